# Optimizing a Trainium2 kernel written in Bass

```python
import math
import jax, jax.numpy as jnp
from jax import lax
import numpy as np

D_MODEL = 2048
BATCH = 4
SEQ = 4096
DEPTH = 1

GRID_W = 64
WIN_H = 8
WIN_W = 16
ATTN_WIDTH = D_MODEL // 2
SSM_WIDTH = D_MODEL - ATTN_WIDTH
MIX_WIDTH = ATTN_WIDTH + SSM_WIDTH
HEAD_DIM = 64
N_HEADS = ATTN_WIDTH // HEAD_DIM
SSM_GROUP_CH = 16
SSM_GROUPS = SSM_WIDTH // SSM_GROUP_CH
SSM_STATE = 64
N_DIRS = 2
D_FF = ((8 * D_MODEL + 3 * 256 - 1) // (3 * 256)) * 256
IN_WIDTH = 3 * ATTN_WIDTH + SSM_WIDTH
RMS_EPS = 1e-6
NEG_INF = -1e30

kernel_name = "hymba_natten_s5_encoder_block"


def rmsnorm(x, g):
    xf = x.astype(jnp.float32)
    y = xf * lax.rsqrt(jnp.mean(xf * xf, axis=-1, keepdims=True) + RMS_EPS)
    return (y * g.astype(jnp.float32)).astype(x.dtype)


def neighbourhood_attention(q, k, v, q_gain, k_gain, rpb):
    b, s, h, dh = q.shape
    rows = s // GRID_W
    kh = min(WIN_H, rows)
    q = rmsnorm(q, q_gain)
    k = rmsnorm(k, k_gain)
    qg = q.reshape(b, rows, GRID_W, h, dh)
    kg = k.reshape(b, rows, GRID_W, h, dh)
    vg = v.reshape(b, rows, GRID_W, h, dh)
    r_idx = jnp.arange(rows)
    row_start = jnp.clip(r_idx - kh // 2, 0, rows - kh)
    key_rows = row_start[:, None] + jnp.arange(kh)[None, :]
    k_blk = kg[:, key_rows]
    v_blk = vg[:, key_rows]
    c_idx = jnp.arange(GRID_W)
    col_start = jnp.clip(c_idx - WIN_W // 2, 0, GRID_W - WIN_W)
    col_in = (c_idx[None, :] >= col_start[:, None]) & (c_idx[None, :] < col_start[:, None] + WIN_W)
    dr_idx = key_rows - r_idx[:, None] + (WIN_H - 1)
    dc_idx = jnp.clip(c_idx[None, :] - c_idx[:, None], -(WIN_W - 1), WIN_W - 1) + (WIN_W - 1)
    bias = rpb[:, dr_idx[:, None, :, None], dc_idx[None, :, None, :]].astype(jnp.float32)
    scale = 1.0 / math.sqrt(dh)
    scores = jnp.einsum('brqhd,brikhd->bhrqik', qg, k_blk).astype(jnp.float32) * scale
    scores = jnp.where(col_in[None, None, None, :, None, :], scores + bias[None], NEG_INF)
    p = jax.nn.softmax(scores.reshape(b, h, rows, GRID_W, kh * GRID_W), axis=-1)
    p = p.reshape(b, h, rows, GRID_W, kh, GRID_W).astype(v.dtype)
    out = jnp.einsum('bhrqik,brikhd->brqhd', p, v_blk)
    return out.reshape(b, s, h * dh)


def _linear_recurrence(e1, e2):
    a1, b1 = e1
    a2, b2 = e2
    return a2 * a1, a2 * b1 + b2


def s5_bidirectional(u, a_re, a_im, b_re, b_im, c_re, c_im, log_step, d_skip, w_glu, b_glu):
    b, s, _ = u.shape
    ug = u.astype(jnp.float32).reshape(b, s, SSM_GROUPS, SSM_GROUP_CH)
    uc = lax.complex(ug, jnp.zeros_like(ug))
    y = jnp.zeros_like(ug)
    for d in range(N_DIRS):
        lam = lax.complex(jnp.minimum(a_re[d].astype(jnp.float32), -1e-4), a_im[d].astype(jnp.float32))
        dt = jnp.exp(log_step[d].astype(jnp.float32))[:, None]
        lam_bar = jnp.exp(lam * dt)
        b_mat = lax.complex(b_re[d].astype(jnp.float32), b_im[d].astype(jnp.float32))
        b_bar = ((lam_bar - 1.0) / lam)[:, :, None] * b_mat
        c_mat = lax.complex(c_re[d].astype(jnp.float32), c_im[d].astype(jnp.float32))
        bu = jnp.einsum('bsgc,gpc->bsgp', uc, b_bar)
        a_seq = jnp.broadcast_to(lam_bar[None, None], (1, s, SSM_GROUPS, SSM_STATE))
        _, states = lax.associative_scan(_linear_recurrence, (a_seq, bu), reverse=(d == 1), axis=1)
        y = y + jnp.real(jnp.einsum('bsgp,gcp->bsgc', states, c_mat))
    y = y.reshape(b, s, SSM_WIDTH) + d_skip.astype(jnp.float32) * ug.reshape(b, s, SSM_WIDTH)
    y = jax.nn.gelu(y.astype(u.dtype))
    return y * jax.nn.sigmoid(y @ w_glu + b_glu)


def setup_inputs(seed: int = 0) -> dict:
    key = jax.random.key(seed)
    ks = jax.random.split(key, 24)
    f32 = jnp.float32
    L, G, P, C = DEPTH, SSM_GROUPS, SSM_STATE, SSM_GROUP_CH
    nrm = lambda k, shp, sc: jax.random.normal(k, shp, f32) * sc
    a_im_base = jnp.pi * jnp.arange(P, dtype=f32)
    return {
        "x": jax.random.normal(ks[0], (BATCH, SEQ, D_MODEL), f32),
        "g_mix": 1.0 + nrm(ks[1], (L, D_MODEL), 0.05),
        "w_in": nrm(ks[2], (L, D_MODEL, IN_WIDTH), D_MODEL ** -0.5),
        "q_gain": 1.0 + nrm(ks[3], (L, HEAD_DIM), 0.05),
        "k_gain": 1.0 + nrm(ks[4], (L, HEAD_DIM), 0.05),
        "rpb": nrm(ks[5], (L, N_HEADS, 2 * WIN_H - 1, 2 * WIN_W - 1), 0.5),
        "ssm_a_re": -0.5 + nrm(ks[6], (L, N_DIRS, G, P), 0.01),
        "ssm_a_im": a_im_base + nrm(ks[7], (L, N_DIRS, G, P), 0.01),
        "ssm_b_re": nrm(ks[8], (L, N_DIRS, G, P, C), (2 * C) ** -0.5),
        "ssm_b_im": nrm(ks[9], (L, N_DIRS, G, P, C), (2 * C) ** -0.5),
        "ssm_c_re": nrm(ks[10], (L, N_DIRS, G, C, P), (2 * P) ** -0.5),
        "ssm_c_im": nrm(ks[11], (L, N_DIRS, G, C, P), (2 * P) ** -0.5),
        "ssm_log_step": jax.random.uniform(ks[12], (L, N_DIRS, G), f32, math.log(1e-3), math.log(1e-1)),
        "ssm_d": nrm(ks[13], (L, SSM_WIDTH), 1.0),
        "w_glu": nrm(ks[14], (L, SSM_WIDTH, SSM_WIDTH), SSM_WIDTH ** -0.5),
        "b_glu": nrm(ks[15], (L, SSM_WIDTH), 0.02),
        "g_out_attn": 1.0 + nrm(ks[16], (L, ATTN_WIDTH), 0.05),
        "g_out_ssm": 1.0 + nrm(ks[17], (L, SSM_WIDTH), 0.05),
        "w_out": nrm(ks[18], (L, MIX_WIDTH, D_MODEL), MIX_WIDTH ** -0.5),
        "g_ffn": 1.0 + nrm(ks[19], (L, D_MODEL), 0.05),
        "w_ffn_gate": nrm(ks[20], (L, D_MODEL, D_FF), D_MODEL ** -0.5),
        "w_ffn_up": nrm(ks[21], (L, D_MODEL, D_FF), D_MODEL ** -0.5),
        "w_ffn_down": nrm(ks[22], (L, D_FF, D_MODEL), D_FF ** -0.5),
    }


def reference(x, g_mix, w_in, q_gain, k_gain, rpb, ssm_a_re, ssm_a_im, ssm_b_re, ssm_b_im,
              ssm_c_re, ssm_c_im, ssm_log_step, ssm_d, w_glu, b_glu, g_out_attn, g_out_ssm,
              w_out, g_ffn, w_ffn_gate, w_ffn_up, w_ffn_down):
    b, s, _ = x.shape
    for l in range(DEPTH):
        h = rmsnorm(x, g_mix[l])
        z = h @ w_in[l]
        q, k, v, u = jnp.split(z, [ATTN_WIDTH, 2 * ATTN_WIDTH, 3 * ATTN_WIDTH], axis=-1)
        q = q.reshape(b, s, N_HEADS, HEAD_DIM)
        k = k.reshape(b, s, N_HEADS, HEAD_DIM)
        v = v.reshape(b, s, N_HEADS, HEAD_DIM)
        ya = neighbourhood_attention(q, k, v, q_gain[l], k_gain[l], rpb[l])
        ys = s5_bidirectional(u, ssm_a_re[l], ssm_a_im[l], ssm_b_re[l], ssm_b_im[l],
                              ssm_c_re[l], ssm_c_im[l], ssm_log_step[l], ssm_d[l],
                              w_glu[l], b_glu[l])
        y = jnp.concatenate([rmsnorm(ya, g_out_attn[l]), rmsnorm(ys, g_out_ssm[l])], axis=-1)
        x = x + y @ w_out[l]
        h = rmsnorm(x, g_ffn[l])
        x = x + (jax.nn.silu(h @ w_ffn_gate[l]) * (h @ w_ffn_up[l])) @ w_ffn_down[l]
    return x
```

```python
import numpy as np
from contextlib import ExitStack
import concourse.bass as bass
import concourse.mybir as mybir
from concourse.bass_utils import run_bass_kernel_spmd

F32 = mybir.dt.float32
BF16 = mybir.dt.bfloat16
I32 = mybir.dt.int32
ALU = mybir.AluOpType
ACT = mybir.ActivationFunctionType

DM = 2048
SEQ = 4096
NH = 16
DFF = 5632
NFT = DFF // 128
EPS = 1e-6
NEG = -30000.0
ENGS = ("pe", "act", "dve", "pool", "sp")
DEBUG = {}


class Buf:
    __slots__ = ("w", "r")

    def __init__(self):
        self.w = None
        self.r = []


class Prog:
    N_DSEM = 32

    def __init__(self, nc, stack):
        self.nc = nc
        self.ops = {e: [] for e in ENGS}
        self.cnt = {e: 0 for e in ENGS}
        self.sems = {e: stack.enter_context(nc.semaphore("s_" + e)) for e in ENGS}
        self.dsems = [stack.enter_context(nc.semaphore("d%d" % i)) for i in range(self.N_DSEM)]
        self.dcnt = [0] * self.N_DSEM
        self.dnext = 0
        self.waited = {e: {} for e in ENGS}

    def _deps(self, eng, reads, writes):
        deps = {}

        def add(tok):
            if tok is not None and deps.get(tok[0], 0) < tok[1]:
                deps[tok[0]] = tok[1]
        for b in reads:
            add(b.w)
        for b in writes:
            add(b.w)
            for t in b.r:
                add(t)
        waits = []
        for k, v in deps.items():
            if k == "pe" and eng == "pe":
                continue
            if self.waited[eng].get(k, 0) >= v:
                continue
            self.waited[eng][k] = v
            waits.append((k, v))
        return waits

    @staticmethod
    def _mark(tok, reads, writes):
        for b in reads:
            b.r.append(tok)
            if len(b.r) > 64:
                b.r = b.r[-48:]
        for b in writes:
            b.w = tok
            b.r = []

    def op(self, eng, fns, reads=(), writes=()):
        if callable(fns):
            fns = [fns]
        waits = self._deps(eng, reads, writes)
        self.cnt[eng] += 1
        tok = (eng, self.cnt[eng])
        self.ops[eng].append((waits, fns, (eng, 1)))
        self._mark(tok, reads, writes)
        return tok

    def dma(self, eng, out, in_, reads=(), writes=()):
        i = self.dnext
        self.dnext = (self.dnext + 1) % self.N_DSEM
        key = "d%d" % i
        waits = self._deps(eng, reads, writes)
        if self.dcnt[i] > 0 and self.waited[eng].get(key, 0) < self.dcnt[i]:
            self.waited[eng][key] = self.dcnt[i]
            waits.append((key, self.dcnt[i]))
        self.dcnt[i] += 16
        tok = (key, self.dcnt[i])
        self.ops[eng].append((waits, [lambda e: e.dma_start(out=out, in_=in_)], (key, 16)))
        self._mark(tok, reads, writes)
        return tok

    def barrier(self):
        toks = [(e, self.cnt[e]) for e in ENGS if self.cnt[e] > 0]
        toks += [("d%d" % i, self.dcnt[i]) for i in range(self.N_DSEM) if self.dcnt[i] > 0]
        for eng in ENGS:
            waits = []
            for k, v in toks:
                if k == eng:
                    continue
                if self.waited[eng].get(k, 0) >= v:
                    continue
                self.waited[eng][k] = v
                waits.append((k, v))
            if waits:
                self.ops[eng].append((waits, [], None))

    def _sem(self, key):
        return self.sems[key] if key in self.sems else self.dsems[int(key[1:])]

    def emit(self):
        prog = self

        def run(engname):
            def body(e):
                for waits, fns, inc in prog.ops[engname]:
                    for k, v in waits:
                        e.wait_ge(prog._sem(k), v)
                    last = None
                    for f in fns:
                        last = f(e)
                    if inc is not None and last is not None:
                        last.then_inc(prog._sem(inc[0]), inc[1])
            return body
        with self.nc.Block() as block:
            block.tensor(run("pe"))
            block.scalar(run("act"))
            block.vector(run("dve"))
            block.gpsimd(run("pool"))
            block.sync(run("sp"))


class Ctx:
    pass


def _prod(s):
    n = 1
    for v in s:
        n *= v
    return n


def aview(K, off, shape, dt):
    esz = 4 if dt in (F32, I32) else 2
    n = _prod(shape)
    assert off % 4 == 0 and off + n * esz <= K.ARENA_BYTES, (off, shape, K.ARENA_BYTES)
    a = K.arena[:, off // 2: off // 2 + n * esz // 2]
    if dt != BF16:
        a = a.bitcast(dt)
    if len(shape) == 2:
        a = a.rearrange("p (a b) -> p a b", a=shape[0], b=shape[1])
    elif len(shape) == 3:
        a = a.rearrange("p (a b c) -> p a b c", a=shape[0], b=shape[1], c=shape[2])
    elif len(shape) == 4:
        a = a.rearrange("p (a b c d) -> p a b c d", a=shape[0], b=shape[1], c=shape[2], d=shape[3])
    return a


KB = 1024


def bc(ap, shape):
    return ap.broadcast_to(list(shape))


def phase_s1(K):
    P = K.P
    XB = [aview(K, 0, [16, 1024], BF16), aview(K, 32 * KB, [16, 1024], BF16)]
    bXB = [Buf(), Buf()]
    WU = aview(K, 64 * KB, [16, 1024], BF16); bWU = Buf()
    U2 = aview(K, 96 * KB, [64, 8, 16], BF16); bU2 = Buf()
    UB = [aview(K, 112 * KB, [64, 128], BF16), aview(K, 128 * KB, [64, 128], BF16)]
    bUB = [Buf(), Buf()]
    SQ = [aview(K, 144 * KB, [1024], BF16), aview(K, 146 * KB, [1024], BF16)]
    bSQ = [Buf(), Buf()]
    RS = aview(K, 148 * KB, [4, 8], F32); bRS = Buf()
    RREP = aview(K, 150 * KB, [1024], F32); bRR = Buf()
    bPS = K.bPS
    for q4 in range(4):
        P.dma("pool", WU[:, q4 * 4:(q4 + 1) * 4, :],
              K.w_in[q4 * 512:(q4 + 1) * 512, 3072:4096].rearrange("(kt p) c -> p kt c", p=128), writes=[bWU])
    for kt in range(16):
        P.op("dve", lambda e, kt=kt: e.tensor_scalar(WU[:, kt, :], WU[:, kt, :], K.gmix[:, kt:kt + 1], None, ALU.mult),
             reads=[bWU, K.bC], writes=[bWU])
    nps = 0
    for b in range(4):
        xb = XB[b % 2]; bx = bXB[b % 2]
        for q4 in range(4):
            P.dma("pool", xb[:, q4 * 4:(q4 + 1) * 4, :],
                  K.xT[q4 * 512:(q4 + 1) * 512, b * 1024:(b + 1) * 1024].rearrange("(kt p) t -> p kt t", p=128),
                  writes=[bx])
        pr0 = K.PSF[4]; pr1 = K.PSF[5]
        for kt in range(16):
            sq = SQ[kt % 2]; bs = bSQ[kt % 2]
            P.op("act", lambda e, kt=kt, sq=sq, xb=xb: e.activation(sq, xb[:, kt, :], ACT.Square), reads=[bx], writes=[bs])
            P.op("pe", [lambda e, kt=kt, sq=sq: e.matmul(pr0[:, :], lhsT=K.onesb[:, :], rhs=sq[:, 0:512], start=(kt == 0), stop=(kt == 15)),
                        lambda e, kt=kt, sq=sq: e.matmul(pr1[:, :], lhsT=K.onesb[:, :], rhs=sq[:, 512:1024], start=(kt == 0), stop=(kt == 15))],
                 reads=[bs, K.bC], writes=[bPS[4], bPS[5]])
        P.op("dve", lambda e: e.tensor_scalar(RREP[:, 0:512], pr0[:, :], 1.0 / DM, EPS, ALU.mult, ALU.add), reads=[bPS[4]], writes=[bRR])
        P.op("dve", lambda e: e.tensor_scalar(RREP[:, 512:1024], pr1[:, :], 1.0 / DM, EPS, ALU.mult, ALU.add), reads=[bPS[5]], writes=[bRR])
        P.op("act", lambda e: e.activation(RREP, RREP, ACT.Sqrt), reads=[bRR], writes=[bRR])
        P.op("dve", lambda e: e.reciprocal(RREP, RREP), reads=[bRR], writes=[bRR])
        fns = []
        for t in range(8):
            fns.append(lambda e, t=t: e.matmul(pr0[:, t:t + 1], lhsT=RREP[:, t::8], rhs=K.inv128[:, 0:1], start=True, stop=True))
        P.op("pe", fns, reads=[bRR, K.bC], writes=[bPS[4]])
        P.op("dve", lambda e, b=b: e.tensor_copy(RS[:, b, :], pr0[:, 0:8]), reads=[bPS[4]], writes=[bRS])
        for t in range(8):
            for ch in range(2):
                ps = K.PSF[nps % 4]; bp = bPS[nps % 4]; nps += 1
                fns = []
                for kt in range(16):
                    fns.append(lambda e, kt=kt, t=t, ch=ch, ps=ps, xb=xb: e.matmul(
                        ps[:, :], lhsT=xb[:, kt, t::8], rhs=WU[:, kt, ch * 512:(ch + 1) * 512], start=(kt == 0), stop=(kt == 15)))
                P.op("pe", fns, reads=[bx, bWU], writes=[bp])
                P.op("dve", lambda e, t=t, ch=ch, ps=ps, b=b: e.tensor_scalar(
                    U2[:, ch * 32:(ch + 1) * 32, t, :], ps[:, :].rearrange("p (g c) -> p g c", c=16), RS[:, b, t:t + 1], None, ALU.mult),
                    reads=[bp, bRS], writes=[bU2])
        ub = UB[b % 2]; bu = bUB[b % 2]
        for g8 in range(8):
            pt = K.PSB[g8 % 2]; bpt = bPS[6 + g8 % 2]
            fns = []
            for gl in range(8):
                g = g8 * 8 + gl
                fns.append(lambda e, g=g, gl=gl, pt=pt: e.transpose(pt[:, gl * 128:(gl + 1) * 128], U2[:, g, :, :], K.identb[:]))
            P.op("pe", fns, reads=[bU2, K.bC], writes=[bpt])
            P.op("act", lambda e, g8=g8, pt=pt, ub=ub: e.activation(
                ub[:, g8 * 8:(g8 + 1) * 8, :].rearrange("p g j -> p (g j)"), pt[:, :], ACT.Identity), reads=[bpt], writes=[bu])
        P.dma("sp", K.Ud[b], ub, reads=[bu], writes=[K.bUd])
        if "U" in DEBUG and b == 2:
            K.dbg("U", ub, [bu])


def ssm_tables(K):
    P = K.P
    T0 = 144 * KB
    o = [T0]

    def al(shape, dt=F32):
        v = aview(K, o[0], shape, dt)
        o[0] += _prod(shape) * 4
        return v
    K.ER = al([2, 32, 24]); K.EI = al([2, 32, 24])
    K.BBR = al([2, 32, 16]); K.BBI = al([2, 32, 16])
    K.MUA = al([2, 2, 32]); K.MUS = al([2, 2, 32])
    AR = al([2, 32]); AI = al([2, 32]); LS = al([2, 32]); DT = al([2, 32])
    ZR = al([2, 32]); ZI = al([2, 32]); E1R = al([2, 32]); E1I = al([2, 32])
    FR = al([2, 32]); FI = al([2, 32]); TA = al([2, 32]); TB = al([2, 32]); DEN = al([2, 32])
    KV = al([2, 24])
    assert o[0] <= 170 * KB, o[0]
    K.bTab = Buf()
    bT = K.bTab
    BR = aview(K, 64 * KB, [2, 32, 16], F32); BI = aview(K, 68 * KB, [2, 32, 16], F32)
    KZ = aview(K, 72 * KB, [2, 32, 24], F32)
    RR = aview(K, 80 * KB, [2, 2, 32, 24], F32)
    RI = aview(K, 92 * KB, [2, 2, 32, 24], I32)
    RF = aview(K, 104 * KB, [2, 2, 32, 24], F32)
    RG = aview(K, 116 * KB, [2, 2, 32, 24], F32)
    TMP = aview(K, 128 * KB, [2, 32, 16], F32)
    bB = Buf(); bW = Buf()
    P.dma("sp", AR, K.p_are, writes=[bT]); P.dma("sp", AI, K.p_aim, writes=[bT])
    P.dma("sp", LS, K.p_ls, writes=[bT]); P.dma("sp", KV, K.p_kv, writes=[bT])
    P.dma("sp", BR, K.p_bre, writes=[bB]); P.dma("sp", BI, K.p_bim, writes=[bB])
    f2 = lambda a: a.rearrange("p a b -> p (a b)")
    f3 = lambda a: a.rearrange("p a b c -> p (a b c)")
    f4 = lambda a: a.rearrange("p a b c d -> p (a b c d)")
    D = lambda fn, r, w: P.op("dve", fn, reads=r, writes=w)
    A = lambda fn, r, w: P.op("act", fn, reads=r, writes=w)
    A(lambda e: e.activation(f2(DT), f2(LS), ACT.Exp), [bT], [bT])
    D(lambda e: e.tensor_scalar(AR, AR, -1e-4, None, ALU.min), [bT], [bT])
    D(lambda e: e.tensor_tensor(ZR, AR, DT, ALU.mult), [bT], [bT])
    D(lambda e: e.tensor_tensor(ZI, AI, DT, ALU.mult), [bT], [bT])
    sh = [128, 2, 32, 24]
    D(lambda e: e.tensor_tensor(KZ, bc(ZR.unsqueeze(3), sh), bc(KV.unsqueeze(2), sh), ALU.mult), [bT], [bW])
    A(lambda e: e.activation(f3(KZ), f3(KZ), ACT.Exp), [bW], [bW])
    D(lambda e: e.tensor_scalar(ZI, ZI, 1.0 / (2 * np.pi), None, ALU.mult), [bT], [bT])
    D(lambda e: e.tensor_tensor(RR[:, 0], bc(ZI.unsqueeze(3), sh), bc(KV.unsqueeze(2), sh), ALU.mult), [bT], [bW])
    D(lambda e: e.tensor_scalar(RR[:, 1], RR[:, 0], 0.25, None, ALU.add), [bW], [bW])
    D(lambda e: e.tensor_copy(f4(RI), f4(RR)), [bW], [bW])
    D(lambda e: e.tensor_copy(f4(RF), f4(RI)), [bW], [bW])
    D(lambda e: e.tensor_tensor(f4(RR), f4(RR), f4(RF), ALU.subtract), [bW], [bW])
    D(lambda e: e.tensor_scalar(f4(RF), f4(RR), 0.5, None, ALU.is_gt), [bW], [bW])
    D(lambda e: e.tensor_scalar(f4(RG), f4(RR), -0.5, None, ALU.is_lt), [bW], [bW])
    D(lambda e: e.tensor_tensor(f4(RR), f4(RR), f4(RF), ALU.subtract), [bW], [bW])
    D(lambda e: e.tensor_tensor(f4(RR), f4(RR), f4(RG), ALU.add), [bW], [bW])
    A(lambda e: e.activation(f4(RR), f4(RR), ACT.Sin, scale=6.283185), [bW], [bW])
    D(lambda e: e.tensor_tensor(K.ER, KZ, RR[:, 1], ALU.mult), [bW], [bT])
    D(lambda e: e.tensor_tensor(K.EI, KZ, RR[:, 0], ALU.mult), [bW], [bT])
    for dd, i1, i8 in ((0, 8, 15), (1, 1, 8)):
        D(lambda e, dd=dd, i1=i1: e.tensor_copy(E1R[:, dd, :], K.ER[:, dd, :, i1]), [bT], [bT])
        D(lambda e, dd=dd, i1=i1: e.tensor_copy(E1I[:, dd, :], K.EI[:, dd, :, i1]), [bT], [bT])
        D(lambda e, dd=dd, i8=i8: e.tensor_copy(K.MUA[:, dd, 0, :], K.ER[:, dd, :, i8]), [bT], [bT])
        D(lambda e, dd=dd, i8=i8: e.tensor_copy(K.MUA[:, dd, 1, :], K.ER[:, dd, :, i8]), [bT], [bT])
        D(lambda e, dd=dd, i8=i8: e.tensor_copy(K.MUS[:, dd, 1, :], K.EI[:, dd, :, i8]), [bT], [bT])
        D(lambda e, dd=dd, i8=i8: e.tensor_scalar(K.MUS[:, dd, 0, :], K.EI[:, dd, :, i8], -1.0, None, ALU.mult), [bT], [bT])
    D(lambda e: e.tensor_scalar(E1R, E1R, -1.0, None, ALU.add), [bT], [bT])
    D(lambda e: e.tensor_tensor(DEN, AR, AR, ALU.mult), [bT], [bT])
    D(lambda e: e.tensor_tensor(TA, AI, AI, ALU.mult), [bT], [bT])
    D(lambda e: e.tensor_tensor(DEN, DEN, TA, ALU.add), [bT], [bT])
    D(lambda e: e.reciprocal(DEN, DEN), [bT], [bT])
    D(lambda e: e.tensor_tensor(TA, E1R, AR, ALU.mult), [bT], [bT])
    D(lambda e: e.tensor_tensor(TB, E1I, AI, ALU.mult), [bT], [bT])
    D(lambda e: e.tensor_tensor(FR, TA, TB, ALU.add), [bT], [bT])
    D(lambda e: e.tensor_tensor(FR, FR, DEN, ALU.mult), [bT], [bT])
    D(lambda e: e.tensor_tensor(TA, E1I, AR, ALU.mult), [bT], [bT])
    D(lambda e: e.tensor_tensor(TB, E1R, AI, ALU.mult), [bT], [bT])
    D(lambda e: e.tensor_tensor(FI, TA, TB, ALU.subtract), [bT], [bT])
    D(lambda e: e.tensor_tensor(FI, FI, DEN, ALU.mult), [bT], [bT])
    s3 = [128, 2, 32, 16]
    D(lambda e: e.tensor_tensor(K.BBR, bc(FR.unsqueeze(3), s3), BR, ALU.mult), [bT, bB], [bT])
    D(lambda e: e.tensor_tensor(TMP, bc(FI.unsqueeze(3), s3), BI, ALU.mult), [bT, bB], [bW])
    D(lambda e: e.tensor_tensor(K.BBR, K.BBR, TMP, ALU.subtract), [bT, bW], [bT])
    D(lambda e: e.tensor_tensor(K.BBI, bc(FR.unsqueeze(3), s3), BI, ALU.mult), [bT, bB], [bT])
    D(lambda e: e.tensor_tensor(TMP, bc(FI.unsqueeze(3), s3), BR, ALU.mult), [bT, bB], [bW])
    D(lambda e: e.tensor_tensor(K.BBI, K.BBI, TMP, ALU.add), [bT, bW], [bT])


def cplx_outer(K, eng, OR, OI, er, ei, xr, xi, t1, bufs_r, bufs_w, neg_im=False):
    P = K.P
    op = lambda fn: P.op(eng, fn, reads=bufs_r, writes=bufs_w)
    op(lambda e: e.tensor_tensor(OR, er, xr, ALU.mult))
    op(lambda e: e.tensor_tensor(t1, ei, xi, ALU.mult))
    op(lambda e: e.tensor_tensor(OR, OR, t1, ALU.subtract))
    op(lambda e: e.tensor_tensor(OI, er, xi, ALU.mult))
    op(lambda e: e.tensor_tensor(t1, ei, xr, ALU.mult))
    if neg_im:
        op(lambda e: e.scalar_tensor_tensor(OI, OI, -1.0, t1, ALU.mult, ALU.subtract))
    else:
        op(lambda e: e.tensor_tensor(OI, OI, t1, ALU.add))


def q_chunk(K, eng, dd, gc, QR, QI, T1, bQ):
    sh = [128, 4, 8, 16]
    gs = slice(gc * 4, gc * 4 + 4)
    er = bc(K.ER[:, dd, gs, 0:8].unsqueeze(3), sh); ei = bc(K.EI[:, dd, gs, 0:8].unsqueeze(3), sh)
    xr = bc(K.BBR[:, dd, gs, :].unsqueeze(2), sh); xi = bc(K.BBI[:, dd, gs, :].unsqueeze(2), sh)
    cplx_outer(K, eng, QR, QI, er, ei, xr, xi, T1, [K.bTab], [bQ])


def phase_gen1(K):
    P = K.P
    K.W1Z = aview(K, 0, [2, 2, 64, 128], BF16); K.bW1 = Buf()
    for q4 in range(4):
        P.op("pool", lambda e, q4=q4: e.memset(K.W1Z[:, q4 // 2, q4 % 2], 0.0), writes=[K.bW1])
    ssm_tables(K)
    QR = aview(K, 178 * KB, [4, 8, 16], F32); QI = aview(K, 180 * KB, [4, 8, 16], F32)
    T1 = aview(K, 182 * KB, [4, 8, 16], F32)
    bQ = Buf()
    n = 0
    for dd in range(2):
        for gc in range(8):
            q_chunk(K, "dve", dd, gc, QR, QI, T1, bQ)
            for ri, Q in ((0, QR), (1, QI)):
                ps = K.PSF[n % 2]; bp = K.bPS[n % 2]; n += 1
                fns = []
                for gl in range(4):
                    fns.append(lambda e, gl=gl, Q=Q, ps=ps: e.transpose(
                        ps[:, gl * 128:(gl + 1) * 128], Q[:, gl].rearrange("p s c -> p (s c)"), K.ident[:]))
                P.op("pe", fns, reads=[bQ, K.bC], writes=[bp])
                g0 = gc * 8
                psv = ps[:, :].rearrange("p (g q) -> p g q", q=128)
                P.op("dve", lambda e, dd=dd, ri=ri, g0=g0, psv=psv: e.tensor_copy(
                    K.W1Z[:, dd, ri, g0:g0 + 8:2, 0:64], psv[:, :, 0:64]), reads=[bp], writes=[K.bW1])
                P.op("dve", lambda e, dd=dd, ri=ri, g0=g0, psv=psv: e.tensor_copy(
                    K.W1Z[:, dd, ri, g0 + 1:g0 + 8:2, 64:128], psv[:, :, 64:128]), reads=[bp], writes=[K.bW1])


def phase_s2(K):
    P = K.P
    S8 = {0: aview(K, 64 * KB, [2, 32, 128], F32), 1: aview(K, 96 * KB, [2, 32, 128], F32)}
    bS8 = {0: Buf(), 1: Buf()}
    UBL = {0: aview(K, 128 * KB, [64, 128], BF16), 1: aview(K, 170 * KB, [32, 128], BF16)}
    bUL = {0: Buf(), 1: Buf()}
    base = 184 * KB
    TT = {}
    for dd in range(2):
        o = base + dd * 2 * KB
        TT[dd] = dict(t1=aview(K, o, [2, 32], F32), t2=aview(K, o + 256, [2, 32], F32),
                      car=aview(K, o + 512, [2, 32], F32), carb=aview(K, o + 768, [2, 32], BF16))
    eng_of = {0: "dve", 1: "pool"}
    bT = {0: Buf(), 1: Buf()}
    for dd in range(2):
        P.op(eng_of[dd], lambda e, dd=dd: e.memset(TT[dd]["car"], 0.0), writes=[bT[dd]])

    def put_carry(dd, blk):
        eng = eng_of[dd]
        P.op(eng, lambda e: e.tensor_copy(TT[dd]["carb"], TT[dd]["car"]), reads=[bT[dd]], writes=[bT[dd]])
        P.dma("sp", K.Hc[dd, blk], TT[dd]["carb"], reads=[bT[dd]], writes=[K.bHd])

    def l1_group(dd, ri, gc, n, ubuf, gmod):
        ps = K.PSF[dd * 2 + n % 2]; bp = K.bPS[dd * 2 + n % 2]
        fns = []
        for gl in range(4):
            gp = gc * 4 + gl
            for g2 in range(2):
                g = 2 * gp + g2
                fns.append(lambda e, gl=gl, g=g, g2=g2, ps=ps: e.matmul(
                    ps[:, gl * 128:(gl + 1) * 128], lhsT=K.W1Z[:, dd, ri, g, :], rhs=ubuf[:, g % gmod, :],
                    start=(g2 == 0), stop=(g2 == 1)))
        P.op("pe", fns, reads=[bUL[dd], K.bW1], writes=[bp])
        P.op("act", lambda e, ps=ps: e.activation(
            S8[dd][:, ri, gc * 4:(gc + 1) * 4, :].rearrange("p g j -> p (g j)"), ps[:, :], ACT.Identity),
            reads=[bp], writes=[bS8[dd]])

    def level1_A(b):
        P.dma("sp", UBL[0], K.Ud[b], reads=[K.bUd], writes=[bUL[0]])
        n = 0
        for ri in range(2):
            for gc in range(8):
                l1_group(0, ri, gc, n, UBL[0], 64); n += 1

    def level1_B(b):
        n = 0
        for half in range(2):
            P.dma("sp", UBL[1], K.Ud[b][:, half * 32:(half + 1) * 32, :], reads=[K.bUd], writes=[bUL[1]])
            for ri in range(2):
                for gcl in range(4):
                    l1_group(1, ri, half * 4 + gcl, n, UBL[1], 32); n += 1

    def scan(dd, order):
        eng = eng_of[dd]
        s8 = S8[dd]; tt = TT[dd]
        mua = K.MUA[:, dd]; mus = K.MUS[:, dd]
        bs = bS8[dd]; bt = bT[dd]
        prevj = None
        for jj in order:
            prev = tt["car"] if prevj is None else s8[:, :, :, prevj]
            cur = s8[:, :, :, jj]
            rd = [bs, bt, K.bTab]
            P.op(eng, lambda e, prev=prev: e.tensor_tensor(tt["t1"], mua, prev, ALU.mult), reads=rd, writes=[bt])
            P.op(eng, lambda e, prev=prev: e.tensor_tensor(tt["t2"][:, 0], mus[:, 0], prev[:, 1], ALU.mult), reads=rd, writes=[bt])
            P.op(eng, lambda e, prev=prev: e.tensor_tensor(tt["t2"][:, 1], mus[:, 1], prev[:, 0], ALU.mult), reads=rd, writes=[bt])
            P.op(eng, lambda e: e.tensor_tensor(tt["t1"], tt["t1"], tt["t2"], ALU.add), reads=[bt], writes=[bt])
            P.op(eng, lambda e, cur=cur: e.tensor_tensor(cur, cur, tt["t1"], ALU.add), reads=[bs, bt], writes=[bs])
            prevj = jj
        P.op(eng, lambda e: e.tensor_copy(tt["car"], s8[:, :, :, prevj]), reads=[bs, bt], writes=[bt])

    def store_H(dd, blk):
        eng = eng_of[dd]
        s8 = S8[dd]
        if dd == 0:
            hb = UBL[0][:, :, :].rearrange("p (r g) j -> p r g j", r=2)
            P.op(eng, lambda e: e.tensor_copy(hb, s8), reads=[bS8[0]], writes=[bUL[0]])
            P.dma("sp", K.Hd[0, blk], hb, reads=[bUL[0]], writes=[K.bHd])
        else:
            for ri in range(2):
                hb = UBL[1]
                P.op(eng, lambda e, ri=ri: e.tensor_copy(hb, s8[:, ri]), reads=[bS8[1]], writes=[bUL[1]])
                P.dma("sp", K.Hd[1, blk][:, ri], hb, reads=[bUL[1]], writes=[K.bHd])

    seqB = [3, 2]
    for step in range(4):
        b = step
        level1_A(b)
        if step < 2:
            level1_B(seqB[step])
        if b >= 2:
            put_carry(0, b - 2)
        scan(0, range(128))
        if b >= 2:
            store_H(0, b - 2)
        if step < 2:
            put_carry(1, seqB[step] - 2)
            scan(1, range(127, -1, -1))
            store_H(1, seqB[step] - 2)


def phase_gen2(K):
    P = K.P
    CR = aview(K, 64 * KB, [2, 32, 16], F32); CI = aview(K, 68 * KB, [2, 32, 16], F32); bCc = Buf()
    P.dma("sp", CR, K.p_cre, writes=[bCc]); P.dma("sp", CI, K.p_cim, writes=[bCc])
    o = [0]

    def al(shape, dt=F32):
        v = aview(K, o[0], shape, dt)
        o[0] += _prod(shape) * (4 if dt == F32 else 2)
        return v
    QR = al([4, 8, 16]); QI = al([4, 8, 16]); T1 = al([4, 8, 16])
    W3R = al([4, 8, 16]); W3I = al([4, 8, 16])
    WPR = {0: al([4, 128]), 1: al([4, 128])}; WPI = {0: al([4, 128]), 1: al([4, 128])}
    QZR = {0: al([8, 128]), 1: al([8, 128])}; QZI = {0: al([8, 128]), 1: al([8, 128])}
    W3S = [al([2, 2, 8, 128], BF16), al([2, 2, 8, 128], BF16)]
    TT1 = al([8, 128]); TT2 = al([8, 128])
    TC = [al([8, 128], BF16), al([8, 128], BF16)]
    assert o[0] <= 64 * KB
    bQ = Buf(); bW = Buf(); bWP = Buf(); bQZ = Buf(); bS = [Buf(), Buf()]; bTT = Buf(); bTC = [Buf(), Buf()]
    sh = [128, 4, 8, 16]
    for gc in range(8):
        gs = slice(gc * 4, gc * 4 + 4)
        w3s = W3S[gc % 2]; bs = bS[gc % 2]
        for dd in range(2):
            q_chunk(K, "dve", dd, gc, QR, QI, T1, bQ)
            for gl in range(4):
                for g2 in range(2):
                    msk = K.m01[:, g2:g2 + 1]
                    P.op("dve", lambda e, gl=gl, g2=g2, msk=msk, dd=dd: e.tensor_scalar(
                        QZR[dd][:, 2 * gl + g2, :], QR[:, gl].rearrange("p s c -> p (s c)"), msk, None, ALU.mult),
                        reads=[bQ, K.bC], writes=[bQZ])
                    P.op("dve", lambda e, gl=gl, g2=g2, msk=msk, dd=dd: e.tensor_scalar(
                        QZI[dd][:, 2 * gl + g2, :], QI[:, gl].rearrange("p s c -> p (s c)"), msk, None, ALU.mult),
                        reads=[bQ, K.bC], writes=[bQZ])
            cr = bc(CR[:, dd, gs, :].unsqueeze(2), sh); ci = bc(CI[:, dd, gs, :].unsqueeze(2), sh)
            er = bc(K.ER[:, dd, gs, 16:24].unsqueeze(3), sh); ei = bc(K.EI[:, dd, gs, 16:24].unsqueeze(3), sh)
            wpr = WPR[dd][:, :, :].rearrange("p g (t c) -> p g t c", c=16)
            wpi = WPI[dd][:, :, :].rearrange("p g (t c) -> p g t c", c=16)
            cplx_outer(K, "dve", wpr, wpi, er, ei, cr, ci, T1, [K.bTab, bCc], [bWP], neg_im=True)
            er = bc(K.ER[:, dd, gs, 8:16].unsqueeze(3), sh); ei = bc(K.EI[:, dd, gs, 8:16].unsqueeze(3), sh)
            cplx_outer(K, "dve", W3R, W3I, er, ei, cr, ci, T1, [K.bTab, bCc], [bW], neg_im=True)
            for ri, W in ((0, W3R), (1, W3I)):
                for g2 in range(2):
                    P.op("dve", lambda e, ri=ri, W=W, g2=g2, dd=dd, w3s=w3s: e.tensor_scalar(
                        w3s[:, dd, ri, g2:8:2, :], W[:, :, :, :].rearrange("p g t c -> p g (t c)"), K.m01[:, g2:g2 + 1], None, ALU.mult),
                        reads=[bW, K.bC], writes=[bs])
        P.dma("sp", K.W3d[:, :, :, gc * 8:(gc + 1) * 8, :], w3s, reads=[bs], writes=[K.bW3d])
        for dd in range(2):
            for hb in range(2):
                ps = K.PSF[dd * 2 + hb]; bp = K.bPS[dd * 2 + hb]
                fns = []
                for gq in range(4):
                    g = hb * 4 + gq
                    gl = g // 2
                    fns.append(lambda e, g=g, gl=gl, gq=gq, ps=ps, dd=dd: e.matmul(
                        ps[:, gq * 128:(gq + 1) * 128], lhsT=QZR[dd][:, g, :], rhs=WPR[dd][:, gl, :], start=True, stop=False))
                    fns.append(lambda e, g=g, gl=gl, gq=gq, ps=ps, dd=dd: e.matmul(
                        ps[:, gq * 128:(gq + 1) * 128], lhsT=QZI[dd][:, g, :], rhs=WPI[dd][:, gl, :], start=False, stop=True))
                P.op("pe", fns, reads=[bQZ, bWP], writes=[bp])
        s4 = [128, 4, 128]
        for hb in range(2):
            pa = K.PSF[hb][:, :].rearrange("p (g q) -> p g q", q=128)
            pb = K.PSF[2 + hb][:, :].rearrange("p (g q) -> p g q", q=128)
            P.op("dve", lambda e, hb=hb, pa=pa: e.tensor_tensor(TT1[:, hb * 4:(hb + 1) * 4, :], pa, bc(K.maskF[:, :].unsqueeze(1), s4), ALU.mult),
                 reads=[K.bPS[hb], K.bC], writes=[bTT])
            P.op("dve", lambda e, hb=hb, pb=pb: e.tensor_tensor(TT2[:, hb * 4:(hb + 1) * 4, :], pb, bc(K.maskB[:, :].unsqueeze(1), s4), ALU.mult),
                 reads=[K.bPS[2 + hb], K.bC], writes=[bTT])
        P.op("dve", lambda e: e.tensor_tensor(TT1, TT1, TT2, ALU.add), reads=[bTT], writes=[bTT])
        tc_ = TC[gc % 2]; btc = bTC[gc % 2]
        for g in range(8):
            gg = gc * 8 + g
            P.op("dve", lambda e, g=g, gg=gg, tc_=tc_: e.scalar_tensor_tensor(
                tc_[:, g, :], K.ident[:], K.dskip[:, gg:gg + 1], TT1[:, g, :], ALU.mult, ALU.add),
                reads=[bTT, K.bC], writes=[btc])
        P.dma("sp", K.Td[:, gc * 8:(gc + 1) * 8, :], tc_, reads=[btc], writes=[K.bTd])
        if "T" in DEBUG and gc == 0:
            K.dbg("T", tc_, [btc])


def phase_s4(K):
    P = K.P
    CH = []
    for i in range(2):
        o = i * 16 * KB
        CH.append(dict(U=aview(K, o, [8, 128], BF16), T=aview(K, o + 2 * KB, [8, 128], BF16),
                       W3=aview(K, o + 4 * KB, [2, 2, 8, 128], BF16),
                       HA=aview(K, o + 12 * KB, [2, 4, 128], BF16), HB=aview(K, o + 14 * KB, [2, 4, 128], BF16),
                       HAc=aview(K, 168 * KB + i * 64, [2, 4], BF16), HBc=aview(K, 168 * KB + 32 + i * 64, [2, 4], BF16), b=Buf()))
    YSG = [aview(K, 32 * KB, [1024], F32), aview(K, 36 * KB, [1024], F32)]; bYSG = [Buf(), Buf()]
    Y2C = aview(K, 40 * KB, [8, 128], F32); G1 = aview(K, 44 * KB, [8, 128], F32)
    G2 = aview(K, 48 * KB, [8, 128], F32); bG = Buf()
    YG = aview(K, 56 * KB, [8, 1024], BF16); bYG = Buf()
    YGT = aview(K, 72 * KB, [8, 1024], BF16); bYGT = Buf()
    WGL = aview(K, 88 * KB, [8, 1024], BF16); bWGL = Buf()
    GT = [aview(K, 104 * KB, [512], F32), aview(K, 106 * KB, [512], F32)]; bGT = [Buf(), Buf()]
    YS = aview(K, 112 * KB, [8, 1024], F32); bYS = Buf()
    YSN = [aview(K, 144 * KB, [1024], BF16), aview(K, 146 * KB, [1024], BF16)]; bYSN = [Buf(), Buf()]
    YST = aview(K, 148 * KB, [8, 1024], BF16); bYST = Buf()
    SS = aview(K, 164 * KB, [8], F32); JNK = aview(K, 165 * KB, [1024], BF16); bSS = Buf()
    for q2 in range(2):
        P.dma("pool", WGL[:, q2 * 4:(q2 + 1) * 4, :], K.w_glu[q2 * 512:(q2 + 1) * 512, :].rearrange("(kt p) c -> p kt c", p=128), writes=[bWGL])
    nch = 0
    for blk in range(2):
        for gc in range(8):
            ch = CH[nch % 2]; nch += 1
            bch = ch["b"]
            P.dma("sp", ch["U"], K.Ud[2 + blk][:, gc * 8:(gc + 1) * 8, :], reads=[K.bUd], writes=[bch])
            P.dma("sp", ch["T"], K.Td[:, gc * 8:(gc + 1) * 8, :], reads=[K.bTd], writes=[bch])
            P.dma("sp", ch["W3"], K.W3d[:, :, :, gc * 8:(gc + 1) * 8, :], reads=[K.bW3d], writes=[bch])
            P.dma("sp", ch["HA"], K.Hd[0, blk][:, :, gc * 4:(gc + 1) * 4, :], reads=[K.bHd], writes=[bch])
            P.dma("sp", ch["HB"], K.Hd[1, blk][:, :, gc * 4:(gc + 1) * 4, :], reads=[K.bHd], writes=[bch])
            P.dma("sp", ch["HAc"], K.Hc[0, blk][:, :, gc * 4:(gc + 1) * 4], reads=[K.bHd], writes=[bch])
            P.dma("sp", ch["HBc"], K.Hc[1, blk][:, :, gc * 4:(gc + 1) * 4], reads=[K.bHd], writes=[bch])
            ysg = YSG[gc % 2]; bys = bYSG[gc % 2]
            for hb in range(2):
                ps = K.PSF[hb]; bp = K.bPS[hb]
                fns = []
                for gq in range(4):
                    g = hb * 4 + gq
                    gpl = g // 2
                    o_ = ps[:, gq * 128:(gq + 1) * 128]
                    fns.append(lambda e, o_=o_, g=g, ch=ch: e.matmul(o_, lhsT=ch["T"][:, g, :], rhs=ch["U"][:, g, :], start=True, stop=False))
                    for ri in range(2):
                        fns.append(lambda e, o_=o_, g=g, gpl=gpl, ch=ch, ri=ri: e.matmul(
                            o_[:, 1:128], lhsT=ch["W3"][:, 0, ri, g, :], rhs=ch["HA"][:, ri, gpl, 0:127], start=False, stop=False))
                        fns.append(lambda e, o_=o_, g=g, gpl=gpl, ch=ch, ri=ri: e.matmul(
                            o_[:, 0:1], lhsT=ch["W3"][:, 0, ri, g, :], rhs=ch["HAc"][:, ri, gpl:gpl + 1], start=False, stop=False))
                        fns.append(lambda e, o_=o_, g=g, gpl=gpl, ch=ch, ri=ri: e.matmul(
                            o_[:, 0:127], lhsT=ch["W3"][:, 1, ri, g, :], rhs=ch["HB"][:, ri, gpl, 1:128], start=False, stop=False))
                        fns.append(lambda e, o_=o_, g=g, gpl=gpl, ch=ch, ri=ri: e.matmul(
                            o_[:, 127:128], lhsT=ch["W3"][:, 1, ri, g, :], rhs=ch["HBc"][:, ri, gpl:gpl + 1], start=False, stop=(ri == 1)))
                P.op("pe", fns, reads=[bch], writes=[bp])
                P.op("act", lambda e, hb=hb, ps=ps, ysg=ysg: e.activation(ysg[:, hb * 512:(hb + 1) * 512], ps[:, :], ACT.Identity),
                     reads=[bp], writes=[bys])
            for hb in range(2):
                ps = K.PSF[2 + hb]; bp = K.bPS[2 + hb]
                fns = []
                for gq in range(4):
                    g = hb * 4 + gq
                    fns.append(lambda e, gq=gq, g=g, ps=ps, ysg=ysg: e.transpose(ps[:, gq * 128:(gq + 1) * 128], ysg[:, g * 128:(g + 1) * 128], K.ident[:]))
                P.op("pe", fns, reads=[bys, K.bC], writes=[bp])
                P.op("dve", lambda e, hb=hb, ps=ps: e.tensor_copy(
                    Y2C[:, :, hb * 64:(hb + 1) * 64].rearrange("p t (g c) -> p g t c", c=16),
                    ps[:, :].rearrange("p (g t c) -> p g t c", t=8, c=16)), reads=[bp], writes=[bG])
            if "ypre" in DEBUG and blk == 0 and gc == 0:
                K.dbg("ypre", Y2C, [bG])
            P.op("dve", lambda e: e.tensor_tensor(G1, Y2C, Y2C, ALU.mult), reads=[bG], writes=[bG])
            P.op("dve", lambda e: e.tensor_scalar(G1, G1, 0.044715, 1.0, ALU.mult, ALU.add), reads=[bG], writes=[bG])
            P.op("dve", lambda e: e.tensor_tensor(G1, G1, Y2C, ALU.mult), reads=[bG], writes=[bG])
            P.op("act", lambda e: e.activation(G2[:, :, :].rearrange("p t c -> p (t c)"), G1[:, :, :].rearrange("p t c -> p (t c)"),
                                               ACT.Sigmoid, scale=1.5957691216057308), reads=[bG], writes=[bG])
            P.op("dve", lambda e, gc=gc: e.tensor_tensor(YS[:, :, gc * 128:(gc + 1) * 128], Y2C, G2, ALU.mult), reads=[bG], writes=[bYS])
            P.op("dve", lambda e, gc=gc: e.tensor_copy(YG[:, :, gc * 128:(gc + 1) * 128], YS[:, :, gc * 128:(gc + 1) * 128]), reads=[bYS], writes=[bYG])
            pt = K.PSB[gc % 2]; bpt = K.bPS[6 + gc % 2]
            fns = []
            for t in range(8):
                fns.append(lambda e, t=t, gc=gc, pt=pt: e.transpose(pt[:, t * 128:(t + 1) * 128], YG[:, t, gc * 128:(gc + 1) * 128], K.identb[:]))
            P.op("pe", fns, reads=[bYG, K.bC], writes=[bpt])
            P.op("dve", lambda e, gc=gc, pt=pt: e.tensor_copy(
                YGT[:, gc, :].rearrange("p (j t) -> p t j", t=8), pt[:, :].rearrange("p (t j) -> p t j", j=128)), reads=[bpt], writes=[bYGT])
        if "yg" in DEBUG and blk == 0:
            K.dbg("yg", YS, [bYS])
        n = 0
        for t in range(8):
            for hf in range(2):
                ps = K.PSF[n % 2]; bp = K.bPS[n % 2]
                gt = GT[n % 2]; bgt = bGT[n % 2]; n += 1
                fns = []
                for kt in range(8):
                    fns.append(lambda e, kt=kt, t=t, hf=hf, ps=ps: e.matmul(
                        ps[:, :], lhsT=YGT[:, kt, t::8], rhs=WGL[:, kt, hf * 512:(hf + 1) * 512], start=(kt == 0), stop=(kt == 7)))
                P.op("pe", fns, reads=[bYGT, bWGL], writes=[bp])
                P.op("dve", lambda e, hf=hf, ps=ps, gt=gt: e.tensor_tensor(gt, ps[:, :], K.bglu[:, hf * 512:(hf + 1) * 512], ALU.add),
                     reads=[bp, K.bC], writes=[bgt])
                P.op("act", lambda e, gt=gt: e.activation(gt, gt, ACT.Sigmoid), reads=[bgt], writes=[bgt])
                P.op("dve", lambda e, t=t, hf=hf, gt=gt: e.tensor_tensor(
                    YS[:, t, hf * 512:(hf + 1) * 512], YS[:, t, hf * 512:(hf + 1) * 512], gt, ALU.mult), reads=[bgt, bYS], writes=[bYS])
        if "ys" in DEBUG and blk == 0:
            K.dbg("ys", YS, [bYS])
        for t in range(8):
            P.op("act", lambda e, t=t: e.activation(JNK, YS[:, t, :], ACT.Square, accum_out=SS[:, t:t + 1]), reads=[bYS], writes=[bSS])
        P.op("dve", lambda e: e.tensor_scalar(SS, SS, 1.0 / 1024, EPS, ALU.mult, ALU.add), reads=[bSS], writes=[bSS])
        P.op("act", lambda e: e.activation(SS, SS, ACT.Sqrt), reads=[bSS], writes=[bSS])
        P.op("dve", lambda e: e.reciprocal(SS, SS), reads=[bSS], writes=[bSS])
        for t in range(8):
            ysn = YSN[t % 2]; bysn = bYSN[t % 2]
            P.op("dve", lambda e, t=t, ysn=ysn: e.tensor_scalar(ysn, YS[:, t, :], SS[:, t:t + 1], None, ALU.mult), reads=[bYS, bSS], writes=[bysn])
            pt = K.PSB[t % 2]; bpt = K.bPS[6 + t % 2]
            fns = []
            for kt in range(8):
                fns.append(lambda e, kt=kt, pt=pt, ysn=ysn: e.transpose(pt[:, kt * 128:(kt + 1) * 128], ysn[:, kt * 128:(kt + 1) * 128], K.identb[:]))
            P.op("pe", fns, reads=[bysn, K.bC], writes=[bpt])
            P.op("dve", lambda e, t=t, pt=pt: e.tensor_copy(YST[:, :, t::8], pt[:, :].rearrange("p (k j) -> p k j", j=128)), reads=[bpt], writes=[bYST])
        P.dma("sp", K.mixT[:, 8:16, blk * 1024:(blk + 1) * 1024], YST, reads=[bYST], writes=[K.bmix])


def phase_a(K):
    P = K.P
    QT = aview(K, 0, [8, 2048], BF16); bQT = Buf()
    KT = aview(K, 32 * KB, [8, 2304], BF16); bKT = Buf()
    VV = aview(K, 68 * KB, [18, 16, 80], BF16); bVV = Buf()
    A0 = 68 * KB + 18 * 16 * 80 * 2
    A0 = (A0 + 63) // 64 * 64
    XC = [aview(K, A0, [16, 512], BF16), aview(K, A0 + 16 * KB, [16, 512], BF16)]; bXC = [Buf(), Buf()]
    WH = aview(K, A0 + 32 * KB, [16, 512], BF16); bWH = Buf()
    o = A0 + 48 * KB
    RX = aview(K, o, [2304], F32); RX2 = aview(K, o + 9 * KB, [2304], F32); bRX = Buf()
    o += 18 * KB
    SQ = [aview(K, o, [512], BF16), aview(K, o + 1 * KB, [512], BF16)]; bSQ = [Buf(), Buf()]
    MS = [aview(K, o + 2 * KB, [512], F32), aview(K, o + 4 * KB, [512], F32)]; bMS = [Buf(), Buf()]
    RK = aview(K, o + 6 * KB, [18], F32); bRK = Buf()
    assert o + 7 * KB <= K.ARENA_BYTES, o
    chunks = [(0, 256), (256, 512), (768, 512), (1280, 512), (1792, 512)]

    def load_x(ci, nb):
        c0, cw = chunks[ci]
        xc = XC[nb % 2]; bx = bXC[nb % 2]
        for q2 in range(2):
            P.dma("pool", xc[:, q2 * 8:(q2 + 1) * 8, 0:cw],
                  K.xT[q2 * 1024:(q2 + 1) * 1024, 1792 + c0:1792 + c0 + cw].rearrange("(kt p) t -> p kt t", p=128), writes=[bx])
        return xc, bx
    nb = 0
    P.op("dve", lambda e: e.memset(VV[:, :, :, 64:65], 1.0), writes=[bVV])
    for ci, (c0, cw) in enumerate(chunks):
        xc, bx = load_x(ci, nb); nb += 1
        ps = K.PSF[0]; bp = K.bPS[0]
        for kt in range(16):
            sq = SQ[kt % 2]; bs = bSQ[kt % 2]
            P.op("act", lambda e, kt=kt, sq=sq, xc=xc, cw=cw: e.activation(sq[:, 0:cw], xc[:, kt, 0:cw], ACT.Square), reads=[bx], writes=[bs])
            P.op("pe", lambda e, kt=kt, sq=sq, cw=cw: e.matmul(ps[:, 0:cw], lhsT=K.onesb[:, :], rhs=sq[:, 0:cw], start=(kt == 0), stop=(kt == 15)),
                 reads=[bs, K.bC], writes=[bp])
        P.op("dve", lambda e, c0=c0, cw=cw: e.tensor_scalar(RX[:, c0:c0 + cw], ps[:, 0:cw], 1.0 / DM, EPS, ALU.mult, ALU.add), reads=[bp], writes=[bRX])
    P.op("act", lambda e: e.activation(RX, RX, ACT.Sqrt), reads=[bRX], writes=[bRX])
    P.op("dve", lambda e: e.reciprocal(RX, RX), reads=[bRX], writes=[bRX])
    P.op("dve", lambda e: e.tensor_tensor(RX2, RX, RX, ALU.mult), reads=[bRX], writes=[bRX])
    pk = K.PSF[1]; bpk = K.bPS[1]
    fns = []
    for tile in range(18):
        fns.append(lambda e, tile=tile: e.matmul(pk[:, tile:tile + 1], lhsT=RX[:, tile * 128:(tile + 1) * 128], rhs=K.inv128[:, 0:1], start=True, stop=True))
    P.op("pe", fns, reads=[bRX, K.bC], writes=[bpk])
    P.op("dve", lambda e: e.tensor_copy(RK, pk[:, 0:18]), reads=[bpk], writes=[bRK])

    def load_w(col0):
        for q2 in range(2):
            P.dma("pool", WH[:, q2 * 8:(q2 + 1) * 8, :], K.w_in[q2 * 1024:(q2 + 1) * 1024, col0:col0 + 512].rearrange("(kt p) c -> p kt c", p=128), writes=[bWH])
        for kt in range(16):
            P.op("dve", lambda e, kt=kt: e.tensor_scalar(WH[:, kt, :], WH[:, kt, :], K.gmix[:, kt:kt + 1], None, ALU.mult), reads=[bWH, K.bC], writes=[bWH])
    npp = 0
    for sel in range(2):
        DST = QT if sel == 0 else KT
        bD = bQT if sel == 0 else bKT
        gain = K.qkg[:, sel:sel + 1]
        for hf in range(2):
            load_w(sel * 1024 + hf * 512)
            for ci, (c0, cw) in enumerate(chunks):
                if sel == 0 and ci == 0:
                    continue
                xc, bx = load_x(ci, nb); nb += 1
                d0 = c0 - 256 if sel == 0 else c0
                for hl in range(4):
                    hp = hf * 4 + hl
                    ps = K.PSF[npp % 2]; bp = K.bPS[npp % 2]
                    p2 = K.PSF[2 + npp % 2]; bp2 = K.bPS[2 + npp % 2]
                    sq = SQ[npp % 2]; bs = bSQ[npp % 2]
                    ms = MS[npp % 2]; bm = bMS[npp % 2]; npp += 1
                    fns = []
                    for kt in range(16):
                        fns.append(lambda e, kt=kt, hl=hl, ps=ps, xc=xc, cw=cw: e.matmul(
                            ps[:, 0:cw], lhsT=WH[:, kt, hl * 128:(hl + 1) * 128], rhs=xc[:, kt, 0:cw], start=(kt == 0), stop=(kt == 15)))
                    P.op("pe", fns, reads=[bWH, bx], writes=[bp])
                    P.op("act", lambda e, ps=ps, sq=sq, cw=cw: e.activation(sq[:, 0:cw], ps[:, 0:cw], ACT.Square), reads=[bp], writes=[bs])
                    P.op("pe", lambda e, p2=p2, sq=sq, cw=cw: e.matmul(p2[:, 0:cw], lhsT=K.blk1[:, :], rhs=sq[:, 0:cw], start=True, stop=True),
                         reads=[bs, K.bC], writes=[bp2])
                    P.op("dve", lambda e, p2=p2, ms=ms, c0=c0, cw=cw: e.tensor_tensor(ms[:, 0:cw], p2[:, 0:cw], RX2[:, c0:c0 + cw], ALU.mult),
                         reads=[bp2, bRX], writes=[bm])
                    P.op("dve", lambda e, ms=ms, cw=cw: e.tensor_scalar(ms[:, 0:cw], ms[:, 0:cw], 1.0 / 64, EPS, ALU.mult, ALU.add), reads=[bm], writes=[bm])
                    P.op("act", lambda e, ms=ms, cw=cw: e.activation(ms[:, 0:cw], ms[:, 0:cw], ACT.Sqrt), reads=[bm], writes=[bm])
                    P.op("dve", lambda e, ms=ms, cw=cw: e.reciprocal(ms[:, 0:cw], ms[:, 0:cw]), reads=[bm], writes=[bm])
                    P.op("dve", lambda e, ms=ms, c0=c0, cw=cw: e.tensor_tensor(ms[:, 0:cw], ms[:, 0:cw], RX[:, c0:c0 + cw], ALU.mult), reads=[bm, bRX], writes=[bm])
                    P.op("dve", lambda e, ps=ps, ms=ms, hp=hp, d0=d0, cw=cw, DST=DST, gain=gain: e.scalar_tensor_tensor(
                        DST[:, hp, d0:d0 + cw], ps[:, 0:cw], gain, ms[:, 0:cw], ALU.mult, ALU.mult), reads=[bp, bm, K.bC], writes=[bD])
    for hf in range(2):
        load_w(2048 + hf * 512)
        for ci, (c0, cw) in enumerate(chunks):
            xc, bx = load_x(ci, nb); nb += 1
            for tl in range(cw // 128):
                tile = c0 // 128 + tl
                ps = K.PSF[npp % 2]; bp = K.bPS[npp % 2]; npp += 1
                fns = []
                for kt in range(16):
                    fns.append(lambda e, kt=kt, tl=tl, ps=ps, xc=xc: e.matmul(
                        ps[:, :], lhsT=xc[:, kt, tl * 128:(tl + 1) * 128], rhs=WH[:, kt, :], start=(kt == 0), stop=(kt == 15)))
                P.op("pe", fns, reads=[bWH, bx], writes=[bp])
                P.op("dve", lambda e, tile=tile, hf=hf, ps=ps: e.tensor_scalar(
                    VV[:, tile, hf * 8:(hf + 1) * 8, 0:64], ps[:, :].rearrange("p (h d) -> p h d", d=64), RK[:, tile:tile + 1], None, ALU.mult),
                    reads=[bp, bRK], writes=[bVV])
    if "qT" in DEBUG:
        K.dbg("qT", QT, [bQT]); K.dbg("kT", KT, [bKT]); K.dbg("vv", VV, [bVV])
    P.barrier()
    B0 = A0
    YA = aview(K, B0, [16, 1024], BF16); bYA = Buf()
    o = B0 + 32 * KB
    BIAS = [aview(K, o, [15, 128], F32), aview(K, o + 7680, [15, 128], F32)]; bBI = [Buf(), Buf()]
    o += 15360
    QZ = [aview(K, o, [2048], BF16), aview(K, o + 4 * KB, [2048], BF16)]; bQZ = [Buf(), Buf()]
    o += 8 * KB
    SB = [aview(K, o, [5, 128], F32), aview(K, o + 2560, [5, 128], F32)]; bSB = [Buf(), Buf()]
    o += 5120
    PT = [aview(K, o, [5, 128], BF16), aview(K, o + 1280, [5, 128], BF16)]; bPT = [Buf(), Buf()]
    o += 2560
    RD = [aview(K, o, [1], F32), aview(K, o + 64, [1], F32)]; bRD = [Buf(), Buf()]
    o += 128
    YN = [aview(K, o, [1024], BF16), aview(K, o + 2 * KB, [1024], BF16)]; bYN = [Buf(), Buf()]
    o += 4 * KB
    YT = [aview(K, o, [8, 128], BF16), aview(K, o + 2 * KB, [8, 128], BF16)]; bYT = [Buf(), Buf()]
    o += 4 * KB
    SSA = aview(K, o, [16], F32); JNK = aview(K, o + 64, [1024], BF16); bSSA = Buf()
    o += 64 + 2 * KB
    assert o <= K.ARENA_BYTES, o
    it = 0
    for head in range(NH):
        hp, hh = head // 2, head % 2
        bi = BIAS[head % 2]; bbi = bBI[head % 2]
        P.dma("sp", bi, K.biasd[head], writes=[bbi])
        qz = QZ[head % 2]; bqz = bQZ[head % 2]
        P.op("pool", lambda e, hp=hp, hh=hh, qz=qz: e.tensor_scalar(qz, QT[:, hp, :], K.m01[:, hh:hh + 1], None, ALU.mult), reads=[bQT, K.bC], writes=[bqz])
        for n in range(16, 32):
            qi = n - 16
            ks = min(max(n - 2, 0), 27)
            v0 = 0 if n <= 29 else (5 if n == 30 else 10)
            psA = K.PSF[(it % 2) * 2]; bpA = K.bPS[(it % 2) * 2]
            psB = K.PSF[(it % 2) * 2 + 1]; bpB = K.bPS[(it % 2) * 2 + 1]
            pso = K.PSF[4 + it % 2]; bpo = K.bPS[4 + it % 2]
            sb = SB[it % 2]; bsb = bSB[it % 2]
            pt = PT[it % 2]; bpt = bPT[it % 2]
            rd = RD[it % 2]; brd = bRD[it % 2]
            it += 1
            fns = []
            for i in range(5):
                kti = ks + i - 14
                dst = psA[:, i * 128:(i + 1) * 128] if i < 4 else psB[:, 0:128]
                fns.append(lambda e, dst=dst, kti=kti, qi=qi, hp=hp, qz=qz: e.matmul(
                    dst, lhsT=KT[:, hp, kti * 128:(kti + 1) * 128], rhs=qz[:, qi * 128:(qi + 1) * 128], start=True, stop=True))
            P.op("pe", fns, reads=[bKT, bqz], writes=[bpA, bpB])
            P.op("dve", lambda e, psA=psA, sb=sb, bi=bi, v0=v0: e.tensor_tensor(
                sb[:, 0:4, :], psA[:, :].rearrange("p (i q) -> p i q", q=128), bi[:, v0:v0 + 4, :], ALU.add), reads=[bpA, bbi], writes=[bsb])
            P.op("dve", lambda e, psB=psB, sb=sb, bi=bi, v0=v0: e.tensor_tensor(sb[:, 4, :], psB[:, 0:128], bi[:, v0 + 4, :], ALU.add),
                 reads=[bpB, bbi], writes=[bsb])
            P.op("act", lambda e, sb=sb, pt=pt: e.activation(pt[:, :, :].rearrange("p i q -> p (i q)"), sb[:, :, :].rearrange("p i q -> p (i q)"), ACT.Exp),
                 reads=[bsb], writes=[bpt])
            fns = []
            for i in range(5):
                kti = ks + i - 14
                fns.append(lambda e, i=i, kti=kti, pso=pso, pt=pt, head=head: e.matmul(
                    pso[:, 0:65], lhsT=pt[:, i, :], rhs=VV[:, kti, head, 0:65], start=(i == 0), stop=(i == 4)))
            P.op("pe", fns, reads=[bpt, bVV], writes=[bpo])
            P.op("dve", lambda e, pso=pso, rd=rd: e.reciprocal(rd, pso[:, 64:65]), reads=[bpo], writes=[brd])
            P.op("dve", lambda e, pso=pso, rd=rd, qi=qi, head=head: e.tensor_scalar(
                YA[:, qi, head * 64:(head + 1) * 64], pso[:, 0:64], rd[:, 0:1], None, ALU.mult), reads=[bpo, brd], writes=[bYA])
    if "ya" in DEBUG:
        K.dbg("ya", YA, [bYA])
    for qi in range(16):
        P.op("act", lambda e, qi=qi: e.activation(JNK, YA[:, qi, :], ACT.Square, accum_out=SSA[:, qi:qi + 1]), reads=[bYA], writes=[bSSA])
    P.op("dve", lambda e: e.tensor_scalar(SSA, SSA, 1.0 / 1024, EPS, ALU.mult, ALU.add), reads=[bSSA], writes=[bSSA])
    P.op("act", lambda e: e.activation(SSA, SSA, ACT.Sqrt), reads=[bSSA], writes=[bSSA])
    P.op("dve", lambda e: e.reciprocal(SSA, SSA), reads=[bSSA], writes=[bSSA])
    for qi in range(16):
        yn = YN[qi % 2]; byn = bYN[qi % 2]
        yt = YT[qi % 2]; byt = bYT[qi % 2]
        P.op("dve", lambda e, qi=qi, yn=yn: e.tensor_scalar(yn, YA[:, qi, :], SSA[:, qi:qi + 1], None, ALU.mult), reads=[bYA, bSSA], writes=[byn])
        pt = K.PSB[qi % 2]; bpt = K.bPS[6 + qi % 2]
        fns = []
        for kt in range(8):
            fns.append(lambda e, kt=kt, pt=pt, yn=yn: e.transpose(pt[:, kt * 128:(kt + 1) * 128], yn[:, kt * 128:(kt + 1) * 128], K.identb[:]))
        P.op("pe", fns, reads=[byn, K.bC], writes=[bpt])
        P.op("act", lambda e, pt=pt, yt=yt: e.activation(yt[:, :, :].rearrange("p k j -> p (k j)"), pt[:, :], ACT.Identity), reads=[bpt], writes=[byt])
        P.dma("sp", K.mixT[:, 0:8, qi * 128:(qi + 1) * 128], yt, reads=[byt], writes=[K.bmix])


def phase_o(K):
    P = K.P
    FG = [6, 6, 6, 6, 5, 5, 5, 5]
    for tb in range(2):
        P.barrier()
        X1 = aview(K, 0, [8, 2048], F32); bX1 = [Buf() for _ in range(8)]
        WO = aview(K, 64 * KB, [16, 2048], BF16); bWO = Buf()
        MIXC = aview(K, 128 * KB, [16, 512], BF16); bMX = Buf()
        XO = [aview(K, 144 * KB, [2048], F32), aview(K, 152 * KB, [2048], F32)]; bXO = [Buf(), Buf()]
        for q4 in range(4):
            P.dma("pool", WO[:, q4 * 4:(q4 + 1) * 4, :], K.w_out[q4 * 512:(q4 + 1) * 512, :].rearrange("(kt p) c -> p kt c", p=128), writes=[bWO])
        for kt in range(16):
            P.op("dve", lambda e, kt=kt: e.tensor_scalar(WO[:, kt, :], WO[:, kt, :], K.gout[:, kt:kt + 1], None, ALU.mult), reads=[bWO, K.bC], writes=[bWO])
        n = 0
        for s in range(2):
            P.dma("sp", MIXC, K.mixT[:, :, tb * 1024 + s * 512: tb * 1024 + (s + 1) * 512], reads=[K.bmix], writes=[bMX])
            for tt in range(4):
                tile = s * 4 + tt
                xo = XO[tile % 2]; bxo = bXO[tile % 2]
                r0 = (tb * 8 + tile) * 128
                P.dma("sp", xo, K.xown[r0:r0 + 128, :], writes=[bxo])
                for dc in range(4):
                    ps = K.PSF[n % 4]; bp = K.bPS[n % 4]; n += 1
                    fns = []
                    for kt in range(16):
                        fns.append(lambda e, kt=kt, tt=tt, dc=dc, ps=ps: e.matmul(
                            ps[:, :], lhsT=MIXC[:, kt, tt * 128:(tt + 1) * 128], rhs=WO[:, kt, dc * 512:(dc + 1) * 512], start=(kt == 0), stop=(kt == 15)))
                    P.op("pe", fns, reads=[bMX, bWO], writes=[bp])
                    P.op("dve", lambda e, tile=tile, dc=dc, ps=ps, xo=xo: e.tensor_tensor(
                        X1[:, tile, dc * 512:(dc + 1) * 512], ps[:, :], xo[:, dc * 512:(dc + 1) * 512], ALU.add), reads=[bp, bxo], writes=[bX1[tile]])
        if "x1" in DEBUG and tb == 0:
            K.dbg("x1", X1, bX1)
        P.barrier()
        H2T = aview(K, 64 * KB, [16, 1024], BF16); bH2T = Buf()
        ACTT = aview(K, 96 * KB, [6, 1024], BF16); bAT = Buf()
        WD = aview(K, 108 * KB, [6, 2048], BF16); bWD = Buf()
        WG = [aview(K, 132 * KB + i * 4 * KB, [16, 128], BF16) for i in range(3)]; bWG = [Buf() for _ in range(3)]
        WUp = [aview(K, 144 * KB + i * 4 * KB, [16, 128], BF16) for i in range(3)]; bWUp = [Buf() for _ in range(3)]
        SG = [aview(K, 156 * KB, [512], BF16), aview(K, 157 * KB, [512], BF16)]; bSG = [Buf(), Buf()]
        H2 = [aview(K, 158 * KB, [2048], BF16), aview(K, 162 * KB, [2048], BF16)]; bH2 = [Buf(), Buf()]
        S2 = aview(K, 166 * KB, [8], F32); JNK = aview(K, 167 * KB, [2048], BF16); bS2 = Buf()
        for tile in range(8):
            P.op("act", lambda e, tile=tile: e.activation(JNK, X1[:, tile, :], ACT.Square, accum_out=S2[:, tile:tile + 1]), reads=[bX1[tile]], writes=[bS2])
        P.op("dve", lambda e: e.tensor_scalar(S2, S2, 1.0 / DM, EPS, ALU.mult, ALU.add), reads=[bS2], writes=[bS2])
        P.op("act", lambda e: e.activation(S2, S2, ACT.Sqrt), reads=[bS2], writes=[bS2])
        P.op("dve", lambda e: e.reciprocal(S2, S2), reads=[bS2], writes=[bS2])
        for tile in range(8):
            h2 = H2[tile % 2]; bh2 = bH2[tile % 2]
            P.op("dve", lambda e, tile=tile, h2=h2: e.scalar_tensor_tensor(h2, X1[:, tile, :], S2[:, tile:tile + 1], K.gffn[:, :], ALU.mult, ALU.mult),
                 reads=[bX1[tile], bS2, K.bC], writes=[bh2])
            for hb in range(2):
                pt = K.PSB[hb]; bpt = K.bPS[6 + hb]
                fns = []
                for k8 in range(8):
                    kt = hb * 8 + k8
                    fns.append(lambda e, k8=k8, kt=kt, pt=pt, h2=h2: e.transpose(pt[:, k8 * 128:(k8 + 1) * 128], h2[:, kt * 128:(kt + 1) * 128], K.identb[:]))
                P.op("pe", fns, reads=[bh2, K.bC], writes=[bpt])
                P.op("dve", lambda e, hb=hb, tile=tile, pt=pt: e.tensor_copy(
                    H2T[:, hb * 8:(hb + 1) * 8, tile * 128:(tile + 1) * 128], pt[:, :].rearrange("p (k j) -> p k j", j=128)), reads=[bpt], writes=[bH2T])
        f0 = 0
        nw = 0
        npg = 0
        for grp, nf in enumerate(FG):
            for q in range(nf):
                r0 = (f0 + q) * 128
                P.dma("pool", WD[:, q, :], K.w_down[r0:r0 + 128, :], writes=[bWD])
            for q in range(nf):
                f = f0 + q
                wg = WG[nw % 3]; bwg = bWG[nw % 3]; wu = WUp[nw % 3]; bwu = bWUp[nw % 3]; nw += 1
                P.dma("pool", wg, K.w_gate[:, f * 128:(f + 1) * 128].rearrange("(kt p) c -> p kt c", p=128), writes=[bwg])
                P.dma("pool", wu, K.w_up[:, f * 128:(f + 1) * 128].rearrange("(kt p) c -> p kt c", p=128), writes=[bwu])
                for s in range(2):
                    pg = K.PSF[(npg % 2) * 2]; bpg = K.bPS[(npg % 2) * 2]
                    pu = K.PSF[(npg % 2) * 2 + 1]; bpu = K.bPS[(npg % 2) * 2 + 1]
                    sg = SG[npg % 2]; bsg = bSG[npg % 2]; npg += 1
                    fg = []
                    fu = []
                    for kt in range(16):
                        fg.append(lambda e, kt=kt, s=s, pg=pg, wg=wg: e.matmul(pg[:, :], lhsT=wg[:, kt, :], rhs=H2T[:, kt, s * 512:(s + 1) * 512], start=(kt == 0), stop=(kt == 15)))
                        fu.append(lambda e, kt=kt, s=s, pu=pu, wu=wu: e.matmul(pu[:, :], lhsT=wu[:, kt, :], rhs=H2T[:, kt, s * 512:(s + 1) * 512], start=(kt == 0), stop=(kt == 15)))
                    P.op("pe", fg, reads=[bwg, bH2T], writes=[bpg])
                    P.op("pe", fu, reads=[bwu, bH2T], writes=[bpu])
                    P.op("act", lambda e, pg=pg, sg=sg: e.activation(sg, pg[:, :], ACT.Silu), reads=[bpg], writes=[bsg])
                    P.op("dve", lambda e, pu=pu, sg=sg, q=q, s=s: e.tensor_tensor(ACTT[:, q, s * 512:(s + 1) * 512], pu[:, :], sg, ALU.mult),
                         reads=[bpu, bsg], writes=[bAT])
            nd = 0
            for tile in range(8):
                for dc in range(4):
                    ps = K.PSF[4 + nd % 2]; bp = K.bPS[4 + nd % 2]; nd += 1
                    fns = []
                    for q in range(nf):
                        fns.append(lambda e, q=q, tile=tile, dc=dc, ps=ps, nf=nf: e.matmul(
                            ps[:, :], lhsT=ACTT[:, q, tile * 128:(tile + 1) * 128], rhs=WD[:, q, dc * 512:(dc + 1) * 512], start=(q == 0), stop=(q == nf - 1)))
                    P.op("pe", fns, reads=[bAT, bWD], writes=[bp])
                    P.op("dve", lambda e, tile=tile, dc=dc, ps=ps: e.tensor_tensor(
                        X1[:, tile, dc * 512:(dc + 1) * 512], X1[:, tile, dc * 512:(dc + 1) * 512], ps[:, :], ALU.add), reads=[bp, bX1[tile]], writes=[bX1[tile]])
            f0 += nf
        for tile in range(8):
            r0 = (tb * 8 + tile) * 128
            P.dma("sp", K.out[r0:r0 + 128, :], X1[:, tile, :], reads=[bX1[tile]], writes=[K.bOut])


def build_program(phases="all"):
    nc = bass.Bass("TRN2", target_bir_lowering=False)
    K = Ctx()
    K.nc = nc
    dram = lambda name, shape, dt=F32, kind="ExternalInput": nc.dram_tensor(name, list(shape), dt, kind=kind).ap()
    K.xT = dram("xT", [DM, SEQ]); K.xown = dram("xown", [2048, DM])
    K.w_in = dram("w_in", [DM, 4096]); K.w_out = dram("w_out", [DM, DM]); K.w_glu = dram("w_glu", [1024, 1024])
    K.w_gate = dram("w_gate", [DM, DFF]); K.w_up = dram("w_up", [DM, DFF]); K.w_down = dram("w_down", [DFF, DM])
    K.p_are = dram("p_are", [128, 2, 32]); K.p_aim = dram("p_aim", [128, 2, 32]); K.p_ls = dram("p_ls", [128, 2, 32])
    K.p_bre = dram("p_bre", [128, 2, 32, 16]); K.p_bim = dram("p_bim", [128, 2, 32, 16])
    K.p_cre = dram("p_cre", [128, 2, 32, 16]); K.p_cim = dram("p_cim", [128, 2, 32, 16])
    K.p_kv = dram("p_kv", [128, 2, 24])
    K.biasd = dram("biasd", [NH, 128, 15, 128])
    cst = dram("cst", [128, CST_COLS])
    K.out = dram("out", [2048, DM], kind="ExternalOutput")
    K.Ud = nc.dram_tensor("Ud", [4, 128, 64, 128], BF16).ap()
    K.Hd = nc.dram_tensor("Hd", [2, 2, 128, 2, 32, 128], BF16).ap()
    K.Hc = nc.dram_tensor("Hc", [2, 2, 128, 2, 32], BF16).ap()
    K.W3d = nc.dram_tensor("W3d", [128, 2, 2, 64, 128], BF16).ap()
    K.Td = nc.dram_tensor("Td", [128, 64, 128], BF16).ap()
    K.mixT = nc.dram_tensor("mixT", [128, 16, 2048], BF16).ap()
    K.bUd = Buf(); K.bHd = Buf(); K.bW3d = Buf(); K.bTd = Buf(); K.bmix = Buf(); K.bOut = Buf()
    K.dbg_out = {}
    for name, shape in DEBUG.items():
        K.dbg_out[name] = dram("dbg_" + name, [128, _prod(shape)], F32, kind="ExternalOutput")
    with ExitStack() as st:
        P = Prog(nc, st)
        K.P = P
        K.ARENA_BYTES = 188 * KB
        K.arena = st.enter_context(nc.sbuf_tensor("arena", [128, K.ARENA_BYTES // 2], BF16))
        CS = st.enter_context(nc.sbuf_tensor("cs", [128, CST_COLS], F32))
        cb16 = st.enter_context(nc.sbuf_tensor("cb16", [128, 3 * 128], BF16))
        K.dbgt = st.enter_context(nc.sbuf_tensor("dbgt", [128, 64], F32))
        K.inv128 = st.enter_context(nc.sbuf_tensor("inv128", [128, 2], F32))
        K.PSF = [st.enter_context(nc.psum_tensor("psf%d" % i, [128, 512], F32)) for i in range(6)]
        K.PSB = [st.enter_context(nc.psum_tensor("psb%d" % i, [128, 1024], BF16)) for i in range(2)]
        K.bPS = [Buf() for _ in range(8)]
        K.bC = Buf()
        P.dma("sp", CS[:], cst, writes=[K.bC])
        c = CST_OFF
        K.ident = CS[:, c["ident"]:c["ident"] + 128]
        K.maskF = CS[:, c["maskF"]:c["maskF"] + 128]
        K.maskB = CS[:, c["maskB"]:c["maskB"] + 128]
        K.gmix = CS[:, c["gmix"]:c["gmix"] + 16]
        K.gout = CS[:, c["gout"]:c["gout"] + 16]
        K.qkg = CS[:, c["qkg"]:c["qkg"] + 2]
        K.m01 = CS[:, c["m01"]:c["m01"] + 2]
        K.dskip = CS[:, c["dskip"]:c["dskip"] + 64]
        K.bglu = CS[:, c["bglu"]:c["bglu"] + 1024]
        K.gffn = CS[:, c["gffn"]:c["gffn"] + 2048]
        K.identb = cb16[:, 0:128]; K.onesb = cb16[:, 128:256]; K.blk1 = cb16[:, 256:384]
        P.op("dve", lambda e: e.tensor_copy(K.identb, K.ident), reads=[K.bC], writes=[K.bC])
        P.op("dve", lambda e: e.memset(K.onesb, 1.0), writes=[K.bC])
        P.op("dve", lambda e: e.memset(K.inv128[:, :], 1.0 / 128), writes=[K.bC])
        P.op("dve", lambda e: e.tensor_copy(K.blk1, CS[:, c["blk1"]:c["blk1"] + 128]), reads=[K.bC], writes=[K.bC])
        P.op("dve", lambda e: e.tensor_scalar(K.qkg[:, 0:1], K.qkg[:, 0:1], 0.125, None, ALU.mult), reads=[K.bC], writes=[K.bC])
        ndbg = [0]

        def dbg(name, ap, bufs):
            shape = DEBUG[name]
            n = _prod(shape)
            dst = K.dbg_out[name]
            flat = ap
            nd = len(shape)
            if nd == 2:
                flat = ap.rearrange("p a b -> p (a b)")
            elif nd == 3:
                flat = ap.rearrange("p a b c -> p (a b c)")
            elif nd == 4:
                flat = ap.rearrange("p a b c d -> p (a b c d)")
            bd = Buf()
            for c0 in range(0, n, 64):
                w = min(64, n - c0)
                P.op("pool", lambda e, c0=c0, w=w: e.tensor_copy(K.dbgt[:, 0:w], flat[:, c0:c0 + w]), reads=list(bufs) + [bd], writes=[bd])
                P.dma("sp", dst[:, c0:c0 + w], K.dbgt[:, 0:w], reads=[bd], writes=[bd, K.bOut])
        K.dbg = dbg

        if phases in ("all", "ssm", "s1"):
            phase_s1(K)
            P.barrier()
        if phases in ("all", "ssm"):
            phase_gen1(K)
            P.barrier()
            phase_s2(K)
            P.barrier()
            phase_gen2(K)
            P.barrier()
            phase_s4(K)
            P.barrier()
        if phases in ("all", "attn"):
            phase_a(K)
            P.barrier()
        if phases in ("all", "out"):
            phase_o(K)
        waits = P._deps("sp", [K.bOut], ())
        P.ops["sp"].append((waits, [], None))
        P.emit()
    return nc


CST_OFF = {}
_c = 0
for _n, _w in (("ident", 128), ("maskF", 128), ("maskB", 128), ("blk1", 128), ("gmix", 16), ("gout", 16), ("qkg", 2),
               ("m01", 2), ("dskip", 64), ("bglu", 1024), ("gffn", 2048)):
    CST_OFF[_n] = _c
    _c += _w
CST_COLS = _c


def _lay_gp(a):
    s = a.shape
    a = a.reshape(2, 32, 2, 64, *s[3:])
    a = np.moveaxis(a, [2, 3], [0, 1])
    return np.ascontiguousarray(a.reshape(128, 2, 32, *s[3:]), dtype=np.float32)


def _bias_table(rpb0, flip):
    out = np.full((NH, 15, 128, 128), NEG, np.float32)
    for v in range(15):
        n, i = (20, v) if v < 5 else ((30, v - 5) if v < 10 else (31, v - 10))
        ks = min(max(n - 2, 0), 27)
        kp = ks + i
        kr = np.repeat(np.array([2 * kp, 2 * kp + 1]), 64); kc = np.tile(np.arange(64), 2)
        qr = np.repeat(np.array([2 * n, 2 * n + 1]), 64); qc = np.tile(np.arange(64), 2)
        if flip:
            kr, kc, qr, qc = 63 - kr, 63 - kc, 63 - qr, 63 - qc
        rs = np.clip(qr - 4, 0, 56); cs = np.clip(qc - 8, 0, 48)
        inwin = ((kr[:, None] >= rs[None, :]) & (kr[:, None] < rs[None, :] + 8) &
                 (kc[:, None] >= cs[None, :]) & (kc[:, None] < cs[None, :] + 16))
        dr = np.clip(kr[:, None] - qr[None, :] + 7, 0, 14)
        dc = np.clip(kc[:, None] - qc[None, :], -15, 15) + 15
        vals = rpb0[:, dr, dc]
        out[:, v] = np.where(inwin[None], vals, np.float32(NEG))
    return np.ascontiguousarray(out.transpose(0, 2, 1, 3))


def _consts(inp):
    cs = np.zeros((128, CST_COLS), np.float32)
    o = CST_OFF
    cs[:, o["ident"]:o["ident"] + 128] = np.eye(128, dtype=np.float32)
    s = np.arange(128) // 16
    cs[:, o["maskF"]:o["maskF"] + 128] = (s[:, None] <= s[None, :])
    cs[:, o["maskB"]:o["maskB"] + 128] = (s[:, None] >= s[None, :])
    hh = np.arange(128) // 64
    cs[:, o["blk1"]:o["blk1"] + 128] = (hh[:, None] == hh[None, :])
    cs[:, o["gmix"]:o["gmix"] + 16] = inp["g_mix"][0].reshape(16, 128).T
    gout = np.concatenate([inp["g_out_attn"][0], inp["g_out_ssm"][0]])
    cs[:, o["gout"]:o["gout"] + 16] = gout.reshape(16, 128).T
    cs[:, o["qkg"]] = np.tile(inp["q_gain"][0], 2)
    cs[:, o["qkg"] + 1] = np.tile(inp["k_gain"][0], 2)
    cs[:, o["m01"]] = (hh == 0)
    cs[:, o["m01"] + 1] = (hh == 1)
    cs[:, o["dskip"]:o["dskip"] + 64] = np.tile(inp["ssm_d"][0].reshape(64, 16).T, (8, 1))
    cs[:, o["bglu"]:o["bglu"] + 1024] = inp["b_glu"][0][None, :]
    cs[:, o["gffn"]:o["gffn"] + 2048] = inp["g_ffn"][0][None, :]
    return cs


def _kvals():
    kvA = np.concatenate([np.arange(7, -1, -1), np.arange(1, 9), np.arange(-7, 1)])
    kvB = np.concatenate([np.arange(0, 8), np.arange(8, 0, -1), -np.arange(0, 8)])
    kv = np.stack([kvA, kvB]).astype(np.float32)
    return np.ascontiguousarray(np.broadcast_to(kv[None], (128, 2, 24)))


def prepare_inputs(inp):
    inp = {k: np.asarray(v) for k, v in inp.items()}
    x = inp["x"]
    shared = dict(
        w_in=np.ascontiguousarray(inp["w_in"][0]), w_out=np.ascontiguousarray(inp["w_out"][0]),
        w_glu=np.ascontiguousarray(inp["w_glu"][0]), w_gate=np.ascontiguousarray(inp["w_ffn_gate"][0]),
        w_up=np.ascontiguousarray(inp["w_ffn_up"][0]), w_down=np.ascontiguousarray(inp["w_ffn_down"][0]),
        cst=_consts(inp), p_kv=_kvals())
    bias_tabs = {f: _bias_table(inp["rpb"][0], f) for f in (False, True)}
    ssm = {}
    for h in (0, 1):
        dirs = [0, 1] if h == 1 else [1, 0]
        ls = np.broadcast_to(inp["ssm_log_step"][0][dirs][:, :, None], (2, 64, 64))
        ssm[h] = dict(
            p_are=_lay_gp(inp["ssm_a_re"][0][dirs]), p_aim=_lay_gp(inp["ssm_a_im"][0][dirs]), p_ls=_lay_gp(ls),
            p_bre=_lay_gp(inp["ssm_b_re"][0][dirs]), p_bim=_lay_gp(inp["ssm_b_im"][0][dirs]),
            p_cre=_lay_gp(inp["ssm_c_re"][0][dirs].transpose(0, 1, 3, 2)), p_cim=_lay_gp(inp["ssm_c_im"][0][dirs].transpose(0, 1, 3, 2)))
    maps = []
    for c in range(8):
        b, h = c // 2, c % 2
        flip = (h == 0)
        xl = x[b][::-1] if flip else x[b]
        m = dict(shared)
        m.update(ssm[h])
        m["xT"] = np.ascontiguousarray(xl.T)
        m["xown"] = np.ascontiguousarray(xl[2048:])
        m["biasd"] = bias_tabs[flip]
        maps.append(m)
    return maps


def assemble(results):
    out = np.empty((4, SEQ, DM), np.float32)
    for c in range(8):
        b, h = c // 2, c % 2
        o = np.asarray(results[c]["out"])
        if h == 0:
            out[b, 0:2048] = o[::-1]
        else:
            out[b, 2048:] = o
    return out


def kernel(**inputs):
    maps = prepare_inputs(inputs)
    nc = build_program("all")
    res = run_bass_kernel_spmd(nc, maps, core_ids=list(range(8)))
    return assemble(res.results)
```

```python
import numpy as np
from contextlib import ExitStack
import concourse.bass as bass
import concourse.mybir as mybir
from concourse.bass_utils import run_bass_kernel_spmd

F32 = mybir.dt.float32
BF16 = mybir.dt.bfloat16
I32 = mybir.dt.int32
ALU = mybir.AluOpType
ACT = mybir.ActivationFunctionType

DM = 2048
SEQ = 4096
NH = 16
DFF = 5632
NFT = DFF // 128
EPS = 1e-6
NEG = -30000.0
ENGS = ("pe", "act", "dve", "pool", "sp")
SAME_ENGINE_SYNC = True
DEBUG = {}


class Buf:
    __slots__ = ("w", "r")

    def __init__(self):
        self.w = None
        self.r = {}


class MBuf:
    __slots__ = ("ws",)

    def __init__(self):
        self.ws = []


class Prog:
    N_DSEM = 32

    def __init__(self, nc, stack):
        self.nc = nc
        self.ops = {e: [] for e in ENGS}
        self.cnt = {e: 0 for e in ENGS}
        self.sems = {e: stack.enter_context(nc.semaphore("s_" + e)) for e in ENGS}
        self.dsems = [stack.enter_context(nc.semaphore("d%d" % i)) for i in range(self.N_DSEM)]
        self.dcnt = [0] * self.N_DSEM
        self.dnext = [0, 0]
        self.waited = {e: {} for e in ENGS}

    def _deps(self, eng, reads, writes, ses=None):
        if ses is None:
            ses = SAME_ENGINE_SYNC
        deps = {}

        def add(tok):
            if tok is not None and deps.get(tok[0], 0) < tok[1]:
                deps[tok[0]] = tok[1]
        for b in reads:
            if isinstance(b, MBuf):
                for t in b.ws:
                    add(t)
            else:
                add(b.w)
        for b in writes:
            if isinstance(b, MBuf):
                continue
            add(b.w)
            for t in b.r.items():
                add(t)
        waits = []
        for k, v in deps.items():
            if k == eng and (eng == "pe" or not ses):
                continue
            if self.waited[eng].get(k, 0) >= v:
                continue
            self.waited[eng][k] = v
            waits.append((k, v))
        return waits

    @staticmethod
    def _mark(tok, reads, writes):
        for b in reads:
            if isinstance(b, MBuf):
                continue
            if b.r.get(tok[0], 0) < tok[1]:
                b.r[tok[0]] = tok[1]
        for b in writes:
            if isinstance(b, MBuf):
                b.ws.append(tok)
                continue
            b.w = tok
            b.r = {}

    def op(self, eng, fns, reads=(), writes=(), ses=None):
        if callable(fns):
            fns = [fns]
        waits = self._deps(eng, reads, writes, ses)
        self.cnt[eng] += 1
        tok = (eng, self.cnt[eng])
        self.ops[eng].append((waits, fns, (eng, 1)))
        self._mark(tok, reads, writes)
        return tok

    def dma(self, eng, out, in_, reads=(), writes=()):
        half = self.N_DSEM // 2
        which = 0 if eng == "pool" else 1
        i = which * half + self.dnext[which]
        self.dnext[which] = (self.dnext[which] + 1) % half
        key = "d%d" % i
        waits = self._deps(eng, reads, writes)
        if self.dcnt[i] > 0 and self.waited[eng].get(key, 0) < self.dcnt[i]:
            self.waited[eng][key] = self.dcnt[i]
            waits.append((key, self.dcnt[i]))
        self.dcnt[i] += 16
        tok = (key, self.dcnt[i])
        self.ops[eng].append((waits, [lambda e: e.dma_start(out=out, in_=in_)], (key, 16)))
        self._mark(tok, reads, writes)
        return tok

    def barrier(self):
        toks = [(e, self.cnt[e]) for e in ENGS if self.cnt[e] > 0]
        toks += [("d%d" % i, self.dcnt[i]) for i in range(self.N_DSEM) if self.dcnt[i] > 0]
        for eng in ENGS:
            waits = []
            for k, v in toks:
                if k == eng:
                    continue
                if self.waited[eng].get(k, 0) >= v:
                    continue
                self.waited[eng][k] = v
                waits.append((k, v))
            if waits:
                self.ops[eng].append((waits, [], None))

    def _sem(self, key):
        return self.sems[key] if key in self.sems else self.dsems[int(key[1:])]

    def emit(self):
        prog = self

        def run(engname):
            def body(e):
                for waits, fns, inc in prog.ops[engname]:
                    for k, v in waits:
                        e.wait_ge(prog._sem(k), v)
                    last = None
                    for f in fns:
                        last = f(e)
                    if inc is not None and last is not None:
                        last.then_inc(prog._sem(inc[0]), inc[1])
            return body
        with self.nc.Block() as block:
            block.tensor(run("pe"))
            block.scalar(run("act"))
            block.vector(run("dve"))
            block.gpsimd(run("pool"))
            block.sync(run("sp"))


class Ctx:
    pass


def _prod(s):
    n = 1
    for v in s:
        n *= v
    return n


def aview(K, off, shape, dt):
    esz = 4 if dt in (F32, I32) else 2
    n = _prod(shape)
    assert off % 4 == 0 and off + n * esz <= K.ARENA_BYTES, (off, shape, K.ARENA_BYTES)
    a = K.arena[:, off // 2: off // 2 + n * esz // 2]
    if dt != BF16:
        a = a.bitcast(dt)
    if len(shape) == 2:
        a = a.rearrange("p (a b) -> p a b", a=shape[0], b=shape[1])
    elif len(shape) == 3:
        a = a.rearrange("p (a b c) -> p a b c", a=shape[0], b=shape[1], c=shape[2])
    elif len(shape) == 4:
        a = a.rearrange("p (a b c d) -> p a b c d", a=shape[0], b=shape[1], c=shape[2], d=shape[3])
    return a


KB = 1024


def bc(ap, shape):
    return ap.broadcast_to(list(shape))


def phase_s1(K):
    P = K.P
    XB = [aview(K, 0, [16, 1024], BF16), aview(K, 32 * KB, [16, 1024], BF16)]
    bXB = [Buf(), Buf()]
    WU = aview(K, 64 * KB, [16, 1024], BF16); bWU = Buf()
    U2 = aview(K, 96 * KB, [64, 8, 16], BF16); bU2 = Buf()
    UB = [aview(K, 112 * KB, [64, 128], BF16), aview(K, 128 * KB, [64, 128], BF16)]
    bUB = [Buf(), Buf()]
    SQ = [aview(K, 144 * KB, [1024], BF16), aview(K, 146 * KB, [1024], BF16)]
    bSQ = [Buf(), Buf()]
    RS = aview(K, 148 * KB, [4, 8], F32); bRS = Buf()
    RREP = aview(K, 150 * KB, [1024], F32); bRR = Buf()
    bPS = K.bPS
    for q4 in range(4):
        P.dma("pool", WU[:, q4 * 4:(q4 + 1) * 4, :],
              K.w_in[q4 * 512:(q4 + 1) * 512, 3072:4096].rearrange("(kt p) c -> p kt c", p=128), writes=[bWU])
    for kt in range(16):
        P.op("dve", lambda e, kt=kt: e.tensor_scalar(WU[:, kt, :], WU[:, kt, :], K.gmix[:, kt:kt + 1], None, ALU.mult),
             reads=[bWU, K.bC], writes=[bWU])
    nps = 0
    for b in range(4):
        xb = XB[b % 2]; bx = bXB[b % 2]
        for q4 in range(4):
            P.dma("pool", xb[:, q4 * 4:(q4 + 1) * 4, :],
                  K.xT[q4 * 512:(q4 + 1) * 512, b * 1024:(b + 1) * 1024].rearrange("(kt p) t -> p kt t", p=128),
                  writes=[bx])
        pr0 = K.PSF[4]; pr1 = K.PSF[5]
        for kt in range(16):
            sq = SQ[kt % 2]; bs = bSQ[kt % 2]
            P.op("act", lambda e, kt=kt, sq=sq, xb=xb: e.activation(sq, xb[:, kt, :], ACT.Square), reads=[bx], writes=[bs])
            P.op("pe", [lambda e, kt=kt, sq=sq: e.matmul(pr0[:, :], lhsT=K.onesb[:, :], rhs=sq[:, 0:512], start=(kt == 0), stop=(kt == 15)),
                        lambda e, kt=kt, sq=sq: e.matmul(pr1[:, :], lhsT=K.onesb[:, :], rhs=sq[:, 512:1024], start=(kt == 0), stop=(kt == 15))],
                 reads=[bs, K.bC], writes=[bPS[4], bPS[5]])
        P.op("dve", lambda e: e.tensor_scalar(RREP[:, 0:512], pr0[:, :], 1.0 / DM, EPS, ALU.mult, ALU.add), reads=[bPS[4]], writes=[bRR])
        P.op("dve", lambda e: e.tensor_scalar(RREP[:, 512:1024], pr1[:, :], 1.0 / DM, EPS, ALU.mult, ALU.add), reads=[bPS[5]], writes=[bRR])
        P.op("act", lambda e: e.activation(RREP, RREP, ACT.Sqrt), reads=[bRR], writes=[bRR])
        P.op("dve", lambda e: e.reciprocal(RREP, RREP), reads=[bRR], writes=[bRR])
        fns = []
        for t in range(8):
            fns.append(lambda e, t=t: e.matmul(pr0[:, t:t + 1], lhsT=RREP[:, t::8], rhs=K.inv128[:, 0:1], start=True, stop=True))
        P.op("pe", fns, reads=[bRR, K.bC], writes=[bPS[4]])
        P.op("dve", lambda e, b=b: e.tensor_copy(RS[:, b, :], pr0[:, 0:8]), reads=[bPS[4]], writes=[bRS])
        for t in range(8):
            for ch in range(2):
                ps = K.PSF[nps % 4]; bp = bPS[nps % 4]; nps += 1
                fns = []
                for kt in range(16):
                    fns.append(lambda e, kt=kt, t=t, ch=ch, ps=ps, xb=xb: e.matmul(
                        ps[:, :], lhsT=xb[:, kt, t::8], rhs=WU[:, kt, ch * 512:(ch + 1) * 512], start=(kt == 0), stop=(kt == 15)))
                P.op("pe", fns, reads=[bx, bWU], writes=[bp])
                P.op("dve", lambda e, t=t, ch=ch, ps=ps, b=b: e.tensor_scalar(
                    U2[:, ch * 32:(ch + 1) * 32, t, :], ps[:, :].rearrange("p (g c) -> p g c", c=16), RS[:, b, t:t + 1], None, ALU.mult),
                    reads=[bp, bRS], writes=[bU2])
        ub = UB[b % 2]; bu = bUB[b % 2]
        for g8 in range(8):
            pt = K.PSB[g8 % 2]; bpt = bPS[6 + g8 % 2]
            fns = []
            for gl in range(8):
                g = g8 * 8 + gl
                fns.append(lambda e, g=g, gl=gl, pt=pt: e.transpose(pt[:, gl * 128:(gl + 1) * 128], U2[:, g, :, :], K.identb[:]))
            P.op("pe", fns, reads=[bU2, K.bC], writes=[bpt])
            P.op("act", lambda e, g8=g8, pt=pt, ub=ub: e.activation(
                ub[:, g8 * 8:(g8 + 1) * 8, :].rearrange("p g j -> p (g j)"), pt[:, :], ACT.Identity), reads=[bpt], writes=[bu])
        P.dma("sp", K.Ud[b], ub, reads=[bu], writes=[K.bUd])
        if "U" in DEBUG and b == 2:
            K.dbg("U", ub, [bu])


def ssm_tables(K):
    P = K.P
    T0 = 144 * KB
    o = [T0]

    def al(shape, dt=F32):
        v = aview(K, o[0], shape, dt)
        o[0] += _prod(shape) * 4
        return v
    K.ER = al([2, 32, 24]); K.EI = al([2, 32, 24])
    K.BBR = al([2, 32, 16]); K.BBI = al([2, 32, 16])
    K.MUA = al([2, 2, 32]); K.MUS = al([2, 2, 32])
    AR = al([2, 32]); AI = al([2, 32]); LS = al([2, 32]); DT = al([2, 32])
    ZR = al([2, 32]); ZI = al([2, 32]); E1R = al([2, 32]); E1I = al([2, 32])
    FR = al([2, 32]); FI = al([2, 32]); TA = al([2, 32]); TB = al([2, 32]); DEN = al([2, 32])
    KV = al([2, 24])
    assert o[0] <= 170 * KB, o[0]
    K.bTab = Buf()
    bT = K.bTab
    BR = aview(K, 64 * KB, [2, 32, 16], F32); BI = aview(K, 68 * KB, [2, 32, 16], F32)
    KZ = aview(K, 72 * KB, [2, 32, 24], F32)
    RR = aview(K, 80 * KB, [2, 2, 32, 24], F32)
    RI = aview(K, 92 * KB, [2, 2, 32, 24], I32)
    RF = aview(K, 104 * KB, [2, 2, 32, 24], F32)
    RG = aview(K, 116 * KB, [2, 2, 32, 24], F32)
    TMP = aview(K, 128 * KB, [2, 32, 16], F32)
    bB = Buf(); bW = Buf()
    P.dma("sp", AR, K.p_are, writes=[bT]); P.dma("sp", AI, K.p_aim, writes=[bT])
    P.dma("sp", LS, K.p_ls, writes=[bT]); P.dma("sp", KV, K.p_kv, writes=[bT])
    P.dma("sp", BR, K.p_bre, writes=[bB]); P.dma("sp", BI, K.p_bim, writes=[bB])
    f2 = lambda a: a.rearrange("p a b -> p (a b)")
    f3 = lambda a: a.rearrange("p a b c -> p (a b c)")
    f4 = lambda a: a.rearrange("p a b c d -> p (a b c d)")
    D = lambda fn, r, w: P.op("dve", fn, reads=r, writes=w)
    A = lambda fn, r, w: P.op("act", fn, reads=r, writes=w)
    A(lambda e: e.activation(f2(DT), f2(LS), ACT.Exp), [bT], [bT])
    D(lambda e: e.tensor_scalar(AR, AR, -1e-4, None, ALU.min), [bT], [bT])
    D(lambda e: e.tensor_tensor(ZR, AR, DT, ALU.mult), [bT], [bT])
    D(lambda e: e.tensor_tensor(ZI, AI, DT, ALU.mult), [bT], [bT])
    sh = [128, 2, 32, 24]
    D(lambda e: e.tensor_tensor(KZ, bc(ZR.unsqueeze(3), sh), bc(KV.unsqueeze(2), sh), ALU.mult), [bT], [bW])
    A(lambda e: e.activation(f3(KZ), f3(KZ), ACT.Exp), [bW], [bW])
    D(lambda e: e.tensor_scalar(ZI, ZI, 1.0 / (2 * np.pi), None, ALU.mult), [bT], [bT])
    D(lambda e: e.tensor_tensor(RR[:, 0], bc(ZI.unsqueeze(3), sh), bc(KV.unsqueeze(2), sh), ALU.mult), [bT], [bW])
    D(lambda e: e.tensor_scalar(RR[:, 1], RR[:, 0], 0.25, None, ALU.add), [bW], [bW])
    D(lambda e: e.tensor_copy(f4(RI), f4(RR)), [bW], [bW])
    D(lambda e: e.tensor_copy(f4(RF), f4(RI)), [bW], [bW])
    D(lambda e: e.tensor_tensor(f4(RR), f4(RR), f4(RF), ALU.subtract), [bW], [bW])
    D(lambda e: e.tensor_scalar(f4(RF), f4(RR), 0.5, None, ALU.is_gt), [bW], [bW])
    D(lambda e: e.tensor_scalar(f4(RG), f4(RR), -0.5, None, ALU.is_lt), [bW], [bW])
    D(lambda e: e.tensor_tensor(f4(RR), f4(RR), f4(RF), ALU.subtract), [bW], [bW])
    D(lambda e: e.tensor_tensor(f4(RR), f4(RR), f4(RG), ALU.add), [bW], [bW])
    A(lambda e: e.activation(f4(RR), f4(RR), ACT.Sin, scale=6.283185), [bW], [bW])
    D(lambda e: e.tensor_tensor(K.ER, KZ, RR[:, 1], ALU.mult), [bW], [bT])
    D(lambda e: e.tensor_tensor(K.EI, KZ, RR[:, 0], ALU.mult), [bW], [bT])
    for dd, i1, i8 in ((0, 8, 15), (1, 1, 8)):
        D(lambda e, dd=dd, i1=i1: e.tensor_copy(E1R[:, dd, :], K.ER[:, dd, :, i1]), [bT], [bT])
        D(lambda e, dd=dd, i1=i1: e.tensor_copy(E1I[:, dd, :], K.EI[:, dd, :, i1]), [bT], [bT])
        D(lambda e, dd=dd, i8=i8: e.tensor_copy(K.MUA[:, dd, 0, :], K.ER[:, dd, :, i8]), [bT], [bT])
        D(lambda e, dd=dd, i8=i8: e.tensor_copy(K.MUA[:, dd, 1, :], K.ER[:, dd, :, i8]), [bT], [bT])
        D(lambda e, dd=dd, i8=i8: e.tensor_copy(K.MUS[:, dd, 1, :], K.EI[:, dd, :, i8]), [bT], [bT])
        D(lambda e, dd=dd, i8=i8: e.tensor_scalar(K.MUS[:, dd, 0, :], K.EI[:, dd, :, i8], -1.0, None, ALU.mult), [bT], [bT])
    D(lambda e: e.tensor_scalar(E1R, E1R, -1.0, None, ALU.add), [bT], [bT])
    D(lambda e: e.tensor_tensor(DEN, AR, AR, ALU.mult), [bT], [bT])
    D(lambda e: e.tensor_tensor(TA, AI, AI, ALU.mult), [bT], [bT])
    D(lambda e: e.tensor_tensor(DEN, DEN, TA, ALU.add), [bT], [bT])
    D(lambda e: e.reciprocal(DEN, DEN), [bT], [bT])
    D(lambda e: e.tensor_tensor(TA, E1R, AR, ALU.mult), [bT], [bT])
    D(lambda e: e.tensor_tensor(TB, E1I, AI, ALU.mult), [bT], [bT])
    D(lambda e: e.tensor_tensor(FR, TA, TB, ALU.add), [bT], [bT])
    D(lambda e: e.tensor_tensor(FR, FR, DEN, ALU.mult), [bT], [bT])
    D(lambda e: e.tensor_tensor(TA, E1I, AR, ALU.mult), [bT], [bT])
    D(lambda e: e.tensor_tensor(TB, E1R, AI, ALU.mult), [bT], [bT])
    D(lambda e: e.tensor_tensor(FI, TA, TB, ALU.subtract), [bT], [bT])
    D(lambda e: e.tensor_tensor(FI, FI, DEN, ALU.mult), [bT], [bT])
    s3 = [128, 2, 32, 16]
    D(lambda e: e.tensor_tensor(K.BBR, bc(FR.unsqueeze(3), s3), BR, ALU.mult), [bT, bB], [bT])
    D(lambda e: e.tensor_tensor(TMP, bc(FI.unsqueeze(3), s3), BI, ALU.mult), [bT, bB], [bW])
    D(lambda e: e.tensor_tensor(K.BBR, K.BBR, TMP, ALU.subtract), [bT, bW], [bT])
    D(lambda e: e.tensor_tensor(K.BBI, bc(FR.unsqueeze(3), s3), BI, ALU.mult), [bT, bB], [bT])
    D(lambda e: e.tensor_tensor(TMP, bc(FI.unsqueeze(3), s3), BR, ALU.mult), [bT, bB], [bW])
    D(lambda e: e.tensor_tensor(K.BBI, K.BBI, TMP, ALU.add), [bT, bW], [bT])


def cplx_outer(K, eng, OR, OI, er, ei, xr, xi, t1, bufs_r, bufs_w, neg_im=False):
    P = K.P
    op = lambda fn: P.op(eng, fn, reads=bufs_r, writes=bufs_w)
    op(lambda e: e.tensor_tensor(OR, er, xr, ALU.mult))
    op(lambda e: e.tensor_tensor(t1, ei, xi, ALU.mult))
    op(lambda e: e.tensor_tensor(OR, OR, t1, ALU.subtract))
    op(lambda e: e.tensor_tensor(OI, er, xi, ALU.mult))
    op(lambda e: e.tensor_tensor(t1, ei, xr, ALU.mult))
    if neg_im:
        op(lambda e: e.scalar_tensor_tensor(OI, OI, -1.0, t1, ALU.mult, ALU.subtract))
    else:
        op(lambda e: e.tensor_tensor(OI, OI, t1, ALU.add))


def q_chunk(K, eng, dd, gc, QR, QI, T1, bQ):
    sh = [128, 4, 8, 16]
    gs = slice(gc * 4, gc * 4 + 4)
    er = bc(K.ER[:, dd, gs, 0:8].unsqueeze(3), sh); ei = bc(K.EI[:, dd, gs, 0:8].unsqueeze(3), sh)
    xr = bc(K.BBR[:, dd, gs, :].unsqueeze(2), sh); xi = bc(K.BBI[:, dd, gs, :].unsqueeze(2), sh)
    cplx_outer(K, eng, QR, QI, er, ei, xr, xi, T1, [K.bTab], [bQ])


def phase_gen1(K):
    P = K.P
    K.W1Z = aview(K, 0, [2, 2, 64, 128], BF16); K.bW1 = Buf()
    for q4 in range(4):
        P.op("pool", lambda e, q4=q4: e.memset(K.W1Z[:, q4 // 2, q4 % 2], 0.0), writes=[K.bW1])
    ssm_tables(K)
    QR = aview(K, 178 * KB, [4, 8, 16], F32); QI = aview(K, 180 * KB, [4, 8, 16], F32)
    T1 = aview(K, 182 * KB, [4, 8, 16], F32)
    bQ = Buf()
    n = 0
    for dd in range(2):
        for gc in range(8):
            q_chunk(K, "dve", dd, gc, QR, QI, T1, bQ)
            for ri, Q in ((0, QR), (1, QI)):
                ps = K.PSF[n % 2]; bp = K.bPS[n % 2]; n += 1
                fns = []
                for gl in range(4):
                    fns.append(lambda e, gl=gl, Q=Q, ps=ps: e.transpose(
                        ps[:, gl * 128:(gl + 1) * 128], Q[:, gl].rearrange("p s c -> p (s c)"), K.ident[:]))
                P.op("pe", fns, reads=[bQ, K.bC], writes=[bp])
                g0 = gc * 8
                psv = ps[:, :].rearrange("p (g q) -> p g q", q=128)
                P.op("dve", lambda e, dd=dd, ri=ri, g0=g0, psv=psv: e.tensor_copy(
                    K.W1Z[:, dd, ri, g0:g0 + 8:2, 0:64], psv[:, :, 0:64]), reads=[bp], writes=[K.bW1])
                P.op("dve", lambda e, dd=dd, ri=ri, g0=g0, psv=psv: e.tensor_copy(
                    K.W1Z[:, dd, ri, g0 + 1:g0 + 8:2, 64:128], psv[:, :, 64:128]), reads=[bp], writes=[K.bW1])


def phase_s2(K):
    P = K.P
    S8 = {0: aview(K, 64 * KB, [2, 32, 128], F32), 1: aview(K, 96 * KB, [2, 32, 128], F32)}
    bS8 = {0: Buf(), 1: Buf()}
    UBL = {0: aview(K, 128 * KB, [64, 128], BF16), 1: aview(K, 170 * KB, [32, 128], BF16)}
    bUL = {0: Buf(), 1: Buf()}
    base = 184 * KB
    TT = {}
    for dd in range(2):
        o = base + dd * 2 * KB
        TT[dd] = dict(t1=aview(K, o, [2, 32], F32), t2=aview(K, o + 256, [2, 32], F32),
                      car=aview(K, o + 512, [2, 32], F32), carb=aview(K, o + 768, [2, 32], BF16))
    eng_of = {0: "dve", 1: "pool"}
    bT = {0: Buf(), 1: Buf()}
    for dd in range(2):
        P.op(eng_of[dd], lambda e, dd=dd: e.memset(TT[dd]["car"], 0.0), writes=[bT[dd]])

    def put_carry(dd, blk):
        eng = eng_of[dd]
        P.op(eng, lambda e: e.tensor_copy(TT[dd]["carb"], TT[dd]["car"]), reads=[bT[dd]], writes=[bT[dd]])
        P.dma("sp" if dd == 0 else "pool", K.Hc[dd, blk], TT[dd]["carb"], reads=[bT[dd]], writes=[K.bHd])

    def l1_group(dd, ri, gc, n, ubuf, gmod):
        ps = K.PSF[dd * 2 + n % 2]; bp = K.bPS[dd * 2 + n % 2]
        fns = []
        for gl in range(4):
            gp = gc * 4 + gl
            for g2 in range(2):
                g = 2 * gp + g2
                fns.append(lambda e, gl=gl, g=g, g2=g2, ps=ps: e.matmul(
                    ps[:, gl * 128:(gl + 1) * 128], lhsT=K.W1Z[:, dd, ri, g, :], rhs=ubuf[:, g % gmod, :],
                    start=(g2 == 0), stop=(g2 == 1)))
        P.op("pe", fns, reads=[bUL[dd], K.bW1], writes=[bp])
        P.op("act", lambda e, ps=ps: e.activation(
            S8[dd][:, ri, gc * 4:(gc + 1) * 4, :].rearrange("p g j -> p (g j)"), ps[:, :], ACT.Identity),
            reads=[bp], writes=[bS8[dd]])

    def level1_A(b):
        P.dma("sp", UBL[0], K.Ud[b], reads=[K.bUd], writes=[bUL[0]])
        n = 0
        for ri in range(2):
            for gc in range(8):
                l1_group(0, ri, gc, n, UBL[0], 64); n += 1

    def level1_B(b):
        n = 0
        for half in range(2):
            P.dma("pool", UBL[1], K.Ud[b][:, half * 32:(half + 1) * 32, :], reads=[K.bUd], writes=[bUL[1]])
            for ri in range(2):
                for gcl in range(4):
                    l1_group(1, ri, half * 4 + gcl, n, UBL[1], 32); n += 1

    def scan(dd, order):
        eng = eng_of[dd]
        s8 = S8[dd]; tt = TT[dd]
        mua = K.MUA[:, dd]; mus = K.MUS[:, dd]
        bs = bS8[dd]; bt = bT[dd]
        prevj = None
        for jj in order:
            prev = tt["car"] if prevj is None else s8[:, :, :, prevj]
            cur = s8[:, :, :, jj]
            rd = [bs, bt, K.bTab]
            P.op(eng, lambda e, prev=prev: e.tensor_tensor(tt["t1"], mua, prev, ALU.mult), reads=rd, writes=[bt], ses=False)
            P.op(eng, lambda e, prev=prev: e.tensor_tensor(tt["t2"][:, 0], mus[:, 0], prev[:, 1], ALU.mult), reads=rd, writes=[bt], ses=False)
            P.op(eng, lambda e, prev=prev: e.tensor_tensor(tt["t2"][:, 1], mus[:, 1], prev[:, 0], ALU.mult), reads=rd, writes=[bt], ses=False)
            P.op(eng, lambda e: e.tensor_tensor(tt["t1"], tt["t1"], tt["t2"], ALU.add), reads=[bt], writes=[bt], ses=False)
            P.op(eng, lambda e, cur=cur: e.tensor_tensor(cur, cur, tt["t1"], ALU.add), reads=[bs, bt], writes=[bs], ses=False)
            prevj = jj
        P.op(eng, lambda e: e.tensor_copy(tt["car"], s8[:, :, :, prevj]), reads=[bs, bt], writes=[bt], ses=False)

    def store_H(dd, blk):
        eng = eng_of[dd]
        s8 = S8[dd]
        if dd == 0:
            hb = UBL[0][:, :, :].rearrange("p (r g) j -> p r g j", r=2)
            P.op(eng, lambda e: e.tensor_copy(hb, s8), reads=[bS8[0]], writes=[bUL[0]])
            P.dma("sp", K.Hd[0, blk], hb, reads=[bUL[0]], writes=[K.bHd])
        else:
            for ri in range(2):
                hb = UBL[1]
                P.op(eng, lambda e, ri=ri: e.tensor_copy(hb, s8[:, ri]), reads=[bS8[1]], writes=[bUL[1]])
                P.dma("pool", K.Hd[1, blk][:, ri], hb, reads=[bUL[1]], writes=[K.bHd])

    def do_A(b):
        level1_A(b)
        if b >= 2:
            put_carry(0, b - 2)
        scan(0, range(128))
        if b >= 2:
            store_H(0, b - 2)

    def do_B(b):
        level1_B(b)
        put_carry(1, b - 2)
        scan(1, range(127, -1, -1))
        store_H(1, b - 2)

    level1_A(0)
    do_B(3)
    scan(0, range(128))
    do_A(1)
    do_A(2)
    do_B(2)
    do_A(3)


def phase_gen2(K):
    P = K.P
    CR = aview(K, 64 * KB, [2, 32, 16], F32); CI = aview(K, 68 * KB, [2, 32, 16], F32); bCc = Buf()
    P.dma("sp", CR, K.p_cre, writes=[bCc]); P.dma("sp", CI, K.p_cim, writes=[bCc])
    o = [0]

    def al(shape, dt=F32):
        v = aview(K, o[0], shape, dt)
        o[0] += _prod(shape) * (4 if dt == F32 else 2)
        return v
    QR = al([4, 8, 16]); QI = al([4, 8, 16]); T1 = al([4, 8, 16])
    W3R = al([4, 8, 16]); W3I = al([4, 8, 16])
    WPR = {0: al([4, 128]), 1: al([4, 128])}; WPI = {0: al([4, 128]), 1: al([4, 128])}
    QZR = {0: al([8, 128]), 1: al([8, 128])}; QZI = {0: al([8, 128]), 1: al([8, 128])}
    W3S = [al([2, 2, 8, 128], BF16), al([2, 2, 8, 128], BF16)]
    TT1 = al([8, 128]); TT2 = al([8, 128])
    TC = [al([8, 128], BF16), al([8, 128], BF16)]
    assert o[0] <= 64 * KB
    bQ = Buf(); bW = Buf(); bWP = Buf(); bQZ = Buf(); bS = [Buf(), Buf()]; bTT = Buf(); bTC = [Buf(), Buf()]
    sh = [128, 4, 8, 16]
    for gc in range(8):
        gs = slice(gc * 4, gc * 4 + 4)
        w3s = W3S[gc % 2]; bs = bS[gc % 2]
        for dd in range(2):
            q_chunk(K, "dve", dd, gc, QR, QI, T1, bQ)
            for gl in range(4):
                for g2 in range(2):
                    msk = K.m01[:, g2:g2 + 1]
                    P.op("dve", lambda e, gl=gl, g2=g2, msk=msk, dd=dd: e.tensor_scalar(
                        QZR[dd][:, 2 * gl + g2, :], QR[:, gl].rearrange("p s c -> p (s c)"), msk, None, ALU.mult),
                        reads=[bQ, K.bC], writes=[bQZ])
                    P.op("dve", lambda e, gl=gl, g2=g2, msk=msk, dd=dd: e.tensor_scalar(
                        QZI[dd][:, 2 * gl + g2, :], QI[:, gl].rearrange("p s c -> p (s c)"), msk, None, ALU.mult),
                        reads=[bQ, K.bC], writes=[bQZ])
            cr = bc(CR[:, dd, gs, :].unsqueeze(2), sh); ci = bc(CI[:, dd, gs, :].unsqueeze(2), sh)
            er = bc(K.ER[:, dd, gs, 16:24].unsqueeze(3), sh); ei = bc(K.EI[:, dd, gs, 16:24].unsqueeze(3), sh)
            wpr = WPR[dd][:, :, :].rearrange("p g (t c) -> p g t c", c=16)
            wpi = WPI[dd][:, :, :].rearrange("p g (t c) -> p g t c", c=16)
            cplx_outer(K, "dve", wpr, wpi, er, ei, cr, ci, T1, [K.bTab, bCc], [bWP], neg_im=True)
            er = bc(K.ER[:, dd, gs, 8:16].unsqueeze(3), sh); ei = bc(K.EI[:, dd, gs, 8:16].unsqueeze(3), sh)
            cplx_outer(K, "dve", W3R, W3I, er, ei, cr, ci, T1, [K.bTab, bCc], [bW], neg_im=True)
            for ri, W in ((0, W3R), (1, W3I)):
                for g2 in range(2):
                    P.op("dve", lambda e, ri=ri, W=W, g2=g2, dd=dd, w3s=w3s: e.tensor_scalar(
                        w3s[:, dd, ri, g2:8:2, :], W[:, :, :, :].rearrange("p g t c -> p g (t c)"), K.m01[:, g2:g2 + 1], None, ALU.mult),
                        reads=[bW, K.bC], writes=[bs])
        P.dma("sp", K.W3d[:, :, :, gc * 8:(gc + 1) * 8, :], w3s, reads=[bs], writes=[K.bW3d])
        for dd in range(2):
            for hb in range(2):
                ps = K.PSF[dd * 2 + hb]; bp = K.bPS[dd * 2 + hb]
                fns = []
                for gq in range(4):
                    g = hb * 4 + gq
                    gl = g // 2
                    fns.append(lambda e, g=g, gl=gl, gq=gq, ps=ps, dd=dd: e.matmul(
                        ps[:, gq * 128:(gq + 1) * 128], lhsT=QZR[dd][:, g, :], rhs=WPR[dd][:, gl, :], start=True, stop=False))
                    fns.append(lambda e, g=g, gl=gl, gq=gq, ps=ps, dd=dd: e.matmul(
                        ps[:, gq * 128:(gq + 1) * 128], lhsT=QZI[dd][:, g, :], rhs=WPI[dd][:, gl, :], start=False, stop=True))
                P.op("pe", fns, reads=[bQZ, bWP], writes=[bp])
        s4 = [128, 4, 128]
        for hb in range(2):
            pa = K.PSF[hb][:, :].rearrange("p (g q) -> p g q", q=128)
            pb = K.PSF[2 + hb][:, :].rearrange("p (g q) -> p g q", q=128)
            P.op("dve", lambda e, hb=hb, pa=pa: e.tensor_tensor(TT1[:, hb * 4:(hb + 1) * 4, :], pa, bc(K.maskF[:, :].unsqueeze(1), s4), ALU.mult),
                 reads=[K.bPS[hb], K.bC], writes=[bTT])
            P.op("dve", lambda e, hb=hb, pb=pb: e.tensor_tensor(TT2[:, hb * 4:(hb + 1) * 4, :], pb, bc(K.maskB[:, :].unsqueeze(1), s4), ALU.mult),
                 reads=[K.bPS[2 + hb], K.bC], writes=[bTT])
        P.op("dve", lambda e: e.tensor_tensor(TT1, TT1, TT2, ALU.add), reads=[bTT], writes=[bTT])
        tc_ = TC[gc % 2]; btc = bTC[gc % 2]
        for g in range(8):
            gg = gc * 8 + g
            P.op("dve", lambda e, g=g, gg=gg, tc_=tc_: e.scalar_tensor_tensor(
                tc_[:, g, :], K.ident[:], K.dskip[:, gg:gg + 1], TT1[:, g, :], ALU.mult, ALU.add),
                reads=[bTT, K.bC], writes=[btc])
        P.dma("sp", K.Td[:, gc * 8:(gc + 1) * 8, :], tc_, reads=[btc], writes=[K.bTd])
        if "T" in DEBUG and gc == 0:
            K.dbg("T", tc_, [btc])


def phase_s4(K):
    P = K.P
    CH = []
    for i in range(2):
        o = i * 16 * KB
        CH.append(dict(U=aview(K, o, [8, 128], BF16), T=aview(K, o + 2 * KB, [8, 128], BF16),
                       W3=aview(K, o + 4 * KB, [2, 2, 8, 128], BF16),
                       HA=aview(K, o + 12 * KB, [2, 4, 128], BF16), HB=aview(K, o + 14 * KB, [2, 4, 128], BF16),
                       HAc=aview(K, 168 * KB + i * 64, [2, 4], BF16), HBc=aview(K, 168 * KB + 32 + i * 64, [2, 4], BF16), b=Buf()))
    YSG = [aview(K, 32 * KB, [1024], F32), aview(K, 36 * KB, [1024], F32)]; bYSG = [Buf(), Buf()]
    Y2C = aview(K, 40 * KB, [8, 128], F32); G1 = aview(K, 44 * KB, [8, 128], F32)
    G2 = aview(K, 48 * KB, [8, 128], F32); bG = Buf()
    YG = aview(K, 56 * KB, [8, 1024], BF16); bYG = Buf()
    YGT = aview(K, 72 * KB, [8, 1024], BF16); bYGT = Buf()
    WGL = aview(K, 88 * KB, [8, 1024], BF16); bWGL = Buf()
    GT = [aview(K, 104 * KB, [512], F32), aview(K, 106 * KB, [512], F32)]; bGT = [Buf(), Buf()]
    YS = aview(K, 112 * KB, [8, 1024], F32); bYS = Buf()
    YSN = [aview(K, 144 * KB, [1024], BF16), aview(K, 146 * KB, [1024], BF16)]; bYSN = [Buf(), Buf()]
    YST = aview(K, 148 * KB, [8, 1024], BF16); bYST = Buf()
    SS = aview(K, 164 * KB, [8], F32); JNK = aview(K, 165 * KB, [1024], BF16); bSS = Buf()
    for q2 in range(2):
        P.dma("pool", WGL[:, q2 * 4:(q2 + 1) * 4, :], K.w_glu[q2 * 512:(q2 + 1) * 512, :].rearrange("(kt p) c -> p kt c", p=128), writes=[bWGL])
    nch = 0
    for blk in range(2):
        for gc in range(8):
            ch = CH[nch % 2]; nch += 1
            bch = ch["b"]
            P.dma("sp", ch["U"], K.Ud[2 + blk][:, gc * 8:(gc + 1) * 8, :], reads=[K.bUd], writes=[bch])
            P.dma("sp", ch["T"], K.Td[:, gc * 8:(gc + 1) * 8, :], reads=[K.bTd], writes=[bch])
            P.dma("sp", ch["W3"], K.W3d[:, :, :, gc * 8:(gc + 1) * 8, :], reads=[K.bW3d], writes=[bch])
            P.dma("sp", ch["HA"], K.Hd[0, blk][:, :, gc * 4:(gc + 1) * 4, :], reads=[K.bHd], writes=[bch])
            P.dma("sp", ch["HB"], K.Hd[1, blk][:, :, gc * 4:(gc + 1) * 4, :], reads=[K.bHd], writes=[bch])
            P.dma("sp", ch["HAc"], K.Hc[0, blk][:, :, gc * 4:(gc + 1) * 4], reads=[K.bHd], writes=[bch])
            P.dma("sp", ch["HBc"], K.Hc[1, blk][:, :, gc * 4:(gc + 1) * 4], reads=[K.bHd], writes=[bch])
            ysg = YSG[gc % 2]; bys = bYSG[gc % 2]
            for hb in range(2):
                ps = K.PSF[hb]; bp = K.bPS[hb]
                fns = []
                for gq in range(4):
                    g = hb * 4 + gq
                    gpl = g // 2
                    o_ = ps[:, gq * 128:(gq + 1) * 128]
                    fns.append(lambda e, o_=o_, g=g, ch=ch: e.matmul(o_, lhsT=ch["T"][:, g, :], rhs=ch["U"][:, g, :], start=True, stop=False))
                    for ri in range(2):
                        fns.append(lambda e, o_=o_, g=g, gpl=gpl, ch=ch, ri=ri: e.matmul(
                            o_[:, 1:128], lhsT=ch["W3"][:, 0, ri, g, :], rhs=ch["HA"][:, ri, gpl, 0:127], start=False, stop=False))
                        fns.append(lambda e, o_=o_, g=g, gpl=gpl, ch=ch, ri=ri: e.matmul(
                            o_[:, 0:1], lhsT=ch["W3"][:, 0, ri, g, :], rhs=ch["HAc"][:, ri, gpl:gpl + 1], start=False, stop=False))
                        fns.append(lambda e, o_=o_, g=g, gpl=gpl, ch=ch, ri=ri: e.matmul(
                            o_[:, 0:127], lhsT=ch["W3"][:, 1, ri, g, :], rhs=ch["HB"][:, ri, gpl, 1:128], start=False, stop=False))
                        fns.append(lambda e, o_=o_, g=g, gpl=gpl, ch=ch, ri=ri: e.matmul(
                            o_[:, 127:128], lhsT=ch["W3"][:, 1, ri, g, :], rhs=ch["HBc"][:, ri, gpl:gpl + 1], start=False, stop=(ri == 1)))
                P.op("pe", fns, reads=[bch], writes=[bp])
                P.op("act", lambda e, hb=hb, ps=ps, ysg=ysg: e.activation(ysg[:, hb * 512:(hb + 1) * 512], ps[:, :], ACT.Identity),
                     reads=[bp], writes=[bys])
            for hb in range(2):
                ps = K.PSF[2 + hb]; bp = K.bPS[2 + hb]
                fns = []
                for gq in range(4):
                    g = hb * 4 + gq
                    fns.append(lambda e, gq=gq, g=g, ps=ps, ysg=ysg: e.transpose(ps[:, gq * 128:(gq + 1) * 128], ysg[:, g * 128:(g + 1) * 128], K.ident[:]))
                P.op("pe", fns, reads=[bys, K.bC], writes=[bp])
                P.op("dve", lambda e, hb=hb, ps=ps: e.tensor_copy(
                    Y2C[:, :, hb * 64:(hb + 1) * 64].rearrange("p t (g c) -> p g t c", c=16),
                    ps[:, :].rearrange("p (g t c) -> p g t c", t=8, c=16)), reads=[bp], writes=[bG])
            if "ypre" in DEBUG and blk == 0 and gc == 0:
                K.dbg("ypre", Y2C, [bG])
            P.op("dve", lambda e: e.tensor_tensor(G1, Y2C, Y2C, ALU.mult), reads=[bG], writes=[bG])
            P.op("dve", lambda e: e.tensor_scalar(G1, G1, 0.044715, 1.0, ALU.mult, ALU.add), reads=[bG], writes=[bG])
            P.op("dve", lambda e: e.tensor_tensor(G1, G1, Y2C, ALU.mult), reads=[bG], writes=[bG])
            P.op("act", lambda e: e.activation(G2[:, :, :].rearrange("p t c -> p (t c)"), G1[:, :, :].rearrange("p t c -> p (t c)"),
                                               ACT.Sigmoid, scale=1.5957691216057308), reads=[bG], writes=[bG])
            P.op("dve", lambda e, gc=gc: e.tensor_tensor(YS[:, :, gc * 128:(gc + 1) * 128], Y2C, G2, ALU.mult), reads=[bG], writes=[bYS])
            P.op("dve", lambda e, gc=gc: e.tensor_copy(YG[:, :, gc * 128:(gc + 1) * 128], YS[:, :, gc * 128:(gc + 1) * 128]), reads=[bYS], writes=[bYG])
            pt = K.PSB[gc % 2]; bpt = K.bPS[6 + gc % 2]
            fns = []
            for t in range(8):
                fns.append(lambda e, t=t, gc=gc, pt=pt: e.transpose(pt[:, t * 128:(t + 1) * 128], YG[:, t, gc * 128:(gc + 1) * 128], K.identb[:]))
            P.op("pe", fns, reads=[bYG, K.bC], writes=[bpt])
            P.op("dve", lambda e, gc=gc, pt=pt: e.tensor_copy(
                YGT[:, gc, :].rearrange("p (j t) -> p t j", t=8), pt[:, :].rearrange("p (t j) -> p t j", j=128)), reads=[bpt], writes=[bYGT])
        if "yg" in DEBUG and blk == 0:
            K.dbg("yg", YS, [bYS])
        n = 0
        for t in range(8):
            for hf in range(2):
                ps = K.PSF[n % 2]; bp = K.bPS[n % 2]
                gt = GT[n % 2]; bgt = bGT[n % 2]; n += 1
                fns = []
                for kt in range(8):
                    fns.append(lambda e, kt=kt, t=t, hf=hf, ps=ps: e.matmul(
                        ps[:, :], lhsT=YGT[:, kt, t::8], rhs=WGL[:, kt, hf * 512:(hf + 1) * 512], start=(kt == 0), stop=(kt == 7)))
                P.op("pe", fns, reads=[bYGT, bWGL], writes=[bp])
                P.op("dve", lambda e, hf=hf, ps=ps, gt=gt: e.tensor_tensor(gt, ps[:, :], K.bglu[:, hf * 512:(hf + 1) * 512], ALU.add),
                     reads=[bp, K.bC], writes=[bgt])
                P.op("act", lambda e, gt=gt: e.activation(gt, gt, ACT.Sigmoid), reads=[bgt], writes=[bgt])
                P.op("dve", lambda e, t=t, hf=hf, gt=gt: e.tensor_tensor(
                    YS[:, t, hf * 512:(hf + 1) * 512], YS[:, t, hf * 512:(hf + 1) * 512], gt, ALU.mult), reads=[bgt, bYS], writes=[bYS])
        if "ys" in DEBUG and blk == 0:
            K.dbg("ys", YS, [bYS])
        for t in range(8):
            P.op("act", lambda e, t=t: e.activation(JNK, YS[:, t, :], ACT.Square, accum_out=SS[:, t:t + 1]), reads=[bYS], writes=[bSS])
        P.op("dve", lambda e: e.tensor_scalar(SS, SS, 1.0 / 1024, EPS, ALU.mult, ALU.add), reads=[bSS], writes=[bSS])
        P.op("act", lambda e: e.activation(SS, SS, ACT.Sqrt), reads=[bSS], writes=[bSS])
        P.op("dve", lambda e: e.reciprocal(SS, SS), reads=[bSS], writes=[bSS])
        for t in range(8):
            ysn = YSN[t % 2]; bysn = bYSN[t % 2]
            P.op("dve", lambda e, t=t, ysn=ysn: e.tensor_scalar(ysn, YS[:, t, :], SS[:, t:t + 1], None, ALU.mult), reads=[bYS, bSS], writes=[bysn])
            pt = K.PSB[t % 2]; bpt = K.bPS[6 + t % 2]
            fns = []
            for kt in range(8):
                fns.append(lambda e, kt=kt, pt=pt, ysn=ysn: e.transpose(pt[:, kt * 128:(kt + 1) * 128], ysn[:, kt * 128:(kt + 1) * 128], K.identb[:]))
            P.op("pe", fns, reads=[bysn, K.bC], writes=[bpt])
            P.op("dve", lambda e, t=t, pt=pt: e.tensor_copy(YST[:, :, t::8], pt[:, :].rearrange("p (k j) -> p k j", j=128)), reads=[bpt], writes=[bYST])
        P.dma("sp", K.mixT[:, 8:16, blk * 1024:(blk + 1) * 1024], YST, reads=[bYST], writes=[K.bmix])


def phase_a(K):
    P = K.P
    QT = aview(K, 0, [8, 2048], BF16); bQT = Buf()
    KT = aview(K, 32 * KB, [8, 2304], BF16); bKT = Buf()
    VV = aview(K, 68 * KB, [18, 16, 80], BF16); bVV = Buf()
    A0 = 68 * KB + 18 * 16 * 80 * 2
    A0 = (A0 + 63) // 64 * 64
    XC = [aview(K, A0, [16, 512], BF16), aview(K, A0 + 16 * KB, [16, 512], BF16)]; bXC = [Buf(), Buf()]
    WH = aview(K, A0 + 32 * KB, [16, 512], BF16); bWH = Buf()
    o = A0 + 48 * KB
    RX = aview(K, o, [2304], F32); RX2 = aview(K, o + 9 * KB, [2304], F32); bRX = Buf()
    o += 18 * KB
    SQ = [aview(K, o, [512], BF16), aview(K, o + 1 * KB, [512], BF16)]; bSQ = [Buf(), Buf()]
    MS = [aview(K, o + 2 * KB, [512], F32), aview(K, o + 4 * KB, [512], F32)]; bMS = [Buf(), Buf()]
    RK = aview(K, o + 6 * KB, [18], F32); bRK = Buf()
    assert o + 7 * KB <= K.ARENA_BYTES, o
    chunks = [(0, 256), (256, 512), (768, 512), (1280, 512), (1792, 512)]

    def load_x(ci, nb):
        c0, cw = chunks[ci]
        xc = XC[nb % 2]; bx = bXC[nb % 2]
        for q2 in range(2):
            P.dma("pool", xc[:, q2 * 8:(q2 + 1) * 8, 0:cw],
                  K.xT[q2 * 1024:(q2 + 1) * 1024, 1792 + c0:1792 + c0 + cw].rearrange("(kt p) t -> p kt t", p=128), writes=[bx])
        return xc, bx
    nb = 0
    P.op("dve", lambda e: e.memset(VV[:, :, :, 64:65], 1.0), writes=[bVV])
    for ci, (c0, cw) in enumerate(chunks):
        xc, bx = load_x(ci, nb); nb += 1
        ps = K.PSF[0]; bp = K.bPS[0]
        for kt in range(16):
            sq = SQ[kt % 2]; bs = bSQ[kt % 2]
            P.op("act", lambda e, kt=kt, sq=sq, xc=xc, cw=cw: e.activation(sq[:, 0:cw], xc[:, kt, 0:cw], ACT.Square), reads=[bx], writes=[bs])
            P.op("pe", lambda e, kt=kt, sq=sq, cw=cw: e.matmul(ps[:, 0:cw], lhsT=K.onesb[:, :], rhs=sq[:, 0:cw], start=(kt == 0), stop=(kt == 15)),
                 reads=[bs, K.bC], writes=[bp])
        P.op("dve", lambda e, c0=c0, cw=cw: e.tensor_scalar(RX[:, c0:c0 + cw], ps[:, 0:cw], 1.0 / DM, EPS, ALU.mult, ALU.add), reads=[bp], writes=[bRX])
    P.op("act", lambda e: e.activation(RX, RX, ACT.Sqrt), reads=[bRX], writes=[bRX])
    P.op("dve", lambda e: e.reciprocal(RX, RX), reads=[bRX], writes=[bRX])
    P.op("dve", lambda e: e.tensor_tensor(RX2, RX, RX, ALU.mult), reads=[bRX], writes=[bRX])
    pk = K.PSF[1]; bpk = K.bPS[1]
    fns = []
    for tile in range(18):
        fns.append(lambda e, tile=tile: e.matmul(pk[:, tile:tile + 1], lhsT=RX[:, tile * 128:(tile + 1) * 128], rhs=K.inv128[:, 0:1], start=True, stop=True))
    P.op("pe", fns, reads=[bRX, K.bC], writes=[bpk])
    P.op("dve", lambda e: e.tensor_copy(RK, pk[:, 0:18]), reads=[bpk], writes=[bRK])

    def load_w(col0):
        for q2 in range(2):
            P.dma("pool", WH[:, q2 * 8:(q2 + 1) * 8, :], K.w_in[q2 * 1024:(q2 + 1) * 1024, col0:col0 + 512].rearrange("(kt p) c -> p kt c", p=128), writes=[bWH])
        for kt in range(16):
            P.op("dve", lambda e, kt=kt: e.tensor_scalar(WH[:, kt, :], WH[:, kt, :], K.gmix[:, kt:kt + 1], None, ALU.mult), reads=[bWH, K.bC], writes=[bWH])
    npp = 0
    for sel in range(2):
        DST = QT if sel == 0 else KT
        bD = bQT if sel == 0 else bKT
        gain = K.qkg[:, sel:sel + 1]
        for hf in range(2):
            load_w(sel * 1024 + hf * 512)
            for ci, (c0, cw) in enumerate(chunks):
                if sel == 0 and ci == 0:
                    continue
                xc, bx = load_x(ci, nb); nb += 1
                d0 = c0 - 256 if sel == 0 else c0
                for hl in range(4):
                    hp = hf * 4 + hl
                    ps = K.PSF[npp % 2]; bp = K.bPS[npp % 2]
                    p2 = K.PSF[2 + npp % 2]; bp2 = K.bPS[2 + npp % 2]
                    sq = SQ[npp % 2]; bs = bSQ[npp % 2]
                    ms = MS[npp % 2]; bm = bMS[npp % 2]; npp += 1
                    fns = []
                    for kt in range(16):
                        fns.append(lambda e, kt=kt, hl=hl, ps=ps, xc=xc, cw=cw: e.matmul(
                            ps[:, 0:cw], lhsT=WH[:, kt, hl * 128:(hl + 1) * 128], rhs=xc[:, kt, 0:cw], start=(kt == 0), stop=(kt == 15)))
                    P.op("pe", fns, reads=[bWH, bx], writes=[bp])
                    P.op("act", lambda e, ps=ps, sq=sq, cw=cw: e.activation(sq[:, 0:cw], ps[:, 0:cw], ACT.Square), reads=[bp], writes=[bs])
                    P.op("pe", lambda e, p2=p2, sq=sq, cw=cw: e.matmul(p2[:, 0:cw], lhsT=K.blk1[:, :], rhs=sq[:, 0:cw], start=True, stop=True),
                         reads=[bs, K.bC], writes=[bp2])
                    P.op("dve", lambda e, p2=p2, ms=ms, c0=c0, cw=cw: e.tensor_tensor(ms[:, 0:cw], p2[:, 0:cw], RX2[:, c0:c0 + cw], ALU.mult),
                         reads=[bp2, bRX], writes=[bm])
                    P.op("dve", lambda e, ms=ms, cw=cw: e.tensor_scalar(ms[:, 0:cw], ms[:, 0:cw], 1.0 / 64, EPS, ALU.mult, ALU.add), reads=[bm], writes=[bm])
                    P.op("act", lambda e, ms=ms, cw=cw: e.activation(ms[:, 0:cw], ms[:, 0:cw], ACT.Sqrt), reads=[bm], writes=[bm])
                    P.op("dve", lambda e, ms=ms, cw=cw: e.reciprocal(ms[:, 0:cw], ms[:, 0:cw]), reads=[bm], writes=[bm])
                    P.op("dve", lambda e, ms=ms, c0=c0, cw=cw: e.tensor_tensor(ms[:, 0:cw], ms[:, 0:cw], RX[:, c0:c0 + cw], ALU.mult), reads=[bm, bRX], writes=[bm])
                    P.op("dve", lambda e, ps=ps, ms=ms, hp=hp, d0=d0, cw=cw, DST=DST, gain=gain: e.scalar_tensor_tensor(
                        DST[:, hp, d0:d0 + cw], ps[:, 0:cw], gain, ms[:, 0:cw], ALU.mult, ALU.mult), reads=[bp, bm, K.bC], writes=[bD])
    for hf in range(2):
        load_w(2048 + hf * 512)
        for ci, (c0, cw) in enumerate(chunks):
            xc, bx = load_x(ci, nb); nb += 1
            for tl in range(cw // 128):
                tile = c0 // 128 + tl
                ps = K.PSF[npp % 2]; bp = K.bPS[npp % 2]; npp += 1
                fns = []
                for kt in range(16):
                    fns.append(lambda e, kt=kt, tl=tl, ps=ps, xc=xc: e.matmul(
                        ps[:, :], lhsT=xc[:, kt, tl * 128:(tl + 1) * 128], rhs=WH[:, kt, :], start=(kt == 0), stop=(kt == 15)))
                P.op("pe", fns, reads=[bWH, bx], writes=[bp])
                P.op("dve", lambda e, tile=tile, hf=hf, ps=ps: e.tensor_scalar(
                    VV[:, tile, hf * 8:(hf + 1) * 8, 0:64], ps[:, :].rearrange("p (h d) -> p h d", d=64), RK[:, tile:tile + 1], None, ALU.mult),
                    reads=[bp, bRK], writes=[bVV])
    if "qT" in DEBUG:
        K.dbg("qT", QT, [bQT]); K.dbg("kT", KT, [bKT]); K.dbg("vv", VV, [bVV])
    P.barrier()
    B0 = A0
    YA = aview(K, B0, [16, 1024], BF16); bYA = Buf()
    o = B0 + 32 * KB
    BIAS = [aview(K, o, [15, 128], F32), aview(K, o + 7680, [15, 128], F32)]; bBI = [Buf(), Buf()]
    o += 15360
    QZ = [aview(K, o, [2048], BF16), aview(K, o + 4 * KB, [2048], BF16)]; bQZ = [Buf(), Buf()]
    o += 8 * KB
    SB = [aview(K, o, [5, 128], F32), aview(K, o + 2560, [5, 128], F32)]; bSB = [Buf(), Buf()]
    o += 5120
    PT = [aview(K, o, [5, 128], BF16), aview(K, o + 1280, [5, 128], BF16)]; bPT = [Buf(), Buf()]
    o += 2560
    RD = [aview(K, o, [1], F32), aview(K, o + 64, [1], F32)]; bRD = [Buf(), Buf()]
    o += 128
    YN = [aview(K, o, [1024], BF16), aview(K, o + 2 * KB, [1024], BF16)]; bYN = [Buf(), Buf()]
    o += 4 * KB
    YT = [aview(K, o, [8, 128], BF16), aview(K, o + 2 * KB, [8, 128], BF16)]; bYT = [Buf(), Buf()]
    o += 4 * KB
    SSA = aview(K, o, [16], F32); JNK = aview(K, o + 64, [1024], BF16); bSSA = Buf()
    o += 64 + 2 * KB
    assert o <= K.ARENA_BYTES, o
    it = 0
    for head in range(NH):
        hp, hh = head // 2, head % 2
        bi = BIAS[head % 2]; bbi = bBI[head % 2]
        P.dma("sp", bi, K.biasd[head], writes=[bbi])
        qz = QZ[head % 2]; bqz = bQZ[head % 2]
        P.op("pool", lambda e, hp=hp, hh=hh, qz=qz: e.tensor_scalar(qz, QT[:, hp, :], K.m01[:, hh:hh + 1], None, ALU.mult), reads=[bQT, K.bC], writes=[bqz])
        for n in range(16, 32):
            qi = n - 16
            ks = min(max(n - 2, 0), 27)
            v0 = 0 if n <= 29 else (5 if n == 30 else 10)
            psA = K.PSF[(it % 2) * 2]; bpA = K.bPS[(it % 2) * 2]
            psB = K.PSF[(it % 2) * 2 + 1]; bpB = K.bPS[(it % 2) * 2 + 1]
            pso = K.PSF[4 + it % 2]; bpo = K.bPS[4 + it % 2]
            sb = SB[it % 2]; bsb = bSB[it % 2]
            pt = PT[it % 2]; bpt = bPT[it % 2]
            rd = RD[it % 2]; brd = bRD[it % 2]
            it += 1
            fns = []
            for i in range(5):
                kti = ks + i - 14
                dst = psA[:, i * 128:(i + 1) * 128] if i < 4 else psB[:, 0:128]
                fns.append(lambda e, dst=dst, kti=kti, qi=qi, hp=hp, qz=qz: e.matmul(
                    dst, lhsT=KT[:, hp, kti * 128:(kti + 1) * 128], rhs=qz[:, qi * 128:(qi + 1) * 128], start=True, stop=True))
            P.op("pe", fns, reads=[bKT, bqz], writes=[bpA, bpB])
            P.op("dve", lambda e, psA=psA, sb=sb, bi=bi, v0=v0: e.tensor_tensor(
                sb[:, 0:4, :], psA[:, :].rearrange("p (i q) -> p i q", q=128), bi[:, v0:v0 + 4, :], ALU.add), reads=[bpA, bbi], writes=[bsb])
            P.op("dve", lambda e, psB=psB, sb=sb, bi=bi, v0=v0: e.tensor_tensor(sb[:, 4, :], psB[:, 0:128], bi[:, v0 + 4, :], ALU.add),
                 reads=[bpB, bbi], writes=[bsb])
            P.op("act", lambda e, sb=sb, pt=pt: e.activation(pt[:, :, :].rearrange("p i q -> p (i q)"), sb[:, :, :].rearrange("p i q -> p (i q)"), ACT.Exp),
                 reads=[bsb], writes=[bpt])
            fns = []
            for i in range(5):
                kti = ks + i - 14
                fns.append(lambda e, i=i, kti=kti, pso=pso, pt=pt, head=head: e.matmul(
                    pso[:, 0:65], lhsT=pt[:, i, :], rhs=VV[:, kti, head, 0:65], start=(i == 0), stop=(i == 4)))
            P.op("pe", fns, reads=[bpt, bVV], writes=[bpo])
            P.op("dve", lambda e, pso=pso, rd=rd: e.reciprocal(rd, pso[:, 64:65]), reads=[bpo], writes=[brd])
            P.op("dve", lambda e, pso=pso, rd=rd, qi=qi, head=head: e.tensor_scalar(
                YA[:, qi, head * 64:(head + 1) * 64], pso[:, 0:64], rd[:, 0:1], None, ALU.mult), reads=[bpo, brd], writes=[bYA])
    if "ya" in DEBUG:
        K.dbg("ya", YA, [bYA])
    for qi in range(16):
        P.op("act", lambda e, qi=qi: e.activation(JNK, YA[:, qi, :], ACT.Square, accum_out=SSA[:, qi:qi + 1]), reads=[bYA], writes=[bSSA])
    P.op("dve", lambda e: e.tensor_scalar(SSA, SSA, 1.0 / 1024, EPS, ALU.mult, ALU.add), reads=[bSSA], writes=[bSSA])
    P.op("act", lambda e: e.activation(SSA, SSA, ACT.Sqrt), reads=[bSSA], writes=[bSSA])
    P.op("dve", lambda e: e.reciprocal(SSA, SSA), reads=[bSSA], writes=[bSSA])
    for qi in range(16):
        yn = YN[qi % 2]; byn = bYN[qi % 2]
        yt = YT[qi % 2]; byt = bYT[qi % 2]
        P.op("dve", lambda e, qi=qi, yn=yn: e.tensor_scalar(yn, YA[:, qi, :], SSA[:, qi:qi + 1], None, ALU.mult), reads=[bYA, bSSA], writes=[byn])
        pt = K.PSB[qi % 2]; bpt = K.bPS[6 + qi % 2]
        fns = []
        for kt in range(8):
            fns.append(lambda e, kt=kt, pt=pt, yn=yn: e.transpose(pt[:, kt * 128:(kt + 1) * 128], yn[:, kt * 128:(kt + 1) * 128], K.identb[:]))
        P.op("pe", fns, reads=[byn, K.bC], writes=[bpt])
        P.op("act", lambda e, pt=pt, yt=yt: e.activation(yt[:, :, :].rearrange("p k j -> p (k j)"), pt[:, :], ACT.Identity), reads=[bpt], writes=[byt])
        P.dma("sp", K.mixT[:, 0:8, qi * 128:(qi + 1) * 128], yt, reads=[byt], writes=[K.bmix])


def phase_o(K):
    P = K.P
    FG = [6, 6, 6, 6, 5, 5, 5, 5]
    for tb in range(2):
        P.barrier()
        X1 = aview(K, 0, [8, 2048], F32); bX1 = [Buf() for _ in range(8)]
        WO = aview(K, 64 * KB, [16, 2048], BF16); bWO = Buf()
        MIXC = aview(K, 128 * KB, [16, 512], BF16); bMX = Buf()
        XO = [aview(K, 144 * KB, [2048], F32), aview(K, 152 * KB, [2048], F32)]; bXO = [Buf(), Buf()]
        for q4 in range(4):
            P.dma("pool", WO[:, q4 * 4:(q4 + 1) * 4, :], K.w_out[q4 * 512:(q4 + 1) * 512, :].rearrange("(kt p) c -> p kt c", p=128), writes=[bWO])
        for kt in range(16):
            P.op("dve", lambda e, kt=kt: e.tensor_scalar(WO[:, kt, :], WO[:, kt, :], K.gout[:, kt:kt + 1], None, ALU.mult), reads=[bWO, K.bC], writes=[bWO])
        n = 0
        for s in range(2):
            P.dma("sp", MIXC, K.mixT[:, :, tb * 1024 + s * 512: tb * 1024 + (s + 1) * 512], reads=[K.bmix], writes=[bMX])
            for tt in range(4):
                tile = s * 4 + tt
                xo = XO[tile % 2]; bxo = bXO[tile % 2]
                r0 = (tb * 8 + tile) * 128
                P.dma("sp", xo, K.xown[r0:r0 + 128, :], writes=[bxo])
                for dc in range(4):
                    ps = K.PSF[n % 4]; bp = K.bPS[n % 4]; n += 1
                    fns = []
                    for kt in range(16):
                        fns.append(lambda e, kt=kt, tt=tt, dc=dc, ps=ps: e.matmul(
                            ps[:, :], lhsT=MIXC[:, kt, tt * 128:(tt + 1) * 128], rhs=WO[:, kt, dc * 512:(dc + 1) * 512], start=(kt == 0), stop=(kt == 15)))
                    P.op("pe", fns, reads=[bMX, bWO], writes=[bp])
                    P.op("dve", lambda e, tile=tile, dc=dc, ps=ps, xo=xo: e.tensor_tensor(
                        X1[:, tile, dc * 512:(dc + 1) * 512], ps[:, :], xo[:, dc * 512:(dc + 1) * 512], ALU.add), reads=[bp, bxo], writes=[bX1[tile]])
        if "x1" in DEBUG and tb == 0:
            K.dbg("x1", X1, bX1)
        P.barrier()
        H2T = aview(K, 64 * KB, [16, 1024], BF16); bH2T = Buf()
        ACTT = aview(K, 96 * KB, [6, 1024], BF16); bAT = Buf()
        WD = aview(K, 108 * KB, [6, 2048], BF16); bWD = Buf()
        WG = [aview(K, 132 * KB + i * 4 * KB, [16, 128], BF16) for i in range(3)]; bWG = [Buf() for _ in range(3)]
        WUp = [aview(K, 144 * KB + i * 4 * KB, [16, 128], BF16) for i in range(3)]; bWUp = [Buf() for _ in range(3)]
        SG = [aview(K, 156 * KB, [512], BF16), aview(K, 157 * KB, [512], BF16)]; bSG = [Buf(), Buf()]
        H2 = [aview(K, 158 * KB, [2048], BF16), aview(K, 162 * KB, [2048], BF16)]; bH2 = [Buf(), Buf()]
        S2 = aview(K, 166 * KB, [8], F32); JNK = aview(K, 167 * KB, [2048], BF16); bS2 = Buf()
        for tile in range(8):
            P.op("act", lambda e, tile=tile: e.activation(JNK, X1[:, tile, :], ACT.Square, accum_out=S2[:, tile:tile + 1]), reads=[bX1[tile]], writes=[bS2])
        P.op("dve", lambda e: e.tensor_scalar(S2, S2, 1.0 / DM, EPS, ALU.mult, ALU.add), reads=[bS2], writes=[bS2])
        P.op("act", lambda e: e.activation(S2, S2, ACT.Sqrt), reads=[bS2], writes=[bS2])
        P.op("dve", lambda e: e.reciprocal(S2, S2), reads=[bS2], writes=[bS2])
        for tile in range(8):
            h2 = H2[tile % 2]; bh2 = bH2[tile % 2]
            P.op("dve", lambda e, tile=tile, h2=h2: e.scalar_tensor_tensor(h2, X1[:, tile, :], S2[:, tile:tile + 1], K.gffn[:, :], ALU.mult, ALU.mult),
                 reads=[bX1[tile], bS2, K.bC], writes=[bh2])
            for hb in range(2):
                pt = K.PSB[hb]; bpt = K.bPS[6 + hb]
                fns = []
                for k8 in range(8):
                    kt = hb * 8 + k8
                    fns.append(lambda e, k8=k8, kt=kt, pt=pt, h2=h2: e.transpose(pt[:, k8 * 128:(k8 + 1) * 128], h2[:, kt * 128:(kt + 1) * 128], K.identb[:]))
                P.op("pe", fns, reads=[bh2, K.bC], writes=[bpt])
                P.op("dve", lambda e, hb=hb, tile=tile, pt=pt: e.tensor_copy(
                    H2T[:, hb * 8:(hb + 1) * 8, tile * 128:(tile + 1) * 128], pt[:, :].rearrange("p (k j) -> p k j", j=128)), reads=[bpt], writes=[bH2T])
        f0 = 0
        nw = 0
        npg = 0
        for grp, nf in enumerate(FG):
            for q in range(nf):
                r0 = (f0 + q) * 128
                P.dma("pool", WD[:, q, :], K.w_down[r0:r0 + 128, :], writes=[bWD])
            for q in range(nf):
                f = f0 + q
                wg = WG[nw % 3]; bwg = bWG[nw % 3]; wu = WUp[nw % 3]; bwu = bWUp[nw % 3]; nw += 1
                P.dma("pool", wg, K.w_gate[:, f * 128:(f + 1) * 128].rearrange("(kt p) c -> p kt c", p=128), writes=[bwg])
                P.dma("pool", wu, K.w_up[:, f * 128:(f + 1) * 128].rearrange("(kt p) c -> p kt c", p=128), writes=[bwu])
                for s in range(2):
                    pg = K.PSF[(npg % 2) * 2]; bpg = K.bPS[(npg % 2) * 2]
                    pu = K.PSF[(npg % 2) * 2 + 1]; bpu = K.bPS[(npg % 2) * 2 + 1]
                    sg = SG[npg % 2]; bsg = bSG[npg % 2]; npg += 1
                    fg = []
                    fu = []
                    for kt in range(16):
                        fg.append(lambda e, kt=kt, s=s, pg=pg, wg=wg: e.matmul(pg[:, :], lhsT=wg[:, kt, :], rhs=H2T[:, kt, s * 512:(s + 1) * 512], start=(kt == 0), stop=(kt == 15)))
                        fu.append(lambda e, kt=kt, s=s, pu=pu, wu=wu: e.matmul(pu[:, :], lhsT=wu[:, kt, :], rhs=H2T[:, kt, s * 512:(s + 1) * 512], start=(kt == 0), stop=(kt == 15)))
                    P.op("pe", fg, reads=[bwg, bH2T], writes=[bpg])
                    P.op("pe", fu, reads=[bwu, bH2T], writes=[bpu])
                    P.op("act", lambda e, pg=pg, sg=sg: e.activation(sg, pg[:, :], ACT.Silu), reads=[bpg], writes=[bsg])
                    P.op("dve", lambda e, pu=pu, sg=sg, q=q, s=s: e.tensor_tensor(ACTT[:, q, s * 512:(s + 1) * 512], pu[:, :], sg, ALU.mult),
                         reads=[bpu, bsg], writes=[bAT])
            nd = 0
            for tile in range(8):
                for dc in range(4):
                    ps = K.PSF[4 + nd % 2]; bp = K.bPS[4 + nd % 2]; nd += 1
                    fns = []
                    for q in range(nf):
                        fns.append(lambda e, q=q, tile=tile, dc=dc, ps=ps, nf=nf: e.matmul(
                            ps[:, :], lhsT=ACTT[:, q, tile * 128:(tile + 1) * 128], rhs=WD[:, q, dc * 512:(dc + 1) * 512], start=(q == 0), stop=(q == nf - 1)))
                    P.op("pe", fns, reads=[bAT, bWD], writes=[bp])
                    P.op("dve", lambda e, tile=tile, dc=dc, ps=ps: e.tensor_tensor(
                        X1[:, tile, dc * 512:(dc + 1) * 512], X1[:, tile, dc * 512:(dc + 1) * 512], ps[:, :], ALU.add), reads=[bp, bX1[tile]], writes=[bX1[tile]])
            f0 += nf
        for tile in range(8):
            r0 = (tb * 8 + tile) * 128
            P.dma("sp", K.out[r0:r0 + 128, :], X1[:, tile, :], reads=[bX1[tile]], writes=[K.bOut])


def build_program(phases="all"):
    nc = bass.Bass("TRN2", target_bir_lowering=False)
    K = Ctx()
    K.nc = nc
    dram = lambda name, shape, dt=F32, kind="ExternalInput": nc.dram_tensor(name, list(shape), dt, kind=kind).ap()
    K.xT = dram("xT", [DM, SEQ]); K.xown = dram("xown", [2048, DM])
    K.w_in = dram("w_in", [DM, 4096]); K.w_out = dram("w_out", [DM, DM]); K.w_glu = dram("w_glu", [1024, 1024])
    K.w_gate = dram("w_gate", [DM, DFF]); K.w_up = dram("w_up", [DM, DFF]); K.w_down = dram("w_down", [DFF, DM])
    K.p_are = dram("p_are", [128, 2, 32]); K.p_aim = dram("p_aim", [128, 2, 32]); K.p_ls = dram("p_ls", [128, 2, 32])
    K.p_bre = dram("p_bre", [128, 2, 32, 16]); K.p_bim = dram("p_bim", [128, 2, 32, 16])
    K.p_cre = dram("p_cre", [128, 2, 32, 16]); K.p_cim = dram("p_cim", [128, 2, 32, 16])
    K.p_kv = dram("p_kv", [128, 2, 24])
    K.biasd = dram("biasd", [NH, 128, 15, 128])
    cst = dram("cst", [128, CST_COLS])
    K.out = dram("out", [2048, DM], kind="ExternalOutput")
    K.Ud = nc.dram_tensor("Ud", [4, 128, 64, 128], BF16).ap()
    K.Hd = nc.dram_tensor("Hd", [2, 2, 128, 2, 32, 128], BF16).ap()
    K.Hc = nc.dram_tensor("Hc", [2, 2, 128, 2, 32], BF16).ap()
    K.W3d = nc.dram_tensor("W3d", [128, 2, 2, 64, 128], BF16).ap()
    K.Td = nc.dram_tensor("Td", [128, 64, 128], BF16).ap()
    K.mixT = nc.dram_tensor("mixT", [128, 16, 2048], BF16).ap()
    K.bUd = MBuf(); K.bHd = MBuf(); K.bW3d = MBuf(); K.bTd = MBuf(); K.bmix = MBuf(); K.bOut = MBuf()
    K.dbg_out = {}
    for name, shape in DEBUG.items():
        K.dbg_out[name] = dram("dbg_" + name, [128, _prod(shape)], F32, kind="ExternalOutput")
    with ExitStack() as st:
        P = Prog(nc, st)
        K.P = P
        K.ARENA_BYTES = 188 * KB
        K.arena = st.enter_context(nc.sbuf_tensor("arena", [128, K.ARENA_BYTES // 2], BF16))
        CS = st.enter_context(nc.sbuf_tensor("cs", [128, CST_COLS], F32))
        cb16 = st.enter_context(nc.sbuf_tensor("cb16", [128, 3 * 128], BF16))
        K.dbgt = st.enter_context(nc.sbuf_tensor("dbgt", [128, 64], F32))
        K.inv128 = st.enter_context(nc.sbuf_tensor("inv128", [128, 2], F32))
        K.PSF = [st.enter_context(nc.psum_tensor("psf%d" % i, [128, 512], F32)) for i in range(6)]
        K.PSB = [st.enter_context(nc.psum_tensor("psb%d" % i, [128, 1024], BF16)) for i in range(2)]
        K.bPS = [Buf() for _ in range(8)]
        K.bC = Buf()
        P.dma("sp", CS[:], cst, writes=[K.bC])
        c = CST_OFF
        K.ident = CS[:, c["ident"]:c["ident"] + 128]
        K.maskF = CS[:, c["maskF"]:c["maskF"] + 128]
        K.maskB = CS[:, c["maskB"]:c["maskB"] + 128]
        K.gmix = CS[:, c["gmix"]:c["gmix"] + 16]
        K.gout = CS[:, c["gout"]:c["gout"] + 16]
        K.qkg = CS[:, c["qkg"]:c["qkg"] + 2]
        K.m01 = CS[:, c["m01"]:c["m01"] + 2]
        K.dskip = CS[:, c["dskip"]:c["dskip"] + 64]
        K.bglu = CS[:, c["bglu"]:c["bglu"] + 1024]
        K.gffn = CS[:, c["gffn"]:c["gffn"] + 2048]
        K.identb = cb16[:, 0:128]; K.onesb = cb16[:, 128:256]; K.blk1 = cb16[:, 256:384]
        P.op("dve", lambda e: e.tensor_copy(K.identb, K.ident), reads=[K.bC], writes=[K.bC])
        P.op("dve", lambda e: e.memset(K.onesb, 1.0), writes=[K.bC])
        P.op("dve", lambda e: e.memset(K.inv128[:, :], 1.0 / 128), writes=[K.bC])
        P.op("dve", lambda e: e.tensor_copy(K.blk1, CS[:, c["blk1"]:c["blk1"] + 128]), reads=[K.bC], writes=[K.bC])
        P.op("dve", lambda e: e.tensor_scalar(K.qkg[:, 0:1], K.qkg[:, 0:1], 0.125, None, ALU.mult), reads=[K.bC], writes=[K.bC])
        ndbg = [0]

        def dbg(name, ap, bufs):
            shape = DEBUG[name]
            n = _prod(shape)
            dst = K.dbg_out[name]
            flat = ap
            nd = len(shape)
            if nd == 2:
                flat = ap.rearrange("p a b -> p (a b)")
            elif nd == 3:
                flat = ap.rearrange("p a b c -> p (a b c)")
            elif nd == 4:
                flat = ap.rearrange("p a b c d -> p (a b c d)")
            bd = Buf()
            for c0 in range(0, n, 64):
                w = min(64, n - c0)
                P.op("pool", lambda e, c0=c0, w=w: e.tensor_copy(K.dbgt[:, 0:w], flat[:, c0:c0 + w]), reads=list(bufs) + [bd], writes=[bd])
                P.dma("sp", dst[:, c0:c0 + w], K.dbgt[:, 0:w], reads=[bd], writes=[bd, K.bOut])
        K.dbg = dbg

        if phases in ("all", "ssm", "s1", "ssm_a", "ssm_b"):
            phase_s1(K)
            P.barrier()
        if phases in ("all", "ssm", "ssm_a", "ssm_b"):
            phase_gen1(K)
            P.barrier()
            phase_s2(K)
            P.barrier()
        if phases in ("all", "ssm", "ssm_b"):
            phase_gen2(K)
            P.barrier()
        if phases in ("all", "ssm"):
            phase_s4(K)
            P.barrier()
        if phases in ("all", "attn"):
            phase_a(K)
            P.barrier()
        if phases in ("all", "out"):
            phase_o(K)
        waits = P._deps("sp", [K.bOut], ())
        P.ops["sp"].append((waits, [], None))
        P.emit()
    return nc


CST_OFF = {}
_c = 0
for _n, _w in (("ident", 128), ("maskF", 128), ("maskB", 128), ("blk1", 128), ("gmix", 16), ("gout", 16), ("qkg", 2),
               ("m01", 2), ("dskip", 64), ("bglu", 1024), ("gffn", 2048)):
    CST_OFF[_n] = _c
    _c += _w
CST_COLS = _c


def _lay_gp(a):
    s = a.shape
    a = a.reshape(2, 32, 2, 64, *s[3:])
    a = np.moveaxis(a, [2, 3], [0, 1])
    return np.ascontiguousarray(a.reshape(128, 2, 32, *s[3:]), dtype=np.float32)


def _bias_table(rpb0, flip):
    out = np.full((NH, 15, 128, 128), NEG, np.float32)
    for v in range(15):
        n, i = (20, v) if v < 5 else ((30, v - 5) if v < 10 else (31, v - 10))
        ks = min(max(n - 2, 0), 27)
        kp = ks + i
        kr = np.repeat(np.array([2 * kp, 2 * kp + 1]), 64); kc = np.tile(np.arange(64), 2)
        qr = np.repeat(np.array([2 * n, 2 * n + 1]), 64); qc = np.tile(np.arange(64), 2)
        if flip:
            kr, kc, qr, qc = 63 - kr, 63 - kc, 63 - qr, 63 - qc
        rs = np.clip(qr - 4, 0, 56); cs = np.clip(qc - 8, 0, 48)
        inwin = ((kr[:, None] >= rs[None, :]) & (kr[:, None] < rs[None, :] + 8) &
                 (kc[:, None] >= cs[None, :]) & (kc[:, None] < cs[None, :] + 16))
        dr = np.clip(kr[:, None] - qr[None, :] + 7, 0, 14)
        dc = np.clip(kc[:, None] - qc[None, :], -15, 15) + 15
        vals = rpb0[:, dr, dc]
        out[:, v] = np.where(inwin[None], vals, np.float32(NEG))
    return np.ascontiguousarray(out.transpose(0, 2, 1, 3))


def _consts(inp):
    cs = np.zeros((128, CST_COLS), np.float32)
    o = CST_OFF
    cs[:, o["ident"]:o["ident"] + 128] = np.eye(128, dtype=np.float32)
    s = np.arange(128) // 16
    cs[:, o["maskF"]:o["maskF"] + 128] = (s[:, None] <= s[None, :])
    cs[:, o["maskB"]:o["maskB"] + 128] = (s[:, None] >= s[None, :])
    hh = np.arange(128) // 64
    cs[:, o["blk1"]:o["blk1"] + 128] = (hh[:, None] == hh[None, :])
    cs[:, o["gmix"]:o["gmix"] + 16] = inp["g_mix"][0].reshape(16, 128).T
    gout = np.concatenate([inp["g_out_attn"][0], inp["g_out_ssm"][0]])
    cs[:, o["gout"]:o["gout"] + 16] = gout.reshape(16, 128).T
    cs[:, o["qkg"]] = np.tile(inp["q_gain"][0], 2)
    cs[:, o["qkg"] + 1] = np.tile(inp["k_gain"][0], 2)
    cs[:, o["m01"]] = (hh == 0)
    cs[:, o["m01"] + 1] = (hh == 1)
    cs[:, o["dskip"]:o["dskip"] + 64] = np.tile(inp["ssm_d"][0].reshape(64, 16).T, (8, 1))
    cs[:, o["bglu"]:o["bglu"] + 1024] = inp["b_glu"][0][None, :]
    cs[:, o["gffn"]:o["gffn"] + 2048] = inp["g_ffn"][0][None, :]
    return cs


def _kvals():
    kvA = np.concatenate([np.arange(7, -1, -1), np.arange(1, 9), np.arange(-7, 1)])
    kvB = np.concatenate([np.arange(0, 8), np.arange(8, 0, -1), -np.arange(0, 8)])
    kv = np.stack([kvA, kvB]).astype(np.float32)
    return np.ascontiguousarray(np.broadcast_to(kv[None], (128, 2, 24)))


def prepare_inputs(inp):
    inp = {k: np.asarray(v) for k, v in inp.items()}
    x = inp["x"]
    shared = dict(
        w_in=np.ascontiguousarray(inp["w_in"][0]), w_out=np.ascontiguousarray(inp["w_out"][0]),
        w_glu=np.ascontiguousarray(inp["w_glu"][0]), w_gate=np.ascontiguousarray(inp["w_ffn_gate"][0]),
        w_up=np.ascontiguousarray(inp["w_ffn_up"][0]), w_down=np.ascontiguousarray(inp["w_ffn_down"][0]),
        cst=_consts(inp), p_kv=_kvals())
    bias_tabs = {f: _bias_table(inp["rpb"][0], f) for f in (False, True)}
    ssm = {}
    for h in (0, 1):
        dirs = [0, 1] if h == 1 else [1, 0]
        ls = np.broadcast_to(inp["ssm_log_step"][0][dirs][:, :, None], (2, 64, 64))
        ssm[h] = dict(
            p_are=_lay_gp(inp["ssm_a_re"][0][dirs]), p_aim=_lay_gp(inp["ssm_a_im"][0][dirs]), p_ls=_lay_gp(ls),
            p_bre=_lay_gp(inp["ssm_b_re"][0][dirs]), p_bim=_lay_gp(inp["ssm_b_im"][0][dirs]),
            p_cre=_lay_gp(inp["ssm_c_re"][0][dirs].transpose(0, 1, 3, 2)), p_cim=_lay_gp(inp["ssm_c_im"][0][dirs].transpose(0, 1, 3, 2)))
    maps = []
    for c in range(8):
        b, h = c // 2, c % 2
        flip = (h == 0)
        xl = x[b][::-1] if flip else x[b]
        m = dict(shared)
        m.update(ssm[h])
        m["xT"] = np.ascontiguousarray(xl.T)
        m["xown"] = np.ascontiguousarray(xl[2048:])
        m["biasd"] = bias_tabs[flip]
        maps.append(m)
    return maps


def assemble(results):
    out = np.empty((4, SEQ, DM), np.float32)
    for c in range(8):
        b, h = c // 2, c % 2
        o = np.asarray(results[c]["out"])
        if h == 0:
            out[b, 0:2048] = o[::-1]
        else:
            out[b, 2048:] = o
    return out


def kernel(**inputs):
    maps = prepare_inputs(inputs)
    nc = build_program("all")
    res = run_bass_kernel_spmd(nc, maps, core_ids=list(range(8)))
    return assemble(res.results)
```

```python
import numpy as np
from contextlib import ExitStack
import concourse.bass as bass
import concourse.mybir as mybir
from concourse.bass_utils import run_bass_kernel_spmd

F32 = mybir.dt.float32
BF16 = mybir.dt.bfloat16
I32 = mybir.dt.int32
ALU = mybir.AluOpType
ACT = mybir.ActivationFunctionType

DM = 2048
SEQ = 4096
NH = 16
DFF = 5632
NFT = DFF // 128
EPS = 1e-6
NEG = -30000.0
ENGS = ("pe", "act", "dve", "pool", "sp")
SAME_ENGINE_SYNC = True
DEBUG = {}


class Buf:
    __slots__ = ("w", "r")

    def __init__(self):
        self.w = None
        self.r = {}


class MBuf:
    __slots__ = ("ws",)

    def __init__(self):
        self.ws = []


class Prog:
    N_DSEM = 32

    def __init__(self, nc, stack):
        self.nc = nc
        self.ops = {e: [] for e in ENGS}
        self.cnt = {e: 0 for e in ENGS}
        self.sems = {e: stack.enter_context(nc.semaphore("s_" + e)) for e in ENGS}
        self.dsems = [stack.enter_context(nc.semaphore("d%d" % i)) for i in range(self.N_DSEM)]
        self.dcnt = [0] * self.N_DSEM
        self.dnext = [0, 0]
        self.waited = {e: {} for e in ENGS}

    def _deps(self, eng, reads, writes, ses=None):
        if ses is None:
            ses = SAME_ENGINE_SYNC
        deps = {}

        def add(tok):
            if tok is not None and deps.get(tok[0], 0) < tok[1]:
                deps[tok[0]] = tok[1]
        for b in reads:
            if isinstance(b, MBuf):
                for t in b.ws:
                    add(t)
            else:
                add(b.w)
        for b in writes:
            if isinstance(b, MBuf):
                continue
            add(b.w)
            for t in b.r.items():
                add(t)
        waits = []
        for k, v in deps.items():
            if k == eng and (eng == "pe" or not ses):
                continue
            if self.waited[eng].get(k, 0) >= v:
                continue
            self.waited[eng][k] = v
            waits.append((k, v))
        return waits

    @staticmethod
    def _mark(tok, reads, writes):
        for b in reads:
            if isinstance(b, MBuf):
                continue
            if b.r.get(tok[0], 0) < tok[1]:
                b.r[tok[0]] = tok[1]
        for b in writes:
            if isinstance(b, MBuf):
                b.ws.append(tok)
                continue
            b.w = tok
            b.r = {}

    def op(self, eng, fns, reads=(), writes=(), ses=None):
        if callable(fns):
            fns = [fns]
        waits = self._deps(eng, reads, writes, ses)
        self.cnt[eng] += 1
        tok = (eng, self.cnt[eng])
        self.ops[eng].append((waits, fns, (eng, 1)))
        self._mark(tok, reads, writes)
        return tok

    def dma(self, eng, out, in_, reads=(), writes=()):
        half = self.N_DSEM // 2
        which = 0 if eng == "pool" else 1
        i = which * half + self.dnext[which]
        self.dnext[which] = (self.dnext[which] + 1) % half
        key = "d%d" % i
        waits = self._deps(eng, reads, writes)
        if self.dcnt[i] > 0 and self.waited[eng].get(key, 0) < self.dcnt[i]:
            self.waited[eng][key] = self.dcnt[i]
            waits.append((key, self.dcnt[i]))
        self.dcnt[i] += 16
        tok = (key, self.dcnt[i])
        self.ops[eng].append((waits, [lambda e: e.dma_start(out=out, in_=in_)], (key, 16)))
        self._mark(tok, reads, writes)
        return tok

    def barrier(self):
        toks = [(e, self.cnt[e]) for e in ENGS if self.cnt[e] > 0]
        toks += [("d%d" % i, self.dcnt[i]) for i in range(self.N_DSEM) if self.dcnt[i] > 0]
        for eng in ENGS:
            waits = []
            for k, v in toks:
                if k == eng:
                    continue
                if self.waited[eng].get(k, 0) >= v:
                    continue
                self.waited[eng][k] = v
                waits.append((k, v))
            if waits:
                self.ops[eng].append((waits, [], None))

    def _sem(self, key):
        return self.sems[key] if key in self.sems else self.dsems[int(key[1:])]

    def emit(self):
        prog = self

        def run(engname):
            def body(e):
                for waits, fns, inc in prog.ops[engname]:
                    for k, v in waits:
                        e.wait_ge(prog._sem(k), v)
                    last = None
                    for f in fns:
                        last = f(e)
                    if inc is not None and last is not None:
                        last.then_inc(prog._sem(inc[0]), inc[1])
            return body
        with self.nc.Block() as block:
            block.tensor(run("pe"))
            block.scalar(run("act"))
            block.vector(run("dve"))
            block.gpsimd(run("pool"))
            block.sync(run("sp"))


class Ctx:
    pass


def _prod(s):
    n = 1
    for v in s:
        n *= v
    return n


def aview(K, off, shape, dt):
    esz = 4 if dt in (F32, I32) else 2
    n = _prod(shape)
    assert off % 4 == 0 and off + n * esz <= K.ARENA_BYTES, (off, shape, K.ARENA_BYTES)
    a = K.arena[:, off // 2: off // 2 + n * esz // 2]
    if dt != BF16:
        a = a.bitcast(dt)
    if len(shape) == 2:
        a = a.rearrange("p (a b) -> p a b", a=shape[0], b=shape[1])
    elif len(shape) == 3:
        a = a.rearrange("p (a b c) -> p a b c", a=shape[0], b=shape[1], c=shape[2])
    elif len(shape) == 4:
        a = a.rearrange("p (a b c d) -> p a b c d", a=shape[0], b=shape[1], c=shape[2], d=shape[3])
    return a


KB = 1024


def bc(ap, shape):
    return ap.broadcast_to(list(shape))


def phase_s1(K):
    P = K.P
    XB = [aview(K, 0, [16, 1024], BF16), aview(K, 32 * KB, [16, 1024], BF16)]
    bXB = [Buf(), Buf()]
    WU = aview(K, 64 * KB, [16, 1024], BF16); bWU = Buf()
    U2 = aview(K, 96 * KB, [64, 8, 16], BF16); bU2 = Buf()
    UB = [aview(K, 112 * KB, [64, 128], BF16), aview(K, 128 * KB, [64, 128], BF16)]
    bUB = [Buf(), Buf()]
    SQ = [aview(K, 144 * KB, [1024], BF16), aview(K, 146 * KB, [1024], BF16)]
    bSQ = [Buf(), Buf()]
    RS = aview(K, 148 * KB, [4, 8], F32); bRS = Buf()
    RREP = aview(K, 150 * KB, [1024], F32); bRR = Buf()
    bPS = K.bPS
    for q4 in range(4):
        P.dma("pool", WU[:, q4 * 4:(q4 + 1) * 4, :],
              K.w_in[q4 * 512:(q4 + 1) * 512, 3072:4096].rearrange("(kt p) c -> p kt c", p=128), writes=[bWU])
    for kt in range(16):
        P.op("dve", lambda e, kt=kt: e.tensor_scalar(WU[:, kt, :], WU[:, kt, :], K.gmix[:, kt:kt + 1], None, ALU.mult),
             reads=[bWU, K.bC], writes=[bWU])
    nps = 0
    for b in range(4):
        xb = XB[b % 2]; bx = bXB[b % 2]
        for q4 in range(4):
            P.dma("pool", xb[:, q4 * 4:(q4 + 1) * 4, :],
                  K.xT[q4 * 512:(q4 + 1) * 512, b * 1024:(b + 1) * 1024].rearrange("(kt p) t -> p kt t", p=128),
                  writes=[bx])
        pr0 = K.PSF[4]; pr1 = K.PSF[5]
        for kt in range(16):
            sq = SQ[kt % 2]; bs = bSQ[kt % 2]
            P.op("act", lambda e, kt=kt, sq=sq, xb=xb: e.activation(sq, xb[:, kt, :], ACT.Square), reads=[bx], writes=[bs])
            P.op("pe", [lambda e, kt=kt, sq=sq: e.matmul(pr0[:, :], lhsT=K.onesb[:, :], rhs=sq[:, 0:512], start=(kt == 0), stop=(kt == 15)),
                        lambda e, kt=kt, sq=sq: e.matmul(pr1[:, :], lhsT=K.onesb[:, :], rhs=sq[:, 512:1024], start=(kt == 0), stop=(kt == 15))],
                 reads=[bs, K.bC], writes=[bPS[4], bPS[5]])
        P.op("dve", lambda e: e.tensor_scalar(RREP[:, 0:512], pr0[:, :], 1.0 / DM, EPS, ALU.mult, ALU.add), reads=[bPS[4]], writes=[bRR])
        P.op("dve", lambda e: e.tensor_scalar(RREP[:, 512:1024], pr1[:, :], 1.0 / DM, EPS, ALU.mult, ALU.add), reads=[bPS[5]], writes=[bRR])
        P.op("act", lambda e: e.activation(RREP, RREP, ACT.Sqrt), reads=[bRR], writes=[bRR])
        P.op("dve", lambda e: e.reciprocal(RREP, RREP), reads=[bRR], writes=[bRR])
        fns = []
        for t in range(8):
            fns.append(lambda e, t=t: e.matmul(pr0[:, t:t + 1], lhsT=RREP[:, t::8], rhs=K.inv128[:, 0:1], start=True, stop=True))
        P.op("pe", fns, reads=[bRR, K.bC], writes=[bPS[4]])
        P.op("dve", lambda e, b=b: e.tensor_copy(RS[:, b, :], pr0[:, 0:8]), reads=[bPS[4]], writes=[bRS])
        for t in range(8):
            for ch in range(2):
                ps = K.PSF[nps % 4]; bp = bPS[nps % 4]; nps += 1
                fns = []
                for kt in range(16):
                    fns.append(lambda e, kt=kt, t=t, ch=ch, ps=ps, xb=xb: e.matmul(
                        ps[:, :], lhsT=xb[:, kt, t::8], rhs=WU[:, kt, ch * 512:(ch + 1) * 512], start=(kt == 0), stop=(kt == 15)))
                P.op("pe", fns, reads=[bx, bWU], writes=[bp])
                P.op("dve", lambda e, t=t, ch=ch, ps=ps, b=b: e.tensor_scalar(
                    U2[:, ch * 32:(ch + 1) * 32, t, :], ps[:, :].rearrange("p (g c) -> p g c", c=16), RS[:, b, t:t + 1], None, ALU.mult),
                    reads=[bp, bRS], writes=[bU2])
        ub = UB[b % 2]; bu = bUB[b % 2]
        for g8 in range(8):
            pt = K.PSB[g8 % 2]; bpt = bPS[6 + g8 % 2]
            fns = []
            for gl in range(8):
                g = g8 * 8 + gl
                fns.append(lambda e, g=g, gl=gl, pt=pt: e.transpose(pt[:, gl * 128:(gl + 1) * 128], U2[:, g, :, :], K.identb[:]))
            P.op("pe", fns, reads=[bU2, K.bC], writes=[bpt])
            P.op("act", lambda e, g8=g8, pt=pt, ub=ub: e.activation(
                ub[:, g8 * 8:(g8 + 1) * 8, :].rearrange("p g j -> p (g j)"), pt[:, :], ACT.Identity), reads=[bpt], writes=[bu])
        P.dma("sp", K.Ud[b], ub, reads=[bu], writes=[K.bUd])
        if "U" in DEBUG and b == 2:
            K.dbg("U", ub, [bu])


def ssm_tables(K):
    P = K.P
    T0 = 144 * KB
    o = [T0]

    def al(shape, dt=F32):
        v = aview(K, o[0], shape, dt)
        o[0] += _prod(shape) * 4
        return v
    K.ER = al([2, 32, 24]); K.EI = al([2, 32, 24])
    K.BBR = al([2, 32, 16]); K.BBI = al([2, 32, 16])
    K.MUA = al([2, 2, 32]); K.MUS = al([2, 2, 32])
    AR = al([2, 32]); AI = al([2, 32]); LS = al([2, 32]); DT = al([2, 32])
    ZR = al([2, 32]); ZI = al([2, 32]); E1R = al([2, 32]); E1I = al([2, 32])
    FR = al([2, 32]); FI = al([2, 32]); TA = al([2, 32]); TB = al([2, 32]); DEN = al([2, 32])
    KV = al([2, 24])
    assert o[0] <= 170 * KB, o[0]
    K.bTab = Buf()
    bT = K.bTab
    BR = aview(K, 64 * KB, [2, 32, 16], F32); BI = aview(K, 68 * KB, [2, 32, 16], F32)
    KZ = aview(K, 72 * KB, [2, 32, 24], F32)
    RR = aview(K, 80 * KB, [2, 2, 32, 24], F32)
    RI = aview(K, 92 * KB, [2, 2, 32, 24], I32)
    RF = aview(K, 104 * KB, [2, 2, 32, 24], F32)
    RG = aview(K, 116 * KB, [2, 2, 32, 24], F32)
    TMP = aview(K, 128 * KB, [2, 32, 16], F32)
    bB = Buf(); bW = Buf()
    P.dma("sp", AR, K.p_are, writes=[bT]); P.dma("sp", AI, K.p_aim, writes=[bT])
    P.dma("sp", LS, K.p_ls, writes=[bT]); P.dma("sp", KV, K.p_kv, writes=[bT])
    P.dma("sp", BR, K.p_bre, writes=[bB]); P.dma("sp", BI, K.p_bim, writes=[bB])
    f2 = lambda a: a.rearrange("p a b -> p (a b)")
    f3 = lambda a: a.rearrange("p a b c -> p (a b c)")
    f4 = lambda a: a.rearrange("p a b c d -> p (a b c d)")
    D = lambda fn, r, w: P.op("dve", fn, reads=r, writes=w)
    A = lambda fn, r, w: P.op("act", fn, reads=r, writes=w)
    A(lambda e: e.activation(f2(DT), f2(LS), ACT.Exp), [bT], [bT])
    D(lambda e: e.tensor_scalar(AR, AR, -1e-4, None, ALU.min), [bT], [bT])
    D(lambda e: e.tensor_tensor(ZR, AR, DT, ALU.mult), [bT], [bT])
    D(lambda e: e.tensor_tensor(ZI, AI, DT, ALU.mult), [bT], [bT])
    sh = [128, 2, 32, 24]
    D(lambda e: e.tensor_tensor(KZ, bc(ZR.unsqueeze(3), sh), bc(KV.unsqueeze(2), sh), ALU.mult), [bT], [bW])
    A(lambda e: e.activation(f3(KZ), f3(KZ), ACT.Exp), [bW], [bW])
    D(lambda e: e.tensor_scalar(ZI, ZI, 1.0 / (2 * np.pi), None, ALU.mult), [bT], [bT])
    D(lambda e: e.tensor_tensor(RR[:, 0], bc(ZI.unsqueeze(3), sh), bc(KV.unsqueeze(2), sh), ALU.mult), [bT], [bW])
    D(lambda e: e.tensor_scalar(RR[:, 1], RR[:, 0], 0.25, None, ALU.add), [bW], [bW])
    D(lambda e: e.tensor_copy(f4(RI), f4(RR)), [bW], [bW])
    D(lambda e: e.tensor_copy(f4(RF), f4(RI)), [bW], [bW])
    D(lambda e: e.tensor_tensor(f4(RR), f4(RR), f4(RF), ALU.subtract), [bW], [bW])
    D(lambda e: e.tensor_scalar(f4(RF), f4(RR), 0.5, None, ALU.is_gt), [bW], [bW])
    D(lambda e: e.tensor_scalar(f4(RG), f4(RR), -0.5, None, ALU.is_lt), [bW], [bW])
    D(lambda e: e.tensor_tensor(f4(RR), f4(RR), f4(RF), ALU.subtract), [bW], [bW])
    D(lambda e: e.tensor_tensor(f4(RR), f4(RR), f4(RG), ALU.add), [bW], [bW])
    A(lambda e: e.activation(f4(RR), f4(RR), ACT.Sin, scale=6.283185), [bW], [bW])
    D(lambda e: e.tensor_tensor(K.ER, KZ, RR[:, 1], ALU.mult), [bW], [bT])
    D(lambda e: e.tensor_tensor(K.EI, KZ, RR[:, 0], ALU.mult), [bW], [bT])
    for dd, i1, i8 in ((0, 8, 15), (1, 1, 8)):
        D(lambda e, dd=dd, i1=i1: e.tensor_copy(E1R[:, dd, :], K.ER[:, dd, :, i1]), [bT], [bT])
        D(lambda e, dd=dd, i1=i1: e.tensor_copy(E1I[:, dd, :], K.EI[:, dd, :, i1]), [bT], [bT])
        D(lambda e, dd=dd, i8=i8: e.tensor_copy(K.MUA[:, dd, 0, :], K.ER[:, dd, :, i8]), [bT], [bT])
        D(lambda e, dd=dd, i8=i8: e.tensor_copy(K.MUA[:, dd, 1, :], K.ER[:, dd, :, i8]), [bT], [bT])
        D(lambda e, dd=dd, i8=i8: e.tensor_copy(K.MUS[:, dd, 1, :], K.EI[:, dd, :, i8]), [bT], [bT])
        D(lambda e, dd=dd, i8=i8: e.tensor_scalar(K.MUS[:, dd, 0, :], K.EI[:, dd, :, i8], -1.0, None, ALU.mult), [bT], [bT])
    D(lambda e: e.tensor_scalar(E1R, E1R, -1.0, None, ALU.add), [bT], [bT])
    D(lambda e: e.tensor_tensor(DEN, AR, AR, ALU.mult), [bT], [bT])
    D(lambda e: e.tensor_tensor(TA, AI, AI, ALU.mult), [bT], [bT])
    D(lambda e: e.tensor_tensor(DEN, DEN, TA, ALU.add), [bT], [bT])
    D(lambda e: e.reciprocal(DEN, DEN), [bT], [bT])
    D(lambda e: e.tensor_tensor(TA, E1R, AR, ALU.mult), [bT], [bT])
    D(lambda e: e.tensor_tensor(TB, E1I, AI, ALU.mult), [bT], [bT])
    D(lambda e: e.tensor_tensor(FR, TA, TB, ALU.add), [bT], [bT])
    D(lambda e: e.tensor_tensor(FR, FR, DEN, ALU.mult), [bT], [bT])
    D(lambda e: e.tensor_tensor(TA, E1I, AR, ALU.mult), [bT], [bT])
    D(lambda e: e.tensor_tensor(TB, E1R, AI, ALU.mult), [bT], [bT])
    D(lambda e: e.tensor_tensor(FI, TA, TB, ALU.subtract), [bT], [bT])
    D(lambda e: e.tensor_tensor(FI, FI, DEN, ALU.mult), [bT], [bT])
    s3 = [128, 2, 32, 16]
    D(lambda e: e.tensor_tensor(K.BBR, bc(FR.unsqueeze(3), s3), BR, ALU.mult), [bT, bB], [bT])
    D(lambda e: e.tensor_tensor(TMP, bc(FI.unsqueeze(3), s3), BI, ALU.mult), [bT, bB], [bW])
    D(lambda e: e.tensor_tensor(K.BBR, K.BBR, TMP, ALU.subtract), [bT, bW], [bT])
    D(lambda e: e.tensor_tensor(K.BBI, bc(FR.unsqueeze(3), s3), BI, ALU.mult), [bT, bB], [bT])
    D(lambda e: e.tensor_tensor(TMP, bc(FI.unsqueeze(3), s3), BR, ALU.mult), [bT, bB], [bW])
    D(lambda e: e.tensor_tensor(K.BBI, K.BBI, TMP, ALU.add), [bT, bW], [bT])


def cplx_outer(K, eng, OR, OI, er, ei, xr, xi, t1, bufs_r, bufs_w, neg_im=False):
    P = K.P
    op = lambda fn: P.op(eng, fn, reads=bufs_r, writes=bufs_w)
    op(lambda e: e.tensor_tensor(OR, er, xr, ALU.mult))
    op(lambda e: e.tensor_tensor(t1, ei, xi, ALU.mult))
    op(lambda e: e.tensor_tensor(OR, OR, t1, ALU.subtract))
    op(lambda e: e.tensor_tensor(OI, er, xi, ALU.mult))
    op(lambda e: e.tensor_tensor(t1, ei, xr, ALU.mult))
    if neg_im:
        op(lambda e: e.scalar_tensor_tensor(OI, OI, -1.0, t1, ALU.mult, ALU.subtract))
    else:
        op(lambda e: e.tensor_tensor(OI, OI, t1, ALU.add))


def q_chunk(K, eng, dd, gc, QR, QI, T1, bQ):
    sh = [128, 4, 8, 16]
    gs = slice(gc * 4, gc * 4 + 4)
    er = bc(K.ER[:, dd, gs, 0:8].unsqueeze(3), sh); ei = bc(K.EI[:, dd, gs, 0:8].unsqueeze(3), sh)
    xr = bc(K.BBR[:, dd, gs, :].unsqueeze(2), sh); xi = bc(K.BBI[:, dd, gs, :].unsqueeze(2), sh)
    cplx_outer(K, eng, QR, QI, er, ei, xr, xi, T1, [K.bTab], [bQ])


def phase_gen1(K):
    P = K.P
    K.W1Z = aview(K, 0, [2, 2, 64, 128], BF16); K.bW1 = Buf()
    for q4 in range(4):
        P.op("pool", lambda e, q4=q4: e.memset(K.W1Z[:, q4 // 2, q4 % 2], 0.0), writes=[K.bW1])
    ssm_tables(K)
    QR = aview(K, 178 * KB, [4, 8, 16], F32); QI = aview(K, 180 * KB, [4, 8, 16], F32)
    T1 = aview(K, 182 * KB, [4, 8, 16], F32)
    bQ = Buf()
    n = 0
    for dd in range(2):
        for gc in range(8):
            q_chunk(K, "dve", dd, gc, QR, QI, T1, bQ)
            for ri, Q in ((0, QR), (1, QI)):
                ps = K.PSF[n % 2]; bp = K.bPS[n % 2]; n += 1
                fns = []
                for gl in range(4):
                    fns.append(lambda e, gl=gl, Q=Q, ps=ps: e.transpose(
                        ps[:, gl * 128:(gl + 1) * 128], Q[:, gl].rearrange("p s c -> p (s c)"), K.ident[:]))
                P.op("pe", fns, reads=[bQ, K.bC], writes=[bp])
                g0 = gc * 8
                psv = ps[:, :].rearrange("p (g q) -> p g q", q=128)
                P.op("dve", lambda e, dd=dd, ri=ri, g0=g0, psv=psv: e.tensor_copy(
                    K.W1Z[:, dd, ri, g0:g0 + 8:2, 0:64], psv[:, :, 0:64]), reads=[bp], writes=[K.bW1])
                P.op("dve", lambda e, dd=dd, ri=ri, g0=g0, psv=psv: e.tensor_copy(
                    K.W1Z[:, dd, ri, g0 + 1:g0 + 8:2, 64:128], psv[:, :, 64:128]), reads=[bp], writes=[K.bW1])


def phase_s2(K):
    P = K.P
    S8 = {0: aview(K, 64 * KB, [2, 32, 128], F32), 1: aview(K, 96 * KB, [2, 32, 128], F32)}
    bS8 = {0: Buf(), 1: Buf()}
    UBL = {0: aview(K, 128 * KB, [64, 128], BF16), 1: aview(K, 170 * KB, [32, 128], BF16)}
    bUL = {0: Buf(), 1: Buf()}
    base = 184 * KB
    TT = {}
    for dd in range(2):
        o = base + dd * 2 * KB
        TT[dd] = dict(t1=aview(K, o, [2, 32], F32), t2=aview(K, o + 256, [2, 32], F32),
                      car=aview(K, o + 512, [2, 32], F32), carb=aview(K, o + 768, [2, 32], BF16))
    eng_of = {0: "dve", 1: "pool"}
    bT = {0: Buf(), 1: Buf()}
    for dd in range(2):
        P.op(eng_of[dd], lambda e, dd=dd: e.memset(TT[dd]["car"], 0.0), writes=[bT[dd]])

    def put_carry(dd, blk):
        eng = eng_of[dd]
        P.op(eng, lambda e: e.tensor_copy(TT[dd]["carb"], TT[dd]["car"]), reads=[bT[dd]], writes=[bT[dd]])
        P.dma("sp" if dd == 0 else "pool", K.Hc[dd, blk], TT[dd]["carb"], reads=[bT[dd]], writes=[K.bHd])

    def l1_group(dd, ri, gc, n, ubuf, gmod):
        ps = K.PSF[dd * 2 + n % 2]; bp = K.bPS[dd * 2 + n % 2]
        fns = []
        for gl in range(4):
            gp = gc * 4 + gl
            for g2 in range(2):
                g = 2 * gp + g2
                fns.append(lambda e, gl=gl, g=g, g2=g2, ps=ps: e.matmul(
                    ps[:, gl * 128:(gl + 1) * 128], lhsT=K.W1Z[:, dd, ri, g, :], rhs=ubuf[:, g % gmod, :],
                    start=(g2 == 0), stop=(g2 == 1)))
        P.op("pe", fns, reads=[bUL[dd], K.bW1], writes=[bp])
        P.op("act", lambda e, ps=ps: e.activation(
            S8[dd][:, ri, gc * 4:(gc + 1) * 4, :].rearrange("p g j -> p (g j)"), ps[:, :], ACT.Identity),
            reads=[bp], writes=[bS8[dd]])

    def level1_A(b):
        P.dma("sp", UBL[0], K.Ud[b], reads=[K.bUd], writes=[bUL[0]])
        n = 0
        for ri in range(2):
            for gc in range(8):
                l1_group(0, ri, gc, n, UBL[0], 64); n += 1

    def level1_B(b):
        n = 0
        for half in range(2):
            P.dma("pool", UBL[1], K.Ud[b][:, half * 32:(half + 1) * 32, :], reads=[K.bUd], writes=[bUL[1]])
            for ri in range(2):
                for gcl in range(4):
                    l1_group(1, ri, half * 4 + gcl, n, UBL[1], 32); n += 1

    def scan(dd, order):
        eng = eng_of[dd]
        s8 = S8[dd]; tt = TT[dd]
        mua = K.MUA[:, dd]; mus = K.MUS[:, dd]
        bs = bS8[dd]; bt = bT[dd]
        prevj = None
        for jj in order:
            prev = tt["car"] if prevj is None else s8[:, :, :, prevj]
            cur = s8[:, :, :, jj]
            rd = [bs, bt, K.bTab]
            P.op(eng, lambda e, prev=prev: e.tensor_tensor(tt["t1"], mua, prev, ALU.mult), reads=rd, writes=[bt], ses=False)
            P.op(eng, lambda e, prev=prev: e.tensor_tensor(tt["t2"][:, 0], mus[:, 0], prev[:, 1], ALU.mult), reads=rd, writes=[bt], ses=False)
            P.op(eng, lambda e, prev=prev: e.tensor_tensor(tt["t2"][:, 1], mus[:, 1], prev[:, 0], ALU.mult), reads=rd, writes=[bt], ses=False)
            P.op(eng, lambda e: e.tensor_tensor(tt["t1"], tt["t1"], tt["t2"], ALU.add), reads=[bt], writes=[bt], ses=False)
            P.op(eng, lambda e, cur=cur: e.tensor_tensor(cur, cur, tt["t1"], ALU.add), reads=[bs, bt], writes=[bs], ses=False)
            prevj = jj
        P.op(eng, lambda e: e.tensor_copy(tt["car"], s8[:, :, :, prevj]), reads=[bs, bt], writes=[bt], ses=False)

    def store_H(dd, blk):
        eng = eng_of[dd]
        s8 = S8[dd]
        if dd == 0:
            hb = UBL[0][:, :, :].rearrange("p (r g) j -> p r g j", r=2)
            P.op(eng, lambda e: e.tensor_copy(hb, s8), reads=[bS8[0]], writes=[bUL[0]])
            P.dma("sp", K.Hd[0, blk], hb, reads=[bUL[0]], writes=[K.bHd])
        else:
            for ri in range(2):
                hb = UBL[1]
                P.op(eng, lambda e, ri=ri: e.tensor_copy(hb, s8[:, ri]), reads=[bS8[1]], writes=[bUL[1]])
                P.dma("pool", K.Hd[1, blk][:, ri], hb, reads=[bUL[1]], writes=[K.bHd])

    def do_A(b):
        level1_A(b)
        if b >= 2:
            put_carry(0, b - 2)
        scan(0, range(128))
        if b >= 2:
            store_H(0, b - 2)

    def do_B(b):
        level1_B(b)
        put_carry(1, b - 2)
        scan(1, range(127, -1, -1))
        store_H(1, b - 2)

    level1_A(0)
    do_B(3)
    scan(0, range(128))
    do_A(1)
    do_A(2)
    do_B(2)
    do_A(3)


def phase_gen2(K):
    P = K.P
    CR = aview(K, 64 * KB, [2, 32, 16], F32); CI = aview(K, 68 * KB, [2, 32, 16], F32); bCc = Buf()
    P.dma("sp", CR, K.p_cre, writes=[bCc]); P.dma("sp", CI, K.p_cim, writes=[bCc])
    o = [0]

    def al(shape, dt=F32):
        v = aview(K, o[0], shape, dt)
        o[0] += _prod(shape) * (4 if dt == F32 else 2)
        return v
    QR = al([4, 8, 16]); QI = al([4, 8, 16]); T1 = al([4, 8, 16])
    W3R = al([4, 8, 16]); W3I = al([4, 8, 16])
    WPR = {0: al([4, 128]), 1: al([4, 128])}; WPI = {0: al([4, 128]), 1: al([4, 128])}
    QZR = {0: al([8, 128]), 1: al([8, 128])}; QZI = {0: al([8, 128]), 1: al([8, 128])}
    W3S = [al([2, 2, 8, 128], BF16), al([2, 2, 8, 128], BF16)]
    TT1 = al([8, 128]); TT2 = al([8, 128])
    TC = [al([8, 128], BF16), al([8, 128], BF16)]
    assert o[0] <= 64 * KB
    bQ = Buf(); bW = Buf(); bWP = Buf(); bQZ = Buf(); bS = [Buf(), Buf()]; bTT = Buf(); bTC = [Buf(), Buf()]
    sh = [128, 4, 8, 16]
    for gc in range(8):
        gs = slice(gc * 4, gc * 4 + 4)
        w3s = W3S[gc % 2]; bs = bS[gc % 2]
        for dd in range(2):
            q_chunk(K, "dve", dd, gc, QR, QI, T1, bQ)
            for gl in range(4):
                for g2 in range(2):
                    msk = K.m01[:, g2:g2 + 1]
                    P.op("dve", lambda e, gl=gl, g2=g2, msk=msk, dd=dd: e.tensor_scalar(
                        QZR[dd][:, 2 * gl + g2, :], QR[:, gl].rearrange("p s c -> p (s c)"), msk, None, ALU.mult),
                        reads=[bQ, K.bC], writes=[bQZ])
                    P.op("dve", lambda e, gl=gl, g2=g2, msk=msk, dd=dd: e.tensor_scalar(
                        QZI[dd][:, 2 * gl + g2, :], QI[:, gl].rearrange("p s c -> p (s c)"), msk, None, ALU.mult),
                        reads=[bQ, K.bC], writes=[bQZ])
            cr = bc(CR[:, dd, gs, :].unsqueeze(2), sh); ci = bc(CI[:, dd, gs, :].unsqueeze(2), sh)
            er = bc(K.ER[:, dd, gs, 16:24].unsqueeze(3), sh); ei = bc(K.EI[:, dd, gs, 16:24].unsqueeze(3), sh)
            wpr = WPR[dd][:, :, :].rearrange("p g (t c) -> p g t c", c=16)
            wpi = WPI[dd][:, :, :].rearrange("p g (t c) -> p g t c", c=16)
            cplx_outer(K, "dve", wpr, wpi, er, ei, cr, ci, T1, [K.bTab, bCc], [bWP], neg_im=True)
            er = bc(K.ER[:, dd, gs, 8:16].unsqueeze(3), sh); ei = bc(K.EI[:, dd, gs, 8:16].unsqueeze(3), sh)
            cplx_outer(K, "dve", W3R, W3I, er, ei, cr, ci, T1, [K.bTab, bCc], [bW], neg_im=True)
            for ri, W in ((0, W3R), (1, W3I)):
                for g2 in range(2):
                    P.op("dve", lambda e, ri=ri, W=W, g2=g2, dd=dd, w3s=w3s: e.tensor_scalar(
                        w3s[:, dd, ri, g2:8:2, :], W[:, :, :, :].rearrange("p g t c -> p g (t c)"), K.m01[:, g2:g2 + 1], None, ALU.mult),
                        reads=[bW, K.bC], writes=[bs])
        P.dma("sp", K.W3d[:, :, :, gc * 8:(gc + 1) * 8, :], w3s, reads=[bs], writes=[K.bW3d])
        for dd in range(2):
            for hb in range(2):
                ps = K.PSF[dd * 2 + hb]; bp = K.bPS[dd * 2 + hb]
                fns = []
                for gq in range(4):
                    g = hb * 4 + gq
                    gl = g // 2
                    fns.append(lambda e, g=g, gl=gl, gq=gq, ps=ps, dd=dd: e.matmul(
                        ps[:, gq * 128:(gq + 1) * 128], lhsT=QZR[dd][:, g, :], rhs=WPR[dd][:, gl, :], start=True, stop=False))
                    fns.append(lambda e, g=g, gl=gl, gq=gq, ps=ps, dd=dd: e.matmul(
                        ps[:, gq * 128:(gq + 1) * 128], lhsT=QZI[dd][:, g, :], rhs=WPI[dd][:, gl, :], start=False, stop=True))
                P.op("pe", fns, reads=[bQZ, bWP], writes=[bp])
        s4 = [128, 4, 128]
        for hb in range(2):
            pa = K.PSF[hb][:, :].rearrange("p (g q) -> p g q", q=128)
            pb = K.PSF[2 + hb][:, :].rearrange("p (g q) -> p g q", q=128)
            P.op("dve", lambda e, hb=hb, pa=pa: e.tensor_tensor(TT1[:, hb * 4:(hb + 1) * 4, :], pa, bc(K.maskF[:, :].unsqueeze(1), s4), ALU.mult),
                 reads=[K.bPS[hb], K.bC], writes=[bTT])
            P.op("dve", lambda e, hb=hb, pb=pb: e.tensor_tensor(TT2[:, hb * 4:(hb + 1) * 4, :], pb, bc(K.maskB[:, :].unsqueeze(1), s4), ALU.mult),
                 reads=[K.bPS[2 + hb], K.bC], writes=[bTT])
        P.op("dve", lambda e: e.tensor_tensor(TT1, TT1, TT2, ALU.add), reads=[bTT], writes=[bTT])
        tc_ = TC[gc % 2]; btc = bTC[gc % 2]
        for g in range(8):
            gg = gc * 8 + g
            P.op("dve", lambda e, g=g, gg=gg, tc_=tc_: e.scalar_tensor_tensor(
                tc_[:, g, :], K.ident[:], K.dskip[:, gg:gg + 1], TT1[:, g, :], ALU.mult, ALU.add),
                reads=[bTT, K.bC], writes=[btc])
        P.dma("sp", K.Td[:, gc * 8:(gc + 1) * 8, :], tc_, reads=[btc], writes=[K.bTd])
        if "T" in DEBUG and gc == 0:
            K.dbg("T", tc_, [btc])


def phase_s4(K):
    P = K.P
    CH = []
    for i in range(2):
        o = i * 16 * KB
        CH.append(dict(U=aview(K, o, [8, 128], BF16), T=aview(K, o + 2 * KB, [8, 128], BF16),
                       W3=aview(K, o + 4 * KB, [2, 2, 8, 128], BF16),
                       HA=aview(K, o + 12 * KB, [2, 4, 128], BF16), HB=aview(K, o + 14 * KB, [2, 4, 128], BF16),
                       HAc=aview(K, 168 * KB + i * 64, [2, 4], BF16), HBc=aview(K, 168 * KB + 32 + i * 64, [2, 4], BF16), b=Buf()))
    YSG = [aview(K, 32 * KB, [1024], F32), aview(K, 36 * KB, [1024], F32)]; bYSG = [Buf(), Buf()]
    Y2C = aview(K, 40 * KB, [8, 128], F32); G1 = aview(K, 44 * KB, [8, 128], F32)
    G2 = aview(K, 48 * KB, [8, 128], F32); bG = Buf()
    YG = aview(K, 56 * KB, [8, 1024], BF16); bYG = Buf()
    YGT = aview(K, 72 * KB, [8, 1024], BF16); bYGT = Buf()
    WGL = aview(K, 88 * KB, [8, 1024], BF16); bWGL = Buf()
    GT = [aview(K, 104 * KB, [512], F32), aview(K, 106 * KB, [512], F32)]; bGT = [Buf(), Buf()]
    YS = aview(K, 112 * KB, [8, 1024], F32); bYS = Buf()
    YSN = [aview(K, 144 * KB, [1024], BF16), aview(K, 146 * KB, [1024], BF16)]; bYSN = [Buf(), Buf()]
    YST = aview(K, 148 * KB, [8, 1024], BF16); bYST = Buf()
    SS = aview(K, 164 * KB, [8], F32); JNK = aview(K, 165 * KB, [1024], BF16); bSS = Buf()
    for q2 in range(2):
        P.dma("pool", WGL[:, q2 * 4:(q2 + 1) * 4, :], K.w_glu[q2 * 512:(q2 + 1) * 512, :].rearrange("(kt p) c -> p kt c", p=128), writes=[bWGL])
    nch = 0
    for blk in range(2):
        for gc in range(8):
            ch = CH[nch % 2]; nch += 1
            bch = ch["b"]
            P.dma("sp", ch["U"], K.Ud[2 + blk][:, gc * 8:(gc + 1) * 8, :], reads=[K.bUd], writes=[bch])
            P.dma("sp", ch["T"], K.Td[:, gc * 8:(gc + 1) * 8, :], reads=[K.bTd], writes=[bch])
            P.dma("sp", ch["W3"], K.W3d[:, :, :, gc * 8:(gc + 1) * 8, :], reads=[K.bW3d], writes=[bch])
            P.dma("sp", ch["HA"], K.Hd[0, blk][:, :, gc * 4:(gc + 1) * 4, :], reads=[K.bHd], writes=[bch])
            P.dma("sp", ch["HB"], K.Hd[1, blk][:, :, gc * 4:(gc + 1) * 4, :], reads=[K.bHd], writes=[bch])
            P.dma("sp", ch["HAc"], K.Hc[0, blk][:, :, gc * 4:(gc + 1) * 4], reads=[K.bHd], writes=[bch])
            P.dma("sp", ch["HBc"], K.Hc[1, blk][:, :, gc * 4:(gc + 1) * 4], reads=[K.bHd], writes=[bch])
            ysg = YSG[gc % 2]; bys = bYSG[gc % 2]
            for hb in range(2):
                ps = K.PSF[hb]; bp = K.bPS[hb]
                fns = []
                for gq in range(4):
                    g = hb * 4 + gq
                    gpl = g // 2
                    o_ = ps[:, gq * 128:(gq + 1) * 128]
                    fns.append(lambda e, o_=o_, g=g, ch=ch: e.matmul(o_, lhsT=ch["T"][:, g, :], rhs=ch["U"][:, g, :], start=True, stop=False))
                    for ri in range(2):
                        fns.append(lambda e, o_=o_, g=g, gpl=gpl, ch=ch, ri=ri: e.matmul(
                            o_[:, 1:128], lhsT=ch["W3"][:, 0, ri, g, :], rhs=ch["HA"][:, ri, gpl, 0:127], start=False, stop=False))
                        fns.append(lambda e, o_=o_, g=g, gpl=gpl, ch=ch, ri=ri: e.matmul(
                            o_[:, 0:1], lhsT=ch["W3"][:, 0, ri, g, :], rhs=ch["HAc"][:, ri, gpl:gpl + 1], start=False, stop=False))
                        fns.append(lambda e, o_=o_, g=g, gpl=gpl, ch=ch, ri=ri: e.matmul(
                            o_[:, 0:127], lhsT=ch["W3"][:, 1, ri, g, :], rhs=ch["HB"][:, ri, gpl, 1:128], start=False, stop=False))
                        fns.append(lambda e, o_=o_, g=g, gpl=gpl, ch=ch, ri=ri: e.matmul(
                            o_[:, 127:128], lhsT=ch["W3"][:, 1, ri, g, :], rhs=ch["HBc"][:, ri, gpl:gpl + 1], start=False, stop=(ri == 1)))
                P.op("pe", fns, reads=[bch], writes=[bp])
                P.op("act", lambda e, hb=hb, ps=ps, ysg=ysg: e.activation(ysg[:, hb * 512:(hb + 1) * 512], ps[:, :], ACT.Identity),
                     reads=[bp], writes=[bys])
            for hb in range(2):
                ps = K.PSF[2 + hb]; bp = K.bPS[2 + hb]
                fns = []
                for gq in range(4):
                    g = hb * 4 + gq
                    fns.append(lambda e, gq=gq, g=g, ps=ps, ysg=ysg: e.transpose(ps[:, gq * 128:(gq + 1) * 128], ysg[:, g * 128:(g + 1) * 128], K.ident[:]))
                P.op("pe", fns, reads=[bys, K.bC], writes=[bp])
                P.op("dve", lambda e, hb=hb, ps=ps: e.tensor_copy(
                    Y2C[:, :, hb * 64:(hb + 1) * 64].rearrange("p t (g c) -> p g t c", c=16),
                    ps[:, :].rearrange("p (g t c) -> p g t c", t=8, c=16)), reads=[bp], writes=[bG])
            if "ypre" in DEBUG and blk == 0 and gc == 0:
                K.dbg("ypre", Y2C, [bG])
            P.op("dve", lambda e: e.tensor_tensor(G1, Y2C, Y2C, ALU.mult), reads=[bG], writes=[bG])
            P.op("dve", lambda e: e.tensor_scalar(G1, G1, 0.044715, 1.0, ALU.mult, ALU.add), reads=[bG], writes=[bG])
            P.op("dve", lambda e: e.tensor_tensor(G1, G1, Y2C, ALU.mult), reads=[bG], writes=[bG])
            P.op("act", lambda e: e.activation(G2[:, :, :].rearrange("p t c -> p (t c)"), G1[:, :, :].rearrange("p t c -> p (t c)"),
                                               ACT.Sigmoid, scale=1.5957691216057308), reads=[bG], writes=[bG])
            P.op("dve", lambda e, gc=gc: e.tensor_tensor(YS[:, :, gc * 128:(gc + 1) * 128], Y2C, G2, ALU.mult), reads=[bG], writes=[bYS])
            P.op("dve", lambda e, gc=gc: e.tensor_copy(YG[:, :, gc * 128:(gc + 1) * 128], YS[:, :, gc * 128:(gc + 1) * 128]), reads=[bYS], writes=[bYG])
            pt = K.PSB[gc % 2]; bpt = K.bPS[6 + gc % 2]
            fns = []
            for t in range(8):
                fns.append(lambda e, t=t, gc=gc, pt=pt: e.transpose(pt[:, t * 128:(t + 1) * 128], YG[:, t, gc * 128:(gc + 1) * 128], K.identb[:]))
            P.op("pe", fns, reads=[bYG, K.bC], writes=[bpt])
            P.op("dve", lambda e, gc=gc, pt=pt: e.tensor_copy(
                YGT[:, gc, :].rearrange("p (j t) -> p t j", t=8), pt[:, :].rearrange("p (t j) -> p t j", j=128)), reads=[bpt], writes=[bYGT])
        if "yg" in DEBUG and blk == 0:
            K.dbg("yg", YS, [bYS])
        n = 0
        for t in range(8):
            for hf in range(2):
                ps = K.PSF[n % 2]; bp = K.bPS[n % 2]
                gt = GT[n % 2]; bgt = bGT[n % 2]; n += 1
                fns = []
                for kt in range(8):
                    fns.append(lambda e, kt=kt, t=t, hf=hf, ps=ps: e.matmul(
                        ps[:, :], lhsT=YGT[:, kt, t::8], rhs=WGL[:, kt, hf * 512:(hf + 1) * 512], start=(kt == 0), stop=(kt == 7)))
                P.op("pe", fns, reads=[bYGT, bWGL], writes=[bp])
                P.op("dve", lambda e, hf=hf, ps=ps, gt=gt: e.tensor_tensor(gt, ps[:, :], K.bglu[:, hf * 512:(hf + 1) * 512], ALU.add),
                     reads=[bp, K.bC], writes=[bgt])
                P.op("act", lambda e, gt=gt: e.activation(gt, gt, ACT.Sigmoid), reads=[bgt], writes=[bgt])
                P.op("dve", lambda e, t=t, hf=hf, gt=gt: e.tensor_tensor(
                    YS[:, t, hf * 512:(hf + 1) * 512], YS[:, t, hf * 512:(hf + 1) * 512], gt, ALU.mult), reads=[bgt, bYS], writes=[bYS])
        if "ys" in DEBUG and blk == 0:
            K.dbg("ys", YS, [bYS])
        for t in range(8):
            P.op("act", lambda e, t=t: e.activation(JNK, YS[:, t, :], ACT.Square, accum_out=SS[:, t:t + 1]), reads=[bYS], writes=[bSS])
        P.op("dve", lambda e: e.tensor_scalar(SS, SS, 1.0 / 1024, EPS, ALU.mult, ALU.add), reads=[bSS], writes=[bSS])
        P.op("act", lambda e: e.activation(SS, SS, ACT.Sqrt), reads=[bSS], writes=[bSS])
        P.op("dve", lambda e: e.reciprocal(SS, SS), reads=[bSS], writes=[bSS])
        for t in range(8):
            ysn = YSN[t % 2]; bysn = bYSN[t % 2]
            P.op("dve", lambda e, t=t, ysn=ysn: e.tensor_scalar(ysn, YS[:, t, :], SS[:, t:t + 1], None, ALU.mult), reads=[bYS, bSS], writes=[bysn])
            pt = K.PSB[t % 2]; bpt = K.bPS[6 + t % 2]
            fns = []
            for kt in range(8):
                fns.append(lambda e, kt=kt, pt=pt, ysn=ysn: e.transpose(pt[:, kt * 128:(kt + 1) * 128], ysn[:, kt * 128:(kt + 1) * 128], K.identb[:]))
            P.op("pe", fns, reads=[bysn, K.bC], writes=[bpt])
            P.op("dve", lambda e, t=t, pt=pt: e.tensor_copy(YST[:, :, t::8], pt[:, :].rearrange("p (k j) -> p k j", j=128)), reads=[bpt], writes=[bYST])
        P.dma("sp", K.mixT[:, 8:16, blk * 1024:(blk + 1) * 1024], YST, reads=[bYST], writes=[K.bmix])


def phase_a(K):
    P = K.P
    QT = aview(K, 0, [8, 2048], BF16); bQT = Buf()
    KT = aview(K, 32 * KB, [8, 2304], BF16); bKT = Buf()
    VV = aview(K, 68 * KB, [18, 16, 80], BF16); bVV = Buf()
    A0 = 68 * KB + 18 * 16 * 80 * 2
    A0 = (A0 + 63) // 64 * 64
    XC = [aview(K, A0, [16, 512], BF16), aview(K, A0 + 16 * KB, [16, 512], BF16)]; bXC = [Buf(), Buf()]
    WH = aview(K, A0 + 32 * KB, [16, 512], BF16); bWH = Buf()
    o = A0 + 48 * KB
    RX = aview(K, o, [2304], F32); RX2 = aview(K, o + 9 * KB, [2304], F32); bRX = Buf()
    o += 18 * KB
    SQ = [aview(K, o, [512], BF16), aview(K, o + 1 * KB, [512], BF16)]; bSQ = [Buf(), Buf()]
    MS = [aview(K, o + 2 * KB, [512], F32), aview(K, o + 4 * KB, [512], F32)]; bMS = [Buf(), Buf()]
    RK = aview(K, o + 6 * KB, [18], F32); bRK = Buf()
    assert o + 7 * KB <= K.ARENA_BYTES, o
    chunks = [(0, 256), (256, 512), (768, 512), (1280, 512), (1792, 512)]

    def load_x(ci, nb):
        c0, cw = chunks[ci]
        xc = XC[nb % 2]; bx = bXC[nb % 2]
        for q2 in range(2):
            P.dma("pool", xc[:, q2 * 8:(q2 + 1) * 8, 0:cw],
                  K.xT[q2 * 1024:(q2 + 1) * 1024, 1792 + c0:1792 + c0 + cw].rearrange("(kt p) t -> p kt t", p=128), writes=[bx])
        return xc, bx
    nb = 0
    P.op("dve", lambda e: e.memset(VV[:, :, :, 64:65], 1.0), writes=[bVV])
    for ci, (c0, cw) in enumerate(chunks):
        xc, bx = load_x(ci, nb); nb += 1
        ps = K.PSF[0]; bp = K.bPS[0]
        for kt in range(16):
            sq = SQ[kt % 2]; bs = bSQ[kt % 2]
            P.op("act", lambda e, kt=kt, sq=sq, xc=xc, cw=cw: e.activation(sq[:, 0:cw], xc[:, kt, 0:cw], ACT.Square), reads=[bx], writes=[bs])
            P.op("pe", lambda e, kt=kt, sq=sq, cw=cw: e.matmul(ps[:, 0:cw], lhsT=K.onesb[:, :], rhs=sq[:, 0:cw], start=(kt == 0), stop=(kt == 15)),
                 reads=[bs, K.bC], writes=[bp])
        P.op("dve", lambda e, c0=c0, cw=cw: e.tensor_scalar(RX[:, c0:c0 + cw], ps[:, 0:cw], 1.0 / DM, EPS, ALU.mult, ALU.add), reads=[bp], writes=[bRX])
        P.op("dve", lambda e, c0=c0, cw=cw: e.tensor_scalar(RX2[:, c0:c0 + cw], RX[:, c0:c0 + cw], EPS, None, ALU.mult), reads=[bRX], writes=[bRX])
    P.op("act", lambda e: e.activation(RX, RX, ACT.Sqrt), reads=[bRX], writes=[bRX])
    P.op("dve", lambda e: e.reciprocal(RX, RX), reads=[bRX], writes=[bRX])
    pk = K.PSF[1]; bpk = K.bPS[1]
    fns = []
    for tile in range(18):
        fns.append(lambda e, tile=tile: e.matmul(pk[:, tile:tile + 1], lhsT=RX[:, tile * 128:(tile + 1) * 128], rhs=K.inv128[:, 0:1], start=True, stop=True))
    P.op("pe", fns, reads=[bRX, K.bC], writes=[bpk])
    P.op("dve", lambda e: e.tensor_copy(RK, pk[:, 0:18]), reads=[bpk], writes=[bRK])

    def load_w(col0):
        for q2 in range(2):
            P.dma("pool", WH[:, q2 * 8:(q2 + 1) * 8, :], K.w_in[q2 * 1024:(q2 + 1) * 1024, col0:col0 + 512].rearrange("(kt p) c -> p kt c", p=128), writes=[bWH])
        for kt in range(16):
            P.op("dve", lambda e, kt=kt: e.tensor_scalar(WH[:, kt, :], WH[:, kt, :], K.gmix[:, kt:kt + 1], None, ALU.mult), reads=[bWH, K.bC], writes=[bWH])
    npp = 0
    for sel in range(2):
        DST = QT if sel == 0 else KT
        bD = bQT if sel == 0 else bKT
        gain = K.qkg[:, sel:sel + 1]
        for hf in range(2):
            load_w(sel * 1024 + hf * 512)
            for ci, (c0, cw) in enumerate(chunks):
                if sel == 0 and ci == 0:
                    continue
                xc, bx = load_x(ci, nb); nb += 1
                d0 = c0 - 256 if sel == 0 else c0
                for hl in range(4):
                    hp = hf * 4 + hl
                    ps = K.PSF[npp % 2]; bp = K.bPS[npp % 2]
                    p2 = K.PSF[2 + npp % 2]; bp2 = K.bPS[2 + npp % 2]
                    sq = SQ[npp % 2]; bs = bSQ[npp % 2]
                    ms = MS[npp % 2]; bm = bMS[npp % 2]; npp += 1
                    fns = []
                    for kt in range(16):
                        fns.append(lambda e, kt=kt, hl=hl, ps=ps, xc=xc, cw=cw: e.matmul(
                            ps[:, 0:cw], lhsT=WH[:, kt, hl * 128:(hl + 1) * 128], rhs=xc[:, kt, 0:cw], start=(kt == 0), stop=(kt == 15)))
                    P.op("pe", fns, reads=[bWH, bx], writes=[bp])
                    P.op("act", lambda e, ps=ps, sq=sq, cw=cw: e.activation(sq[:, 0:cw], ps[:, 0:cw], ACT.Square), reads=[bp], writes=[bs])
                    P.op("pe", lambda e, p2=p2, sq=sq, cw=cw: e.matmul(p2[:, 0:cw], lhsT=K.blk1[:, :], rhs=sq[:, 0:cw], start=True, stop=True),
                         reads=[bs, K.bC], writes=[bp2])
                    P.op("dve", lambda e, p2=p2, ms=ms, c0=c0, cw=cw: e.scalar_tensor_tensor(
                        ms[:, 0:cw], p2[:, 0:cw], 1.0 / 64, RX2[:, c0:c0 + cw], ALU.mult, ALU.add), reads=[bp2, bRX], writes=[bm])
                    P.op("act", lambda e, ms=ms, cw=cw: e.activation(ms[:, 0:cw], ms[:, 0:cw], ACT.Sqrt), reads=[bm], writes=[bm])
                    P.op("dve", lambda e, ms=ms, cw=cw: e.reciprocal(ms[:, 0:cw], ms[:, 0:cw]), reads=[bm], writes=[bm])
                    P.op("dve", lambda e, ps=ps, ms=ms, hp=hp, d0=d0, cw=cw, DST=DST, gain=gain: e.scalar_tensor_tensor(
                        DST[:, hp, d0:d0 + cw], ps[:, 0:cw], gain, ms[:, 0:cw], ALU.mult, ALU.mult), reads=[bp, bm, K.bC], writes=[bD])
    for hf in range(2):
        load_w(2048 + hf * 512)
        for ci, (c0, cw) in enumerate(chunks):
            xc, bx = load_x(ci, nb); nb += 1
            for tl in range(cw // 128):
                tile = c0 // 128 + tl
                ps = K.PSF[npp % 2]; bp = K.bPS[npp % 2]; npp += 1
                fns = []
                for kt in range(16):
                    fns.append(lambda e, kt=kt, tl=tl, ps=ps, xc=xc: e.matmul(
                        ps[:, :], lhsT=xc[:, kt, tl * 128:(tl + 1) * 128], rhs=WH[:, kt, :], start=(kt == 0), stop=(kt == 15)))
                P.op("pe", fns, reads=[bWH, bx], writes=[bp])
                P.op("dve", lambda e, tile=tile, hf=hf, ps=ps: e.tensor_scalar(
                    VV[:, tile, hf * 8:(hf + 1) * 8, 0:64], ps[:, :].rearrange("p (h d) -> p h d", d=64), RK[:, tile:tile + 1], None, ALU.mult),
                    reads=[bp, bRK], writes=[bVV])
    if "qT" in DEBUG:
        K.dbg("qT", QT, [bQT]); K.dbg("kT", KT, [bKT]); K.dbg("vv", VV, [bVV])
    P.barrier()
    B0 = A0
    YA = aview(K, B0, [16, 1024], BF16); bYA = Buf()
    o = B0 + 32 * KB
    BIAS = [aview(K, o, [15, 128], F32), aview(K, o + 7680, [15, 128], F32)]; bBI = [Buf(), Buf()]
    o += 15360
    QZ = [aview(K, o, [2048], BF16), aview(K, o + 4 * KB, [2048], BF16)]; bQZ = [Buf(), Buf()]
    o += 8 * KB
    SB = [aview(K, o, [5, 128], F32), aview(K, o + 2560, [5, 128], F32)]; bSB = [Buf(), Buf()]
    o += 5120
    PT = [aview(K, o, [5, 128], BF16), aview(K, o + 1280, [5, 128], BF16)]; bPT = [Buf(), Buf()]
    o += 2560
    RD = [aview(K, o, [1], F32), aview(K, o + 64, [1], F32)]; bRD = [Buf(), Buf()]
    o += 128
    YN = [aview(K, o, [1024], BF16), aview(K, o + 2 * KB, [1024], BF16)]; bYN = [Buf(), Buf()]
    o += 4 * KB
    YT = [aview(K, o, [8, 128], BF16), aview(K, o + 2 * KB, [8, 128], BF16)]; bYT = [Buf(), Buf()]
    o += 4 * KB
    SSA = aview(K, o, [16], F32); JNK = aview(K, o + 64, [1024], BF16); bSSA = Buf()
    o += 64 + 2 * KB
    assert o <= K.ARENA_BYTES, o
    iters = [(head, n) for head in range(NH) for n in range(16, 32)]
    st = {}

    def stage_front(it):
        head, n = iters[it]
        hp, hh = head // 2, head % 2
        bi = BIAS[head % 2]; bbi = bBI[head % 2]
        qz = QZ[head % 2]; bqz = bQZ[head % 2]
        if n == 16:
            P.dma("sp", bi, K.biasd[head], writes=[bbi])
            P.op("act", lambda e: e.activation(qz, QT[:, hp, :], ACT.Identity, scale=K.m01[:, hh:hh + 1]), reads=[bQT, K.bC], writes=[bqz])
        qi = n - 16
        ks = min(max(n - 2, 0), 27)
        v0 = 0 if n <= 29 else (5 if n == 30 else 10)
        psA = K.PSF[(it % 2) * 2]; bpA = K.bPS[(it % 2) * 2]
        psB = K.PSF[(it % 2) * 2 + 1]; bpB = K.bPS[(it % 2) * 2 + 1]
        sb = SB[it % 2]; bsb = bSB[it % 2]
        pt = PT[it % 2]; bpt = bPT[it % 2]
        fns = []
        for i in range(5):
            kti = ks + i - 14
            dst = psA[:, i * 128:(i + 1) * 128] if i < 4 else psB[:, 0:128]
            fns.append(lambda e, dst=dst, kti=kti: e.matmul(
                dst, lhsT=KT[:, hp, kti * 128:(kti + 1) * 128], rhs=qz[:, qi * 128:(qi + 1) * 128], start=True, stop=True))
        P.op("pe", fns, reads=[bKT, bqz], writes=[bpA, bpB])
        P.op("dve", lambda e: e.tensor_tensor(
            sb[:, 0:4, :], psA[:, :].rearrange("p (i q) -> p i q", q=128), bi[:, v0:v0 + 4, :], ALU.add), reads=[bpA, bbi], writes=[bsb])
        P.op("dve", lambda e: e.tensor_tensor(sb[:, 4, :], psB[:, 0:128], bi[:, v0 + 4, :], ALU.add),
             reads=[bpB, bbi], writes=[bsb])
        P.op("act", lambda e: e.activation(pt[:, :, :].rearrange("p i q -> p (i q)"), sb[:, :, :].rearrange("p i q -> p (i q)"), ACT.Exp),
             reads=[bsb], writes=[bpt])

    def stage_back(it):
        head, n = iters[it]
        qi = n - 16
        ks = min(max(n - 2, 0), 27)
        pso = K.PSF[4 + it % 2]; bpo = K.bPS[4 + it % 2]
        pt = PT[it % 2]; bpt = bPT[it % 2]
        rd = RD[it % 2]; brd = bRD[it % 2]
        fns = []
        for i in range(5):
            kti = ks + i - 14
            fns.append(lambda e, i=i, kti=kti: e.matmul(
                pso[:, 0:65], lhsT=pt[:, i, :], rhs=VV[:, kti, head, 0:65], start=(i == 0), stop=(i == 4)))
        P.op("pe", fns, reads=[bpt, bVV], writes=[bpo])
        P.op("dve", lambda e: e.reciprocal(rd, pso[:, 64:65]), reads=[bpo], writes=[brd])
        P.op("act", lambda e: e.activation(YA[:, qi, head * 64:(head + 1) * 64], pso[:, 0:64], ACT.Identity, scale=rd[:, 0:1]),
             reads=[bpo, brd], writes=[bYA])

    for it in range(len(iters) + 1):
        if it < len(iters):
            stage_front(it)
        if it >= 1:
            stage_back(it - 1)
    if "ya" in DEBUG:
        K.dbg("ya", YA, [bYA])
    for qi in range(16):
        P.op("act", lambda e, qi=qi: e.activation(JNK, YA[:, qi, :], ACT.Square, accum_out=SSA[:, qi:qi + 1]), reads=[bYA], writes=[bSSA])
    P.op("dve", lambda e: e.tensor_scalar(SSA, SSA, 1.0 / 1024, EPS, ALU.mult, ALU.add), reads=[bSSA], writes=[bSSA])
    P.op("act", lambda e: e.activation(SSA, SSA, ACT.Sqrt), reads=[bSSA], writes=[bSSA])
    P.op("dve", lambda e: e.reciprocal(SSA, SSA), reads=[bSSA], writes=[bSSA])
    for qi in range(16):
        yn = YN[qi % 2]; byn = bYN[qi % 2]
        yt = YT[qi % 2]; byt = bYT[qi % 2]
        P.op("dve", lambda e, qi=qi, yn=yn: e.tensor_scalar(yn, YA[:, qi, :], SSA[:, qi:qi + 1], None, ALU.mult), reads=[bYA, bSSA], writes=[byn])
        pt = K.PSB[qi % 2]; bpt = K.bPS[6 + qi % 2]
        fns = []
        for kt in range(8):
            fns.append(lambda e, kt=kt, pt=pt, yn=yn: e.transpose(pt[:, kt * 128:(kt + 1) * 128], yn[:, kt * 128:(kt + 1) * 128], K.identb[:]))
        P.op("pe", fns, reads=[byn, K.bC], writes=[bpt])
        P.op("act", lambda e, pt=pt, yt=yt: e.activation(yt[:, :, :].rearrange("p k j -> p (k j)"), pt[:, :], ACT.Identity), reads=[bpt], writes=[byt])
        P.dma("sp", K.mixT[:, 0:8, qi * 128:(qi + 1) * 128], yt, reads=[byt], writes=[K.bmix])


def phase_o(K):
    P = K.P
    FG = [6, 6, 6, 6, 5, 5, 5, 5]
    for tb in range(2):
        P.barrier()
        X1 = aview(K, 0, [8, 2048], F32); bX1 = [Buf() for _ in range(8)]
        WO = aview(K, 64 * KB, [16, 2048], BF16); bWO = Buf()
        MIXC = aview(K, 128 * KB, [16, 512], BF16); bMX = Buf()
        XO = [aview(K, 144 * KB, [2048], F32), aview(K, 152 * KB, [2048], F32)]; bXO = [Buf(), Buf()]
        for q4 in range(4):
            P.dma("pool", WO[:, q4 * 4:(q4 + 1) * 4, :], K.w_out[q4 * 512:(q4 + 1) * 512, :].rearrange("(kt p) c -> p kt c", p=128), writes=[bWO])
        for kt in range(16):
            P.op("dve", lambda e, kt=kt: e.tensor_scalar(WO[:, kt, :], WO[:, kt, :], K.gout[:, kt:kt + 1], None, ALU.mult), reads=[bWO, K.bC], writes=[bWO])
        n = 0
        for s in range(2):
            P.dma("sp", MIXC, K.mixT[:, :, tb * 1024 + s * 512: tb * 1024 + (s + 1) * 512], reads=[K.bmix], writes=[bMX])
            for tt in range(4):
                tile = s * 4 + tt
                xo = XO[tile % 2]; bxo = bXO[tile % 2]
                r0 = (tb * 8 + tile) * 128
                P.dma("sp", xo, K.xown[r0:r0 + 128, :], writes=[bxo])
                for dc in range(4):
                    ps = K.PSF[n % 4]; bp = K.bPS[n % 4]; n += 1
                    fns = []
                    for kt in range(16):
                        fns.append(lambda e, kt=kt, tt=tt, dc=dc, ps=ps: e.matmul(
                            ps[:, :], lhsT=MIXC[:, kt, tt * 128:(tt + 1) * 128], rhs=WO[:, kt, dc * 512:(dc + 1) * 512], start=(kt == 0), stop=(kt == 15)))
                    P.op("pe", fns, reads=[bMX, bWO], writes=[bp])
                    P.op("dve", lambda e, tile=tile, dc=dc, ps=ps, xo=xo: e.tensor_tensor(
                        X1[:, tile, dc * 512:(dc + 1) * 512], ps[:, :], xo[:, dc * 512:(dc + 1) * 512], ALU.add), reads=[bp, bxo], writes=[bX1[tile]])
        if "x1" in DEBUG and tb == 0:
            K.dbg("x1", X1, bX1)
        NB = 3
        WG = [aview(K, 160 * KB + i * 4 * KB, [16, 128], BF16) for i in range(NB)]; bWG = [Buf() for _ in range(NB)]
        WUp = [aview(K, 172 * KB + i * 4 * KB, [16, 128], BF16) for i in range(NB)]; bWUp = [Buf() for _ in range(NB)]

        def load_gu(f):
            P.dma("pool", WG[f % NB], K.w_gate[f], writes=[bWG[f % NB]])
            P.dma("pool", WUp[f % NB], K.w_up[f], writes=[bWUp[f % NB]])
        for f in range(NB):
            load_gu(f)
        P.barrier()
        H2T = aview(K, 64 * KB, [16, 1024], BF16); bH2T = Buf()
        ACTT = aview(K, 96 * KB, [6, 1024], BF16); bAT = Buf()
        WD = aview(K, 108 * KB, [6, 2048], BF16); bWD = Buf()
        SG = [aview(K, 132 * KB, [512], BF16), aview(K, 133 * KB, [512], BF16)]; bSG = [Buf(), Buf()]
        H2 = [aview(K, 134 * KB, [2048], BF16), aview(K, 138 * KB, [2048], BF16)]; bH2 = [Buf(), Buf()]
        S2 = aview(K, 142 * KB, [8], F32); JNK = aview(K, 143 * KB, [2048], BF16); bS2 = Buf()
        for tile in range(8):
            P.op("act", lambda e, tile=tile: e.activation(JNK, X1[:, tile, :], ACT.Square, accum_out=S2[:, tile:tile + 1]), reads=[bX1[tile]], writes=[bS2])
        P.op("dve", lambda e: e.tensor_scalar(S2, S2, 1.0 / DM, EPS, ALU.mult, ALU.add), reads=[bS2], writes=[bS2])
        P.op("act", lambda e: e.activation(S2, S2, ACT.Sqrt), reads=[bS2], writes=[bS2])
        P.op("dve", lambda e: e.reciprocal(S2, S2), reads=[bS2], writes=[bS2])
        for tile in range(8):
            h2 = H2[tile % 2]; bh2 = bH2[tile % 2]
            P.op("dve", lambda e, tile=tile, h2=h2: e.scalar_tensor_tensor(h2, X1[:, tile, :], S2[:, tile:tile + 1], K.gffn[:, :], ALU.mult, ALU.mult),
                 reads=[bX1[tile], bS2, K.bC], writes=[bh2])
            for hb in range(2):
                pt = K.PSB[hb]; bpt = K.bPS[6 + hb]
                fns = []
                for k8 in range(8):
                    kt = hb * 8 + k8
                    fns.append(lambda e, k8=k8, kt=kt, pt=pt, h2=h2: e.transpose(pt[:, k8 * 128:(k8 + 1) * 128], h2[:, kt * 128:(kt + 1) * 128], K.identb[:]))
                P.op("pe", fns, reads=[bh2, K.bC], writes=[bpt])
                P.op("dve", lambda e, hb=hb, tile=tile, pt=pt: e.tensor_copy(
                    H2T[:, hb * 8:(hb + 1) * 8, tile * 128:(tile + 1) * 128], pt[:, :].rearrange("p (k j) -> p k j", j=128)), reads=[bpt], writes=[bH2T])
        f0 = 0
        npg = 0
        for grp, nf in enumerate(FG):
            for q in range(nf):
                r0 = (f0 + q) * 128
                P.dma("pool", WD[:, q, :], K.w_down[r0:r0 + 128, :], writes=[bWD])
            for q in range(nf):
                f = f0 + q
                wg = WG[f % NB]; bwg = bWG[f % NB]; wu = WUp[f % NB]; bwu = bWUp[f % NB]
                for s in range(2):
                    pg = K.PSF[(npg % 2) * 2]; bpg = K.bPS[(npg % 2) * 2]
                    pu = K.PSF[(npg % 2) * 2 + 1]; bpu = K.bPS[(npg % 2) * 2 + 1]
                    sg = SG[npg % 2]; bsg = bSG[npg % 2]; npg += 1
                    fg = []
                    fu = []
                    for kt in range(16):
                        fg.append(lambda e, kt=kt, s=s, pg=pg, wg=wg: e.matmul(pg[:, :], lhsT=wg[:, kt, :], rhs=H2T[:, kt, s * 512:(s + 1) * 512], start=(kt == 0), stop=(kt == 15)))
                        fu.append(lambda e, kt=kt, s=s, pu=pu, wu=wu: e.matmul(pu[:, :], lhsT=wu[:, kt, :], rhs=H2T[:, kt, s * 512:(s + 1) * 512], start=(kt == 0), stop=(kt == 15)))
                    P.op("pe", fg, reads=[bwg, bH2T], writes=[bpg])
                    P.op("pe", fu, reads=[bwu, bH2T], writes=[bpu])
                    P.op("act", lambda e, pg=pg, sg=sg: e.activation(sg, pg[:, :], ACT.Silu), reads=[bpg], writes=[bsg])
                    P.op("dve", lambda e, pu=pu, sg=sg, q=q, s=s: e.tensor_tensor(ACTT[:, q, s * 512:(s + 1) * 512], pu[:, :], sg, ALU.mult),
                         reads=[bpu, bsg], writes=[bAT])
                if f + NB < NFT:
                    load_gu(f + NB)
            nd = 0
            for tile in range(8):
                for dc in range(4):
                    ps = K.PSF[4 + nd % 2]; bp = K.bPS[4 + nd % 2]; nd += 1
                    fns = []
                    for q in range(nf):
                        fns.append(lambda e, q=q, tile=tile, dc=dc, ps=ps, nf=nf: e.matmul(
                            ps[:, :], lhsT=ACTT[:, q, tile * 128:(tile + 1) * 128], rhs=WD[:, q, dc * 512:(dc + 1) * 512], start=(q == 0), stop=(q == nf - 1)))
                    P.op("pe", fns, reads=[bAT, bWD], writes=[bp])
                    P.op("dve", lambda e, tile=tile, dc=dc, ps=ps: e.tensor_tensor(
                        X1[:, tile, dc * 512:(dc + 1) * 512], X1[:, tile, dc * 512:(dc + 1) * 512], ps[:, :], ALU.add), reads=[bp, bX1[tile]], writes=[bX1[tile]])
            f0 += nf
        for tile in range(8):
            r0 = (tb * 8 + tile) * 128
            P.dma("sp", K.out[r0:r0 + 128, :], X1[:, tile, :], reads=[bX1[tile]], writes=[K.bOut])


def build_program(phases="all"):
    nc = bass.Bass("TRN2", target_bir_lowering=False)
    K = Ctx()
    K.nc = nc
    dram = lambda name, shape, dt=F32, kind="ExternalInput": nc.dram_tensor(name, list(shape), dt, kind=kind).ap()
    K.xT = dram("xT", [DM, SEQ]); K.xown = dram("xown", [2048, DM])
    K.w_in = dram("w_in", [DM, 4096]); K.w_out = dram("w_out", [DM, DM]); K.w_glu = dram("w_glu", [1024, 1024])
    K.w_gate = dram("w_gate", [NFT, 128, 16, 128]); K.w_up = dram("w_up", [NFT, 128, 16, 128]); K.w_down = dram("w_down", [DFF, DM])
    K.p_are = dram("p_are", [128, 2, 32]); K.p_aim = dram("p_aim", [128, 2, 32]); K.p_ls = dram("p_ls", [128, 2, 32])
    K.p_bre = dram("p_bre", [128, 2, 32, 16]); K.p_bim = dram("p_bim", [128, 2, 32, 16])
    K.p_cre = dram("p_cre", [128, 2, 32, 16]); K.p_cim = dram("p_cim", [128, 2, 32, 16])
    K.p_kv = dram("p_kv", [128, 2, 24])
    K.biasd = dram("biasd", [NH, 128, 15, 128])
    cst = dram("cst", [128, CST_COLS])
    K.out = dram("out", [2048, DM], kind="ExternalOutput")
    K.Ud = nc.dram_tensor("Ud", [4, 128, 64, 128], BF16).ap()
    K.Hd = nc.dram_tensor("Hd", [2, 2, 128, 2, 32, 128], BF16).ap()
    K.Hc = nc.dram_tensor("Hc", [2, 2, 128, 2, 32], BF16).ap()
    K.W3d = nc.dram_tensor("W3d", [128, 2, 2, 64, 128], BF16).ap()
    K.Td = nc.dram_tensor("Td", [128, 64, 128], BF16).ap()
    K.mixT = nc.dram_tensor("mixT", [128, 16, 2048], BF16).ap()
    K.bUd = MBuf(); K.bHd = MBuf(); K.bW3d = MBuf(); K.bTd = MBuf(); K.bmix = MBuf(); K.bOut = MBuf()
    K.dbg_out = {}
    for name, shape in DEBUG.items():
        K.dbg_out[name] = dram("dbg_" + name, [128, _prod(shape)], F32, kind="ExternalOutput")
    with ExitStack() as st:
        P = Prog(nc, st)
        K.P = P
        K.ARENA_BYTES = 188 * KB
        K.arena = st.enter_context(nc.sbuf_tensor("arena", [128, K.ARENA_BYTES // 2], BF16))
        CS = st.enter_context(nc.sbuf_tensor("cs", [128, CST_COLS], F32))
        cb16 = st.enter_context(nc.sbuf_tensor("cb16", [128, 3 * 128], BF16))
        K.dbgt = st.enter_context(nc.sbuf_tensor("dbgt", [128, 64], F32))
        K.inv128 = st.enter_context(nc.sbuf_tensor("inv128", [128, 2], F32))
        K.PSF = [st.enter_context(nc.psum_tensor("psf%d" % i, [128, 512], F32)) for i in range(6)]
        K.PSB = [st.enter_context(nc.psum_tensor("psb%d" % i, [128, 1024], BF16)) for i in range(2)]
        K.bPS = [Buf() for _ in range(8)]
        K.bC = Buf()
        P.dma("sp", CS[:], cst, writes=[K.bC])
        c = CST_OFF
        K.ident = CS[:, c["ident"]:c["ident"] + 128]
        K.maskF = CS[:, c["maskF"]:c["maskF"] + 128]
        K.maskB = CS[:, c["maskB"]:c["maskB"] + 128]
        K.gmix = CS[:, c["gmix"]:c["gmix"] + 16]
        K.gout = CS[:, c["gout"]:c["gout"] + 16]
        K.qkg = CS[:, c["qkg"]:c["qkg"] + 2]
        K.m01 = CS[:, c["m01"]:c["m01"] + 2]
        K.dskip = CS[:, c["dskip"]:c["dskip"] + 64]
        K.bglu = CS[:, c["bglu"]:c["bglu"] + 1024]
        K.gffn = CS[:, c["gffn"]:c["gffn"] + 2048]
        K.identb = cb16[:, 0:128]; K.onesb = cb16[:, 128:256]; K.blk1 = cb16[:, 256:384]
        P.op("dve", lambda e: e.tensor_copy(K.identb, K.ident), reads=[K.bC], writes=[K.bC])
        P.op("dve", lambda e: e.memset(K.onesb, 1.0), writes=[K.bC])
        P.op("dve", lambda e: e.memset(K.inv128[:, :], 1.0 / 128), writes=[K.bC])
        P.op("dve", lambda e: e.tensor_copy(K.blk1, CS[:, c["blk1"]:c["blk1"] + 128]), reads=[K.bC], writes=[K.bC])
        P.op("dve", lambda e: e.tensor_scalar(K.qkg[:, 0:1], K.qkg[:, 0:1], 0.125, None, ALU.mult), reads=[K.bC], writes=[K.bC])
        ndbg = [0]

        def dbg(name, ap, bufs):
            shape = DEBUG[name]
            n = _prod(shape)
            dst = K.dbg_out[name]
            flat = ap
            nd = len(shape)
            if nd == 2:
                flat = ap.rearrange("p a b -> p (a b)")
            elif nd == 3:
                flat = ap.rearrange("p a b c -> p (a b c)")
            elif nd == 4:
                flat = ap.rearrange("p a b c d -> p (a b c d)")
            bd = Buf()
            for c0 in range(0, n, 64):
                w = min(64, n - c0)
                P.op("pool", lambda e, c0=c0, w=w: e.tensor_copy(K.dbgt[:, 0:w], flat[:, c0:c0 + w]), reads=list(bufs) + [bd], writes=[bd])
                P.dma("sp", dst[:, c0:c0 + w], K.dbgt[:, 0:w], reads=[bd], writes=[bd, K.bOut])
        K.dbg = dbg

        if phases in ("all", "ssm", "s1", "ssm_a", "ssm_b"):
            phase_s1(K)
            P.barrier()
        if phases in ("all", "ssm", "ssm_a", "ssm_b"):
            phase_gen1(K)
            P.barrier()
            phase_s2(K)
            P.barrier()
        if phases in ("all", "ssm", "ssm_b"):
            phase_gen2(K)
            P.barrier()
        if phases in ("all", "ssm"):
            phase_s4(K)
            P.barrier()
        if phases in ("all", "attn"):
            phase_a(K)
            P.barrier()
        if phases in ("all", "out"):
            phase_o(K)
        waits = P._deps("sp", [K.bOut], ())
        P.ops["sp"].append((waits, [], None))
        P.emit()
    return nc


CST_OFF = {}
_c = 0
for _n, _w in (("ident", 128), ("maskF", 128), ("maskB", 128), ("blk1", 128), ("gmix", 16), ("gout", 16), ("qkg", 2),
               ("m01", 2), ("dskip", 64), ("bglu", 1024), ("gffn", 2048)):
    CST_OFF[_n] = _c
    _c += _w
CST_COLS = _c


def _lay_gp(a):
    s = a.shape
    a = a.reshape(2, 32, 2, 64, *s[3:])
    a = np.moveaxis(a, [2, 3], [0, 1])
    return np.ascontiguousarray(a.reshape(128, 2, 32, *s[3:]), dtype=np.float32)


def _bias_table(rpb0, flip):
    out = np.full((NH, 15, 128, 128), NEG, np.float32)
    for v in range(15):
        n, i = (20, v) if v < 5 else ((30, v - 5) if v < 10 else (31, v - 10))
        ks = min(max(n - 2, 0), 27)
        kp = ks + i
        kr = np.repeat(np.array([2 * kp, 2 * kp + 1]), 64); kc = np.tile(np.arange(64), 2)
        qr = np.repeat(np.array([2 * n, 2 * n + 1]), 64); qc = np.tile(np.arange(64), 2)
        if flip:
            kr, kc, qr, qc = 63 - kr, 63 - kc, 63 - qr, 63 - qc
        rs = np.clip(qr - 4, 0, 56); cs = np.clip(qc - 8, 0, 48)
        inwin = ((kr[:, None] >= rs[None, :]) & (kr[:, None] < rs[None, :] + 8) &
                 (kc[:, None] >= cs[None, :]) & (kc[:, None] < cs[None, :] + 16))
        dr = np.clip(kr[:, None] - qr[None, :] + 7, 0, 14)
        dc = np.clip(kc[:, None] - qc[None, :], -15, 15) + 15
        vals = rpb0[:, dr, dc]
        out[:, v] = np.where(inwin[None], vals, np.float32(NEG))
    return np.ascontiguousarray(out.transpose(0, 2, 1, 3))


def _consts(inp):
    cs = np.zeros((128, CST_COLS), np.float32)
    o = CST_OFF
    cs[:, o["ident"]:o["ident"] + 128] = np.eye(128, dtype=np.float32)
    s = np.arange(128) // 16
    cs[:, o["maskF"]:o["maskF"] + 128] = (s[:, None] <= s[None, :])
    cs[:, o["maskB"]:o["maskB"] + 128] = (s[:, None] >= s[None, :])
    hh = np.arange(128) // 64
    cs[:, o["blk1"]:o["blk1"] + 128] = (hh[:, None] == hh[None, :])
    cs[:, o["gmix"]:o["gmix"] + 16] = inp["g_mix"][0].reshape(16, 128).T
    gout = np.concatenate([inp["g_out_attn"][0], inp["g_out_ssm"][0]])
    cs[:, o["gout"]:o["gout"] + 16] = gout.reshape(16, 128).T
    cs[:, o["qkg"]] = np.tile(inp["q_gain"][0], 2)
    cs[:, o["qkg"] + 1] = np.tile(inp["k_gain"][0], 2)
    cs[:, o["m01"]] = (hh == 0)
    cs[:, o["m01"] + 1] = (hh == 1)
    cs[:, o["dskip"]:o["dskip"] + 64] = np.tile(inp["ssm_d"][0].reshape(64, 16).T, (8, 1))
    cs[:, o["bglu"]:o["bglu"] + 1024] = inp["b_glu"][0][None, :]
    cs[:, o["gffn"]:o["gffn"] + 2048] = inp["g_ffn"][0][None, :]
    return cs


def _kvals():
    kvA = np.concatenate([np.arange(7, -1, -1), np.arange(1, 9), np.arange(-7, 1)])
    kvB = np.concatenate([np.arange(0, 8), np.arange(8, 0, -1), -np.arange(0, 8)])
    kv = np.stack([kvA, kvB]).astype(np.float32)
    return np.ascontiguousarray(np.broadcast_to(kv[None], (128, 2, 24)))


def prepare_inputs(inp):
    inp = {k: np.asarray(v) for k, v in inp.items()}
    x = inp["x"]
    shared = dict(
        w_in=np.ascontiguousarray(inp["w_in"][0]), w_out=np.ascontiguousarray(inp["w_out"][0]),
        w_glu=np.ascontiguousarray(inp["w_glu"][0]),
        w_gate=np.ascontiguousarray(inp["w_ffn_gate"][0].reshape(16, 128, NFT, 128).transpose(2, 1, 0, 3)),
        w_up=np.ascontiguousarray(inp["w_ffn_up"][0].reshape(16, 128, NFT, 128).transpose(2, 1, 0, 3)), w_down=np.ascontiguousarray(inp["w_ffn_down"][0]),
        cst=_consts(inp), p_kv=_kvals())
    bias_tabs = {f: _bias_table(inp["rpb"][0], f) for f in (False, True)}
    ssm = {}
    for h in (0, 1):
        dirs = [0, 1] if h == 1 else [1, 0]
        ls = np.broadcast_to(inp["ssm_log_step"][0][dirs][:, :, None], (2, 64, 64))
        ssm[h] = dict(
            p_are=_lay_gp(inp["ssm_a_re"][0][dirs]), p_aim=_lay_gp(inp["ssm_a_im"][0][dirs]), p_ls=_lay_gp(ls),
            p_bre=_lay_gp(inp["ssm_b_re"][0][dirs]), p_bim=_lay_gp(inp["ssm_b_im"][0][dirs]),
            p_cre=_lay_gp(inp["ssm_c_re"][0][dirs].transpose(0, 1, 3, 2)), p_cim=_lay_gp(inp["ssm_c_im"][0][dirs].transpose(0, 1, 3, 2)))
    maps = []
    for c in range(8):
        b, h = c // 2, c % 2
        flip = (h == 0)
        xl = x[b][::-1] if flip else x[b]
        m = dict(shared)
        m.update(ssm[h])
        m["xT"] = np.ascontiguousarray(xl.T)
        m["xown"] = np.ascontiguousarray(xl[2048:])
        m["biasd"] = bias_tabs[flip]
        maps.append(m)
    return maps


def assemble(results):
    out = np.empty((4, SEQ, DM), np.float32)
    for c in range(8):
        b, h = c // 2, c % 2
        o = np.asarray(results[c]["out"])
        if h == 0:
            out[b, 0:2048] = o[::-1]
        else:
            out[b, 2048:] = o
    return out


def kernel(**inputs):
    maps = prepare_inputs(inputs)
    nc = build_program("all")
    res = run_bass_kernel_spmd(nc, maps, core_ids=list(range(8)))
    return assemble(res.results)
```

```python
import numpy as np
from contextlib import ExitStack
import concourse.bass as bass
import concourse.mybir as mybir
from concourse.bass_utils import run_bass_kernel_spmd

F32 = mybir.dt.float32
BF16 = mybir.dt.bfloat16
I32 = mybir.dt.int32
ALU = mybir.AluOpType
ACT = mybir.ActivationFunctionType

DM = 2048
SEQ = 4096
NH = 16
DFF = 5632
NFT = DFF // 128
EPS = 1e-6
NEG = -30000.0
ENGS = ("pe", "act", "dve", "pool", "sp")
SAME_ENGINE_SYNC = True
DEBUG = {}


class Buf:
    __slots__ = ("w", "r")

    def __init__(self):
        self.w = None
        self.r = {}


class MBuf:
    __slots__ = ("ws",)

    def __init__(self):
        self.ws = []


class Prog:
    N_DSEM = 32

    def __init__(self, nc, stack):
        self.nc = nc
        self.ops = {e: [] for e in ENGS}
        self.cnt = {e: 0 for e in ENGS}
        self.sems = {e: stack.enter_context(nc.semaphore("s_" + e)) for e in ENGS}
        self.dsems = [stack.enter_context(nc.semaphore("d%d" % i)) for i in range(self.N_DSEM)]
        self.dcnt = [0] * self.N_DSEM
        self.dnext = [0, 0]
        self.waited = {e: {} for e in ENGS}

    def _deps(self, eng, reads, writes, ses=None):
        if ses is None:
            ses = SAME_ENGINE_SYNC
        deps = {}

        def add(tok):
            if tok is not None and deps.get(tok[0], 0) < tok[1]:
                deps[tok[0]] = tok[1]
        for b in reads:
            if isinstance(b, MBuf):
                for t in b.ws:
                    add(t)
            else:
                add(b.w)
        for b in writes:
            if isinstance(b, MBuf):
                continue
            add(b.w)
            for t in b.r.items():
                add(t)
        waits = []
        for k, v in deps.items():
            if k == eng and (eng == "pe" or not ses):
                continue
            if self.waited[eng].get(k, 0) >= v:
                continue
            self.waited[eng][k] = v
            waits.append((k, v))
        return waits

    @staticmethod
    def _mark(tok, reads, writes):
        for b in reads:
            if isinstance(b, MBuf):
                continue
            if b.r.get(tok[0], 0) < tok[1]:
                b.r[tok[0]] = tok[1]
        for b in writes:
            if isinstance(b, MBuf):
                b.ws.append(tok)
                continue
            b.w = tok
            b.r = {}

    def op(self, eng, fns, reads=(), writes=(), ses=None):
        if callable(fns):
            fns = [fns]
        waits = self._deps(eng, reads, writes, ses)
        self.cnt[eng] += 1
        tok = (eng, self.cnt[eng])
        self.ops[eng].append((waits, fns, (eng, 1)))
        self._mark(tok, reads, writes)
        return tok

    def dma(self, eng, out, in_, reads=(), writes=()):
        half = self.N_DSEM // 2
        which = 0 if eng == "pool" else 1
        i = which * half + self.dnext[which]
        self.dnext[which] = (self.dnext[which] + 1) % half
        key = "d%d" % i
        waits = self._deps(eng, reads, writes)
        if self.dcnt[i] > 0 and self.waited[eng].get(key, 0) < self.dcnt[i]:
            self.waited[eng][key] = self.dcnt[i]
            waits.append((key, self.dcnt[i]))
        self.dcnt[i] += 16
        tok = (key, self.dcnt[i])
        self.ops[eng].append((waits, [lambda e: e.dma_start(out=out, in_=in_)], (key, 16)))
        self._mark(tok, reads, writes)
        return tok

    def barrier(self):
        toks = [(e, self.cnt[e]) for e in ENGS if self.cnt[e] > 0]
        toks += [("d%d" % i, self.dcnt[i]) for i in range(self.N_DSEM) if self.dcnt[i] > 0]
        for eng in ENGS:
            waits = []
            for k, v in toks:
                if k == eng:
                    continue
                if self.waited[eng].get(k, 0) >= v:
                    continue
                self.waited[eng][k] = v
                waits.append((k, v))
            if waits:
                self.ops[eng].append((waits, [], None))

    def _sem(self, key):
        return self.sems[key] if key in self.sems else self.dsems[int(key[1:])]

    def emit(self):
        prog = self

        def run(engname):
            def body(e):
                for waits, fns, inc in prog.ops[engname]:
                    for k, v in waits:
                        e.wait_ge(prog._sem(k), v)
                    last = None
                    for f in fns:
                        last = f(e)
                    if inc is not None and last is not None:
                        last.then_inc(prog._sem(inc[0]), inc[1])
            return body
        with self.nc.Block() as block:
            block.tensor(run("pe"))
            block.scalar(run("act"))
            block.vector(run("dve"))
            block.gpsimd(run("pool"))
            block.sync(run("sp"))


class Ctx:
    pass


def _prod(s):
    n = 1
    for v in s:
        n *= v
    return n


def aview(K, off, shape, dt):
    esz = 4 if dt in (F32, I32) else 2
    n = _prod(shape)
    assert off % 4 == 0 and off + n * esz <= K.ARENA_BYTES, (off, shape, K.ARENA_BYTES)
    a = K.arena[:, off // 2: off // 2 + n * esz // 2]
    if dt != BF16:
        a = a.bitcast(dt)
    if len(shape) == 2:
        a = a.rearrange("p (a b) -> p a b", a=shape[0], b=shape[1])
    elif len(shape) == 3:
        a = a.rearrange("p (a b c) -> p a b c", a=shape[0], b=shape[1], c=shape[2])
    elif len(shape) == 4:
        a = a.rearrange("p (a b c d) -> p a b c d", a=shape[0], b=shape[1], c=shape[2], d=shape[3])
    return a


KB = 1024


def bc(ap, shape):
    return ap.broadcast_to(list(shape))


def phase_s1(K):
    P = K.P
    XB = [aview(K, 0, [16, 1024], BF16), aview(K, 32 * KB, [16, 1024], BF16)]
    bXB = [Buf(), Buf()]
    WU = aview(K, 64 * KB, [16, 1024], BF16); bWU = Buf()
    U2 = aview(K, 96 * KB, [64, 8, 16], BF16); bU2 = Buf()
    UB = [aview(K, 112 * KB, [64, 128], BF16), aview(K, 128 * KB, [64, 128], BF16)]
    bUB = [Buf(), Buf()]
    SQ = [aview(K, 144 * KB, [1024], BF16), aview(K, 146 * KB, [1024], BF16)]
    bSQ = [Buf(), Buf()]
    RS = aview(K, 148 * KB, [4, 8], F32); bRS = Buf()
    RREP = aview(K, 150 * KB, [1024], F32); bRR = Buf()
    bPS = K.bPS
    for q4 in range(4):
        P.dma("pool", WU[:, q4 * 4:(q4 + 1) * 4, :],
              K.w_in[q4 * 512:(q4 + 1) * 512, 3072:4096].rearrange("(kt p) c -> p kt c", p=128), writes=[bWU])
    for kt in range(16):
        P.op("dve", lambda e, kt=kt: e.tensor_scalar(WU[:, kt, :], WU[:, kt, :], K.gmix[:, kt:kt + 1], None, ALU.mult),
             reads=[bWU, K.bC], writes=[bWU])
    nps = 0
    for b in range(4):
        xb = XB[b % 2]; bx = bXB[b % 2]
        for q4 in range(4):
            P.dma("pool", xb[:, q4 * 4:(q4 + 1) * 4, :],
                  K.xT[q4 * 512:(q4 + 1) * 512, b * 1024:(b + 1) * 1024].rearrange("(kt p) t -> p kt t", p=128),
                  writes=[bx])
        pr0 = K.PSF[4]; pr1 = K.PSF[5]
        for kt in range(16):
            sq = SQ[kt % 2]; bs = bSQ[kt % 2]
            P.op("act", lambda e, kt=kt, sq=sq, xb=xb: e.activation(sq, xb[:, kt, :], ACT.Square), reads=[bx], writes=[bs])
            P.op("pe", [lambda e, kt=kt, sq=sq: e.matmul(pr0[:, :], lhsT=K.onesb[:, :], rhs=sq[:, 0:512], start=(kt == 0), stop=(kt == 15)),
                        lambda e, kt=kt, sq=sq: e.matmul(pr1[:, :], lhsT=K.onesb[:, :], rhs=sq[:, 512:1024], start=(kt == 0), stop=(kt == 15))],
                 reads=[bs, K.bC], writes=[bPS[4], bPS[5]])
        P.op("dve", lambda e: e.tensor_scalar(RREP[:, 0:512], pr0[:, :], 1.0 / DM, EPS, ALU.mult, ALU.add), reads=[bPS[4]], writes=[bRR])
        P.op("dve", lambda e: e.tensor_scalar(RREP[:, 512:1024], pr1[:, :], 1.0 / DM, EPS, ALU.mult, ALU.add), reads=[bPS[5]], writes=[bRR])
        P.op("act", lambda e: e.activation(RREP, RREP, ACT.Sqrt), reads=[bRR], writes=[bRR])
        P.op("dve", lambda e: e.reciprocal(RREP, RREP), reads=[bRR], writes=[bRR])
        fns = []
        for t in range(8):
            fns.append(lambda e, t=t: e.matmul(pr0[:, t:t + 1], lhsT=RREP[:, t::8], rhs=K.inv128[:, 0:1], start=True, stop=True))
        P.op("pe", fns, reads=[bRR, K.bC], writes=[bPS[4]])
        P.op("dve", lambda e, b=b: e.tensor_copy(RS[:, b, :], pr0[:, 0:8]), reads=[bPS[4]], writes=[bRS])
        for t in range(8):
            for ch in range(2):
                ps = K.PSF[nps % 4]; bp = bPS[nps % 4]; nps += 1
                fns = []
                for kt in range(16):
                    fns.append(lambda e, kt=kt, t=t, ch=ch, ps=ps, xb=xb: e.matmul(
                        ps[:, :], lhsT=xb[:, kt, t::8], rhs=WU[:, kt, ch * 512:(ch + 1) * 512], start=(kt == 0), stop=(kt == 15)))
                P.op("pe", fns, reads=[bx, bWU], writes=[bp])
                P.op("dve", lambda e, t=t, ch=ch, ps=ps, b=b: e.tensor_scalar(
                    U2[:, ch * 32:(ch + 1) * 32, t, :], ps[:, :].rearrange("p (g c) -> p g c", c=16), RS[:, b, t:t + 1], None, ALU.mult),
                    reads=[bp, bRS], writes=[bU2])
        ub = UB[b % 2]; bu = bUB[b % 2]
        for g8 in range(8):
            pt = K.PSB[g8 % 2]; bpt = bPS[6 + g8 % 2]
            fns = []
            for gl in range(8):
                g = g8 * 8 + gl
                fns.append(lambda e, g=g, gl=gl, pt=pt: e.transpose(pt[:, gl * 128:(gl + 1) * 128], U2[:, g, :, :], K.identb[:]))
            P.op("pe", fns, reads=[bU2, K.bC], writes=[bpt])
            P.op("act", lambda e, g8=g8, pt=pt, ub=ub: e.activation(
                ub[:, g8 * 8:(g8 + 1) * 8, :].rearrange("p g j -> p (g j)"), pt[:, :], ACT.Identity), reads=[bpt], writes=[bu])
        P.dma("sp", K.Ud[b], ub, reads=[bu], writes=[K.bUd])
        if "U" in DEBUG and b == 2:
            K.dbg("U", ub, [bu])


def ssm_tables(K):
    P = K.P
    T0 = 144 * KB
    o = [T0]

    def al(shape, dt=F32):
        v = aview(K, o[0], shape, dt)
        o[0] += _prod(shape) * 4
        return v
    K.ER = al([2, 32, 24]); K.EI = al([2, 32, 24])
    K.BBR = al([2, 32, 16]); K.BBI = al([2, 32, 16])
    K.MUA = al([2, 2, 32]); K.MUS = al([2, 2, 32])
    AR = al([2, 32]); AI = al([2, 32]); LS = al([2, 32]); DT = al([2, 32])
    ZR = al([2, 32]); ZI = al([2, 32]); E1R = al([2, 32]); E1I = al([2, 32])
    FR = al([2, 32]); FI = al([2, 32]); TA = al([2, 32]); TB = al([2, 32]); DEN = al([2, 32])
    KV = al([2, 24])
    assert o[0] <= 170 * KB, o[0]
    K.bTab = Buf()
    bT = K.bTab
    BR = aview(K, 64 * KB, [2, 32, 16], F32); BI = aview(K, 68 * KB, [2, 32, 16], F32)
    KZ = aview(K, 72 * KB, [2, 32, 24], F32)
    RR = aview(K, 80 * KB, [2, 2, 32, 24], F32)
    RI = aview(K, 92 * KB, [2, 2, 32, 24], I32)
    RF = aview(K, 104 * KB, [2, 2, 32, 24], F32)
    RG = aview(K, 116 * KB, [2, 2, 32, 24], F32)
    TMP = aview(K, 128 * KB, [2, 32, 16], F32)
    bB = Buf(); bW = Buf()
    P.dma("sp", AR, K.p_are, writes=[bT]); P.dma("sp", AI, K.p_aim, writes=[bT])
    P.dma("sp", LS, K.p_ls, writes=[bT]); P.dma("sp", KV, K.p_kv, writes=[bT])
    P.dma("sp", BR, K.p_bre, writes=[bB]); P.dma("sp", BI, K.p_bim, writes=[bB])
    f2 = lambda a: a.rearrange("p a b -> p (a b)")
    f3 = lambda a: a.rearrange("p a b c -> p (a b c)")
    f4 = lambda a: a.rearrange("p a b c d -> p (a b c d)")
    D = lambda fn, r, w: P.op("dve", fn, reads=r, writes=w)
    A = lambda fn, r, w: P.op("act", fn, reads=r, writes=w)
    A(lambda e: e.activation(f2(DT), f2(LS), ACT.Exp), [bT], [bT])
    D(lambda e: e.tensor_scalar(AR, AR, -1e-4, None, ALU.min), [bT], [bT])
    D(lambda e: e.tensor_tensor(ZR, AR, DT, ALU.mult), [bT], [bT])
    D(lambda e: e.tensor_tensor(ZI, AI, DT, ALU.mult), [bT], [bT])
    sh = [128, 2, 32, 24]
    D(lambda e: e.tensor_tensor(KZ, bc(ZR.unsqueeze(3), sh), bc(KV.unsqueeze(2), sh), ALU.mult), [bT], [bW])
    A(lambda e: e.activation(f3(KZ), f3(KZ), ACT.Exp), [bW], [bW])
    D(lambda e: e.tensor_scalar(ZI, ZI, 1.0 / (2 * np.pi), None, ALU.mult), [bT], [bT])
    D(lambda e: e.tensor_tensor(RR[:, 0], bc(ZI.unsqueeze(3), sh), bc(KV.unsqueeze(2), sh), ALU.mult), [bT], [bW])
    D(lambda e: e.tensor_scalar(RR[:, 1], RR[:, 0], 0.25, None, ALU.add), [bW], [bW])
    D(lambda e: e.tensor_copy(f4(RI), f4(RR)), [bW], [bW])
    D(lambda e: e.tensor_copy(f4(RF), f4(RI)), [bW], [bW])
    D(lambda e: e.tensor_tensor(f4(RR), f4(RR), f4(RF), ALU.subtract), [bW], [bW])
    D(lambda e: e.tensor_scalar(f4(RF), f4(RR), 0.5, None, ALU.is_gt), [bW], [bW])
    D(lambda e: e.tensor_scalar(f4(RG), f4(RR), -0.5, None, ALU.is_lt), [bW], [bW])
    D(lambda e: e.tensor_tensor(f4(RR), f4(RR), f4(RF), ALU.subtract), [bW], [bW])
    D(lambda e: e.tensor_tensor(f4(RR), f4(RR), f4(RG), ALU.add), [bW], [bW])
    A(lambda e: e.activation(f4(RR), f4(RR), ACT.Sin, scale=6.283185), [bW], [bW])
    D(lambda e: e.tensor_tensor(K.ER, KZ, RR[:, 1], ALU.mult), [bW], [bT])
    D(lambda e: e.tensor_tensor(K.EI, KZ, RR[:, 0], ALU.mult), [bW], [bT])
    for dd, i1, i8 in ((0, 8, 15), (1, 1, 8)):
        D(lambda e, dd=dd, i1=i1: e.tensor_copy(E1R[:, dd, :], K.ER[:, dd, :, i1]), [bT], [bT])
        D(lambda e, dd=dd, i1=i1: e.tensor_copy(E1I[:, dd, :], K.EI[:, dd, :, i1]), [bT], [bT])
        D(lambda e, dd=dd, i8=i8: e.tensor_copy(K.MUA[:, dd, 0, :], K.ER[:, dd, :, i8]), [bT], [bT])
        D(lambda e, dd=dd, i8=i8: e.tensor_copy(K.MUA[:, dd, 1, :], K.ER[:, dd, :, i8]), [bT], [bT])
        D(lambda e, dd=dd, i8=i8: e.tensor_copy(K.MUS[:, dd, 1, :], K.EI[:, dd, :, i8]), [bT], [bT])
        D(lambda e, dd=dd, i8=i8: e.tensor_scalar(K.MUS[:, dd, 0, :], K.EI[:, dd, :, i8], -1.0, None, ALU.mult), [bT], [bT])
    D(lambda e: e.tensor_scalar(E1R, E1R, -1.0, None, ALU.add), [bT], [bT])
    D(lambda e: e.tensor_tensor(DEN, AR, AR, ALU.mult), [bT], [bT])
    D(lambda e: e.tensor_tensor(TA, AI, AI, ALU.mult), [bT], [bT])
    D(lambda e: e.tensor_tensor(DEN, DEN, TA, ALU.add), [bT], [bT])
    D(lambda e: e.reciprocal(DEN, DEN), [bT], [bT])
    D(lambda e: e.tensor_tensor(TA, E1R, AR, ALU.mult), [bT], [bT])
    D(lambda e: e.tensor_tensor(TB, E1I, AI, ALU.mult), [bT], [bT])
    D(lambda e: e.tensor_tensor(FR, TA, TB, ALU.add), [bT], [bT])
    D(lambda e: e.tensor_tensor(FR, FR, DEN, ALU.mult), [bT], [bT])
    D(lambda e: e.tensor_tensor(TA, E1I, AR, ALU.mult), [bT], [bT])
    D(lambda e: e.tensor_tensor(TB, E1R, AI, ALU.mult), [bT], [bT])
    D(lambda e: e.tensor_tensor(FI, TA, TB, ALU.subtract), [bT], [bT])
    D(lambda e: e.tensor_tensor(FI, FI, DEN, ALU.mult), [bT], [bT])
    s3 = [128, 2, 32, 16]
    D(lambda e: e.tensor_tensor(K.BBR, bc(FR.unsqueeze(3), s3), BR, ALU.mult), [bT, bB], [bT])
    D(lambda e: e.tensor_tensor(TMP, bc(FI.unsqueeze(3), s3), BI, ALU.mult), [bT, bB], [bW])
    D(lambda e: e.tensor_tensor(K.BBR, K.BBR, TMP, ALU.subtract), [bT, bW], [bT])
    D(lambda e: e.tensor_tensor(K.BBI, bc(FR.unsqueeze(3), s3), BI, ALU.mult), [bT, bB], [bT])
    D(lambda e: e.tensor_tensor(TMP, bc(FI.unsqueeze(3), s3), BR, ALU.mult), [bT, bB], [bW])
    D(lambda e: e.tensor_tensor(K.BBI, K.BBI, TMP, ALU.add), [bT, bW], [bT])


def cplx_outer(K, eng, OR, OI, er, ei, xr, xi, t1, bufs_r, bufs_w, neg_im=False):
    P = K.P
    op = lambda fn: P.op(eng, fn, reads=bufs_r, writes=bufs_w)
    op(lambda e: e.tensor_tensor(OR, er, xr, ALU.mult))
    op(lambda e: e.tensor_tensor(t1, ei, xi, ALU.mult))
    op(lambda e: e.tensor_tensor(OR, OR, t1, ALU.subtract))
    op(lambda e: e.tensor_tensor(OI, er, xi, ALU.mult))
    op(lambda e: e.tensor_tensor(t1, ei, xr, ALU.mult))
    if neg_im and eng == "dve":
        op(lambda e: e.scalar_tensor_tensor(OI, OI, -1.0, t1, ALU.mult, ALU.subtract))
    elif neg_im:
        op(lambda e: e.tensor_tensor(OI, OI, t1, ALU.add))
        op(lambda e: e.tensor_scalar(OI, OI, -1.0, None, ALU.mult))
    else:
        op(lambda e: e.tensor_tensor(OI, OI, t1, ALU.add))


def q_chunk(K, eng, dd, gc, QR, QI, T1, bQ):
    sh = [128, 4, 8, 16]
    gs = slice(gc * 4, gc * 4 + 4)
    er = bc(K.ER[:, dd, gs, 0:8].unsqueeze(3), sh); ei = bc(K.EI[:, dd, gs, 0:8].unsqueeze(3), sh)
    xr = bc(K.BBR[:, dd, gs, :].unsqueeze(2), sh); xi = bc(K.BBI[:, dd, gs, :].unsqueeze(2), sh)
    cplx_outer(K, eng, QR, QI, er, ei, xr, xi, T1, [K.bTab], [bQ])


def phase_gen1(K):
    P = K.P
    K.W1Z = aview(K, 0, [2, 2, 64, 128], BF16); K.bW1 = Buf()
    for q4 in range(4):
        P.op("pool", lambda e, q4=q4: e.memset(K.W1Z[:, q4 // 2, q4 % 2], 0.0), writes=[K.bW1])
    ssm_tables(K)
    QR = aview(K, 178 * KB, [4, 8, 16], F32); QI = aview(K, 180 * KB, [4, 8, 16], F32)
    T1 = aview(K, 182 * KB, [4, 8, 16], F32)
    bQ = Buf()
    n = 0
    for dd in range(2):
        for gc in range(8):
            q_chunk(K, "dve", dd, gc, QR, QI, T1, bQ)
            for ri, Q in ((0, QR), (1, QI)):
                ps = K.PSF[n % 2]; bp = K.bPS[n % 2]; n += 1
                fns = []
                for gl in range(4):
                    fns.append(lambda e, gl=gl, Q=Q, ps=ps: e.transpose(
                        ps[:, gl * 128:(gl + 1) * 128], Q[:, gl].rearrange("p s c -> p (s c)"), K.ident[:]))
                P.op("pe", fns, reads=[bQ, K.bC], writes=[bp])
                g0 = gc * 8
                psv = ps[:, :].rearrange("p (g q) -> p g q", q=128)
                P.op("dve", lambda e, dd=dd, ri=ri, g0=g0, psv=psv: e.tensor_copy(
                    K.W1Z[:, dd, ri, g0:g0 + 8:2, 0:64], psv[:, :, 0:64]), reads=[bp], writes=[K.bW1])
                P.op("dve", lambda e, dd=dd, ri=ri, g0=g0, psv=psv: e.tensor_copy(
                    K.W1Z[:, dd, ri, g0 + 1:g0 + 8:2, 64:128], psv[:, :, 64:128]), reads=[bp], writes=[K.bW1])


def phase_s2(K):
    P = K.P
    S8 = {0: aview(K, 64 * KB, [2, 32, 128], F32), 1: aview(K, 96 * KB, [2, 32, 128], F32)}
    bS8 = {0: Buf(), 1: Buf()}
    UBL = {0: aview(K, 128 * KB, [64, 128], BF16), 1: aview(K, 170 * KB, [32, 128], BF16)}
    bUL = {0: Buf(), 1: Buf()}
    base = 184 * KB
    TT = {}
    for dd in range(2):
        o = base + dd * 2 * KB
        TT[dd] = dict(t1=aview(K, o, [2, 32], F32), t2=aview(K, o + 256, [2, 32], F32),
                      car=aview(K, o + 512, [2, 32], F32), carb=aview(K, o + 768, [2, 32], BF16))
    eng_of = {0: "dve", 1: "pool"}
    bT = {0: Buf(), 1: Buf()}
    for dd in range(2):
        P.op(eng_of[dd], lambda e, dd=dd: e.memset(TT[dd]["car"], 0.0), writes=[bT[dd]])

    def put_carry(dd, blk):
        eng = eng_of[dd]
        P.op(eng, lambda e: e.tensor_copy(TT[dd]["carb"], TT[dd]["car"]), reads=[bT[dd]], writes=[bT[dd]])
        P.dma("sp" if dd == 0 else "pool", K.Hc[dd, blk], TT[dd]["carb"], reads=[bT[dd]], writes=[K.bHd])

    def l1_group(dd, ri, gc, n, ubuf, gmod):
        ps = K.PSF[dd * 2 + n % 2]; bp = K.bPS[dd * 2 + n % 2]
        fns = []
        for gl in range(4):
            gp = gc * 4 + gl
            for g2 in range(2):
                g = 2 * gp + g2
                fns.append(lambda e, gl=gl, g=g, g2=g2, ps=ps: e.matmul(
                    ps[:, gl * 128:(gl + 1) * 128], lhsT=K.W1Z[:, dd, ri, g, :], rhs=ubuf[:, g % gmod, :],
                    start=(g2 == 0), stop=(g2 == 1)))
        P.op("pe", fns, reads=[bUL[dd], K.bW1], writes=[bp])
        P.op("act", lambda e, ps=ps: e.activation(
            S8[dd][:, ri, gc * 4:(gc + 1) * 4, :].rearrange("p g j -> p (g j)"), ps[:, :], ACT.Identity),
            reads=[bp], writes=[bS8[dd]])

    def level1_A(b):
        P.dma("sp", UBL[0], K.Ud[b], reads=[K.bUd], writes=[bUL[0]])
        n = 0
        for ri in range(2):
            for gc in range(8):
                l1_group(0, ri, gc, n, UBL[0], 64); n += 1

    def level1_B(b):
        n = 0
        for half in range(2):
            P.dma("pool", UBL[1], K.Ud[b][:, half * 32:(half + 1) * 32, :], reads=[K.bUd], writes=[bUL[1]])
            for ri in range(2):
                for gcl in range(4):
                    l1_group(1, ri, half * 4 + gcl, n, UBL[1], 32); n += 1

    def scan(dd, order):
        eng = eng_of[dd]
        s8 = S8[dd]; tt = TT[dd]
        mua = K.MUA[:, dd]; mus = K.MUS[:, dd]
        bs = bS8[dd]; bt = bT[dd]
        prevj = None
        for jj in order:
            prev = tt["car"] if prevj is None else s8[:, :, :, prevj]
            cur = s8[:, :, :, jj]
            rd = [bs, bt, K.bTab]
            P.op(eng, lambda e, prev=prev: e.tensor_tensor(tt["t1"], mua, prev, ALU.mult), reads=rd, writes=[bt], ses=False)
            P.op(eng, lambda e, prev=prev: e.tensor_tensor(tt["t2"][:, 0], mus[:, 0], prev[:, 1], ALU.mult), reads=rd, writes=[bt], ses=False)
            P.op(eng, lambda e, prev=prev: e.tensor_tensor(tt["t2"][:, 1], mus[:, 1], prev[:, 0], ALU.mult), reads=rd, writes=[bt], ses=False)
            P.op(eng, lambda e: e.tensor_tensor(tt["t1"], tt["t1"], tt["t2"], ALU.add), reads=[bt], writes=[bt], ses=False)
            P.op(eng, lambda e, cur=cur: e.tensor_tensor(cur, cur, tt["t1"], ALU.add), reads=[bs, bt], writes=[bs], ses=False)
            prevj = jj
        P.op(eng, lambda e: e.tensor_copy(tt["car"], s8[:, :, :, prevj]), reads=[bs, bt], writes=[bt], ses=False)

    def store_H(dd, blk):
        eng = eng_of[dd]
        s8 = S8[dd]
        if dd == 0:
            hb = UBL[0][:, :, :].rearrange("p (r g) j -> p r g j", r=2)
            P.op(eng, lambda e: e.tensor_copy(hb, s8), reads=[bS8[0]], writes=[bUL[0]])
            P.dma("sp", K.Hd[0, blk], hb, reads=[bUL[0]], writes=[K.bHd])
        else:
            for ri in range(2):
                hb = UBL[1]
                P.op(eng, lambda e, ri=ri: e.tensor_copy(hb, s8[:, ri]), reads=[bS8[1]], writes=[bUL[1]])
                P.dma("pool", K.Hd[1, blk][:, ri], hb, reads=[bUL[1]], writes=[K.bHd])

    def do_A(b):
        level1_A(b)
        if b >= 2:
            put_carry(0, b - 2)
        scan(0, range(128))
        if b >= 2:
            store_H(0, b - 2)

    def do_B(b):
        level1_B(b)
        put_carry(1, b - 2)
        scan(1, range(127, -1, -1))
        store_H(1, b - 2)

    level1_A(0)
    do_B(3)
    scan(0, range(128))
    do_A(1)
    do_A(2)
    do_B(2)
    do_A(3)


def phase_gen2(K):
    P = K.P
    CR = aview(K, 130 * KB, [2, 32, 16], F32); CI = aview(K, 134 * KB, [2, 32, 16], F32); bCc = Buf()
    P.dma("sp", CR, K.p_cre, writes=[bCc]); P.dma("sp", CI, K.p_cim, writes=[bCc])
    o = [0]

    def al(shape, dt=F32):
        v = aview(K, o[0], shape, dt)
        o[0] += _prod(shape) * (4 if dt == F32 else 2)
        return v
    SCR = {dd: (al([4, 8, 16]), al([4, 8, 16]), al([4, 8, 16]), al([4, 8, 16]), al([4, 8, 16])) for dd in range(2)}
    bQs = {0: Buf(), 1: Buf()}; bWs = {0: Buf(), 1: Buf()}
    WPR = {0: al([4, 128]), 1: al([4, 128])}; WPI = {0: al([4, 128]), 1: al([4, 128])}
    QZR = {0: al([8, 128]), 1: al([8, 128])}; QZI = {0: al([8, 128]), 1: al([8, 128])}
    W3S = [al([2, 2, 8, 128], BF16), al([2, 2, 8, 128], BF16)]
    TT1 = al([8, 128]); TT2 = al([8, 128])
    TC = [al([8, 128], BF16), al([8, 128], BF16)]
    assert o[0] <= 128 * KB
    bWPs = {0: Buf(), 1: Buf()}; bQZs = {0: Buf(), 1: Buf()}; bS = [Buf(), Buf()]; bTT = Buf(); bTC = [Buf(), Buf()]
    sh = [128, 4, 8, 16]
    for gc in range(8):
        gs = slice(gc * 4, gc * 4 + 4)
        w3s = W3S[gc % 2]; bs = bS[gc % 2]
        for dd in range(2):
            eng = "dve"
            QR, QI, T1, W3R, W3I = SCR[dd]
            bQ, bW = bQs[dd], bWs[dd]
            bWP, bQZ = bWPs[dd], bQZs[dd]
            q_chunk(K, eng, dd, gc, QR, QI, T1, bQ)
            for gl in range(4):
                for g2 in range(2):
                    msk = K.m01[:, g2:g2 + 1]
                    P.op(eng, lambda e, gl=gl, g2=g2, msk=msk, dd=dd, QR=QR: e.tensor_scalar(
                        QZR[dd][:, 2 * gl + g2, :], QR[:, gl].rearrange("p s c -> p (s c)"), msk, None, ALU.mult),
                        reads=[bQ, K.bC], writes=[bQZ])
                    P.op(eng, lambda e, gl=gl, g2=g2, msk=msk, dd=dd, QI=QI: e.tensor_scalar(
                        QZI[dd][:, 2 * gl + g2, :], QI[:, gl].rearrange("p s c -> p (s c)"), msk, None, ALU.mult),
                        reads=[bQ, K.bC], writes=[bQZ])
            cr = bc(CR[:, dd, gs, :].unsqueeze(2), sh); ci = bc(CI[:, dd, gs, :].unsqueeze(2), sh)
            er = bc(K.ER[:, dd, gs, 16:24].unsqueeze(3), sh); ei = bc(K.EI[:, dd, gs, 16:24].unsqueeze(3), sh)
            wpr = WPR[dd][:, :, :].rearrange("p g (t c) -> p g t c", c=16)
            wpi = WPI[dd][:, :, :].rearrange("p g (t c) -> p g t c", c=16)
            cplx_outer(K, eng, wpr, wpi, er, ei, cr, ci, T1, [K.bTab, bCc], [bWP], neg_im=True)
            er = bc(K.ER[:, dd, gs, 8:16].unsqueeze(3), sh); ei = bc(K.EI[:, dd, gs, 8:16].unsqueeze(3), sh)
            cplx_outer(K, eng, W3R, W3I, er, ei, cr, ci, T1, [K.bTab, bCc], [bW], neg_im=True)
            for ri, W in ((0, W3R), (1, W3I)):
                for g2 in range(2):
                    P.op(eng, lambda e, ri=ri, W=W, g2=g2, dd=dd, w3s=w3s: e.tensor_scalar(
                        w3s[:, dd, ri, g2:8:2, :], W[:, :, :, :].rearrange("p g t c -> p g (t c)"), K.m01[:, g2:g2 + 1], None, ALU.mult),
                        reads=[bW, K.bC], writes=[bs])
        P.dma("sp", K.W3d[:, :, :, gc * 8:(gc + 1) * 8, :], w3s, reads=[bs], writes=[K.bW3d])
        for dd in range(2):
            for hb in range(2):
                ps = K.PSF[dd * 2 + hb]; bp = K.bPS[dd * 2 + hb]
                fns = []
                for gq in range(4):
                    g = hb * 4 + gq
                    gl = g // 2
                    fns.append(lambda e, g=g, gl=gl, gq=gq, ps=ps, dd=dd: e.matmul(
                        ps[:, gq * 128:(gq + 1) * 128], lhsT=QZR[dd][:, g, :], rhs=WPR[dd][:, gl, :], start=True, stop=False))
                    fns.append(lambda e, g=g, gl=gl, gq=gq, ps=ps, dd=dd: e.matmul(
                        ps[:, gq * 128:(gq + 1) * 128], lhsT=QZI[dd][:, g, :], rhs=WPI[dd][:, gl, :], start=False, stop=True))
                P.op("pe", fns, reads=[bQZs[dd], bWPs[dd]], writes=[bp])
        s4 = [128, 4, 128]
        for hb in range(2):
            pa = K.PSF[hb][:, :].rearrange("p (g q) -> p g q", q=128)
            pb = K.PSF[2 + hb][:, :].rearrange("p (g q) -> p g q", q=128)
            P.op("dve", lambda e, hb=hb, pa=pa: e.tensor_tensor(TT1[:, hb * 4:(hb + 1) * 4, :], pa, bc(K.maskF[:, :].unsqueeze(1), s4), ALU.mult),
                 reads=[K.bPS[hb], K.bC], writes=[bTT])
            P.op("dve", lambda e, hb=hb, pb=pb: e.tensor_tensor(TT2[:, hb * 4:(hb + 1) * 4, :], pb, bc(K.maskB[:, :].unsqueeze(1), s4), ALU.mult),
                 reads=[K.bPS[2 + hb], K.bC], writes=[bTT])
        P.op("dve", lambda e: e.tensor_tensor(TT1, TT1, TT2, ALU.add), reads=[bTT], writes=[bTT])
        tc_ = TC[gc % 2]; btc = bTC[gc % 2]
        for g in range(8):
            gg = gc * 8 + g
            P.op("dve", lambda e, g=g, gg=gg, tc_=tc_: e.scalar_tensor_tensor(
                tc_[:, g, :], K.ident[:], K.dskip[:, gg:gg + 1], TT1[:, g, :], ALU.mult, ALU.add),
                reads=[bTT, K.bC], writes=[btc])
        P.dma("sp", K.Td[:, gc * 8:(gc + 1) * 8, :], tc_, reads=[btc], writes=[K.bTd])
        if "T" in DEBUG and gc == 0:
            K.dbg("T", tc_, [btc])


def phase_s4(K):
    P = K.P
    CH = []
    for i in range(2):
        o = i * 16 * KB
        CH.append(dict(U=aview(K, o, [8, 128], BF16), T=aview(K, o + 2 * KB, [8, 128], BF16),
                       W3=aview(K, o + 4 * KB, [2, 2, 8, 128], BF16),
                       HA=aview(K, o + 12 * KB, [2, 4, 128], BF16), HB=aview(K, o + 14 * KB, [2, 4, 128], BF16),
                       HAc=aview(K, 168 * KB + i * 64, [2, 4], BF16), HBc=aview(K, 168 * KB + 32 + i * 64, [2, 4], BF16), b=Buf()))
    YSG = [aview(K, 32 * KB, [1024], F32), aview(K, 36 * KB, [1024], F32)]; bYSG = [Buf(), Buf()]
    Y2C = aview(K, 40 * KB, [8, 128], F32); G1 = aview(K, 44 * KB, [8, 128], F32)
    G2 = aview(K, 48 * KB, [8, 128], F32); bG = Buf()
    YG = aview(K, 56 * KB, [8, 1024], BF16); bYG = Buf()
    YGT = aview(K, 72 * KB, [8, 1024], BF16); bYGT = Buf()
    WGL = aview(K, 88 * KB, [8, 1024], BF16); bWGL = Buf()
    GT = [aview(K, 104 * KB, [512], F32), aview(K, 106 * KB, [512], F32)]; bGT = [Buf(), Buf()]
    YS = aview(K, 112 * KB, [8, 1024], F32); bYS = Buf()
    YSN = [aview(K, 144 * KB, [1024], BF16), aview(K, 146 * KB, [1024], BF16)]; bYSN = [Buf(), Buf()]
    YST = aview(K, 148 * KB, [8, 1024], BF16); bYST = Buf()
    SS = aview(K, 164 * KB, [8], F32); JNK = aview(K, 165 * KB, [1024], BF16); bSS = Buf()
    for q2 in range(2):
        P.dma("pool", WGL[:, q2 * 4:(q2 + 1) * 4, :], K.w_glu[q2 * 512:(q2 + 1) * 512, :].rearrange("(kt p) c -> p kt c", p=128), writes=[bWGL])
    Y2Cs = [Y2C, aview(K, 52 * KB, [8, 128], F32)]; bY2 = [Buf(), Buf()]
    for blk in range(2):
        def stage_a(gc, blk=blk):
            ch = CH[gc % 2]
            bch = ch["b"]
            P.dma("sp", ch["U"], K.Ud[2 + blk][:, gc * 8:(gc + 1) * 8, :], reads=[K.bUd], writes=[bch])
            P.dma("sp", ch["T"], K.Td[:, gc * 8:(gc + 1) * 8, :], reads=[K.bTd], writes=[bch])
            P.dma("sp", ch["W3"], K.W3d[:, :, :, gc * 8:(gc + 1) * 8, :], reads=[K.bW3d], writes=[bch])
            P.dma("sp", ch["HA"], K.Hd[0, blk][:, :, gc * 4:(gc + 1) * 4, :], reads=[K.bHd], writes=[bch])
            P.dma("sp", ch["HB"], K.Hd[1, blk][:, :, gc * 4:(gc + 1) * 4, :], reads=[K.bHd], writes=[bch])
            P.dma("sp", ch["HAc"], K.Hc[0, blk][:, :, gc * 4:(gc + 1) * 4], reads=[K.bHd], writes=[bch])
            P.dma("sp", ch["HBc"], K.Hc[1, blk][:, :, gc * 4:(gc + 1) * 4], reads=[K.bHd], writes=[bch])
            ysg = YSG[gc % 2]; bys = bYSG[gc % 2]
            y2c = Y2Cs[gc % 2]; by2 = bY2[gc % 2]
            for hb in range(2):
                ps = K.PSF[hb]; bp = K.bPS[hb]
                fns = []
                for gq in range(4):
                    g = hb * 4 + gq
                    gpl = g // 2
                    o_ = ps[:, gq * 128:(gq + 1) * 128]
                    fns.append(lambda e, o_=o_, g=g, ch=ch: e.matmul(o_, lhsT=ch["T"][:, g, :], rhs=ch["U"][:, g, :], start=True, stop=False))
                    for ri in range(2):
                        fns.append(lambda e, o_=o_, g=g, gpl=gpl, ch=ch, ri=ri: e.matmul(
                            o_[:, 1:128], lhsT=ch["W3"][:, 0, ri, g, :], rhs=ch["HA"][:, ri, gpl, 0:127], start=False, stop=False))
                        fns.append(lambda e, o_=o_, g=g, gpl=gpl, ch=ch, ri=ri: e.matmul(
                            o_[:, 0:1], lhsT=ch["W3"][:, 0, ri, g, :], rhs=ch["HAc"][:, ri, gpl:gpl + 1], start=False, stop=False))
                        fns.append(lambda e, o_=o_, g=g, gpl=gpl, ch=ch, ri=ri: e.matmul(
                            o_[:, 0:127], lhsT=ch["W3"][:, 1, ri, g, :], rhs=ch["HB"][:, ri, gpl, 1:128], start=False, stop=False))
                        fns.append(lambda e, o_=o_, g=g, gpl=gpl, ch=ch, ri=ri: e.matmul(
                            o_[:, 127:128], lhsT=ch["W3"][:, 1, ri, g, :], rhs=ch["HBc"][:, ri, gpl:gpl + 1], start=False, stop=(ri == 1)))
                P.op("pe", fns, reads=[bch], writes=[bp])
                P.op("act", lambda e, hb=hb, ps=ps, ysg=ysg: e.activation(ysg[:, hb * 512:(hb + 1) * 512], ps[:, :], ACT.Identity),
                     reads=[bp], writes=[bys])
            for hb in range(2):
                ps = K.PSF[2 + hb]; bp = K.bPS[2 + hb]
                fns = []
                for gq in range(4):
                    g = hb * 4 + gq
                    fns.append(lambda e, gq=gq, g=g, ps=ps, ysg=ysg: e.transpose(ps[:, gq * 128:(gq + 1) * 128], ysg[:, g * 128:(g + 1) * 128], K.ident[:]))
                P.op("pe", fns, reads=[bys, K.bC], writes=[bp])
                P.op("dve", lambda e, hb=hb, ps=ps, y2c=y2c: e.tensor_copy(
                    y2c[:, :, hb * 64:(hb + 1) * 64].rearrange("p t (g c) -> p g t c", c=16),
                    ps[:, :].rearrange("p (g t c) -> p g t c", t=8, c=16)), reads=[bp], writes=[by2])

        def stage_b(gc, blk=blk):
            y2c = Y2Cs[gc % 2]; by2 = bY2[gc % 2]
            if "ypre" in DEBUG and blk == 0 and gc == 0:
                K.dbg("ypre", y2c, [by2])
            P.op("dve", lambda e: e.tensor_tensor(G1, y2c, y2c, ALU.mult), reads=[by2], writes=[bG])
            P.op("dve", lambda e: e.tensor_scalar(G1, G1, 0.044715, 1.0, ALU.mult, ALU.add), reads=[bG], writes=[bG])
            P.op("dve", lambda e: e.tensor_tensor(G1, G1, y2c, ALU.mult), reads=[bG, by2], writes=[bG])
            P.op("act", lambda e: e.activation(G2[:, :, :].rearrange("p t c -> p (t c)"), G1[:, :, :].rearrange("p t c -> p (t c)"),
                                               ACT.Sigmoid, scale=1.5957691216057308), reads=[bG], writes=[bG])
            P.op("dve", lambda e: e.tensor_tensor(YS[:, :, gc * 128:(gc + 1) * 128], y2c, G2, ALU.mult), reads=[bG, by2], writes=[bYS])
            P.op("dve", lambda e: e.tensor_copy(YG[:, :, gc * 128:(gc + 1) * 128], YS[:, :, gc * 128:(gc + 1) * 128]), reads=[bYS], writes=[bYG])

        def stage_c(gc):
            pt = K.PSB[gc % 2]; bpt = K.bPS[6 + gc % 2]
            fns = []
            for t in range(8):
                fns.append(lambda e, t=t, pt=pt: e.transpose(pt[:, t * 128:(t + 1) * 128], YG[:, t, gc * 128:(gc + 1) * 128], K.identb[:]))
            P.op("pe", fns, reads=[bYG, K.bC], writes=[bpt])
            P.op("dve", lambda e, pt=pt: e.tensor_copy(
                YGT[:, gc, :].rearrange("p (j t) -> p t j", t=8), pt[:, :].rearrange("p (t j) -> p t j", j=128)), reads=[bpt], writes=[bYGT])

        stage_a(0)
        for gc in range(8):
            if gc + 1 < 8:
                stage_a(gc + 1)
            stage_b(gc)
            stage_c(gc)
        if "yg" in DEBUG and blk == 0:
            K.dbg("yg", YS, [bYS])
        n = 0
        for t in range(8):
            for hf in range(2):
                ps = K.PSF[n % 2]; bp = K.bPS[n % 2]
                gt = GT[n % 2]; bgt = bGT[n % 2]; n += 1
                fns = []
                for kt in range(8):
                    fns.append(lambda e, kt=kt, t=t, hf=hf, ps=ps: e.matmul(
                        ps[:, :], lhsT=YGT[:, kt, t::8], rhs=WGL[:, kt, hf * 512:(hf + 1) * 512], start=(kt == 0), stop=(kt == 7)))
                P.op("pe", fns, reads=[bYGT, bWGL], writes=[bp])
                P.op("dve", lambda e, hf=hf, ps=ps, gt=gt: e.tensor_tensor(gt, ps[:, :], K.bglu[:, hf * 512:(hf + 1) * 512], ALU.add),
                     reads=[bp, K.bC], writes=[bgt])
                P.op("act", lambda e, gt=gt: e.activation(gt, gt, ACT.Sigmoid), reads=[bgt], writes=[bgt])
                P.op("dve", lambda e, t=t, hf=hf, gt=gt: e.tensor_tensor(
                    YS[:, t, hf * 512:(hf + 1) * 512], YS[:, t, hf * 512:(hf + 1) * 512], gt, ALU.mult), reads=[bgt, bYS], writes=[bYS])
        if "ys" in DEBUG and blk == 0:
            K.dbg("ys", YS, [bYS])
        for t in range(8):
            P.op("act", lambda e, t=t: e.activation(JNK, YS[:, t, :], ACT.Square, accum_out=SS[:, t:t + 1]), reads=[bYS], writes=[bSS])
        P.op("dve", lambda e: e.tensor_scalar(SS, SS, 1.0 / 1024, EPS, ALU.mult, ALU.add), reads=[bSS], writes=[bSS])
        P.op("act", lambda e: e.activation(SS, SS, ACT.Sqrt), reads=[bSS], writes=[bSS])
        P.op("dve", lambda e: e.reciprocal(SS, SS), reads=[bSS], writes=[bSS])
        for t in range(8):
            ysn = YSN[t % 2]; bysn = bYSN[t % 2]
            P.op("dve", lambda e, t=t, ysn=ysn: e.tensor_scalar(ysn, YS[:, t, :], SS[:, t:t + 1], None, ALU.mult), reads=[bYS, bSS], writes=[bysn])
            pt = K.PSB[t % 2]; bpt = K.bPS[6 + t % 2]
            fns = []
            for kt in range(8):
                fns.append(lambda e, kt=kt, pt=pt, ysn=ysn: e.transpose(pt[:, kt * 128:(kt + 1) * 128], ysn[:, kt * 128:(kt + 1) * 128], K.identb[:]))
            P.op("pe", fns, reads=[bysn, K.bC], writes=[bpt])
            P.op("dve", lambda e, t=t, pt=pt: e.tensor_copy(YST[:, :, t::8], pt[:, :].rearrange("p (k j) -> p k j", j=128)), reads=[bpt], writes=[bYST])
        P.dma("sp", K.mixT[:, 8:16, blk * 1024:(blk + 1) * 1024], YST, reads=[bYST], writes=[K.bmix])


def phase_a(K):
    P = K.P
    QT = aview(K, 0, [8, 2048], BF16); bQT = Buf()
    KT = aview(K, 32 * KB, [8, 2304], BF16); bKT = Buf()
    VV = aview(K, 68 * KB, [18, 16, 80], BF16); bVV = Buf()
    A0 = 68 * KB + 18 * 16 * 80 * 2
    A0 = (A0 + 63) // 64 * 64
    XC = [aview(K, A0, [16, 512], BF16), aview(K, A0 + 16 * KB, [16, 512], BF16)]; bXC = [Buf(), Buf()]
    WH = aview(K, A0 + 32 * KB, [16, 512], BF16); bWH = Buf()
    o = A0 + 48 * KB
    RX = aview(K, o, [2304], F32); RX2 = aview(K, o + 9 * KB, [2304], F32); bRX = Buf()
    o += 18 * KB
    SQ = [aview(K, o, [512], BF16), aview(K, o + 1 * KB, [512], BF16)]; bSQ = [Buf(), Buf()]
    MS = [aview(K, o + 2 * KB, [512], F32), aview(K, o + 4 * KB, [512], F32)]; bMS = [Buf(), Buf()]
    RK = aview(K, o + 6 * KB, [18], F32); bRK = Buf()
    assert o + 7 * KB <= K.ARENA_BYTES, o
    chunks = [(0, 256), (256, 512), (768, 512), (1280, 512), (1792, 512)]

    def load_x(ci, nb):
        c0, cw = chunks[ci]
        xc = XC[nb % 2]; bx = bXC[nb % 2]
        for q2 in range(2):
            P.dma("pool", xc[:, q2 * 8:(q2 + 1) * 8, 0:cw],
                  K.xT[q2 * 1024:(q2 + 1) * 1024, 1792 + c0:1792 + c0 + cw].rearrange("(kt p) t -> p kt t", p=128), writes=[bx])
        return xc, bx
    nb = 0
    P.op("dve", lambda e: e.memset(VV[:, :, :, 64:65], 1.0), writes=[bVV])
    for ci, (c0, cw) in enumerate(chunks):
        xc, bx = load_x(ci, nb); nb += 1
        ps = K.PSF[0]; bp = K.bPS[0]
        for kt in range(16):
            sq = SQ[kt % 2]; bs = bSQ[kt % 2]
            P.op("act", lambda e, kt=kt, sq=sq, xc=xc, cw=cw: e.activation(sq[:, 0:cw], xc[:, kt, 0:cw], ACT.Square), reads=[bx], writes=[bs])
            P.op("pe", lambda e, kt=kt, sq=sq, cw=cw: e.matmul(ps[:, 0:cw], lhsT=K.onesb[:, :], rhs=sq[:, 0:cw], start=(kt == 0), stop=(kt == 15)),
                 reads=[bs, K.bC], writes=[bp])
        P.op("dve", lambda e, c0=c0, cw=cw: e.tensor_scalar(RX[:, c0:c0 + cw], ps[:, 0:cw], 1.0 / DM, EPS, ALU.mult, ALU.add), reads=[bp], writes=[bRX])
        P.op("dve", lambda e, c0=c0, cw=cw: e.tensor_scalar(RX2[:, c0:c0 + cw], RX[:, c0:c0 + cw], EPS, None, ALU.mult), reads=[bRX], writes=[bRX])
    P.op("act", lambda e: e.activation(RX, RX, ACT.Sqrt), reads=[bRX], writes=[bRX])
    P.op("dve", lambda e: e.reciprocal(RX, RX), reads=[bRX], writes=[bRX])
    pk = K.PSF[1]; bpk = K.bPS[1]
    fns = []
    for tile in range(18):
        fns.append(lambda e, tile=tile: e.matmul(pk[:, tile:tile + 1], lhsT=RX[:, tile * 128:(tile + 1) * 128], rhs=K.inv128[:, 0:1], start=True, stop=True))
    P.op("pe", fns, reads=[bRX, K.bC], writes=[bpk])
    P.op("dve", lambda e: e.tensor_copy(RK, pk[:, 0:18]), reads=[bpk], writes=[bRK])

    def load_w(col0):
        for q2 in range(2):
            P.dma("pool", WH[:, q2 * 8:(q2 + 1) * 8, :], K.w_in[q2 * 1024:(q2 + 1) * 1024, col0:col0 + 512].rearrange("(kt p) c -> p kt c", p=128), writes=[bWH])
        for kt in range(16):
            P.op("dve", lambda e, kt=kt: e.tensor_scalar(WH[:, kt, :], WH[:, kt, :], K.gmix[:, kt:kt + 1], None, ALU.mult), reads=[bWH, K.bC], writes=[bWH])
    npp = 0
    tiles = []
    for sel in range(2):
        for hf in range(2):
            for ci, (c0, cw) in enumerate(chunks):
                if sel == 0 and ci == 0:
                    continue
                for hl in range(4):
                    tiles.append((sel, hf, ci, hl))
    cur_x = [None, None]

    def qk_front(ti):
        sel, hf, ci, hl = tiles[ti]
        c0, cw = chunks[ci]
        first_ci = 1 if sel == 0 else 0
        if hl == 0 and ci == first_ci:
            load_w(sel * 1024 + hf * 512)
        if hl == 0:
            cur_x[0], cur_x[1] = load_x(ci, qk_front.nb); qk_front.nb += 1
        xc, bx = cur_x
        ps = K.PSF[ti % 3]; bp = K.bPS[ti % 3]
        sq = SQ[ti % 2]; bs = bSQ[ti % 2]
        fns = []
        for kt in range(16):
            fns.append(lambda e, kt=kt: e.matmul(
                ps[:, 0:cw], lhsT=WH[:, kt, hl * 128:(hl + 1) * 128], rhs=xc[:, kt, 0:cw], start=(kt == 0), stop=(kt == 15)))
        P.op("pe", fns, reads=[bWH, bx], writes=[bp])
        P.op("act", lambda e: e.activation(sq[:, 0:cw], ps[:, 0:cw], ACT.Square), reads=[bp], writes=[bs])
    qk_front.nb = nb

    def qk_back(ti):
        sel, hf, ci, hl = tiles[ti]
        c0, cw = chunks[ci]
        DST = QT if sel == 0 else KT
        bD = bQT if sel == 0 else bKT
        gain = K.qkg[:, sel:sel + 1]
        d0 = c0 - 256 if sel == 0 else c0
        hp = hf * 4 + hl
        ps = K.PSF[ti % 3]; bp = K.bPS[ti % 3]
        p2 = K.PSF[3 + ti % 2]; bp2 = K.bPS[3 + ti % 2]
        sq = SQ[ti % 2]; bs = bSQ[ti % 2]
        ms = MS[ti % 2]; bm = bMS[ti % 2]
        P.op("pe", lambda e: e.matmul(p2[:, 0:cw], lhsT=K.blk1[:, :], rhs=sq[:, 0:cw], start=True, stop=True),
             reads=[bs, K.bC], writes=[bp2])
        P.op("dve", lambda e: e.scalar_tensor_tensor(
            ms[:, 0:cw], p2[:, 0:cw], 1.0 / 64, RX2[:, c0:c0 + cw], ALU.mult, ALU.add), reads=[bp2, bRX], writes=[bm])
        P.op("act", lambda e: e.activation(ms[:, 0:cw], ms[:, 0:cw], ACT.Sqrt), reads=[bm], writes=[bm])
        P.op("dve", lambda e: e.reciprocal(ms[:, 0:cw], ms[:, 0:cw]), reads=[bm], writes=[bm])
        P.op("dve", lambda e: e.scalar_tensor_tensor(
            DST[:, hp, d0:d0 + cw], ps[:, 0:cw], gain, ms[:, 0:cw], ALU.mult, ALU.mult), reads=[bp, bm, K.bC], writes=[bD])

    for ti in range(len(tiles) + 1):
        if ti < len(tiles):
            qk_front(ti)
        if ti >= 1:
            qk_back(ti - 1)
    nb = qk_front.nb
    npp = len(tiles)
    for hf in range(2):
        load_w(2048 + hf * 512)
        for ci, (c0, cw) in enumerate(chunks):
            xc, bx = load_x(ci, nb); nb += 1
            for tl in range(cw // 128):
                tile = c0 // 128 + tl
                ps = K.PSF[npp % 2]; bp = K.bPS[npp % 2]; npp += 1
                fns = []
                for kt in range(16):
                    fns.append(lambda e, kt=kt, tl=tl, ps=ps, xc=xc: e.matmul(
                        ps[:, :], lhsT=xc[:, kt, tl * 128:(tl + 1) * 128], rhs=WH[:, kt, :], start=(kt == 0), stop=(kt == 15)))
                P.op("pe", fns, reads=[bWH, bx], writes=[bp])
                P.op("dve", lambda e, tile=tile, hf=hf, ps=ps: e.tensor_scalar(
                    VV[:, tile, hf * 8:(hf + 1) * 8, 0:64], ps[:, :].rearrange("p (h d) -> p h d", d=64), RK[:, tile:tile + 1], None, ALU.mult),
                    reads=[bp, bRK], writes=[bVV])
    if "qT" in DEBUG:
        K.dbg("qT", QT, [bQT]); K.dbg("kT", KT, [bKT]); K.dbg("vv", VV, [bVV])
    P.barrier()
    B0 = A0
    YA = aview(K, B0, [16, 1024], BF16); bYA = Buf()
    o = B0 + 32 * KB
    BIAS = [aview(K, o, [15, 128], F32), aview(K, o + 7680, [15, 128], F32)]; bBI = [Buf(), Buf()]
    o += 15360
    QZ = [aview(K, o, [2048], BF16), aview(K, o + 4 * KB, [2048], BF16)]; bQZ = [Buf(), Buf()]
    o += 8 * KB
    SB = [aview(K, o + i * 2560, [5, 128], F32) for i in range(3)]; bSB = [Buf() for _ in range(3)]
    o += 7680
    PT = [aview(K, o + i * 1280, [5, 128], BF16) for i in range(3)]; bPT = [Buf() for _ in range(3)]
    o += 3840
    RD = [aview(K, o, [1], F32), aview(K, o + 64, [1], F32)]; bRD = [Buf(), Buf()]
    o += 128
    YN = [aview(K, o, [1024], BF16), aview(K, o + 2 * KB, [1024], BF16)]; bYN = [Buf(), Buf()]
    o += 4 * KB
    YT = [aview(K, o, [8, 128], BF16), aview(K, o + 2 * KB, [8, 128], BF16)]; bYT = [Buf(), Buf()]
    o += 4 * KB
    SSA = aview(K, o, [16], F32); JNK = aview(K, o + 64, [1024], BF16); bSSA = Buf()
    o += 64 + 2 * KB
    assert o <= K.ARENA_BYTES, o
    iters = [(head, n) for head in range(NH) for n in range(16, 32)]
    st = {}

    def stage_front(it):
        head, n = iters[it]
        hp, hh = head // 2, head % 2
        bi = BIAS[head % 2]; bbi = bBI[head % 2]
        qz = QZ[head % 2]; bqz = bQZ[head % 2]
        if n == 16:
            P.dma("sp", bi, K.biasd[head], writes=[bbi])
            P.op("act", lambda e: e.activation(qz, QT[:, hp, :], ACT.Identity, scale=K.m01[:, hh:hh + 1]), reads=[bQT, K.bC], writes=[bqz])
        qi = n - 16
        ks = min(max(n - 2, 0), 27)
        v0 = 0 if n <= 29 else (5 if n == 30 else 10)
        psA = K.PSF[(it % 3) * 2]; bpA = K.bPS[(it % 3) * 2]
        psB = K.PSF[(it % 3) * 2 + 1]; bpB = K.bPS[(it % 3) * 2 + 1]
        sb = SB[it % 3]; bsb = bSB[it % 3]
        pt = PT[it % 3]; bpt = bPT[it % 3]
        fns = []
        for i in range(5):
            kti = ks + i - 14
            dst = psA[:, i * 128:(i + 1) * 128] if i < 4 else psB[:, 0:128]
            fns.append(lambda e, dst=dst, kti=kti: e.matmul(
                dst, lhsT=KT[:, hp, kti * 128:(kti + 1) * 128], rhs=qz[:, qi * 128:(qi + 1) * 128], start=True, stop=True))
        P.op("pe", fns, reads=[bKT, bqz], writes=[bpA, bpB])
        P.op("dve", lambda e: e.tensor_tensor(
            sb[:, 0:4, :], psA[:, :].rearrange("p (i q) -> p i q", q=128), bi[:, v0:v0 + 4, :], ALU.add), reads=[bpA, bbi], writes=[bsb])
        P.op("dve", lambda e: e.tensor_tensor(sb[:, 4, :], psB[:, 0:128], bi[:, v0 + 4, :], ALU.add),
             reads=[bpB, bbi], writes=[bsb])
        P.op("act", lambda e: e.activation(pt[:, :, :].rearrange("p i q -> p (i q)"), sb[:, :, :].rearrange("p i q -> p (i q)"), ACT.Exp),
             reads=[bsb], writes=[bpt])

    def stage_back(it):
        head, n = iters[it]
        qi = n - 16
        ks = min(max(n - 2, 0), 27)
        pso = K.PSB[it % 2][:, :].bitcast(F32); bpo = K.bPS[6 + it % 2]
        pt = PT[it % 3]; bpt = bPT[it % 3]
        rd = RD[it % 2]; brd = bRD[it % 2]
        fns = []
        for i in range(5):
            kti = ks + i - 14
            fns.append(lambda e, i=i, kti=kti: e.matmul(
                pso[:, 0:65], lhsT=pt[:, i, :], rhs=VV[:, kti, head, 0:65], start=(i == 0), stop=(i == 4)))
        P.op("pe", fns, reads=[bpt, bVV], writes=[bpo])
        P.op("dve", lambda e: e.reciprocal(rd, pso[:, 64:65]), reads=[bpo], writes=[brd])
        P.op("act", lambda e: e.activation(YA[:, qi, head * 64:(head + 1) * 64], pso[:, 0:64], ACT.Identity, scale=rd[:, 0:1]),
             reads=[bpo, brd], writes=[bYA])

    for it in range(len(iters) + 2):
        if it < len(iters):
            stage_front(it)
        if it >= 2:
            stage_back(it - 2)
    if "ya" in DEBUG:
        K.dbg("ya", YA, [bYA])
    for qi in range(16):
        P.op("act", lambda e, qi=qi: e.activation(JNK, YA[:, qi, :], ACT.Square, accum_out=SSA[:, qi:qi + 1]), reads=[bYA], writes=[bSSA])
    P.op("dve", lambda e: e.tensor_scalar(SSA, SSA, 1.0 / 1024, EPS, ALU.mult, ALU.add), reads=[bSSA], writes=[bSSA])
    P.op("act", lambda e: e.activation(SSA, SSA, ACT.Sqrt), reads=[bSSA], writes=[bSSA])
    P.op("dve", lambda e: e.reciprocal(SSA, SSA), reads=[bSSA], writes=[bSSA])
    for qi in range(16):
        yn = YN[qi % 2]; byn = bYN[qi % 2]
        yt = YT[qi % 2]; byt = bYT[qi % 2]
        P.op("dve", lambda e, qi=qi, yn=yn: e.tensor_scalar(yn, YA[:, qi, :], SSA[:, qi:qi + 1], None, ALU.mult), reads=[bYA, bSSA], writes=[byn])
        pt = K.PSB[qi % 2]; bpt = K.bPS[6 + qi % 2]
        fns = []
        for kt in range(8):
            fns.append(lambda e, kt=kt, pt=pt, yn=yn: e.transpose(pt[:, kt * 128:(kt + 1) * 128], yn[:, kt * 128:(kt + 1) * 128], K.identb[:]))
        P.op("pe", fns, reads=[byn, K.bC], writes=[bpt])
        P.op("act", lambda e, pt=pt, yt=yt: e.activation(yt[:, :, :].rearrange("p k j -> p (k j)"), pt[:, :], ACT.Identity), reads=[bpt], writes=[byt])
        P.dma("sp", K.mixT[:, 0:8, qi * 128:(qi + 1) * 128], yt, reads=[byt], writes=[K.bmix])


def phase_o(K):
    P = K.P
    FG = [6, 6, 6, 6, 5, 5, 5, 5]
    for tb in range(2):
        P.barrier()
        X1 = aview(K, 0, [8, 2048], F32); bX1 = [Buf() for _ in range(8)]
        WO = aview(K, 64 * KB, [16, 2048], BF16); bWO = Buf()
        MIXC = aview(K, 128 * KB, [16, 512], BF16); bMX = Buf()
        XO = [aview(K, 144 * KB, [2048], F32), aview(K, 152 * KB, [2048], F32)]; bXO = [Buf(), Buf()]
        for q4 in range(4):
            P.dma("pool", WO[:, q4 * 4:(q4 + 1) * 4, :], K.w_out[q4 * 512:(q4 + 1) * 512, :].rearrange("(kt p) c -> p kt c", p=128), writes=[bWO])
        for kt in range(16):
            P.op("dve", lambda e, kt=kt: e.tensor_scalar(WO[:, kt, :], WO[:, kt, :], K.gout[:, kt:kt + 1], None, ALU.mult), reads=[bWO, K.bC], writes=[bWO])
        n = 0
        for s in range(2):
            P.dma("sp", MIXC, K.mixT[:, :, tb * 1024 + s * 512: tb * 1024 + (s + 1) * 512], reads=[K.bmix], writes=[bMX])
            for tt in range(4):
                tile = s * 4 + tt
                xo = XO[tile % 2]; bxo = bXO[tile % 2]
                r0 = (tb * 8 + tile) * 128
                P.dma("sp", xo, K.xown[r0:r0 + 128, :], writes=[bxo])
                for dc in range(4):
                    ps = K.PSF[n % 4]; bp = K.bPS[n % 4]; n += 1
                    fns = []
                    for kt in range(16):
                        fns.append(lambda e, kt=kt, tt=tt, dc=dc, ps=ps: e.matmul(
                            ps[:, :], lhsT=MIXC[:, kt, tt * 128:(tt + 1) * 128], rhs=WO[:, kt, dc * 512:(dc + 1) * 512], start=(kt == 0), stop=(kt == 15)))
                    P.op("pe", fns, reads=[bMX, bWO], writes=[bp])
                    P.op("dve", lambda e, tile=tile, dc=dc, ps=ps, xo=xo: e.tensor_tensor(
                        X1[:, tile, dc * 512:(dc + 1) * 512], ps[:, :], xo[:, dc * 512:(dc + 1) * 512], ALU.add), reads=[bp, bxo], writes=[bX1[tile]])
        if "x1" in DEBUG and tb == 0:
            K.dbg("x1", X1, bX1)
        NB = 3
        WG = [aview(K, 160 * KB + i * 4 * KB, [16, 128], BF16) for i in range(NB)]; bWG = [Buf() for _ in range(NB)]
        WUp = [aview(K, 172 * KB + i * 4 * KB, [16, 128], BF16) for i in range(NB)]; bWUp = [Buf() for _ in range(NB)]

        def load_gu(f):
            P.dma("pool", WG[f % NB], K.w_gate[f], writes=[bWG[f % NB]])
            P.dma("pool", WUp[f % NB], K.w_up[f], writes=[bWUp[f % NB]])
        for f in range(NB):
            load_gu(f)
        P.barrier()
        H2T = aview(K, 64 * KB, [16, 1024], BF16); bH2T = Buf()
        ACTT = aview(K, 96 * KB, [6, 1024], BF16); bAT = Buf()
        WD = aview(K, 108 * KB, [6, 2048], BF16); bWD = Buf()
        SG = [aview(K, 132 * KB, [512], BF16), aview(K, 133 * KB, [512], BF16)]; bSG = [Buf(), Buf()]
        H2 = [aview(K, 134 * KB, [2048], BF16), aview(K, 138 * KB, [2048], BF16)]; bH2 = [Buf(), Buf()]
        S2 = aview(K, 142 * KB, [8], F32); JNK = aview(K, 143 * KB, [2048], BF16); bS2 = Buf()
        for tile in range(8):
            P.op("act", lambda e, tile=tile: e.activation(JNK, X1[:, tile, :], ACT.Square, accum_out=S2[:, tile:tile + 1]), reads=[bX1[tile]], writes=[bS2])
        P.op("dve", lambda e: e.tensor_scalar(S2, S2, 1.0 / DM, EPS, ALU.mult, ALU.add), reads=[bS2], writes=[bS2])
        P.op("act", lambda e: e.activation(S2, S2, ACT.Sqrt), reads=[bS2], writes=[bS2])
        P.op("dve", lambda e: e.reciprocal(S2, S2), reads=[bS2], writes=[bS2])
        for tile in range(8):
            h2 = H2[tile % 2]; bh2 = bH2[tile % 2]
            P.op("dve", lambda e, tile=tile, h2=h2: e.scalar_tensor_tensor(h2, X1[:, tile, :], S2[:, tile:tile + 1], K.gffn[:, :], ALU.mult, ALU.mult),
                 reads=[bX1[tile], bS2, K.bC], writes=[bh2])
            for hb in range(2):
                pt = K.PSB[hb]; bpt = K.bPS[6 + hb]
                fns = []
                for k8 in range(8):
                    kt = hb * 8 + k8
                    fns.append(lambda e, k8=k8, kt=kt, pt=pt, h2=h2: e.transpose(pt[:, k8 * 128:(k8 + 1) * 128], h2[:, kt * 128:(kt + 1) * 128], K.identb[:]))
                P.op("pe", fns, reads=[bh2, K.bC], writes=[bpt])
                P.op("dve", lambda e, hb=hb, tile=tile, pt=pt: e.tensor_copy(
                    H2T[:, hb * 8:(hb + 1) * 8, tile * 128:(tile + 1) * 128], pt[:, :].rearrange("p (k j) -> p k j", j=128)), reads=[bpt], writes=[bH2T])
        f0 = 0
        npg = 0
        for grp, nf in enumerate(FG):
            for q in range(nf):
                r0 = (f0 + q) * 128
                P.dma("pool", WD[:, q, :], K.w_down[r0:r0 + 128, :], writes=[bWD])
            for q in range(nf):
                f = f0 + q
                wg = WG[f % NB]; bwg = bWG[f % NB]; wu = WUp[f % NB]; bwu = bWUp[f % NB]
                for s in range(2):
                    pg = K.PSF[(npg % 2) * 2]; bpg = K.bPS[(npg % 2) * 2]
                    pu = K.PSF[(npg % 2) * 2 + 1]; bpu = K.bPS[(npg % 2) * 2 + 1]
                    sg = SG[npg % 2]; bsg = bSG[npg % 2]; npg += 1
                    fg = []
                    fu = []
                    for kt in range(16):
                        fg.append(lambda e, kt=kt, s=s, pg=pg, wg=wg: e.matmul(pg[:, :], lhsT=wg[:, kt, :], rhs=H2T[:, kt, s * 512:(s + 1) * 512], start=(kt == 0), stop=(kt == 15)))
                        fu.append(lambda e, kt=kt, s=s, pu=pu, wu=wu: e.matmul(pu[:, :], lhsT=wu[:, kt, :], rhs=H2T[:, kt, s * 512:(s + 1) * 512], start=(kt == 0), stop=(kt == 15)))
                    P.op("pe", fg, reads=[bwg, bH2T], writes=[bpg])
                    P.op("pe", fu, reads=[bwu, bH2T], writes=[bpu])
                    P.op("act", lambda e, pg=pg, sg=sg: e.activation(sg, pg[:, :], ACT.Silu), reads=[bpg], writes=[bsg])
                    P.op("dve", lambda e, pu=pu, sg=sg, q=q, s=s: e.tensor_tensor(ACTT[:, q, s * 512:(s + 1) * 512], pu[:, :], sg, ALU.mult),
                         reads=[bpu, bsg], writes=[bAT])
                if f + NB < NFT:
                    load_gu(f + NB)
            nd = 0
            for tile in range(8):
                for dc in range(4):
                    ps = K.PSF[4 + nd % 2]; bp = K.bPS[4 + nd % 2]; nd += 1
                    fns = []
                    for q in range(nf):
                        fns.append(lambda e, q=q, tile=tile, dc=dc, ps=ps, nf=nf: e.matmul(
                            ps[:, :], lhsT=ACTT[:, q, tile * 128:(tile + 1) * 128], rhs=WD[:, q, dc * 512:(dc + 1) * 512], start=(q == 0), stop=(q == nf - 1)))
                    P.op("pe", fns, reads=[bAT, bWD], writes=[bp])
                    P.op("dve", lambda e, tile=tile, dc=dc, ps=ps: e.tensor_tensor(
                        X1[:, tile, dc * 512:(dc + 1) * 512], X1[:, tile, dc * 512:(dc + 1) * 512], ps[:, :], ALU.add), reads=[bp, bX1[tile]], writes=[bX1[tile]])
            f0 += nf
        for tile in range(8):
            r0 = (tb * 8 + tile) * 128
            P.dma("sp", K.out[r0:r0 + 128, :], X1[:, tile, :], reads=[bX1[tile]], writes=[K.bOut])


def build_program(phases="all"):
    nc = bass.Bass("TRN2", target_bir_lowering=False)
    K = Ctx()
    K.nc = nc
    dram = lambda name, shape, dt=F32, kind="ExternalInput": nc.dram_tensor(name, list(shape), dt, kind=kind).ap()
    K.xT = dram("xT", [DM, SEQ]); K.xown = dram("xown", [2048, DM])
    K.w_in = dram("w_in", [DM, 4096]); K.w_out = dram("w_out", [DM, DM]); K.w_glu = dram("w_glu", [1024, 1024])
    K.w_gate = dram("w_gate", [NFT, 128, 16, 128]); K.w_up = dram("w_up", [NFT, 128, 16, 128]); K.w_down = dram("w_down", [DFF, DM])
    K.p_are = dram("p_are", [128, 2, 32]); K.p_aim = dram("p_aim", [128, 2, 32]); K.p_ls = dram("p_ls", [128, 2, 32])
    K.p_bre = dram("p_bre", [128, 2, 32, 16]); K.p_bim = dram("p_bim", [128, 2, 32, 16])
    K.p_cre = dram("p_cre", [128, 2, 32, 16]); K.p_cim = dram("p_cim", [128, 2, 32, 16])
    K.p_kv = dram("p_kv", [128, 2, 24])
    K.biasd = dram("biasd", [NH, 128, 15, 128])
    cst = dram("cst", [128, CST_COLS])
    K.out = dram("out", [2048, DM], kind="ExternalOutput")
    K.Ud = nc.dram_tensor("Ud", [4, 128, 64, 128], BF16).ap()
    K.Hd = nc.dram_tensor("Hd", [2, 2, 128, 2, 32, 128], BF16).ap()
    K.Hc = nc.dram_tensor("Hc", [2, 2, 128, 2, 32], BF16).ap()
    K.W3d = nc.dram_tensor("W3d", [128, 2, 2, 64, 128], BF16).ap()
    K.Td = nc.dram_tensor("Td", [128, 64, 128], BF16).ap()
    K.mixT = nc.dram_tensor("mixT", [128, 16, 2048], BF16).ap()
    K.bUd = MBuf(); K.bHd = MBuf(); K.bW3d = MBuf(); K.bTd = MBuf(); K.bmix = MBuf(); K.bOut = MBuf()
    K.dbg_out = {}
    for name, shape in DEBUG.items():
        K.dbg_out[name] = dram("dbg_" + name, [128, _prod(shape)], F32, kind="ExternalOutput")
    with ExitStack() as st:
        P = Prog(nc, st)
        K.P = P
        K.ARENA_BYTES = 190 * KB
        K.arena = st.enter_context(nc.sbuf_tensor("arena", [128, K.ARENA_BYTES // 2], BF16))
        CS = st.enter_context(nc.sbuf_tensor("cs", [128, CST_COLS], F32))
        cb16 = st.enter_context(nc.sbuf_tensor("cb16", [128, 3 * 128], BF16))
        K.dbgt = st.enter_context(nc.sbuf_tensor("dbgt", [128, 64], F32))
        K.inv128 = st.enter_context(nc.sbuf_tensor("inv128", [128, 2], F32))
        K.PSF = [st.enter_context(nc.psum_tensor("psf%d" % i, [128, 512], F32)) for i in range(6)]
        K.PSB = [st.enter_context(nc.psum_tensor("psb%d" % i, [128, 1024], BF16)) for i in range(2)]
        K.bPS = [Buf() for _ in range(8)]
        K.bC = Buf()
        P.dma("sp", CS[:], cst, writes=[K.bC])
        c = CST_OFF
        K.ident = CS[:, c["ident"]:c["ident"] + 128]
        K.maskF = CS[:, c["maskF"]:c["maskF"] + 128]
        K.maskB = CS[:, c["maskB"]:c["maskB"] + 128]
        K.gmix = CS[:, c["gmix"]:c["gmix"] + 16]
        K.gout = CS[:, c["gout"]:c["gout"] + 16]
        K.qkg = CS[:, c["qkg"]:c["qkg"] + 2]
        K.m01 = CS[:, c["m01"]:c["m01"] + 2]
        K.dskip = CS[:, c["dskip"]:c["dskip"] + 64]
        K.bglu = CS[:, c["bglu"]:c["bglu"] + 1024]
        K.gffn = CS[:, c["gffn"]:c["gffn"] + 2048]
        K.identb = cb16[:, 0:128]; K.onesb = cb16[:, 128:256]; K.blk1 = cb16[:, 256:384]
        P.op("dve", lambda e: e.tensor_copy(K.identb, K.ident), reads=[K.bC], writes=[K.bC])
        P.op("dve", lambda e: e.memset(K.onesb, 1.0), writes=[K.bC])
        P.op("dve", lambda e: e.memset(K.inv128[:, :], 1.0 / 128), writes=[K.bC])
        P.op("dve", lambda e: e.tensor_copy(K.blk1, CS[:, c["blk1"]:c["blk1"] + 128]), reads=[K.bC], writes=[K.bC])
        P.op("dve", lambda e: e.tensor_scalar(K.qkg[:, 0:1], K.qkg[:, 0:1], 0.125, None, ALU.mult), reads=[K.bC], writes=[K.bC])
        ndbg = [0]

        def dbg(name, ap, bufs):
            shape = DEBUG[name]
            n = _prod(shape)
            dst = K.dbg_out[name]
            flat = ap
            nd = len(shape)
            if nd == 2:
                flat = ap.rearrange("p a b -> p (a b)")
            elif nd == 3:
                flat = ap.rearrange("p a b c -> p (a b c)")
            elif nd == 4:
                flat = ap.rearrange("p a b c d -> p (a b c d)")
            bd = Buf()
            for c0 in range(0, n, 64):
                w = min(64, n - c0)
                P.op("pool", lambda e, c0=c0, w=w: e.tensor_copy(K.dbgt[:, 0:w], flat[:, c0:c0 + w]), reads=list(bufs) + [bd], writes=[bd])
                P.dma("sp", dst[:, c0:c0 + w], K.dbgt[:, 0:w], reads=[bd], writes=[bd, K.bOut])
        K.dbg = dbg

        if phases in ("all", "ssm", "s1", "ssm_a", "ssm_b"):
            phase_s1(K)
            P.barrier()
        if phases in ("all", "ssm", "ssm_a", "ssm_b"):
            phase_gen1(K)
            P.barrier()
            phase_s2(K)
            P.barrier()
        if phases in ("all", "ssm", "ssm_b"):
            phase_gen2(K)
            P.barrier()
        if phases in ("all", "ssm"):
            phase_s4(K)
            P.barrier()
        if phases in ("all", "attn"):
            phase_a(K)
            P.barrier()
        if phases in ("all", "out"):
            phase_o(K)
        waits = P._deps("sp", [K.bOut], ())
        P.ops["sp"].append((waits, [], None))
        P.emit()
    return nc


CST_OFF = {}
_c = 0
for _n, _w in (("ident", 128), ("maskF", 128), ("maskB", 128), ("blk1", 128), ("gmix", 16), ("gout", 16), ("qkg", 2),
               ("m01", 2), ("dskip", 64), ("bglu", 1024), ("gffn", 2048)):
    CST_OFF[_n] = _c
    _c += _w
CST_COLS = _c


def _lay_gp(a):
    s = a.shape
    a = a.reshape(2, 32, 2, 64, *s[3:])
    a = np.moveaxis(a, [2, 3], [0, 1])
    return np.ascontiguousarray(a.reshape(128, 2, 32, *s[3:]), dtype=np.float32)


def _bias_table(rpb0, flip):
    out = np.full((NH, 15, 128, 128), NEG, np.float32)
    for v in range(15):
        n, i = (20, v) if v < 5 else ((30, v - 5) if v < 10 else (31, v - 10))
        ks = min(max(n - 2, 0), 27)
        kp = ks + i
        kr = np.repeat(np.array([2 * kp, 2 * kp + 1]), 64); kc = np.tile(np.arange(64), 2)
        qr = np.repeat(np.array([2 * n, 2 * n + 1]), 64); qc = np.tile(np.arange(64), 2)
        if flip:
            kr, kc, qr, qc = 63 - kr, 63 - kc, 63 - qr, 63 - qc
        rs = np.clip(qr - 4, 0, 56); cs = np.clip(qc - 8, 0, 48)
        inwin = ((kr[:, None] >= rs[None, :]) & (kr[:, None] < rs[None, :] + 8) &
                 (kc[:, None] >= cs[None, :]) & (kc[:, None] < cs[None, :] + 16))
        dr = np.clip(kr[:, None] - qr[None, :] + 7, 0, 14)
        dc = np.clip(kc[:, None] - qc[None, :], -15, 15) + 15
        vals = rpb0[:, dr, dc]
        out[:, v] = np.where(inwin[None], vals, np.float32(NEG))
    return np.ascontiguousarray(out.transpose(0, 2, 1, 3))


def _consts(inp):
    cs = np.zeros((128, CST_COLS), np.float32)
    o = CST_OFF
    cs[:, o["ident"]:o["ident"] + 128] = np.eye(128, dtype=np.float32)
    s = np.arange(128) // 16
    cs[:, o["maskF"]:o["maskF"] + 128] = (s[:, None] <= s[None, :])
    cs[:, o["maskB"]:o["maskB"] + 128] = (s[:, None] >= s[None, :])
    hh = np.arange(128) // 64
    cs[:, o["blk1"]:o["blk1"] + 128] = (hh[:, None] == hh[None, :])
    cs[:, o["gmix"]:o["gmix"] + 16] = inp["g_mix"][0].reshape(16, 128).T
    gout = np.concatenate([inp["g_out_attn"][0], inp["g_out_ssm"][0]])
    cs[:, o["gout"]:o["gout"] + 16] = gout.reshape(16, 128).T
    cs[:, o["qkg"]] = np.tile(inp["q_gain"][0], 2)
    cs[:, o["qkg"] + 1] = np.tile(inp["k_gain"][0], 2)
    cs[:, o["m01"]] = (hh == 0)
    cs[:, o["m01"] + 1] = (hh == 1)
    cs[:, o["dskip"]:o["dskip"] + 64] = np.tile(inp["ssm_d"][0].reshape(64, 16).T, (8, 1))
    cs[:, o["bglu"]:o["bglu"] + 1024] = inp["b_glu"][0][None, :]
    cs[:, o["gffn"]:o["gffn"] + 2048] = inp["g_ffn"][0][None, :]
    return cs


def _kvals():
    kvA = np.concatenate([np.arange(7, -1, -1), np.arange(1, 9), np.arange(-7, 1)])
    kvB = np.concatenate([np.arange(0, 8), np.arange(8, 0, -1), -np.arange(0, 8)])
    kv = np.stack([kvA, kvB]).astype(np.float32)
    return np.ascontiguousarray(np.broadcast_to(kv[None], (128, 2, 24)))


def prepare_inputs(inp):
    inp = {k: np.asarray(v) for k, v in inp.items()}
    x = inp["x"]
    shared = dict(
        w_in=np.ascontiguousarray(inp["w_in"][0]), w_out=np.ascontiguousarray(inp["w_out"][0]),
        w_glu=np.ascontiguousarray(inp["w_glu"][0]),
        w_gate=np.ascontiguousarray(inp["w_ffn_gate"][0].reshape(16, 128, NFT, 128).transpose(2, 1, 0, 3)),
        w_up=np.ascontiguousarray(inp["w_ffn_up"][0].reshape(16, 128, NFT, 128).transpose(2, 1, 0, 3)), w_down=np.ascontiguousarray(inp["w_ffn_down"][0]),
        cst=_consts(inp), p_kv=_kvals())
    bias_tabs = {f: _bias_table(inp["rpb"][0], f) for f in (False, True)}
    ssm = {}
    for h in (0, 1):
        dirs = [0, 1] if h == 1 else [1, 0]
        ls = np.broadcast_to(inp["ssm_log_step"][0][dirs][:, :, None], (2, 64, 64))
        ssm[h] = dict(
            p_are=_lay_gp(inp["ssm_a_re"][0][dirs]), p_aim=_lay_gp(inp["ssm_a_im"][0][dirs]), p_ls=_lay_gp(ls),
            p_bre=_lay_gp(inp["ssm_b_re"][0][dirs]), p_bim=_lay_gp(inp["ssm_b_im"][0][dirs]),
            p_cre=_lay_gp(inp["ssm_c_re"][0][dirs].transpose(0, 1, 3, 2)), p_cim=_lay_gp(inp["ssm_c_im"][0][dirs].transpose(0, 1, 3, 2)))
    maps = []
    for c in range(8):
        b, h = c // 2, c % 2
        flip = (h == 0)
        xl = x[b][::-1] if flip else x[b]
        m = dict(shared)
        m.update(ssm[h])
        m["xT"] = np.ascontiguousarray(xl.T)
        m["xown"] = np.ascontiguousarray(xl[2048:])
        m["biasd"] = bias_tabs[flip]
        maps.append(m)
    return maps


def assemble(results):
    out = np.empty((4, SEQ, DM), np.float32)
    for c in range(8):
        b, h = c // 2, c % 2
        o = np.asarray(results[c]["out"])
        if h == 0:
            out[b, 0:2048] = o[::-1]
        else:
            out[b, 2048:] = o
    return out


def kernel(**inputs):
    maps = prepare_inputs(inputs)
    nc = build_program("all")
    res = run_bass_kernel_spmd(nc, maps, core_ids=list(range(8)))
    return assemble(res.results)
```

```python
import numpy as np
from contextlib import ExitStack
import concourse.bass as bass
import concourse.mybir as mybir
from concourse.bass_utils import run_bass_kernel_spmd

F32 = mybir.dt.float32
BF16 = mybir.dt.bfloat16
I32 = mybir.dt.int32
ALU = mybir.AluOpType
ACT = mybir.ActivationFunctionType

DM = 2048
SEQ = 4096
NH = 16
DFF = 5632
NFT = DFF // 128
EPS = 1e-6
NEG = -30000.0
ENGS = ("pe", "act", "dve", "pool", "sp")
SAME_ENGINE_SYNC = True
DEBUG = {}


class Buf:
    __slots__ = ("w", "r")

    def __init__(self):
        self.w = {}
        self.r = {}


class MBuf:
    __slots__ = ("ws",)

    def __init__(self):
        self.ws = []


class Prog:
    N_DSEM = 32

    def __init__(self, nc, stack):
        self.nc = nc
        self.ops = {e: [] for e in ENGS}
        self.cnt = {e: 0 for e in ENGS}
        self.sems = {e: stack.enter_context(nc.semaphore("s_" + e)) for e in ENGS}
        self.dsems = [stack.enter_context(nc.semaphore("d%d" % i)) for i in range(self.N_DSEM)]
        self.dcnt = [0] * self.N_DSEM
        self.dnext = [0, 0]
        self.waited = {e: {} for e in ENGS}

    def _deps(self, eng, reads, writes, ses=None, is_dma=False):
        if ses is None:
            ses = SAME_ENGINE_SYNC
        deps = {}

        def add(tok):
            if tok is not None and deps.get(tok[0], 0) < tok[1]:
                deps[tok[0]] = tok[1]
        for b in reads:
            if isinstance(b, MBuf):
                for t in b.ws:
                    add(t)
            else:
                for t in b.w.items():
                    add(t)
        for b in writes:
            if isinstance(b, MBuf):
                continue
            for t in b.w.items():
                if is_dma and not b.r and t[0][1:].isdigit():
                    continue
                add(t)
            for t in b.r.items():
                add(t)
        waits = []
        for k, v in deps.items():
            if k == eng and (eng == "pe" or not ses):
                continue
            if self.waited[eng].get(k, 0) >= v:
                continue
            self.waited[eng][k] = v
            waits.append((k, v))
        return waits

    @staticmethod
    def _mark(tok, reads, writes, is_dma=False):
        for b in reads:
            if isinstance(b, MBuf):
                continue
            if b.r.get(tok[0], 0) < tok[1]:
                b.r[tok[0]] = tok[1]
        for b in writes:
            if isinstance(b, MBuf):
                b.ws.append(tok)
                continue
            if is_dma and not b.r:
                keep = {k: v for k, v in b.w.items() if k[1:].isdigit()}
            else:
                keep = {}
            keep[tok[0]] = max(keep.get(tok[0], 0), tok[1])
            b.w = keep
            b.r = {}

    def op(self, eng, fns, reads=(), writes=(), ses=None):
        if callable(fns):
            fns = [fns]
        waits = self._deps(eng, reads, writes, ses)
        self.cnt[eng] += 1
        tok = (eng, self.cnt[eng])
        self.ops[eng].append((waits, fns, (eng, 1)))
        self._mark(tok, reads, writes)
        return tok

    def dma(self, eng, out, in_, reads=(), writes=()):
        half = self.N_DSEM // 2
        which = 0 if eng == "pool" else 1
        i = which * half + self.dnext[which]
        self.dnext[which] = (self.dnext[which] + 1) % half
        key = "d%d" % i
        waits = self._deps(eng, reads, writes, is_dma=True)
        if self.dcnt[i] > 0 and self.waited[eng].get(key, 0) < self.dcnt[i]:
            self.waited[eng][key] = self.dcnt[i]
            waits.append((key, self.dcnt[i]))
        self.dcnt[i] += 16
        tok = (key, self.dcnt[i])
        self.ops[eng].append((waits, [lambda e: e.dma_start(out=out, in_=in_)], (key, 16)))
        self._mark(tok, reads, writes, is_dma=True)
        return tok

    def barrier(self):
        toks = [(e, self.cnt[e]) for e in ENGS if self.cnt[e] > 0]
        toks += [("d%d" % i, self.dcnt[i]) for i in range(self.N_DSEM) if self.dcnt[i] > 0]
        for eng in ENGS:
            waits = []
            for k, v in toks:
                if k == eng:
                    continue
                if self.waited[eng].get(k, 0) >= v:
                    continue
                self.waited[eng][k] = v
                waits.append((k, v))
            if waits:
                self.ops[eng].append((waits, [], None))

    def _sem(self, key):
        return self.sems[key] if key in self.sems else self.dsems[int(key[1:])]

    def emit(self):
        prog = self

        def run(engname):
            def body(e):
                for waits, fns, inc in prog.ops[engname]:
                    for k, v in waits:
                        e.wait_ge(prog._sem(k), v)
                    last = None
                    for f in fns:
                        last = f(e)
                    if inc is not None and last is not None:
                        last.then_inc(prog._sem(inc[0]), inc[1])
            return body
        with self.nc.Block() as block:
            block.tensor(run("pe"))
            block.scalar(run("act"))
            block.vector(run("dve"))
            block.gpsimd(run("pool"))
            block.sync(run("sp"))


class Ctx:
    pass


def _prod(s):
    n = 1
    for v in s:
        n *= v
    return n


def aview(K, off, shape, dt):
    esz = 4 if dt in (F32, I32) else 2
    n = _prod(shape)
    assert off % 4 == 0 and off + n * esz <= K.ARENA_BYTES, (off, shape, K.ARENA_BYTES)
    a = K.arena[:, off // 2: off // 2 + n * esz // 2]
    if dt != BF16:
        a = a.bitcast(dt)
    if len(shape) == 2:
        a = a.rearrange("p (a b) -> p a b", a=shape[0], b=shape[1])
    elif len(shape) == 3:
        a = a.rearrange("p (a b c) -> p a b c", a=shape[0], b=shape[1], c=shape[2])
    elif len(shape) == 4:
        a = a.rearrange("p (a b c d) -> p a b c d", a=shape[0], b=shape[1], c=shape[2], d=shape[3])
    return a


KB = 1024


def bc(ap, shape):
    return ap.broadcast_to(list(shape))


def phase_s1(K):
    P = K.P
    XB = [aview(K, 0, [16, 1024], BF16), aview(K, 32 * KB, [16, 1024], BF16)]
    bXB = [Buf(), Buf()]
    WU = aview(K, 64 * KB, [16, 1024], BF16); bWU = Buf()
    U2 = aview(K, 96 * KB, [64, 8, 16], BF16); bU2 = Buf()
    UB = [aview(K, 112 * KB, [64, 128], BF16), aview(K, 128 * KB, [64, 128], BF16)]
    bUB = [Buf(), Buf()]
    SQ = [aview(K, 144 * KB, [1024], BF16), aview(K, 146 * KB, [1024], BF16)]
    bSQ = [Buf(), Buf()]
    RS = aview(K, 148 * KB, [4, 8], F32); bRS = Buf()
    RREP = aview(K, 150 * KB, [1024], F32); bRR = Buf()
    bPS = K.bPS
    for q4 in range(4):
        P.dma("pool", WU[:, q4 * 4:(q4 + 1) * 4, :],
              K.w_in[q4 * 512:(q4 + 1) * 512, 3072:4096].rearrange("(kt p) c -> p kt c", p=128), writes=[bWU])
    for kt in range(16):
        P.op("dve", lambda e, kt=kt: e.tensor_scalar(WU[:, kt, :], WU[:, kt, :], K.gmix[:, kt:kt + 1], None, ALU.mult),
             reads=[bWU, K.bC], writes=[bWU])
    nps = 0
    for b in range(4):
        xb = XB[b % 2]; bx = bXB[b % 2]
        for q4 in range(4):
            P.dma("pool", xb[:, q4 * 4:(q4 + 1) * 4, :],
                  K.xT[q4 * 512:(q4 + 1) * 512, b * 1024:(b + 1) * 1024].rearrange("(kt p) t -> p kt t", p=128),
                  writes=[bx])
        pr0 = K.PSF[4]; pr1 = K.PSF[5]
        for kt in range(16):
            sq = SQ[kt % 2]; bs = bSQ[kt % 2]
            P.op("act", lambda e, kt=kt, sq=sq, xb=xb: e.activation(sq, xb[:, kt, :], ACT.Square), reads=[bx], writes=[bs])
            P.op("pe", [lambda e, kt=kt, sq=sq: e.matmul(pr0[:, :], lhsT=K.onesb[:, :], rhs=sq[:, 0:512], start=(kt == 0), stop=(kt == 15)),
                        lambda e, kt=kt, sq=sq: e.matmul(pr1[:, :], lhsT=K.onesb[:, :], rhs=sq[:, 512:1024], start=(kt == 0), stop=(kt == 15))],
                 reads=[bs, K.bC], writes=[bPS[4], bPS[5]])
        P.op("dve", lambda e: e.tensor_scalar(RREP[:, 0:512], pr0[:, :], 1.0 / DM, EPS, ALU.mult, ALU.add), reads=[bPS[4]], writes=[bRR])
        P.op("dve", lambda e: e.tensor_scalar(RREP[:, 512:1024], pr1[:, :], 1.0 / DM, EPS, ALU.mult, ALU.add), reads=[bPS[5]], writes=[bRR])
        P.op("act", lambda e: e.activation(RREP, RREP, ACT.Sqrt), reads=[bRR], writes=[bRR])
        P.op("dve", lambda e: e.reciprocal(RREP, RREP), reads=[bRR], writes=[bRR])
        fns = []
        for t in range(8):
            fns.append(lambda e, t=t: e.matmul(pr0[:, t:t + 1], lhsT=RREP[:, t::8], rhs=K.inv128[:, 0:1], start=True, stop=True))
        P.op("pe", fns, reads=[bRR, K.bC], writes=[bPS[4]])
        P.op("dve", lambda e, b=b: e.tensor_copy(RS[:, b, :], pr0[:, 0:8]), reads=[bPS[4]], writes=[bRS])
        for t in range(8):
            for ch in range(2):
                ps = K.PSF[nps % 4]; bp = bPS[nps % 4]; nps += 1
                fns = []
                for kt in range(16):
                    fns.append(lambda e, kt=kt, t=t, ch=ch, ps=ps, xb=xb: e.matmul(
                        ps[:, :], lhsT=xb[:, kt, t::8], rhs=WU[:, kt, ch * 512:(ch + 1) * 512], start=(kt == 0), stop=(kt == 15)))
                P.op("pe", fns, reads=[bx, bWU], writes=[bp])
                P.op("dve", lambda e, t=t, ch=ch, ps=ps, b=b: e.tensor_scalar(
                    U2[:, ch * 32:(ch + 1) * 32, t, :], ps[:, :].rearrange("p (g c) -> p g c", c=16), RS[:, b, t:t + 1], None, ALU.mult),
                    reads=[bp, bRS], writes=[bU2])
        ub = UB[b % 2]; bu = bUB[b % 2]
        for g8 in range(8):
            pt = K.PSB[g8 % 2]; bpt = bPS[6 + g8 % 2]
            fns = []
            for gl in range(8):
                g = g8 * 8 + gl
                fns.append(lambda e, g=g, gl=gl, pt=pt: e.transpose(pt[:, gl * 128:(gl + 1) * 128], U2[:, g, :, :], K.identb[:]))
            P.op("pe", fns, reads=[bU2, K.bC], writes=[bpt])
            P.op("act", lambda e, g8=g8, pt=pt, ub=ub: e.activation(
                ub[:, g8 * 8:(g8 + 1) * 8, :].rearrange("p g j -> p (g j)"), pt[:, :], ACT.Identity), reads=[bpt], writes=[bu])
        P.dma("sp", K.Ud[b], ub, reads=[bu], writes=[K.bUd])
        if "U" in DEBUG and b == 2:
            K.dbg("U", ub, [bu])


def ssm_tables(K):
    P = K.P
    T0 = 144 * KB
    o = [T0]

    def al(shape, dt=F32):
        v = aview(K, o[0], shape, dt)
        o[0] += _prod(shape) * 4
        return v
    K.ER = al([2, 32, 24]); K.EI = al([2, 32, 24])
    K.BBR = al([2, 32, 16]); K.BBI = al([2, 32, 16])
    K.MUA = al([2, 2, 32]); K.MUS = al([2, 2, 32])
    AR = al([2, 32]); AI = al([2, 32]); LS = al([2, 32]); DT = al([2, 32])
    ZR = al([2, 32]); ZI = al([2, 32]); E1R = al([2, 32]); E1I = al([2, 32])
    FR = al([2, 32]); FI = al([2, 32]); TA = al([2, 32]); TB = al([2, 32]); DEN = al([2, 32])
    KV = al([2, 24])
    assert o[0] <= 170 * KB, o[0]
    K.bTab = Buf()
    bT = K.bTab
    BR = aview(K, 64 * KB, [2, 32, 16], F32); BI = aview(K, 68 * KB, [2, 32, 16], F32)
    KZ = aview(K, 72 * KB, [2, 32, 24], F32)
    RR = aview(K, 80 * KB, [2, 2, 32, 24], F32)
    RI = aview(K, 92 * KB, [2, 2, 32, 24], I32)
    RF = aview(K, 104 * KB, [2, 2, 32, 24], F32)
    RG = aview(K, 116 * KB, [2, 2, 32, 24], F32)
    TMP = aview(K, 128 * KB, [2, 32, 16], F32)
    bB = Buf(); bW = Buf()
    P.dma("sp", AR, K.p_are, writes=[bT]); P.dma("sp", AI, K.p_aim, writes=[bT])
    P.dma("sp", LS, K.p_ls, writes=[bT]); P.dma("sp", KV, K.p_kv, writes=[bT])
    P.dma("sp", BR, K.p_bre, writes=[bB]); P.dma("sp", BI, K.p_bim, writes=[bB])
    f2 = lambda a: a.rearrange("p a b -> p (a b)")
    f3 = lambda a: a.rearrange("p a b c -> p (a b c)")
    f4 = lambda a: a.rearrange("p a b c d -> p (a b c d)")
    D = lambda fn, r, w: P.op("dve", fn, reads=r, writes=w)
    A = lambda fn, r, w: P.op("act", fn, reads=r, writes=w)
    A(lambda e: e.activation(f2(DT), f2(LS), ACT.Exp), [bT], [bT])
    D(lambda e: e.tensor_scalar(AR, AR, -1e-4, None, ALU.min), [bT], [bT])
    D(lambda e: e.tensor_tensor(ZR, AR, DT, ALU.mult), [bT], [bT])
    D(lambda e: e.tensor_tensor(ZI, AI, DT, ALU.mult), [bT], [bT])
    sh = [128, 2, 32, 24]
    D(lambda e: e.tensor_tensor(KZ, bc(ZR.unsqueeze(3), sh), bc(KV.unsqueeze(2), sh), ALU.mult), [bT], [bW])
    A(lambda e: e.activation(f3(KZ), f3(KZ), ACT.Exp), [bW], [bW])
    D(lambda e: e.tensor_scalar(ZI, ZI, 1.0 / (2 * np.pi), None, ALU.mult), [bT], [bT])
    D(lambda e: e.tensor_tensor(RR[:, 0], bc(ZI.unsqueeze(3), sh), bc(KV.unsqueeze(2), sh), ALU.mult), [bT], [bW])
    D(lambda e: e.tensor_scalar(RR[:, 1], RR[:, 0], 0.25, None, ALU.add), [bW], [bW])
    D(lambda e: e.tensor_copy(f4(RI), f4(RR)), [bW], [bW])
    D(lambda e: e.tensor_copy(f4(RF), f4(RI)), [bW], [bW])
    D(lambda e: e.tensor_tensor(f4(RR), f4(RR), f4(RF), ALU.subtract), [bW], [bW])
    D(lambda e: e.tensor_scalar(f4(RF), f4(RR), 0.5, None, ALU.is_gt), [bW], [bW])
    D(lambda e: e.tensor_scalar(f4(RG), f4(RR), -0.5, None, ALU.is_lt), [bW], [bW])
    D(lambda e: e.tensor_tensor(f4(RR), f4(RR), f4(RF), ALU.subtract), [bW], [bW])
    D(lambda e: e.tensor_tensor(f4(RR), f4(RR), f4(RG), ALU.add), [bW], [bW])
    A(lambda e: e.activation(f4(RR), f4(RR), ACT.Sin, scale=6.283185), [bW], [bW])
    D(lambda e: e.tensor_tensor(K.ER, KZ, RR[:, 1], ALU.mult), [bW], [bT])
    D(lambda e: e.tensor_tensor(K.EI, KZ, RR[:, 0], ALU.mult), [bW], [bT])
    for dd, i1, i8 in ((0, 8, 15), (1, 1, 8)):
        D(lambda e, dd=dd, i1=i1: e.tensor_copy(E1R[:, dd, :], K.ER[:, dd, :, i1]), [bT], [bT])
        D(lambda e, dd=dd, i1=i1: e.tensor_copy(E1I[:, dd, :], K.EI[:, dd, :, i1]), [bT], [bT])
        D(lambda e, dd=dd, i8=i8: e.tensor_copy(K.MUA[:, dd, 0, :], K.ER[:, dd, :, i8]), [bT], [bT])
        D(lambda e, dd=dd, i8=i8: e.tensor_copy(K.MUA[:, dd, 1, :], K.ER[:, dd, :, i8]), [bT], [bT])
        D(lambda e, dd=dd, i8=i8: e.tensor_copy(K.MUS[:, dd, 1, :], K.EI[:, dd, :, i8]), [bT], [bT])
        D(lambda e, dd=dd, i8=i8: e.tensor_scalar(K.MUS[:, dd, 0, :], K.EI[:, dd, :, i8], -1.0, None, ALU.mult), [bT], [bT])
    D(lambda e: e.tensor_scalar(E1R, E1R, -1.0, None, ALU.add), [bT], [bT])
    D(lambda e: e.tensor_tensor(DEN, AR, AR, ALU.mult), [bT], [bT])
    D(lambda e: e.tensor_tensor(TA, AI, AI, ALU.mult), [bT], [bT])
    D(lambda e: e.tensor_tensor(DEN, DEN, TA, ALU.add), [bT], [bT])
    D(lambda e: e.reciprocal(DEN, DEN), [bT], [bT])
    D(lambda e: e.tensor_tensor(TA, E1R, AR, ALU.mult), [bT], [bT])
    D(lambda e: e.tensor_tensor(TB, E1I, AI, ALU.mult), [bT], [bT])
    D(lambda e: e.tensor_tensor(FR, TA, TB, ALU.add), [bT], [bT])
    D(lambda e: e.tensor_tensor(FR, FR, DEN, ALU.mult), [bT], [bT])
    D(lambda e: e.tensor_tensor(TA, E1I, AR, ALU.mult), [bT], [bT])
    D(lambda e: e.tensor_tensor(TB, E1R, AI, ALU.mult), [bT], [bT])
    D(lambda e: e.tensor_tensor(FI, TA, TB, ALU.subtract), [bT], [bT])
    D(lambda e: e.tensor_tensor(FI, FI, DEN, ALU.mult), [bT], [bT])
    s3 = [128, 2, 32, 16]
    D(lambda e: e.tensor_tensor(K.BBR, bc(FR.unsqueeze(3), s3), BR, ALU.mult), [bT, bB], [bT])
    D(lambda e: e.tensor_tensor(TMP, bc(FI.unsqueeze(3), s3), BI, ALU.mult), [bT, bB], [bW])
    D(lambda e: e.tensor_tensor(K.BBR, K.BBR, TMP, ALU.subtract), [bT, bW], [bT])
    D(lambda e: e.tensor_tensor(K.BBI, bc(FR.unsqueeze(3), s3), BI, ALU.mult), [bT, bB], [bT])
    D(lambda e: e.tensor_tensor(TMP, bc(FI.unsqueeze(3), s3), BR, ALU.mult), [bT, bB], [bW])
    D(lambda e: e.tensor_tensor(K.BBI, K.BBI, TMP, ALU.add), [bT, bW], [bT])


def cplx_outer(K, eng, OR, OI, er, ei, xr, xi, t1, bufs_r, bufs_w, neg_im=False):
    P = K.P
    op = lambda fn: P.op(eng, fn, reads=bufs_r, writes=bufs_w)
    op(lambda e: e.tensor_tensor(OR, er, xr, ALU.mult))
    op(lambda e: e.tensor_tensor(t1, ei, xi, ALU.mult))
    op(lambda e: e.tensor_tensor(OR, OR, t1, ALU.subtract))
    op(lambda e: e.tensor_tensor(OI, er, xi, ALU.mult))
    op(lambda e: e.tensor_tensor(t1, ei, xr, ALU.mult))
    if neg_im and eng == "dve":
        op(lambda e: e.scalar_tensor_tensor(OI, OI, -1.0, t1, ALU.mult, ALU.subtract))
    elif neg_im:
        op(lambda e: e.tensor_tensor(OI, OI, t1, ALU.add))
        op(lambda e: e.tensor_scalar(OI, OI, -1.0, None, ALU.mult))
    else:
        op(lambda e: e.tensor_tensor(OI, OI, t1, ALU.add))


def q_chunk(K, eng, dd, gc, QR, QI, T1, bQ):
    sh = [128, 4, 8, 16]
    gs = slice(gc * 4, gc * 4 + 4)
    er = bc(K.ER[:, dd, gs, 0:8].unsqueeze(3), sh); ei = bc(K.EI[:, dd, gs, 0:8].unsqueeze(3), sh)
    xr = bc(K.BBR[:, dd, gs, :].unsqueeze(2), sh); xi = bc(K.BBI[:, dd, gs, :].unsqueeze(2), sh)
    cplx_outer(K, eng, QR, QI, er, ei, xr, xi, T1, [K.bTab], [bQ])


def phase_gen1(K):
    P = K.P
    K.W1Z = aview(K, 0, [2, 2, 64, 128], BF16); K.bW1 = Buf()
    for q4 in range(4):
        P.op("pool", lambda e, q4=q4: e.memset(K.W1Z[:, q4 // 2, q4 % 2], 0.0), writes=[K.bW1])
    ssm_tables(K)
    QR = aview(K, 178 * KB, [4, 8, 16], F32); QI = aview(K, 180 * KB, [4, 8, 16], F32)
    T1 = aview(K, 182 * KB, [4, 8, 16], F32)
    bQ = Buf()
    n = 0
    for dd in range(2):
        for gc in range(8):
            q_chunk(K, "dve", dd, gc, QR, QI, T1, bQ)
            for ri, Q in ((0, QR), (1, QI)):
                ps = K.PSF[n % 2]; bp = K.bPS[n % 2]; n += 1
                fns = []
                for gl in range(4):
                    fns.append(lambda e, gl=gl, Q=Q, ps=ps: e.transpose(
                        ps[:, gl * 128:(gl + 1) * 128], Q[:, gl].rearrange("p s c -> p (s c)"), K.ident[:]))
                P.op("pe", fns, reads=[bQ, K.bC], writes=[bp])
                g0 = gc * 8
                psv = ps[:, :].rearrange("p (g q) -> p g q", q=128)
                P.op("dve", lambda e, dd=dd, ri=ri, g0=g0, psv=psv: e.tensor_copy(
                    K.W1Z[:, dd, ri, g0:g0 + 8:2, 0:64], psv[:, :, 0:64]), reads=[bp], writes=[K.bW1])
                P.op("dve", lambda e, dd=dd, ri=ri, g0=g0, psv=psv: e.tensor_copy(
                    K.W1Z[:, dd, ri, g0 + 1:g0 + 8:2, 64:128], psv[:, :, 64:128]), reads=[bp], writes=[K.bW1])


def phase_s2(K):
    P = K.P
    S8 = {0: aview(K, 64 * KB, [2, 32, 128], F32), 1: aview(K, 96 * KB, [2, 32, 128], F32)}
    bS8 = {0: Buf(), 1: Buf()}
    UBL = {0: aview(K, 128 * KB, [64, 128], BF16), 1: aview(K, 170 * KB, [32, 128], BF16)}
    bUL = {0: Buf(), 1: Buf()}
    base = 184 * KB
    TT = {}
    for dd in range(2):
        o = base + dd * 2 * KB
        TT[dd] = dict(t1=aview(K, o, [2, 32], F32), t2=aview(K, o + 256, [2, 32], F32),
                      car=aview(K, o + 512, [2, 32], F32), carb=aview(K, o + 768, [2, 32], BF16))
    eng_of = {0: "dve", 1: "pool"}
    bT = {0: Buf(), 1: Buf()}
    for dd in range(2):
        P.op(eng_of[dd], lambda e, dd=dd: e.memset(TT[dd]["car"], 0.0), writes=[bT[dd]])

    def put_carry(dd, blk):
        eng = eng_of[dd]
        P.op(eng, lambda e: e.tensor_copy(TT[dd]["carb"], TT[dd]["car"]), reads=[bT[dd]], writes=[bT[dd]])
        P.dma("sp" if dd == 0 else "pool", K.Hc[dd, blk], TT[dd]["carb"], reads=[bT[dd]], writes=[K.bHd])

    def l1_group(dd, ri, gc, n, ubuf, gmod):
        ps = K.PSF[dd * 2 + n % 2]; bp = K.bPS[dd * 2 + n % 2]
        fns = []
        for gl in range(4):
            gp = gc * 4 + gl
            for g2 in range(2):
                g = 2 * gp + g2
                fns.append(lambda e, gl=gl, g=g, g2=g2, ps=ps: e.matmul(
                    ps[:, gl * 128:(gl + 1) * 128], lhsT=K.W1Z[:, dd, ri, g, :], rhs=ubuf[:, g % gmod, :],
                    start=(g2 == 0), stop=(g2 == 1)))
        P.op("pe", fns, reads=[bUL[dd], K.bW1], writes=[bp])
        P.op("act", lambda e, ps=ps: e.activation(
            S8[dd][:, ri, gc * 4:(gc + 1) * 4, :].rearrange("p g j -> p (g j)"), ps[:, :], ACT.Identity),
            reads=[bp], writes=[bS8[dd]])

    def level1_A(b):
        P.dma("sp", UBL[0], K.Ud[b], reads=[K.bUd], writes=[bUL[0]])
        n = 0
        for ri in range(2):
            for gc in range(8):
                l1_group(0, ri, gc, n, UBL[0], 64); n += 1

    def level1_B(b):
        n = 0
        for half in range(2):
            P.dma("pool", UBL[1], K.Ud[b][:, half * 32:(half + 1) * 32, :], reads=[K.bUd], writes=[bUL[1]])
            for ri in range(2):
                for gcl in range(4):
                    l1_group(1, ri, half * 4 + gcl, n, UBL[1], 32); n += 1

    def scan(dd, order):
        eng = eng_of[dd]
        s8 = S8[dd]; tt = TT[dd]
        mua = K.MUA[:, dd]; mus = K.MUS[:, dd]
        bs = bS8[dd]; bt = bT[dd]
        prevj = None
        for jj in order:
            prev = tt["car"] if prevj is None else s8[:, :, :, prevj]
            cur = s8[:, :, :, jj]
            rd = [bs, bt, K.bTab]
            P.op(eng, lambda e, prev=prev: e.tensor_tensor(tt["t1"], mua, prev, ALU.mult), reads=rd, writes=[bt], ses=False)
            P.op(eng, lambda e, prev=prev: e.tensor_tensor(tt["t2"][:, 0], mus[:, 0], prev[:, 1], ALU.mult), reads=rd, writes=[bt], ses=False)
            P.op(eng, lambda e, prev=prev: e.tensor_tensor(tt["t2"][:, 1], mus[:, 1], prev[:, 0], ALU.mult), reads=rd, writes=[bt], ses=False)
            P.op(eng, lambda e: e.tensor_tensor(tt["t1"], tt["t1"], tt["t2"], ALU.add), reads=[bt], writes=[bt], ses=False)
            P.op(eng, lambda e, cur=cur: e.tensor_tensor(cur, cur, tt["t1"], ALU.add), reads=[bs, bt], writes=[bs], ses=False)
            prevj = jj
        P.op(eng, lambda e: e.tensor_copy(tt["car"], s8[:, :, :, prevj]), reads=[bs, bt], writes=[bt], ses=False)

    def store_H(dd, blk):
        eng = eng_of[dd]
        s8 = S8[dd]
        if dd == 0:
            hb = UBL[0][:, :, :].rearrange("p (r g) j -> p r g j", r=2)
            P.op(eng, lambda e: e.tensor_copy(hb, s8), reads=[bS8[0]], writes=[bUL[0]])
            P.dma("sp", K.Hd[0, blk], hb, reads=[bUL[0]], writes=[K.bHd])
        else:
            for ri in range(2):
                hb = UBL[1]
                P.op(eng, lambda e, ri=ri: e.tensor_copy(hb, s8[:, ri]), reads=[bS8[1]], writes=[bUL[1]])
                P.dma("pool", K.Hd[1, blk][:, ri], hb, reads=[bUL[1]], writes=[K.bHd])

    def do_A(b):
        level1_A(b)
        if b >= 2:
            put_carry(0, b - 2)
        scan(0, range(128))
        if b >= 2:
            store_H(0, b - 2)

    def do_B(b):
        level1_B(b)
        put_carry(1, b - 2)
        scan(1, range(127, -1, -1))
        store_H(1, b - 2)

    level1_A(0)
    do_B(3)
    scan(0, range(128))
    do_A(1)
    do_A(2)
    do_B(2)
    do_A(3)


def phase_gen2(K):
    P = K.P
    CR = aview(K, 130 * KB, [2, 32, 16], F32); CI = aview(K, 134 * KB, [2, 32, 16], F32); bCc = Buf()
    P.dma("sp", CR, K.p_cre, writes=[bCc]); P.dma("sp", CI, K.p_cim, writes=[bCc])
    o = [0]

    def al(shape, dt=F32):
        v = aview(K, o[0], shape, dt)
        o[0] += _prod(shape) * (4 if dt == F32 else 2)
        return v
    SCR = {dd: (al([4, 8, 16]), al([4, 8, 16]), al([4, 8, 16]), al([4, 8, 16]), al([4, 8, 16])) for dd in range(2)}
    bQs = {0: Buf(), 1: Buf()}; bWs = {0: Buf(), 1: Buf()}
    WPR = {0: al([4, 128]), 1: al([4, 128])}; WPI = {0: al([4, 128]), 1: al([4, 128])}
    QZR = {0: al([8, 128]), 1: al([8, 128])}; QZI = {0: al([8, 128]), 1: al([8, 128])}
    W3S = [al([2, 2, 8, 128], BF16), al([2, 2, 8, 128], BF16)]
    TT1 = al([8, 128]); TT2 = al([8, 128])
    TC = [al([8, 128], BF16), al([8, 128], BF16)]
    assert o[0] <= 128 * KB
    bWPs = {0: Buf(), 1: Buf()}; bQZs = {0: Buf(), 1: Buf()}; bS = [Buf(), Buf()]; bTT = Buf(); bTC = [Buf(), Buf()]
    sh = [128, 4, 8, 16]
    for gc in range(8):
        gs = slice(gc * 4, gc * 4 + 4)
        w3s = W3S[gc % 2]; bs = bS[gc % 2]
        for dd in range(2):
            eng = "dve"
            QR, QI, T1, W3R, W3I = SCR[dd]
            bQ, bW = bQs[dd], bWs[dd]
            bWP, bQZ = bWPs[dd], bQZs[dd]
            q_chunk(K, eng, dd, gc, QR, QI, T1, bQ)
            for gl in range(4):
                for g2 in range(2):
                    msk = K.m01[:, g2:g2 + 1]
                    P.op(eng, lambda e, gl=gl, g2=g2, msk=msk, dd=dd, QR=QR: e.tensor_scalar(
                        QZR[dd][:, 2 * gl + g2, :], QR[:, gl].rearrange("p s c -> p (s c)"), msk, None, ALU.mult),
                        reads=[bQ, K.bC], writes=[bQZ])
                    P.op(eng, lambda e, gl=gl, g2=g2, msk=msk, dd=dd, QI=QI: e.tensor_scalar(
                        QZI[dd][:, 2 * gl + g2, :], QI[:, gl].rearrange("p s c -> p (s c)"), msk, None, ALU.mult),
                        reads=[bQ, K.bC], writes=[bQZ])
            cr = bc(CR[:, dd, gs, :].unsqueeze(2), sh); ci = bc(CI[:, dd, gs, :].unsqueeze(2), sh)
            er = bc(K.ER[:, dd, gs, 16:24].unsqueeze(3), sh); ei = bc(K.EI[:, dd, gs, 16:24].unsqueeze(3), sh)
            wpr = WPR[dd][:, :, :].rearrange("p g (t c) -> p g t c", c=16)
            wpi = WPI[dd][:, :, :].rearrange("p g (t c) -> p g t c", c=16)
            cplx_outer(K, eng, wpr, wpi, er, ei, cr, ci, T1, [K.bTab, bCc], [bWP], neg_im=True)
            er = bc(K.ER[:, dd, gs, 8:16].unsqueeze(3), sh); ei = bc(K.EI[:, dd, gs, 8:16].unsqueeze(3), sh)
            cplx_outer(K, eng, W3R, W3I, er, ei, cr, ci, T1, [K.bTab, bCc], [bW], neg_im=True)
            for ri, W in ((0, W3R), (1, W3I)):
                for g2 in range(2):
                    P.op(eng, lambda e, ri=ri, W=W, g2=g2, dd=dd, w3s=w3s: e.tensor_scalar(
                        w3s[:, dd, ri, g2:8:2, :], W[:, :, :, :].rearrange("p g t c -> p g (t c)"), K.m01[:, g2:g2 + 1], None, ALU.mult),
                        reads=[bW, K.bC], writes=[bs])
        P.dma("sp", K.W3d[:, :, :, gc * 8:(gc + 1) * 8, :], w3s, reads=[bs], writes=[K.bW3d])
        for dd in range(2):
            for hb in range(2):
                ps = K.PSF[dd * 2 + hb]; bp = K.bPS[dd * 2 + hb]
                fns = []
                for gq in range(4):
                    g = hb * 4 + gq
                    gl = g // 2
                    fns.append(lambda e, g=g, gl=gl, gq=gq, ps=ps, dd=dd: e.matmul(
                        ps[:, gq * 128:(gq + 1) * 128], lhsT=QZR[dd][:, g, :], rhs=WPR[dd][:, gl, :], start=True, stop=False))
                    fns.append(lambda e, g=g, gl=gl, gq=gq, ps=ps, dd=dd: e.matmul(
                        ps[:, gq * 128:(gq + 1) * 128], lhsT=QZI[dd][:, g, :], rhs=WPI[dd][:, gl, :], start=False, stop=True))
                P.op("pe", fns, reads=[bQZs[dd], bWPs[dd]], writes=[bp])
        s4 = [128, 4, 128]
        for hb in range(2):
            pa = K.PSF[hb][:, :].rearrange("p (g q) -> p g q", q=128)
            pb = K.PSF[2 + hb][:, :].rearrange("p (g q) -> p g q", q=128)
            P.op("dve", lambda e, hb=hb, pa=pa: e.tensor_tensor(TT1[:, hb * 4:(hb + 1) * 4, :], pa, bc(K.maskF[:, :].unsqueeze(1), s4), ALU.mult),
                 reads=[K.bPS[hb], K.bC], writes=[bTT])
            P.op("dve", lambda e, hb=hb, pb=pb: e.tensor_tensor(TT2[:, hb * 4:(hb + 1) * 4, :], pb, bc(K.maskB[:, :].unsqueeze(1), s4), ALU.mult),
                 reads=[K.bPS[2 + hb], K.bC], writes=[bTT])
        P.op("dve", lambda e: e.tensor_tensor(TT1, TT1, TT2, ALU.add), reads=[bTT], writes=[bTT])
        tc_ = TC[gc % 2]; btc = bTC[gc % 2]
        for g in range(8):
            gg = gc * 8 + g
            P.op("dve", lambda e, g=g, gg=gg, tc_=tc_: e.scalar_tensor_tensor(
                tc_[:, g, :], K.ident[:], K.dskip[:, gg:gg + 1], TT1[:, g, :], ALU.mult, ALU.add),
                reads=[bTT, K.bC], writes=[btc])
        P.dma("sp", K.Td[:, gc * 8:(gc + 1) * 8, :], tc_, reads=[btc], writes=[K.bTd])
        if "T" in DEBUG and gc == 0:
            K.dbg("T", tc_, [btc])


def phase_s4(K):
    P = K.P
    CH = []
    for i in range(2):
        o = i * 16 * KB
        CH.append(dict(U=aview(K, o, [8, 128], BF16), T=aview(K, o + 2 * KB, [8, 128], BF16),
                       W3=aview(K, o + 4 * KB, [2, 2, 8, 128], BF16),
                       HA=aview(K, o + 12 * KB, [2, 4, 128], BF16), HB=aview(K, o + 14 * KB, [2, 4, 128], BF16),
                       HAc=aview(K, 168 * KB + i * 64, [2, 4], BF16), HBc=aview(K, 168 * KB + 32 + i * 64, [2, 4], BF16),
                       b={k: Buf() for k in ("U", "T", "W3", "HA", "HB", "HAc", "HBc")}))
    YSG = [aview(K, 32 * KB, [1024], F32), aview(K, 36 * KB, [1024], F32)]; bYSG = [Buf(), Buf()]
    Y2C = aview(K, 40 * KB, [8, 128], F32); G1 = aview(K, 44 * KB, [8, 128], F32)
    G2 = aview(K, 48 * KB, [8, 128], F32); bG = Buf()
    YG = aview(K, 56 * KB, [8, 1024], BF16); bYG = Buf()
    YGT = aview(K, 72 * KB, [8, 1024], BF16); bYGT = Buf()
    WGL = aview(K, 88 * KB, [8, 1024], BF16); bWGL = Buf()
    GT = [aview(K, 104 * KB, [512], F32), aview(K, 106 * KB, [512], F32)]; bGT = [Buf(), Buf()]
    YS = aview(K, 112 * KB, [8, 1024], F32); bYS = Buf()
    YSN = [aview(K, 144 * KB, [1024], BF16), aview(K, 146 * KB, [1024], BF16)]; bYSN = [Buf(), Buf()]
    YST = aview(K, 148 * KB, [8, 1024], BF16); bYST = Buf()
    SS = aview(K, 164 * KB, [8], F32); JNK = aview(K, 165 * KB, [1024], BF16); bSS = Buf()
    for q2 in range(2):
        P.dma("pool", WGL[:, q2 * 4:(q2 + 1) * 4, :], K.w_glu[q2 * 512:(q2 + 1) * 512, :].rearrange("(kt p) c -> p kt c", p=128), writes=[bWGL])
    Y2Cs = [Y2C, aview(K, 52 * KB, [8, 128], F32)]; bY2 = [Buf(), Buf()]
    for blk in range(2):
        def stage_a(gc, blk=blk):
            ch = CH[gc % 2]
            bch = list(ch["b"].values())
            P.dma("sp", ch["U"], K.Ud[2 + blk][:, gc * 8:(gc + 1) * 8, :], reads=[K.bUd], writes=[ch["b"]['U']])
            P.dma("sp", ch["T"], K.Td[:, gc * 8:(gc + 1) * 8, :], reads=[K.bTd], writes=[ch["b"]['T']])
            P.dma("sp", ch["W3"], K.W3d[:, :, :, gc * 8:(gc + 1) * 8, :], reads=[K.bW3d], writes=[ch["b"]['W3']])
            P.dma("sp", ch["HA"], K.Hd[0, blk][:, :, gc * 4:(gc + 1) * 4, :], reads=[K.bHd], writes=[ch["b"]['HA']])
            P.dma("sp", ch["HB"], K.Hd[1, blk][:, :, gc * 4:(gc + 1) * 4, :], reads=[K.bHd], writes=[ch["b"]['HB']])
            P.dma("sp", ch["HAc"], K.Hc[0, blk][:, :, gc * 4:(gc + 1) * 4], reads=[K.bHd], writes=[ch["b"]['HAc']])
            P.dma("sp", ch["HBc"], K.Hc[1, blk][:, :, gc * 4:(gc + 1) * 4], reads=[K.bHd], writes=[ch["b"]['HBc']])
            ysg = YSG[gc % 2]; bys = bYSG[gc % 2]
            y2c = Y2Cs[gc % 2]; by2 = bY2[gc % 2]
            for hb in range(2):
                ps = K.PSF[hb]; bp = K.bPS[hb]
                fns = []
                for gq in range(4):
                    g = hb * 4 + gq
                    gpl = g // 2
                    o_ = ps[:, gq * 128:(gq + 1) * 128]
                    fns.append(lambda e, o_=o_, g=g, ch=ch: e.matmul(o_, lhsT=ch["T"][:, g, :], rhs=ch["U"][:, g, :], start=True, stop=False))
                    for ri in range(2):
                        fns.append(lambda e, o_=o_, g=g, gpl=gpl, ch=ch, ri=ri: e.matmul(
                            o_[:, 1:128], lhsT=ch["W3"][:, 0, ri, g, :], rhs=ch["HA"][:, ri, gpl, 0:127], start=False, stop=False))
                        fns.append(lambda e, o_=o_, g=g, gpl=gpl, ch=ch, ri=ri: e.matmul(
                            o_[:, 0:1], lhsT=ch["W3"][:, 0, ri, g, :], rhs=ch["HAc"][:, ri, gpl:gpl + 1], start=False, stop=False))
                        fns.append(lambda e, o_=o_, g=g, gpl=gpl, ch=ch, ri=ri: e.matmul(
                            o_[:, 0:127], lhsT=ch["W3"][:, 1, ri, g, :], rhs=ch["HB"][:, ri, gpl, 1:128], start=False, stop=False))
                        fns.append(lambda e, o_=o_, g=g, gpl=gpl, ch=ch, ri=ri: e.matmul(
                            o_[:, 127:128], lhsT=ch["W3"][:, 1, ri, g, :], rhs=ch["HBc"][:, ri, gpl:gpl + 1], start=False, stop=(ri == 1)))
                P.op("pe", fns, reads=bch, writes=[bp])
                P.op("act", lambda e, hb=hb, ps=ps, ysg=ysg: e.activation(ysg[:, hb * 512:(hb + 1) * 512], ps[:, :], ACT.Identity),
                     reads=[bp], writes=[bys])
            for hb in range(2):
                ps = K.PSF[2 + hb]; bp = K.bPS[2 + hb]
                fns = []
                for gq in range(4):
                    g = hb * 4 + gq
                    fns.append(lambda e, gq=gq, g=g, ps=ps, ysg=ysg: e.transpose(ps[:, gq * 128:(gq + 1) * 128], ysg[:, g * 128:(g + 1) * 128], K.ident[:]))
                P.op("pe", fns, reads=[bys, K.bC], writes=[bp])

        def stage_a2(gc):
            y2c = Y2Cs[gc % 2]; by2 = bY2[gc % 2]
            for hb in range(2):
                ps = K.PSF[2 + hb]; bp = K.bPS[2 + hb]
                P.op("dve", lambda e, hb=hb, ps=ps, y2c=y2c: e.tensor_copy(
                    y2c[:, :, hb * 64:(hb + 1) * 64].rearrange("p t (g c) -> p g t c", c=16),
                    ps[:, :].rearrange("p (g t c) -> p g t c", t=8, c=16)), reads=[bp], writes=[by2])

        def stage_b(gc, blk=blk):
            y2c = Y2Cs[gc % 2]; by2 = bY2[gc % 2]
            if "ypre" in DEBUG and blk == 0 and gc == 0:
                K.dbg("ypre", y2c, [by2])
            P.op("dve", lambda e: e.tensor_tensor(G1, y2c, y2c, ALU.mult), reads=[by2], writes=[bG])
            P.op("dve", lambda e: e.tensor_scalar(G1, G1, 0.044715, 1.0, ALU.mult, ALU.add), reads=[bG], writes=[bG])
            P.op("dve", lambda e: e.tensor_tensor(G1, G1, y2c, ALU.mult), reads=[bG, by2], writes=[bG])
            P.op("act", lambda e: e.activation(G2[:, :, :].rearrange("p t c -> p (t c)"), G1[:, :, :].rearrange("p t c -> p (t c)"),
                                               ACT.Sigmoid, scale=1.5957691216057308), reads=[bG], writes=[bG])
            P.op("dve", lambda e: e.tensor_tensor(YS[:, :, gc * 128:(gc + 1) * 128], y2c, G2, ALU.mult), reads=[bG, by2], writes=[bYS])
            P.op("dve", lambda e: e.tensor_copy(YG[:, :, gc * 128:(gc + 1) * 128], YS[:, :, gc * 128:(gc + 1) * 128]), reads=[bYS], writes=[bYG])

        def stage_c(gc):
            pt = K.PSB[gc % 2]; bpt = K.bPS[6 + gc % 2]
            fns = []
            for t in range(8):
                fns.append(lambda e, t=t, pt=pt: e.transpose(pt[:, t * 128:(t + 1) * 128], YG[:, t, gc * 128:(gc + 1) * 128], K.identb[:]))
            P.op("pe", fns, reads=[bYG, K.bC], writes=[bpt])
            P.op("dve", lambda e, pt=pt: e.tensor_copy(
                YGT[:, gc, :].rearrange("p (j t) -> p t j", t=8), pt[:, :].rearrange("p (t j) -> p t j", j=128)), reads=[bpt], writes=[bYGT])

        stage_a(0)
        stage_a2(0)
        for gc in range(8):
            if gc + 1 < 8:
                stage_a(gc + 1)
            stage_b(gc)
            if gc + 1 < 8:
                stage_a2(gc + 1)
            stage_c(gc)
        if "yg" in DEBUG and blk == 0:
            K.dbg("yg", YS, [bYS])
        n = 0
        for t in range(8):
            for hf in range(2):
                ps = K.PSF[n % 2]; bp = K.bPS[n % 2]
                gt = GT[n % 2]; bgt = bGT[n % 2]; n += 1
                fns = []
                for kt in range(8):
                    fns.append(lambda e, kt=kt, t=t, hf=hf, ps=ps: e.matmul(
                        ps[:, :], lhsT=YGT[:, kt, t::8], rhs=WGL[:, kt, hf * 512:(hf + 1) * 512], start=(kt == 0), stop=(kt == 7)))
                P.op("pe", fns, reads=[bYGT, bWGL], writes=[bp])
                P.op("dve", lambda e, hf=hf, ps=ps, gt=gt: e.tensor_tensor(gt, ps[:, :], K.bglu[:, hf * 512:(hf + 1) * 512], ALU.add),
                     reads=[bp, K.bC], writes=[bgt])
                P.op("act", lambda e, gt=gt: e.activation(gt, gt, ACT.Sigmoid), reads=[bgt], writes=[bgt])
                P.op("dve", lambda e, t=t, hf=hf, gt=gt: e.tensor_tensor(
                    YS[:, t, hf * 512:(hf + 1) * 512], YS[:, t, hf * 512:(hf + 1) * 512], gt, ALU.mult), reads=[bgt, bYS], writes=[bYS])
        if "ys" in DEBUG and blk == 0:
            K.dbg("ys", YS, [bYS])
        for t in range(8):
            P.op("act", lambda e, t=t: e.activation(JNK, YS[:, t, :], ACT.Square, accum_out=SS[:, t:t + 1]), reads=[bYS], writes=[bSS])
        P.op("dve", lambda e: e.tensor_scalar(SS, SS, 1.0 / 1024, EPS, ALU.mult, ALU.add), reads=[bSS], writes=[bSS])
        P.op("act", lambda e: e.activation(SS, SS, ACT.Sqrt), reads=[bSS], writes=[bSS])
        P.op("dve", lambda e: e.reciprocal(SS, SS), reads=[bSS], writes=[bSS])
        for t in range(8):
            ysn = YSN[t % 2]; bysn = bYSN[t % 2]
            P.op("dve", lambda e, t=t, ysn=ysn: e.tensor_scalar(ysn, YS[:, t, :], SS[:, t:t + 1], None, ALU.mult), reads=[bYS, bSS], writes=[bysn])
            pt = K.PSB[t % 2]; bpt = K.bPS[6 + t % 2]
            fns = []
            for kt in range(8):
                fns.append(lambda e, kt=kt, pt=pt, ysn=ysn: e.transpose(pt[:, kt * 128:(kt + 1) * 128], ysn[:, kt * 128:(kt + 1) * 128], K.identb[:]))
            P.op("pe", fns, reads=[bysn, K.bC], writes=[bpt])
            P.op("dve", lambda e, t=t, pt=pt: e.tensor_copy(YST[:, :, t::8], pt[:, :].rearrange("p (k j) -> p k j", j=128)), reads=[bpt], writes=[bYST])
        P.dma("sp", K.mixT[:, 8:16, blk * 1024:(blk + 1) * 1024], YST, reads=[bYST], writes=[K.bmix])


def phase_a(K):
    P = K.P
    QT = aview(K, 0, [8, 2048], BF16); bQT = Buf()
    KT = aview(K, 32 * KB, [8, 2304], BF16); bKT = Buf()
    VV = aview(K, 68 * KB, [18, 16, 80], BF16); bVV = Buf()
    A0 = 68 * KB + 18 * 16 * 80 * 2
    A0 = (A0 + 63) // 64 * 64
    XC = [aview(K, A0, [16, 512], BF16), aview(K, A0 + 16 * KB, [16, 512], BF16)]; bXC = [Buf(), Buf()]
    WH = aview(K, A0 + 32 * KB, [16, 512], BF16); bWH = Buf()
    o = A0 + 48 * KB
    RX = aview(K, o, [2304], F32); RX2 = aview(K, o + 9 * KB, [2304], F32); bRX = Buf()
    o += 18 * KB
    SQ = [aview(K, o, [512], BF16), aview(K, o + 1 * KB, [512], BF16)]; bSQ = [Buf(), Buf()]
    MS = [aview(K, o + 2 * KB, [512], F32), aview(K, o + 4 * KB, [512], F32)]; bMS = [Buf(), Buf()]
    RK = aview(K, o + 6 * KB, [18], F32); bRK = Buf()
    assert o + 7 * KB <= K.ARENA_BYTES, o
    chunks = [(0, 256), (256, 512), (768, 512), (1280, 512), (1792, 512)]

    def load_x(ci, nb):
        c0, cw = chunks[ci]
        xc = XC[nb % 2]; bx = bXC[nb % 2]
        for q2 in range(2):
            P.dma("pool", xc[:, q2 * 8:(q2 + 1) * 8, 0:cw],
                  K.xT[q2 * 1024:(q2 + 1) * 1024, 1792 + c0:1792 + c0 + cw].rearrange("(kt p) t -> p kt t", p=128), writes=[bx])
        return xc, bx
    nb = 0
    P.op("dve", lambda e: e.memset(VV[:, :, :, 64:65], 1.0), writes=[bVV])
    for ci, (c0, cw) in enumerate(chunks):
        xc, bx = load_x(ci, nb); nb += 1
        ps = K.PSF[0]; bp = K.bPS[0]
        for kt in range(16):
            sq = SQ[kt % 2]; bs = bSQ[kt % 2]
            P.op("act", lambda e, kt=kt, sq=sq, xc=xc, cw=cw: e.activation(sq[:, 0:cw], xc[:, kt, 0:cw], ACT.Square), reads=[bx], writes=[bs])
            P.op("pe", lambda e, kt=kt, sq=sq, cw=cw: e.matmul(ps[:, 0:cw], lhsT=K.onesb[:, :], rhs=sq[:, 0:cw], start=(kt == 0), stop=(kt == 15)),
                 reads=[bs, K.bC], writes=[bp])
        P.op("dve", lambda e, c0=c0, cw=cw: e.tensor_scalar(RX[:, c0:c0 + cw], ps[:, 0:cw], 1.0 / DM, EPS, ALU.mult, ALU.add), reads=[bp], writes=[bRX])
        P.op("dve", lambda e, c0=c0, cw=cw: e.tensor_scalar(RX2[:, c0:c0 + cw], RX[:, c0:c0 + cw], EPS, None, ALU.mult), reads=[bRX], writes=[bRX])
    P.op("act", lambda e: e.activation(RX, RX, ACT.Sqrt), reads=[bRX], writes=[bRX])
    P.op("dve", lambda e: e.reciprocal(RX, RX), reads=[bRX], writes=[bRX])
    pk = K.PSF[1]; bpk = K.bPS[1]
    fns = []
    for tile in range(18):
        fns.append(lambda e, tile=tile: e.matmul(pk[:, tile:tile + 1], lhsT=RX[:, tile * 128:(tile + 1) * 128], rhs=K.inv128[:, 0:1], start=True, stop=True))
    P.op("pe", fns, reads=[bRX, K.bC], writes=[bpk])
    P.op("dve", lambda e: e.tensor_copy(RK, pk[:, 0:18]), reads=[bpk], writes=[bRK])

    def load_w(col0):
        for q2 in range(2):
            P.dma("pool", WH[:, q2 * 8:(q2 + 1) * 8, :], K.w_in[q2 * 1024:(q2 + 1) * 1024, col0:col0 + 512].rearrange("(kt p) c -> p kt c", p=128), writes=[bWH])
        for kt in range(16):
            P.op("dve", lambda e, kt=kt: e.tensor_scalar(WH[:, kt, :], WH[:, kt, :], K.gmix[:, kt:kt + 1], None, ALU.mult), reads=[bWH, K.bC], writes=[bWH])
    npp = 0
    tiles = []
    for sel in range(2):
        for hf in range(2):
            for ci, (c0, cw) in enumerate(chunks):
                if sel == 0 and ci == 0:
                    continue
                for hl in range(4):
                    tiles.append((sel, hf, ci, hl))
    cur_x = [None, None]

    def qk_front(ti):
        sel, hf, ci, hl = tiles[ti]
        c0, cw = chunks[ci]
        first_ci = 1 if sel == 0 else 0
        if hl == 0 and ci == first_ci:
            load_w(sel * 1024 + hf * 512)
        if hl == 0:
            cur_x[0], cur_x[1] = load_x(ci, qk_front.nb); qk_front.nb += 1
        xc, bx = cur_x
        ps = K.PSF[ti % 3]; bp = K.bPS[ti % 3]
        sq = SQ[ti % 2]; bs = bSQ[ti % 2]
        fns = []
        for kt in range(16):
            fns.append(lambda e, kt=kt: e.matmul(
                ps[:, 0:cw], lhsT=WH[:, kt, hl * 128:(hl + 1) * 128], rhs=xc[:, kt, 0:cw], start=(kt == 0), stop=(kt == 15)))
        P.op("pe", fns, reads=[bWH, bx], writes=[bp])
        P.op("act", lambda e: e.activation(sq[:, 0:cw], ps[:, 0:cw], ACT.Square), reads=[bp], writes=[bs])
    qk_front.nb = nb

    def qk_back(ti):
        sel, hf, ci, hl = tiles[ti]
        c0, cw = chunks[ci]
        DST = QT if sel == 0 else KT
        bD = bQT if sel == 0 else bKT
        gain = K.qkg[:, sel:sel + 1]
        d0 = c0 - 256 if sel == 0 else c0
        hp = hf * 4 + hl
        ps = K.PSF[ti % 3]; bp = K.bPS[ti % 3]
        p2 = K.PSF[3 + ti % 2]; bp2 = K.bPS[3 + ti % 2]
        sq = SQ[ti % 2]; bs = bSQ[ti % 2]
        ms = MS[ti % 2]; bm = bMS[ti % 2]
        P.op("pe", lambda e: e.matmul(p2[:, 0:cw], lhsT=K.blk1[:, :], rhs=sq[:, 0:cw], start=True, stop=True),
             reads=[bs, K.bC], writes=[bp2])
        P.op("dve", lambda e: e.scalar_tensor_tensor(
            ms[:, 0:cw], p2[:, 0:cw], 1.0 / 64, RX2[:, c0:c0 + cw], ALU.mult, ALU.add), reads=[bp2, bRX], writes=[bm])
        P.op("act", lambda e: e.activation(ms[:, 0:cw], ms[:, 0:cw], ACT.Sqrt), reads=[bm], writes=[bm])
        P.op("dve", lambda e: e.reciprocal(ms[:, 0:cw], ms[:, 0:cw]), reads=[bm], writes=[bm])
        P.op("dve", lambda e: e.scalar_tensor_tensor(
            DST[:, hp, d0:d0 + cw], ps[:, 0:cw], gain, ms[:, 0:cw], ALU.mult, ALU.mult), reads=[bp, bm, K.bC], writes=[bD])

    for ti in range(len(tiles) + 1):
        if ti < len(tiles):
            qk_front(ti)
        if ti >= 1:
            qk_back(ti - 1)
    nb = qk_front.nb
    npp = len(tiles)
    for hf in range(2):
        load_w(2048 + hf * 512)
        for ci, (c0, cw) in enumerate(chunks):
            xc, bx = load_x(ci, nb); nb += 1
            for tl in range(cw // 128):
                tile = c0 // 128 + tl
                ps = K.PSF[npp % 2]; bp = K.bPS[npp % 2]; npp += 1
                fns = []
                for kt in range(16):
                    fns.append(lambda e, kt=kt, tl=tl, ps=ps, xc=xc: e.matmul(
                        ps[:, :], lhsT=xc[:, kt, tl * 128:(tl + 1) * 128], rhs=WH[:, kt, :], start=(kt == 0), stop=(kt == 15)))
                P.op("pe", fns, reads=[bWH, bx], writes=[bp])
                P.op("dve", lambda e, tile=tile, hf=hf, ps=ps: e.tensor_scalar(
                    VV[:, tile, hf * 8:(hf + 1) * 8, 0:64], ps[:, :].rearrange("p (h d) -> p h d", d=64), RK[:, tile:tile + 1], None, ALU.mult),
                    reads=[bp, bRK], writes=[bVV])
    if "qT" in DEBUG:
        K.dbg("qT", QT, [bQT]); K.dbg("kT", KT, [bKT]); K.dbg("vv", VV, [bVV])
    P.barrier()
    B0 = A0
    YA = aview(K, B0, [16, 1024], BF16); bYA = Buf()
    o = B0 + 32 * KB
    BIAS = [aview(K, o, [15, 128], F32), aview(K, o + 7680, [15, 128], F32)]; bBI = [Buf(), Buf()]
    o += 15360
    QZ = [aview(K, o, [2048], BF16), aview(K, o + 4 * KB, [2048], BF16)]; bQZ = [Buf(), Buf()]
    o += 8 * KB
    SB = [aview(K, o + i * 2560, [5, 128], F32) for i in range(3)]; bSB = [Buf() for _ in range(3)]
    o += 7680
    PT = [aview(K, o + i * 1280, [5, 128], BF16) for i in range(3)]; bPT = [Buf() for _ in range(3)]
    o += 3840
    RD = [aview(K, o, [1], F32), aview(K, o + 64, [1], F32)]; bRD = [Buf(), Buf()]
    o += 128
    YN = [aview(K, o, [1024], BF16), aview(K, o + 2 * KB, [1024], BF16)]; bYN = [Buf(), Buf()]
    o += 4 * KB
    YT = [aview(K, o, [8, 128], BF16), aview(K, o + 2 * KB, [8, 128], BF16)]; bYT = [Buf(), Buf()]
    o += 4 * KB
    SSA = aview(K, o, [16], F32); JNK = aview(K, o + 64, [1024], BF16); bSSA = Buf()
    o += 64 + 2 * KB
    assert o <= K.ARENA_BYTES, o
    iters = [(head, n) for head in range(NH) for n in range(16, 32)]
    st = {}

    def stage_front(it):
        head, n = iters[it]
        hp, hh = head // 2, head % 2
        bi = BIAS[head % 2]; bbi = bBI[head % 2]
        qz = QZ[head % 2]; bqz = bQZ[head % 2]
        if n == 16:
            P.dma("sp", bi, K.biasd[head], writes=[bbi])
            P.op("act", lambda e: e.activation(qz, QT[:, hp, :], ACT.Identity, scale=K.m01[:, hh:hh + 1]), reads=[bQT, K.bC], writes=[bqz])
        qi = n - 16
        ks = min(max(n - 2, 0), 27)
        v0 = 0 if n <= 29 else (5 if n == 30 else 10)
        psA = K.PSF[(it % 3) * 2]; bpA = K.bPS[(it % 3) * 2]
        psB = K.PSF[(it % 3) * 2 + 1]; bpB = K.bPS[(it % 3) * 2 + 1]
        sb = SB[it % 3]; bsb = bSB[it % 3]
        pt = PT[it % 3]; bpt = bPT[it % 3]
        fns = []
        for i in range(5):
            kti = ks + i - 14
            dst = psA[:, i * 128:(i + 1) * 128] if i < 4 else psB[:, 0:128]
            fns.append(lambda e, dst=dst, kti=kti: e.matmul(
                dst, lhsT=KT[:, hp, kti * 128:(kti + 1) * 128], rhs=qz[:, qi * 128:(qi + 1) * 128], start=True, stop=True))
        P.op("pe", fns, reads=[bKT, bqz], writes=[bpA, bpB])
        P.op("dve", lambda e: e.tensor_tensor(
            sb[:, 0:4, :], psA[:, :].rearrange("p (i q) -> p i q", q=128), bi[:, v0:v0 + 4, :], ALU.add), reads=[bpA, bbi], writes=[bsb])
        P.op("dve", lambda e: e.tensor_tensor(sb[:, 4, :], psB[:, 0:128], bi[:, v0 + 4, :], ALU.add),
             reads=[bpB, bbi], writes=[bsb])
        P.op("act", lambda e: e.activation(pt[:, :, :].rearrange("p i q -> p (i q)"), sb[:, :, :].rearrange("p i q -> p (i q)"), ACT.Exp),
             reads=[bsb], writes=[bpt])

    def stage_back(it):
        head, n = iters[it]
        qi = n - 16
        ks = min(max(n - 2, 0), 27)
        pso = K.PSB[it % 2][:, :].bitcast(F32); bpo = K.bPS[6 + it % 2]
        pt = PT[it % 3]; bpt = bPT[it % 3]
        rd = RD[it % 2]; brd = bRD[it % 2]
        fns = []
        for i in range(5):
            kti = ks + i - 14
            fns.append(lambda e, i=i, kti=kti: e.matmul(
                pso[:, 0:65], lhsT=pt[:, i, :], rhs=VV[:, kti, head, 0:65], start=(i == 0), stop=(i == 4)))
        P.op("pe", fns, reads=[bpt, bVV], writes=[bpo])
        P.op("dve", lambda e: e.reciprocal(rd, pso[:, 64:65]), reads=[bpo], writes=[brd])
        P.op("act", lambda e: e.activation(YA[:, qi, head * 64:(head + 1) * 64], pso[:, 0:64], ACT.Identity, scale=rd[:, 0:1]),
             reads=[bpo, brd], writes=[bYA])

    for it in range(len(iters) + 2):
        if it < len(iters):
            stage_front(it)
        if it >= 2:
            stage_back(it - 2)
    if "ya" in DEBUG:
        K.dbg("ya", YA, [bYA])
    for qi in range(16):
        P.op("act", lambda e, qi=qi: e.activation(JNK, YA[:, qi, :], ACT.Square, accum_out=SSA[:, qi:qi + 1]), reads=[bYA], writes=[bSSA])
    P.op("dve", lambda e: e.tensor_scalar(SSA, SSA, 1.0 / 1024, EPS, ALU.mult, ALU.add), reads=[bSSA], writes=[bSSA])
    P.op("act", lambda e: e.activation(SSA, SSA, ACT.Sqrt), reads=[bSSA], writes=[bSSA])
    P.op("dve", lambda e: e.reciprocal(SSA, SSA), reads=[bSSA], writes=[bSSA])
    for qi in range(16):
        yn = YN[qi % 2]; byn = bYN[qi % 2]
        yt = YT[qi % 2]; byt = bYT[qi % 2]
        P.op("dve", lambda e, qi=qi, yn=yn: e.tensor_scalar(yn, YA[:, qi, :], SSA[:, qi:qi + 1], None, ALU.mult), reads=[bYA, bSSA], writes=[byn])
        pt = K.PSB[qi % 2]; bpt = K.bPS[6 + qi % 2]
        fns = []
        for kt in range(8):
            fns.append(lambda e, kt=kt, pt=pt, yn=yn: e.transpose(pt[:, kt * 128:(kt + 1) * 128], yn[:, kt * 128:(kt + 1) * 128], K.identb[:]))
        P.op("pe", fns, reads=[byn, K.bC], writes=[bpt])
        P.op("act", lambda e, pt=pt, yt=yt: e.activation(yt[:, :, :].rearrange("p k j -> p (k j)"), pt[:, :], ACT.Identity), reads=[bpt], writes=[byt])
        P.dma("sp", K.mixT[:, 0:8, qi * 128:(qi + 1) * 128], yt, reads=[byt], writes=[K.bmix])


def phase_o(K):
    P = K.P
    FG = [6, 6, 6, 6, 5, 5, 5, 5]
    for tb in range(2):
        P.barrier()
        X1 = aview(K, 0, [8, 2048], F32); bX1 = [Buf() for _ in range(8)]
        WO = aview(K, 64 * KB, [16, 2048], BF16); bWO = Buf()
        MIXC = aview(K, 128 * KB, [16, 512], BF16); bMX = Buf()
        XO = [aview(K, 144 * KB, [2048], F32), aview(K, 152 * KB, [2048], F32)]; bXO = [Buf(), Buf()]
        for q4 in range(4):
            P.dma("pool", WO[:, q4 * 4:(q4 + 1) * 4, :], K.w_out[q4 * 512:(q4 + 1) * 512, :].rearrange("(kt p) c -> p kt c", p=128), writes=[bWO])
        for kt in range(16):
            P.op("dve", lambda e, kt=kt: e.tensor_scalar(WO[:, kt, :], WO[:, kt, :], K.gout[:, kt:kt + 1], None, ALU.mult), reads=[bWO, K.bC], writes=[bWO])
        n = 0
        for s in range(2):
            P.dma("sp", MIXC, K.mixT[:, :, tb * 1024 + s * 512: tb * 1024 + (s + 1) * 512], reads=[K.bmix], writes=[bMX])
            for tt in range(4):
                tile = s * 4 + tt
                xo = XO[tile % 2]; bxo = bXO[tile % 2]
                r0 = (tb * 8 + tile) * 128
                P.dma("sp", xo, K.xown[r0:r0 + 128, :], writes=[bxo])
                for dc in range(4):
                    ps = K.PSF[n % 4]; bp = K.bPS[n % 4]; n += 1
                    fns = []
                    for kt in range(16):
                        fns.append(lambda e, kt=kt, tt=tt, dc=dc, ps=ps: e.matmul(
                            ps[:, :], lhsT=MIXC[:, kt, tt * 128:(tt + 1) * 128], rhs=WO[:, kt, dc * 512:(dc + 1) * 512], start=(kt == 0), stop=(kt == 15)))
                    P.op("pe", fns, reads=[bMX, bWO], writes=[bp])
                    P.op("dve", lambda e, tile=tile, dc=dc, ps=ps, xo=xo: e.tensor_tensor(
                        X1[:, tile, dc * 512:(dc + 1) * 512], ps[:, :], xo[:, dc * 512:(dc + 1) * 512], ALU.add), reads=[bp, bxo], writes=[bX1[tile]])
        if "x1" in DEBUG and tb == 0:
            K.dbg("x1", X1, bX1)
        NB = 3
        WG = [aview(K, 160 * KB + i * 4 * KB, [16, 128], BF16) for i in range(NB)]; bWG = [Buf() for _ in range(NB)]
        WUp = [aview(K, 172 * KB + i * 4 * KB, [16, 128], BF16) for i in range(NB)]; bWUp = [Buf() for _ in range(NB)]

        def load_gu(f):
            P.dma("pool", WG[f % NB], K.w_gate[f], writes=[bWG[f % NB]])
            P.dma("pool", WUp[f % NB], K.w_up[f], writes=[bWUp[f % NB]])
        for f in range(NB):
            load_gu(f)
        P.barrier()
        H2T = aview(K, 64 * KB, [16, 1024], BF16); bH2T = Buf()
        ACTT = aview(K, 96 * KB, [6, 1024], BF16); bAT = Buf()
        WD = aview(K, 108 * KB, [6, 2048], BF16); bWD = Buf()
        SG = [aview(K, 132 * KB, [512], BF16), aview(K, 133 * KB, [512], BF16)]; bSG = [Buf(), Buf()]
        H2 = [aview(K, 134 * KB, [2048], BF16), aview(K, 138 * KB, [2048], BF16)]; bH2 = [Buf(), Buf()]
        S2 = aview(K, 142 * KB, [8], F32); JNK = aview(K, 143 * KB, [2048], BF16); bS2 = Buf()
        for tile in range(8):
            P.op("act", lambda e, tile=tile: e.activation(JNK, X1[:, tile, :], ACT.Square, accum_out=S2[:, tile:tile + 1]), reads=[bX1[tile]], writes=[bS2])
        P.op("dve", lambda e: e.tensor_scalar(S2, S2, 1.0 / DM, EPS, ALU.mult, ALU.add), reads=[bS2], writes=[bS2])
        P.op("act", lambda e: e.activation(S2, S2, ACT.Sqrt), reads=[bS2], writes=[bS2])
        P.op("dve", lambda e: e.reciprocal(S2, S2), reads=[bS2], writes=[bS2])
        for tile in range(8):
            h2 = H2[tile % 2]; bh2 = bH2[tile % 2]
            P.op("dve", lambda e, tile=tile, h2=h2: e.scalar_tensor_tensor(h2, X1[:, tile, :], S2[:, tile:tile + 1], K.gffn[:, :], ALU.mult, ALU.mult),
                 reads=[bX1[tile], bS2, K.bC], writes=[bh2])
            for hb in range(2):
                pt = K.PSB[hb]; bpt = K.bPS[6 + hb]
                fns = []
                for k8 in range(8):
                    kt = hb * 8 + k8
                    fns.append(lambda e, k8=k8, kt=kt, pt=pt, h2=h2: e.transpose(pt[:, k8 * 128:(k8 + 1) * 128], h2[:, kt * 128:(kt + 1) * 128], K.identb[:]))
                P.op("pe", fns, reads=[bh2, K.bC], writes=[bpt])
                P.op("dve", lambda e, hb=hb, tile=tile, pt=pt: e.tensor_copy(
                    H2T[:, hb * 8:(hb + 1) * 8, tile * 128:(tile + 1) * 128], pt[:, :].rearrange("p (k j) -> p k j", j=128)), reads=[bpt], writes=[bH2T])
        f0 = 0
        npg = 0
        for grp, nf in enumerate(FG):
            for q in range(nf):
                r0 = (f0 + q) * 128
                P.dma("pool", WD[:, q, :], K.w_down[r0:r0 + 128, :], writes=[bWD])
            for q in range(nf):
                f = f0 + q
                wg = WG[f % NB]; bwg = bWG[f % NB]; wu = WUp[f % NB]; bwu = bWUp[f % NB]
                for s in range(2):
                    pg = K.PSF[(npg % 2) * 2]; bpg = K.bPS[(npg % 2) * 2]
                    pu = K.PSF[(npg % 2) * 2 + 1]; bpu = K.bPS[(npg % 2) * 2 + 1]
                    sg = SG[npg % 2]; bsg = bSG[npg % 2]; npg += 1
                    fg = []
                    fu = []
                    for kt in range(16):
                        fg.append(lambda e, kt=kt, s=s, pg=pg, wg=wg: e.matmul(pg[:, :], lhsT=wg[:, kt, :], rhs=H2T[:, kt, s * 512:(s + 1) * 512], start=(kt == 0), stop=(kt == 15)))
                        fu.append(lambda e, kt=kt, s=s, pu=pu, wu=wu: e.matmul(pu[:, :], lhsT=wu[:, kt, :], rhs=H2T[:, kt, s * 512:(s + 1) * 512], start=(kt == 0), stop=(kt == 15)))
                    P.op("pe", fg, reads=[bwg, bH2T], writes=[bpg])
                    P.op("pe", fu, reads=[bwu, bH2T], writes=[bpu])
                    P.op("act", lambda e, pg=pg, sg=sg: e.activation(sg, pg[:, :], ACT.Silu), reads=[bpg], writes=[bsg])
                    P.op("dve", lambda e, pu=pu, sg=sg, q=q, s=s: e.tensor_tensor(ACTT[:, q, s * 512:(s + 1) * 512], pu[:, :], sg, ALU.mult),
                         reads=[bpu, bsg], writes=[bAT])
                if f + NB < NFT:
                    load_gu(f + NB)
            nd = 0
            for tile in range(8):
                for dc in range(4):
                    ps = K.PSF[4 + nd % 2]; bp = K.bPS[4 + nd % 2]; nd += 1
                    fns = []
                    for q in range(nf):
                        fns.append(lambda e, q=q, tile=tile, dc=dc, ps=ps, nf=nf: e.matmul(
                            ps[:, :], lhsT=ACTT[:, q, tile * 128:(tile + 1) * 128], rhs=WD[:, q, dc * 512:(dc + 1) * 512], start=(q == 0), stop=(q == nf - 1)))
                    P.op("pe", fns, reads=[bAT, bWD], writes=[bp])
                    P.op("dve", lambda e, tile=tile, dc=dc, ps=ps: e.tensor_tensor(
                        X1[:, tile, dc * 512:(dc + 1) * 512], X1[:, tile, dc * 512:(dc + 1) * 512], ps[:, :], ALU.add), reads=[bp, bX1[tile]], writes=[bX1[tile]])
            f0 += nf
        for tile in range(8):
            r0 = (tb * 8 + tile) * 128
            P.dma("sp", K.out[r0:r0 + 128, :], X1[:, tile, :], reads=[bX1[tile]], writes=[K.bOut])


def build_program(phases="all"):
    nc = bass.Bass("TRN2", target_bir_lowering=False)
    K = Ctx()
    K.nc = nc
    dram = lambda name, shape, dt=F32, kind="ExternalInput": nc.dram_tensor(name, list(shape), dt, kind=kind).ap()
    K.xT = dram("xT", [DM, SEQ]); K.xown = dram("xown", [2048, DM])
    K.w_in = dram("w_in", [DM, 4096]); K.w_out = dram("w_out", [DM, DM]); K.w_glu = dram("w_glu", [1024, 1024])
    K.w_gate = dram("w_gate", [NFT, 128, 16, 128]); K.w_up = dram("w_up", [NFT, 128, 16, 128]); K.w_down = dram("w_down", [DFF, DM])
    K.p_are = dram("p_are", [128, 2, 32]); K.p_aim = dram("p_aim", [128, 2, 32]); K.p_ls = dram("p_ls", [128, 2, 32])
    K.p_bre = dram("p_bre", [128, 2, 32, 16]); K.p_bim = dram("p_bim", [128, 2, 32, 16])
    K.p_cre = dram("p_cre", [128, 2, 32, 16]); K.p_cim = dram("p_cim", [128, 2, 32, 16])
    K.p_kv = dram("p_kv", [128, 2, 24])
    K.biasd = dram("biasd", [NH, 128, 15, 128])
    cst = dram("cst", [128, CST_COLS])
    K.out = dram("out", [2048, DM], kind="ExternalOutput")
    K.Ud = nc.dram_tensor("Ud", [4, 128, 64, 128], BF16).ap()
    K.Hd = nc.dram_tensor("Hd", [2, 2, 128, 2, 32, 128], BF16).ap()
    K.Hc = nc.dram_tensor("Hc", [2, 2, 128, 2, 32], BF16).ap()
    K.W3d = nc.dram_tensor("W3d", [128, 2, 2, 64, 128], BF16).ap()
    K.Td = nc.dram_tensor("Td", [128, 64, 128], BF16).ap()
    K.mixT = nc.dram_tensor("mixT", [128, 16, 2048], BF16).ap()
    K.bUd = MBuf(); K.bHd = MBuf(); K.bW3d = MBuf(); K.bTd = MBuf(); K.bmix = MBuf(); K.bOut = MBuf()
    K.dbg_out = {}
    for name, shape in DEBUG.items():
        K.dbg_out[name] = dram("dbg_" + name, [128, _prod(shape)], F32, kind="ExternalOutput")
    with ExitStack() as st:
        P = Prog(nc, st)
        K.P = P
        K.ARENA_BYTES = 190 * KB
        K.arena = st.enter_context(nc.sbuf_tensor("arena", [128, K.ARENA_BYTES // 2], BF16))
        CS = st.enter_context(nc.sbuf_tensor("cs", [128, CST_COLS], F32))
        cb16 = st.enter_context(nc.sbuf_tensor("cb16", [128, 3 * 128], BF16))
        K.dbgt = st.enter_context(nc.sbuf_tensor("dbgt", [128, 64], F32))
        K.inv128 = st.enter_context(nc.sbuf_tensor("inv128", [128, 2], F32))
        K.PSF = [st.enter_context(nc.psum_tensor("psf%d" % i, [128, 512], F32)) for i in range(6)]
        K.PSB = [st.enter_context(nc.psum_tensor("psb%d" % i, [128, 1024], BF16)) for i in range(2)]
        K.bPS = [Buf() for _ in range(8)]
        K.bC = Buf()
        P.dma("sp", CS[:], cst, writes=[K.bC])
        c = CST_OFF
        K.ident = CS[:, c["ident"]:c["ident"] + 128]
        K.maskF = CS[:, c["maskF"]:c["maskF"] + 128]
        K.maskB = CS[:, c["maskB"]:c["maskB"] + 128]
        K.gmix = CS[:, c["gmix"]:c["gmix"] + 16]
        K.gout = CS[:, c["gout"]:c["gout"] + 16]
        K.qkg = CS[:, c["qkg"]:c["qkg"] + 2]
        K.m01 = CS[:, c["m01"]:c["m01"] + 2]
        K.dskip = CS[:, c["dskip"]:c["dskip"] + 64]
        K.bglu = CS[:, c["bglu"]:c["bglu"] + 1024]
        K.gffn = CS[:, c["gffn"]:c["gffn"] + 2048]
        K.identb = cb16[:, 0:128]; K.onesb = cb16[:, 128:256]; K.blk1 = cb16[:, 256:384]
        P.op("dve", lambda e: e.tensor_copy(K.identb, K.ident), reads=[K.bC], writes=[K.bC])
        P.op("dve", lambda e: e.memset(K.onesb, 1.0), writes=[K.bC])
        P.op("dve", lambda e: e.memset(K.inv128[:, :], 1.0 / 128), writes=[K.bC])
        P.op("dve", lambda e: e.tensor_copy(K.blk1, CS[:, c["blk1"]:c["blk1"] + 128]), reads=[K.bC], writes=[K.bC])
        P.op("dve", lambda e: e.tensor_scalar(K.qkg[:, 0:1], K.qkg[:, 0:1], 0.125, None, ALU.mult), reads=[K.bC], writes=[K.bC])
        ndbg = [0]

        def dbg(name, ap, bufs):
            shape = DEBUG[name]
            n = _prod(shape)
            dst = K.dbg_out[name]
            flat = ap
            nd = len(shape)
            if nd == 2:
                flat = ap.rearrange("p a b -> p (a b)")
            elif nd == 3:
                flat = ap.rearrange("p a b c -> p (a b c)")
            elif nd == 4:
                flat = ap.rearrange("p a b c d -> p (a b c d)")
            bd = Buf()
            for c0 in range(0, n, 64):
                w = min(64, n - c0)
                P.op("pool", lambda e, c0=c0, w=w: e.tensor_copy(K.dbgt[:, 0:w], flat[:, c0:c0 + w]), reads=list(bufs) + [bd], writes=[bd])
                P.dma("sp", dst[:, c0:c0 + w], K.dbgt[:, 0:w], reads=[bd], writes=[bd, K.bOut])
        K.dbg = dbg

        if phases in ("all", "ssm", "s1", "ssm_a", "ssm_b"):
            phase_s1(K)
            P.barrier()
        if phases in ("all", "ssm", "ssm_a", "ssm_b"):
            phase_gen1(K)
            P.barrier()
            phase_s2(K)
            P.barrier()
        if phases in ("all", "ssm", "ssm_b"):
            phase_gen2(K)
            P.barrier()
        if phases in ("all", "ssm"):
            phase_s4(K)
            P.barrier()
        if phases in ("all", "attn"):
            phase_a(K)
            P.barrier()
        if phases in ("all", "out"):
            phase_o(K)
        waits = P._deps("sp", [K.bOut], ())
        P.ops["sp"].append((waits, [], None))
        P.emit()
    return nc


CST_OFF = {}
_c = 0
for _n, _w in (("ident", 128), ("maskF", 128), ("maskB", 128), ("blk1", 128), ("gmix", 16), ("gout", 16), ("qkg", 2),
               ("m01", 2), ("dskip", 64), ("bglu", 1024), ("gffn", 2048)):
    CST_OFF[_n] = _c
    _c += _w
CST_COLS = _c


def _lay_gp(a):
    s = a.shape
    a = a.reshape(2, 32, 2, 64, *s[3:])
    a = np.moveaxis(a, [2, 3], [0, 1])
    return np.ascontiguousarray(a.reshape(128, 2, 32, *s[3:]), dtype=np.float32)


def _bias_table(rpb0, flip):
    out = np.full((NH, 15, 128, 128), NEG, np.float32)
    for v in range(15):
        n, i = (20, v) if v < 5 else ((30, v - 5) if v < 10 else (31, v - 10))
        ks = min(max(n - 2, 0), 27)
        kp = ks + i
        kr = np.repeat(np.array([2 * kp, 2 * kp + 1]), 64); kc = np.tile(np.arange(64), 2)
        qr = np.repeat(np.array([2 * n, 2 * n + 1]), 64); qc = np.tile(np.arange(64), 2)
        if flip:
            kr, kc, qr, qc = 63 - kr, 63 - kc, 63 - qr, 63 - qc
        rs = np.clip(qr - 4, 0, 56); cs = np.clip(qc - 8, 0, 48)
        inwin = ((kr[:, None] >= rs[None, :]) & (kr[:, None] < rs[None, :] + 8) &
                 (kc[:, None] >= cs[None, :]) & (kc[:, None] < cs[None, :] + 16))
        dr = np.clip(kr[:, None] - qr[None, :] + 7, 0, 14)
        dc = np.clip(kc[:, None] - qc[None, :], -15, 15) + 15
        vals = rpb0[:, dr, dc]
        out[:, v] = np.where(inwin[None], vals, np.float32(NEG))
    return np.ascontiguousarray(out.transpose(0, 2, 1, 3))


def _consts(inp):
    cs = np.zeros((128, CST_COLS), np.float32)
    o = CST_OFF
    cs[:, o["ident"]:o["ident"] + 128] = np.eye(128, dtype=np.float32)
    s = np.arange(128) // 16
    cs[:, o["maskF"]:o["maskF"] + 128] = (s[:, None] <= s[None, :])
    cs[:, o["maskB"]:o["maskB"] + 128] = (s[:, None] >= s[None, :])
    hh = np.arange(128) // 64
    cs[:, o["blk1"]:o["blk1"] + 128] = (hh[:, None] == hh[None, :])
    cs[:, o["gmix"]:o["gmix"] + 16] = inp["g_mix"][0].reshape(16, 128).T
    gout = np.concatenate([inp["g_out_attn"][0], inp["g_out_ssm"][0]])
    cs[:, o["gout"]:o["gout"] + 16] = gout.reshape(16, 128).T
    cs[:, o["qkg"]] = np.tile(inp["q_gain"][0], 2)
    cs[:, o["qkg"] + 1] = np.tile(inp["k_gain"][0], 2)
    cs[:, o["m01"]] = (hh == 0)
    cs[:, o["m01"] + 1] = (hh == 1)
    cs[:, o["dskip"]:o["dskip"] + 64] = np.tile(inp["ssm_d"][0].reshape(64, 16).T, (8, 1))
    cs[:, o["bglu"]:o["bglu"] + 1024] = inp["b_glu"][0][None, :]
    cs[:, o["gffn"]:o["gffn"] + 2048] = inp["g_ffn"][0][None, :]
    return cs


def _kvals():
    kvA = np.concatenate([np.arange(7, -1, -1), np.arange(1, 9), np.arange(-7, 1)])
    kvB = np.concatenate([np.arange(0, 8), np.arange(8, 0, -1), -np.arange(0, 8)])
    kv = np.stack([kvA, kvB]).astype(np.float32)
    return np.ascontiguousarray(np.broadcast_to(kv[None], (128, 2, 24)))


def prepare_inputs(inp):
    inp = {k: np.asarray(v) for k, v in inp.items()}
    x = inp["x"]
    shared = dict(
        w_in=np.ascontiguousarray(inp["w_in"][0]), w_out=np.ascontiguousarray(inp["w_out"][0]),
        w_glu=np.ascontiguousarray(inp["w_glu"][0]),
        w_gate=np.ascontiguousarray(inp["w_ffn_gate"][0].reshape(16, 128, NFT, 128).transpose(2, 1, 0, 3)),
        w_up=np.ascontiguousarray(inp["w_ffn_up"][0].reshape(16, 128, NFT, 128).transpose(2, 1, 0, 3)), w_down=np.ascontiguousarray(inp["w_ffn_down"][0]),
        cst=_consts(inp), p_kv=_kvals())
    bias_tabs = {f: _bias_table(inp["rpb"][0], f) for f in (False, True)}
    ssm = {}
    for h in (0, 1):
        dirs = [0, 1] if h == 1 else [1, 0]
        ls = np.broadcast_to(inp["ssm_log_step"][0][dirs][:, :, None], (2, 64, 64))
        ssm[h] = dict(
            p_are=_lay_gp(inp["ssm_a_re"][0][dirs]), p_aim=_lay_gp(inp["ssm_a_im"][0][dirs]), p_ls=_lay_gp(ls),
            p_bre=_lay_gp(inp["ssm_b_re"][0][dirs]), p_bim=_lay_gp(inp["ssm_b_im"][0][dirs]),
            p_cre=_lay_gp(inp["ssm_c_re"][0][dirs].transpose(0, 1, 3, 2)), p_cim=_lay_gp(inp["ssm_c_im"][0][dirs].transpose(0, 1, 3, 2)))
    maps = []
    for c in range(8):
        b, h = c // 2, c % 2
        flip = (h == 0)
        xl = x[b][::-1] if flip else x[b]
        m = dict(shared)
        m.update(ssm[h])
        m["xT"] = np.ascontiguousarray(xl.T)
        m["xown"] = np.ascontiguousarray(xl[2048:])
        m["biasd"] = bias_tabs[flip]
        maps.append(m)
    return maps


def assemble(results):
    out = np.empty((4, SEQ, DM), np.float32)
    for c in range(8):
        b, h = c // 2, c % 2
        o = np.asarray(results[c]["out"])
        if h == 0:
            out[b, 0:2048] = o[::-1]
        else:
            out[b, 2048:] = o
    return out


def kernel(**inputs):
    maps = prepare_inputs(inputs)
    nc = build_program("all")
    res = run_bass_kernel_spmd(nc, maps, core_ids=list(range(8)))
    return assemble(res.results)
```
